# Optimizing a Trainium2 kernel written in Bass

```python
import jax, jax.numpy as jnp
from jax import lax
import numpy as np

D_MODEL = 1024
BATCH = 8
SEQ = 2048
DEPTH = 2
DEC_BATCH = 32
DEC_SEQ = 4
PAST_LEN = 8192
PAGE_SIZE = 128

D_MIX = D_MODEL
D_GROUP = D_MIX // 4
HEAD_DIM = 64
N_HEADS = D_GROUP // HEAD_DIM
GLA_DK = HEAD_DIM // 2
GLA_DV = HEAD_DIM
GLA_RANK = 16
GLA_TAU = 16.0
GLA_CHUNK = 64
SGU_CHUNK = 128
FOX_BLOCK = 128
FOX_BF_INIT = 7.0
CONV_WIDTH = 31
FFN_CONV_WIDTH = 3
D_FF = 2688
ALPHA = (2 * DEPTH) ** 0.25
BETA = (8 * DEPTH) ** -0.25
EPS = 1e-5
F32 = jnp.float32
IN_SPLIT = (N_HEADS * GLA_DK, N_HEADS * GLA_DK, N_HEADS * GLA_DV, N_HEADS * GLA_DV, GLA_RANK,
            D_GROUP, D_GROUP,
            D_GROUP, D_GROUP, D_GROUP, N_HEADS,
            2 * D_GROUP)
D_IN = sum(IN_SPLIT)

kernel_name = 'hybrid_gla_sgu_fox_conformer_decode_step'


def _layer_norm(x, g, b):
    xf = x.astype(F32)
    mu = jnp.mean(xf, -1, keepdims=True)
    var = jnp.mean(jnp.square(xf - mu), -1, keepdims=True)
    return ((xf - mu) * lax.rsqrt(var + EPS) * g + b).astype(x.dtype)


def _group_layer_norm(x, g, b, groups):
    shp = x.shape
    xg = x.reshape(*shp[:-1], groups, shp[-1] // groups)
    return _layer_norm(xg, g.reshape(groups, -1), b.reshape(groups, -1)).reshape(shp)


def _causal_dwconv(x, buf, w, b):
    xx = jnp.concatenate([buf.astype(x.dtype), x], axis=1)
    y = lax.conv_general_dilated(xx, w[:, None, :].astype(xx.dtype), window_strides=(1,), padding='VALID',
                                 dimension_numbers=('NWC', 'WIO', 'NWC'), feature_group_count=x.shape[-1])
    return y + b, xx[:, -(w.shape[0] - 1):]


def _gla(q, k, v, log_a, s0, chunk):
    B, T, H, _ = q.shape
    n = T // chunk

    def rs(t):
        return t.astype(F32).reshape(B, n, chunk, H, -1).transpose(1, 0, 3, 2, 4)

    qc, kc, vc = rs(q), rs(k), rs(v)
    gc = jnp.cumsum(rs(log_a), axis=3)
    causal = jnp.tril(jnp.ones((chunk, chunk), bool))

    def step(S, inp):
        qi, ki, vi, gi = inp
        diff = gi[..., :, None, :] - gi[..., None, :, :]
        decay = jnp.exp(jnp.where(causal[:, :, None], diff, -jnp.inf))
        attn = jnp.einsum('bhtd,bhsd,bhtsd->bhts', qi, ki, decay)
        o = jnp.einsum('bhtd,bhde->bhte', qi * jnp.exp(gi), S) + jnp.einsum('bhts,bhse->bhte', attn, vi)
        g_last = gi[..., -1:, :]
        S = S * jnp.exp(g_last[..., 0, :])[..., None] + jnp.einsum('bhsd,bhse->bhde', ki * jnp.exp(g_last - gi), vi)
        return S, o

    S, o = lax.scan(step, s0.astype(F32), (qc, kc, vc, gc))
    return o.transpose(1, 0, 3, 2, 4).reshape(B, T, H, -1), S


def _sgu(u, v, w_s, b_s, chunk):
    B, T, _ = u.shape
    n = T // chunk
    w = jnp.tril(w_s[:, :chunk, :chunk])
    vc = v.reshape(B, n, chunk, N_HEADS, -1)
    mixed = jnp.einsum('gts,bnsgc->bntgc', w, vc) + b_s[:, :chunk].T[None, None, :, :, None]
    return u * mixed.reshape(B, T, -1)


def _fox_prompt(q, k, v, logf):
    B, T, H, D = q.shape
    dcum = jnp.cumsum(logf, axis=1).transpose(0, 2, 1)
    key_pos = jnp.arange(T)
    scale = D ** -0.5

    def block(i):
        start = i * FOX_BLOCK
        q_blk = lax.dynamic_slice_in_dim(q, start, FOX_BLOCK, axis=1)
        d_blk = lax.dynamic_slice_in_dim(dcum, start, FOX_BLOCK, axis=2)
        s = jnp.einsum('bqhd,bkhd->bhqk', q_blk, k, preferred_element_type=F32) * scale
        s = s + d_blk[..., :, None] - dcum[..., None, :]
        q_pos = start + jnp.arange(FOX_BLOCK)
        s = jnp.where(key_pos[None, :] <= q_pos[:, None], s, -jnp.inf)
        p = jax.nn.softmax(s, axis=-1)
        return jnp.einsum('bhqk,bkhd->bqhd', p.astype(v.dtype), v)

    o = lax.map(block, jnp.arange(T // FOX_BLOCK))
    return o.transpose(1, 0, 2, 3, 4).reshape(B, T, H, D)


def _fox_sample(q, k, v, logf, k_past, v_past, logf_past):
    B, S, H, D = q.shape
    P = k_past.shape[1]
    lp = logf_past.astype(F32)
    suffix = lax.cumsum(lp, axis=1, reverse=True) - lp
    d_new = jnp.cumsum(logf, axis=1)
    key_bias = jnp.concatenate([suffix, -d_new], axis=1).transpose(0, 2, 1)
    k_all = jnp.concatenate([k_past.astype(k.dtype), k], axis=1)
    v_all = jnp.concatenate([v_past.astype(v.dtype), v], axis=1)
    s = jnp.einsum('bqhd,bkhd->bhqk', q, k_all, preferred_element_type=F32) * D ** -0.5
    s = s + d_new.transpose(0, 2, 1)[..., :, None] + key_bias[..., None, :]
    key_pos = jnp.arange(P + S)
    q_pos = P + jnp.arange(S)
    s = jnp.where(key_pos[None, :] <= q_pos[:, None], s, -jnp.inf)
    p = jax.nn.softmax(s, axis=-1)
    return jnp.einsum('bhqk,bkhd->bqhd', p.astype(v_all.dtype), v_all)


def _mixer(x, p, gla_s0, conv_buf, fox_past, chunk_a, chunk_b):
    B, T, _ = x.shape
    h = jnp.einsum('btd,de->bte', x, p['w_in'])
    points = [int(c) for c in np.cumsum(IN_SPLIT)[:-1]]
    (a_q, a_k, a_v, a_g, a_lr, b_u, b_v, c_q, c_k, c_v, c_f, d_in) = jnp.split(h, points, axis=-1)
    q = a_q.reshape(B, T, N_HEADS, GLA_DK) * GLA_DK ** -0.5
    k = a_k.reshape(B, T, N_HEADS, GLA_DK)
    v = a_v.reshape(B, T, N_HEADS, GLA_DV)
    log_a = jax.nn.log_sigmoid((jnp.einsum('btr,re->bte', a_lr, p['gla_w_a']) + p['gla_b_a']).astype(F32)) / GLA_TAU
    o_a, gla_state = _gla(q, k, v, log_a.reshape(B, T, N_HEADS, GLA_DK), gla_s0, chunk_a)
    o_a = o_a * lax.rsqrt(jnp.mean(jnp.square(o_a), -1, keepdims=True) + EPS)
    y_a = (o_a.reshape(B, T, -1) * p['gla_norm_g']).astype(x.dtype) * jax.nn.silu(a_g)
    v_ln = _group_layer_norm(b_v, p['sgu_ln_g'], p['sgu_ln_b'], N_HEADS)
    y_b = _sgu(b_u, v_ln, p['sgu_w'], p['sgu_b'], chunk_b)
    fq = c_q.reshape(B, T, N_HEADS, HEAD_DIM)
    fk = c_k.reshape(B, T, N_HEADS, HEAD_DIM)
    fv = c_v.reshape(B, T, N_HEADS, HEAD_DIM)
    logf = jax.nn.log_sigmoid((c_f + p['fox_b_f']).astype(F32))
    if fox_past is None:
        o_c = _fox_prompt(fq, fk, fv, logf)
    else:
        o_c = _fox_sample(fq, fk, fv, logf, *fox_past)
    y_c = o_c.reshape(B, T, -1)
    glu = d_in[..., :D_GROUP] * jax.nn.sigmoid(d_in[..., D_GROUP:])
    conv_out, conv_state = _causal_dwconv(glu, conv_buf, p['conv_w'], p['conv_b'])
    y_d = jax.nn.silu(_group_layer_norm(conv_out, p['conv_norm_g'], p['conv_norm_b'], N_HEADS))
    y = jnp.concatenate([y_a, y_b, y_c.astype(x.dtype), y_d], axis=-1)
    out = jnp.einsum('bte,ed->btd', y, p['w_o'])
    return out, (fk, fv, logf, gla_state, conv_state, v_ln)


def _ffn(x, p, buf):
    h = jnp.einsum('btd,df->btf', x, p['ffn_w_up'])
    gate, val = h[..., :D_FF], h[..., D_FF:]
    gate_c, new_buf = _causal_dwconv(gate, buf, p['ffn_conv_w'], p['ffn_conv_b'])
    return jnp.einsum('btf,fd->btd', jax.nn.silu(gate_c) * val, p['ffn_w_down']), new_buf


def _layer(x, p, gla_s0, conv_buf, ffn_buf, fox_past, chunk_a, chunk_b):
    m, st = _mixer(x, p, gla_s0, conv_buf, fox_past, chunk_a, chunk_b)
    x = _layer_norm(ALPHA * x + m, p['ln1_g'], p['ln1_b'])
    f, ffn_state = _ffn(x, p, ffn_buf)
    x = _layer_norm(ALPHA * x + f, p['ln2_g'], p['ln2_b'])
    return x, st + (ffn_state,)


def _gather_pages(pool, page_table):
    g = pool[page_table]
    return g.reshape(page_table.shape[0], -1, *pool.shape[2:])


def _stack(states, i):
    return jnp.stack([s[i] for s in states])


def setup_inputs(seed: int = 0) -> dict:
    key = jax.random.key(seed)
    ks = iter(jax.random.split(key, 48))

    def nrm(shape, scale):
        return jax.random.normal(next(ks), shape, F32) * scale

    n_pages = PAST_LEN // PAGE_SIZE
    n_used = DEC_BATCH * n_pages
    n_pool = n_used + n_used // 4
    return {
        'x_prompt': nrm((BATCH, SEQ, D_MODEL), 1.0),
        'x_sample': nrm((DEC_BATCH, DEC_SEQ, D_MODEL), 1.0),
        'cache_fox_k': nrm((DEPTH, n_pool, PAGE_SIZE, N_HEADS, HEAD_DIM), 1.0),
        'cache_fox_v': nrm((DEPTH, n_pool, PAGE_SIZE, N_HEADS, HEAD_DIM), 1.0),
        'cache_fox_logf': jax.nn.log_sigmoid(FOX_BF_INIT + nrm((DEPTH, n_pool, PAGE_SIZE, N_HEADS), 0.5)),
        'state_gla': nrm((DEPTH, DEC_BATCH, N_HEADS, GLA_DK, GLA_DV), 1.0),
        'state_conv': nrm((DEPTH, DEC_BATCH, CONV_WIDTH - 1, D_GROUP), 0.5),
        'state_ffn_conv': nrm((DEPTH, DEC_BATCH, FFN_CONV_WIDTH - 1, D_FF), 1.0),
        'page_table': jax.random.permutation(next(ks), n_pool)[:n_used].reshape(DEC_BATCH, n_pages).astype(jnp.int32),
        'w_in': nrm((DEPTH, D_MODEL, D_IN), D_MODEL ** -0.5),
        'gla_w_a': nrm((DEPTH, GLA_RANK, N_HEADS * GLA_DK), GLA_RANK ** -0.5),
        'gla_b_a': nrm((DEPTH, N_HEADS * GLA_DK), 0.1),
        'gla_norm_g': 1.0 + nrm((DEPTH, N_HEADS * GLA_DV), 0.1),
        'sgu_ln_g': 1.0 + nrm((DEPTH, D_GROUP), 0.1),
        'sgu_ln_b': nrm((DEPTH, D_GROUP), 0.01),
        'sgu_w': nrm((DEPTH, N_HEADS, SGU_CHUNK, SGU_CHUNK), SGU_CHUNK ** -0.5),
        'sgu_b': 1.0 + nrm((DEPTH, N_HEADS, SGU_CHUNK), 0.1),
        'fox_b_f': FOX_BF_INIT + nrm((DEPTH, N_HEADS), 0.5),
        'conv_w': nrm((DEPTH, CONV_WIDTH, D_GROUP), CONV_WIDTH ** -0.5),
        'conv_b': nrm((DEPTH, D_GROUP), 0.01),
        'conv_norm_g': 1.0 + nrm((DEPTH, D_GROUP), 0.1),
        'conv_norm_b': nrm((DEPTH, D_GROUP), 0.01),
        'w_o': nrm((DEPTH, D_MIX, D_MODEL), BETA * D_MIX ** -0.5),
        'ln1_g': 1.0 + nrm((DEPTH, D_MODEL), 0.1),
        'ln1_b': nrm((DEPTH, D_MODEL), 0.01),
        'ffn_w_up': nrm((DEPTH, D_MODEL, 2 * D_FF), D_MODEL ** -0.5),
        'ffn_conv_w': nrm((DEPTH, FFN_CONV_WIDTH, D_FF), FFN_CONV_WIDTH ** -0.5),
        'ffn_conv_b': nrm((DEPTH, D_FF), 0.01),
        'ffn_w_down': nrm((DEPTH, D_FF, D_MODEL), BETA * D_FF ** -0.5),
        'ln2_g': 1.0 + nrm((DEPTH, D_MODEL), 0.1),
        'ln2_b': nrm((DEPTH, D_MODEL), 0.01),
    }


def reference(x_prompt, x_sample, cache_fox_k, cache_fox_v, cache_fox_logf, state_gla, state_conv,
              state_ffn_conv, page_table, w_in, gla_w_a, gla_b_a, gla_norm_g, sgu_ln_g, sgu_ln_b, sgu_w,
              sgu_b, fox_b_f, conv_w, conv_b, conv_norm_g, conv_norm_b, w_o, ln1_g, ln1_b, ffn_w_up,
              ffn_conv_w, ffn_conv_b, ffn_w_down, ln2_g, ln2_b):
    B = x_prompt.shape[0]
    S = x_sample.shape[1]
    xp, xs = x_prompt, x_sample
    st_p, st_s = [], []
    for l in range(DEPTH):
        p = dict(w_in=w_in[l], gla_w_a=gla_w_a[l], gla_b_a=gla_b_a[l], gla_norm_g=gla_norm_g[l],
                 sgu_ln_g=sgu_ln_g[l], sgu_ln_b=sgu_ln_b[l], sgu_w=sgu_w[l], sgu_b=sgu_b[l],
                 fox_b_f=fox_b_f[l], conv_w=conv_w[l], conv_b=conv_b[l], conv_norm_g=conv_norm_g[l],
                 conv_norm_b=conv_norm_b[l], w_o=w_o[l], ln1_g=ln1_g[l], ln1_b=ln1_b[l],
                 ffn_w_up=ffn_w_up[l], ffn_conv_w=ffn_conv_w[l], ffn_conv_b=ffn_conv_b[l],
                 ffn_w_down=ffn_w_down[l], ln2_g=ln2_g[l], ln2_b=ln2_b[l])
        xp, sp = _layer(xp, p,
                        jnp.zeros((B, N_HEADS, GLA_DK, GLA_DV), F32),
                        jnp.zeros((B, CONV_WIDTH - 1, D_GROUP), xp.dtype),
                        jnp.zeros((B, FFN_CONV_WIDTH - 1, D_FF), xp.dtype),
                        None, GLA_CHUNK, SGU_CHUNK)
        fox_past = (_gather_pages(cache_fox_k[l], page_table),
                    _gather_pages(cache_fox_v[l], page_table),
                    _gather_pages(cache_fox_logf[l], page_table))
        xs, ss = _layer(xs, p, state_gla[l], state_conv[l], state_ffn_conv[l], fox_past, S, S)
        st_p.append(sp)
        st_s.append(ss)
    p_fox_k, p_fox_v, p_fox_logf = _stack(st_p, 0), _stack(st_p, 1), _stack(st_p, 2)
    p_gla, p_conv, p_ffn_conv = _stack(st_p, 3), _stack(st_p, 4), _stack(st_p, 6)
    s_fox_k, s_fox_v, s_fox_logf = _stack(st_s, 0), _stack(st_s, 1), _stack(st_s, 2)
    s_gla, s_conv, s_sgu_v, s_ffn_conv = _stack(st_s, 3), _stack(st_s, 4), _stack(st_s, 5), _stack(st_s, 6)
    return (xp, xs, p_fox_k, p_fox_v, p_fox_logf, p_gla, p_conv, p_ffn_conv,
            s_fox_k, s_fox_v, s_fox_logf, s_gla, s_conv, s_ffn_conv, s_sgu_v)
```

```python
import numpy as np
import os
GSTOP = int(os.environ.get('GSTOP', '99'))
SKIP = set(os.environ.get('SKIP', '').split(','))
from contextlib import ExitStack
import concourse.bass as bass
import concourse.mybir as mybir
from concourse.bass_utils import run_bass_kernel_spmd

F32 = mybir.dt.float32
BF16 = mybir.dt.bfloat16
I32 = mybir.dt.int32
AF = mybir.ActivationFunctionType
ALU = mybir.AluOpType
AX = mybir.AxisListType

D = 1024
SEQ = 2048
NT = 16
DEPTH = 2
DIN = 2580
DFF = 2688
NFC = 21
EPS = 1e-5
ALPHA = (2 * DEPTH) ** 0.25
NS = 16

O_AQ, O_AK, O_AV, O_AG, O_ALR, O_BU, O_BV, O_CQ, O_CK, O_CV, O_CF, O_DIN = (
    0, 128, 256, 512, 768, 784, 1040, 1296, 1552, 1808, 2064, 2068)


class Res:
    __slots__ = ("name", "lw", "rd", "excl")

    def __init__(self, name="", excl=False):
        self.name = name
        self.lw = None
        self.rd = []
        self.excl = excl


class Op:
    __slots__ = ("eng", "fn", "raw", "oth", "pos", "dma", "key", "inc", "val", "users")


class Prog:
    ENGS = ("pe", "act", "dve", "pool", "sp")

    def __init__(self, nc):
        self.nc = nc
        self.by = {e: [] for e in self.ENGS}
        self.dcount = {}
        self.all_dma = []

    def op(self, eng, fn, r=(), w=(), dma=False, key=None):
        o = Op()
        o.eng, o.fn, o.dma, o.key = eng, fn, dma, key
        o.raw, o.oth = set(), set()
        o.inc, o.val, o.users = False, 0, 0
        o.pos = len(self.by[eng])
        for x in r:
            if x.lw is not None:
                o.raw.add(x.lw)
            if x.excl:
                for q in x.rd:
                    if q.eng != eng:
                        o.oth.add(q)
        for x in w:
            if x.lw is not None:
                if not (dma and x.lw.dma and x.lw.key is key and x.lw.eng == eng):
                    o.oth.add(x.lw)
            for q in x.rd:
                o.oth.add(q)
        for x in r:
            x.rd.append(o)
        for x in w:
            x.lw = o
            x.rd = []
        if dma:
            assert key is not None
            c = self.dcount.get(key, 0) + 16
            self.dcount[key] = c
            o.val = c
            self.all_dma.append(o)
        self.by[eng].append(o)
        return o

    def _needs(self, p, q):
        if p.dma:
            return True
        if p.eng != q.eng:
            return True
        if p.eng == "pe":
            return False
        return (p in q.raw) and (q.pos - p.pos <= 3)

    def emit(self, stack):
        nc = self.nc
        fin = self.op("sp", None)
        for d in self.all_dma:
            fin.oth.add(d)
        for e in self.ENGS:
            for q in self.by[e]:
                for p in (q.raw | q.oth):
                    if p is q:
                        continue
                    if self._needs(p, q):
                        p.inc = True
        esem = {}
        for e in self.ENGS:
            esem[e] = stack.enter_context(nc.semaphore("e_" + e))
            c = 0
            for o in self.by[e]:
                if o.inc and not o.dma:
                    c += 1
                    o.val = c
        dsem = {}
        for k in self.dcount:
            dsem[k] = stack.enter_context(nc.semaphore("d%d" % len(dsem)))
        print("sems used", len(dsem) + 5, "ops", {e: len(self.by[e]) for e in self.ENGS})

        def run(e, eng):
            waited = {}
            for q in self.by[e]:
                need = {}
                for p in (q.raw | q.oth):
                    if p is q or not self._needs(p, q):
                        continue
                    s = dsem[p.key] if p.dma else esem[p.eng]
                    if need.get(s, (0,))[0] < p.val:
                        need[s] = (p.val, s)
                for s, (v, _) in need.items():
                    if waited.get(s, 0) < v:
                        eng.wait_ge(s, v)
                        waited[s] = v
                if q.fn is None:
                    continue
                ins = q.fn(eng)
                if q.dma:
                    ins.then_inc(dsem[q.key], 16)
                elif q.inc:
                    ins.then_inc(esem[e], 1)

        with nc.Block() as block:
            @block.tensor
            def _(eng):
                run("pe", eng)

            @block.scalar
            def _(eng):
                run("act", eng)

            @block.vector
            def _(eng):
                run("dve", eng)

            @block.gpsimd
            def _(eng):
                run("pool", eng)

            @block.sync
            def _(eng):
                run("sp", eng)


class B:
    def __init__(self, nc, stack):
        self.nc, self.stack = nc, stack
        self.P = Prog(nc)
        self.nps = 0
        self.psb = []
        for i in range(8):
            t = stack.enter_context(nc.psum_tensor("ps%d" % i, [128, 512], F32))
            self.psb.append((t, Res("ps%d" % i, excl=True)))
        self.cnt = 0

    def sb(self, shape, dt, name=None):
        self.cnt += 1
        t = self.stack.enter_context(self.nc.sbuf_tensor(name or ("t%d" % self.cnt), list(shape), dt))
        return t

    def ps(self):
        t, r = self.psb[self.nps % 6]
        self.nps += 1
        return t, r


NPOOL = 2560


def build(has_cache=True, dbg=False, nlayers=DEPTH):
    nc = bass.Bass("TRN2", target_bir_lowering=False)
    stack = ExitStack()
    b = B(nc, stack)
    P = b.P

    def din(name, shape, dt=F32):
        return nc.dram_tensor(name, list(shape), dt, kind="ExternalInput").ap()

    def dout(name, shape, dt=F32):
        return nc.dram_tensor(name, list(shape), dt, kind="ExternalOutput").ap()

    def mm(out, lhsT, rhs, st, sp, r, w, **kw):
        P.op("pe", lambda e: e.matmul(out, lhsT, rhs, start=st, stop=sp, **kw), r=r, w=w)

    def tr(out, in_, ident, r, w):
        P.op("pe", lambda e: e.transpose(out=out, in_=in_, identity=ident), r=r, w=w)

    def act(out, in_, func, r, w, **kw):
        P.op("act", lambda e: e.activation(out=out, in_=in_, func=func, **kw), r=r, w=w)

    def tt(eng, out, in0, in1, op, r, w):
        P.op(eng, lambda e: e.tensor_tensor(out=out, in0=in0, in1=in1, op=op), r=r, w=w)

    def ts(eng, out, in0, s1, s2, op0, op1, r, w):
        if op1 == ALU.pow:
            act(out, in0, AF.Ln, r, w, bias=float(s1), scale=1.0)
            act(out, out, AF.Exp, list(r) + list(w), w, scale=float(s2))
            return
        if op0 == ALU.pow:
            act(out, in0, AF.Ln, r, w)
            act(out, out, AF.Exp, list(r) + list(w), w, scale=float(s1))
            return
        if s2 is None:
            P.op(eng, lambda e: e.tensor_scalar(out=out, in0=in0, scalar1=s1, scalar2=None, op0=op0), r=r, w=w)
        else:
            P.op(eng, lambda e: e.tensor_scalar(out=out, in0=in0, scalar1=s1, scalar2=s2, op0=op0, op1=op1), r=r, w=w)

    def stt(eng, out, in0, sc, in1, op0, op1, r, w):
        P.op("dve", lambda e: e.scalar_tensor_tensor(out=out, in0=in0, scalar=sc, in1=in1, op0=op0, op1=op1), r=r, w=w)

    def cp(eng, out, in_, r, w):
        if eng == "act":
            P.op(eng, lambda e: e.copy(out=out, in_=in_), r=r, w=w)
        else:
            P.op(eng, lambda e: e.tensor_copy(out=out, in_=in_), r=r, w=w)

    def rsum(eng, out, in_, r, w):
        P.op(eng, lambda e: e.reduce_sum(out=out, in_=in_, axis=AX.X), r=r, w=w)

    def mset(eng, ap, v, w):
        P.op(eng, lambda e: e.memset(ap, v), w=w)

    def asel(out, in_, pattern, cmp_, fill, base, cm, r, w):
        P.op("pool", lambda e: e.affine_select(out=out, in_=in_, pattern=pattern, compare_op=cmp_, fill=fill,
                                               base=base, channel_multiplier=cm), r=r, w=w)

    def dma(eng, out, in_, r, w, key, **kw):
        P.op(eng, lambda e: e.dma_start(out=out, in_=in_, **kw), r=r, w=w, dma=True, key=key)

    MUL, ADD, SUB, POW = ALU.mult, ALU.add, ALU.subtract, ALU.pow

    xp = din("xp", [SEQ, D])
    w_in = din("w_in", [DEPTH, D, DIN])
    w_o = din("w_o", [DEPTH, D, D])
    w_up = din("w_up", [DEPTH, D, 2 * DFF])
    w_dn = din("w_dn", [DEPTH, DFF, D])
    gla_w_a = din("gla_w_a", [DEPTH, 16, 128])
    gla_b_a = din("gla_b_a", [DEPTH, 128])
    gla_g = din("gla_norm_g", [DEPTH, 256])
    sgu_g = din("sgu_ln_g", [DEPTH, 256])
    sgu_bb = din("sgu_ln_b", [DEPTH, 256])
    sgu_w = din("sgu_w", [DEPTH, 4, 128, 128])
    sgu_bs = din("sgu_b", [DEPTH, 4, 128])
    fox_bf = din("fox_b_f", [DEPTH, 4])
    conv_w = din("conv_w", [DEPTH, 31, 256])
    conv_b = din("conv_b", [DEPTH, 256])
    cn_g = din("conv_norm_g", [DEPTH, 256])
    cn_b = din("conv_norm_b", [DEPTH, 256])
    ln1_g = din("ln1_g", [DEPTH, D])
    ln1_b = din("ln1_b", [DEPTH, D])
    ln2_g = din("ln2_g", [DEPTH, D])
    ln2_b = din("ln2_b", [DEPTH, D])
    fcw = din("ffn_conv_w", [DEPTH, 3, DFF])
    fcb = din("ffn_conv_b", [DEPTH, DFF])

    o_y = dout("o_y", [SEQ, D])
    o_fk = dout("o_fk", [DEPTH, SEQ, 256])
    o_fv = dout("o_fv", [DEPTH, SEQ, 256])
    o_flf = dout("o_flf", [DEPTH, SEQ, 4])
    o_gla = dout("o_gla", [DEPTH, 4, 32, 64])
    o_conv = dout("o_conv", [DEPTH, 30, 256])
    o_ffc = dout("o_ffc", [DEPTH, 2, DFF])
    if has_cache:
        pt_d = din("pt", [4, 64], I32)
        ck = din("ck", [DEPTH, NPOOL, 128, 256])
        cv = din("cv", [DEPTH, NPOOL, 128, 256])
        clf = din("clf", [DEPTH, NPOOL, 128, 4])
    xs_d = din("xs", [NS, D])
    st_gla = din("st_gla", [DEPTH, 4, 4, 32, 64])
    st_conv = din("st_conv", [DEPTH, 4, 30, 256])
    st_ffc = din("st_ffc", [DEPTH, 4, 2, DFF])
    o_ys = dout("o_ys", [NS, D])
    o_sfk = dout("o_sfk", [DEPTH, NS, 256])
    o_sfv = dout("o_sfv", [DEPTH, NS, 256])
    o_sflf = dout("o_sflf", [DEPTH, NS, 4])
    o_sgla = dout("o_sgla", [DEPTH, 4, 4, 32, 64])
    o_sconv = dout("o_sconv", [DEPTH, 4, 30, 256])
    o_sffc = dout("o_sffc", [DEPTH, 4, 2, DFF])
    o_ssgu = dout("o_ssgu", [DEPTH, NS, 256])
    if dbg:
        d_y = dout("d_y", [SEQ, 768])
        d_yd = dout("d_yd", [256, SEQ])
        d_x1 = dout("d_x1", [SEQ, D])
        d_h = dout("d_h", [DFF, 512])
        d_x2 = dout("d_x2", [SEQ, D])

    rc = Res("const")
    ident_f = b.sb([128, 128], F32, "ident_f")
    ident_bf = b.sb([128, 128], BF16, "ident_bf")
    revM = b.sb([128, 128], F32, "revM")
    triO = b.sb([128, 128], F32, "triO")
    ones_f = b.sb([128, 128], F32, "ones_f")
    ones_bf = b.sb([1, 128], BF16, "ones_bf")
    mask_bf = b.sb([128, 128], BF16, "mask_bf")
    mask4 = b.sb([128, 512], BF16, "mask4")
    blk64 = b.sb([128, 128], F32, "blk64")
    blkmask = b.sb([128, 256], F32, "blkmask")
    hm = b.sb([128, 4], F32, "hm")

    mset("pool", ident_f[:], 0.0, [rc])
    asel(ident_f[:], ident_f[:], [[-1, 128]], ALU.not_equal, 1.0, 0, 1, [rc], [rc])
    cp("dve", ident_bf[:], ident_f[:], [rc], [rc])
    mset("pool", revM[:], -1.0 / 16.0, [rc])
    asel(revM[:], revM[:], [[-1, 128]], ALU.is_gt, 0.0, 0, 1, [rc], [rc])
    mset("pool", triO[:], 1.0, [rc])
    asel(triO[:], triO[:], [[1, 128]], ALU.is_ge, 0.0, 0, -1, [rc], [rc])
    mset("pool", ones_f[:], 1.0, [rc])
    mset("pool", ones_bf[:], 1.0, [rc])
    cp("dve", mask_bf[:], triO[:], [rc], [rc])
    triMb = b.sb([128, 128], BF16, "triMb")
    revMb = b.sb([128, 128], BF16, "revMb")
    ts("dve", triMb[:], triO[:], -1.0 / 16.0, None, MUL, None, [rc], [rc])
    cp("dve", revMb[:], revM[:], [rc], [rc])
    for h in range(4):
        cp("dve", mask4[:, h * 128:(h + 1) * 128], triO[:], [rc], [rc])
    mset("pool", blk64[:], 0.0, [rc])
    mset("pool", blk64[0:64, 0:64], 1.0 / 64.0, [rc])
    mset("pool", blk64[64:128, 64:128], 1.0 / 64.0, [rc])
    mset("pool", blkmask[:], 1.0, [rc])
    mset("pool", hm[:], 1.0, [rc])
    for h in range(4):
        v = blkmask[:, 64 * h:64 * h + 64]
        asel(v, v, [[0, 64]], ALU.is_ge, 0.0, -32 * h, 1, [rc], [rc])
        asel(v, v, [[0, 64]], ALU.is_ge, 0.0, 32 * h + 31, -1, [rc], [rc])
        v = hm[:, h:h + 1]
        asel(v, v, [[0, 1]], ALU.is_ge, 0.0, -32 * h, 1, [rc], [rc])
        asel(v, v, [[0, 1]], ALU.is_ge, 0.0, 32 * h + 31, -1, [rc], [rc])

    R = b.sb([128, NT, D], BF16, "R")
    r_R = [Res("R%d" % t) for t in range(NT)]
    Wi = b.sb([128, 8, DIN], BF16, "Wi")
    Wo = b.sb([128, 8, D], BF16, "Wo")
    rW = Res("W")
    Wa = b.sb([16, 128], BF16, "Wa")
    ba = b.sb([1, 128], BF16, "ba")
    bfb = b.sb([1, 4], BF16, "bfb")
    glag = b.sb([128, 256], F32, "glag")
    sgg = b.sb([128, 256], F32, "sgg")
    sgb = b.sb([128, 256], F32, "sgb")
    l1g = b.sb([128, D], F32, "l1g")
    l1b = b.sb([128, D], F32, "l1b")
    l2g, l2b = l1g, l1b
    dummy = b.sb([128, 2], F32, "dummy_t")
    r_dummy = Res("dummy")
    cw = b.sb([128, 2, 31], F32, "cw")
    cb = b.sb([128, 2], F32, "cb")
    cng = b.sb([128, 2], F32, "cng")
    cnb = b.sb([128, 2], F32, "cnb")
    fw = b.sb([128, NFC, 3], F32, "fw")
    fb = b.sb([128, NFC], F32, "fb")
    bs4 = b.sb([4, 128], BF16, "bs4")
    bsT = b.sb([128, 4], F32, "bsT")
    WT = b.sb([128, 4, 128], BF16, "WT")

    xT = b.sb([128, 8, 512], BF16, "xT")
    r_xT = Res("xT")
    xTf = xT[:].rearrange("p k n -> p (k n)").bitcast(F32)
    sw = xTf[:, 0:512].rearrange("p (g s) -> p g s", g=4)
    FK = b.sb([128, 2, SEQ], BF16, "FK")
    r_FK = [Res("FK%d" % i) for i in range(4)]
    VAflat = b.sb([128, NT * 4 * 65], BF16, "VA")
    VA = VAflat[:].rearrange("p (t h e) -> p t h e", t=NT, h=4)
    r_VA = [Res("VA%d" % t) for t in range(NT)]
    fq = b.sb([128, 2, 512], BF16, "fq")
    r_fq = Res("fq")
    qTf = b.sb([128, 512], F32, "qTf")
    kTf = b.sb([128, 512], F32, "kTf")
    r_qk = Res("qk")
    alr = b.sb([16, 512], BF16, "alr")
    r_alr = Res("alr")
    gext = [b.sb([128, 2, 542], BF16, "gext%d" % i) for i in range(2)]
    r_gext = [Res("gext%d" % i) for i in range(2)]
    glast = b.sb([128, 2, 30], F32, "glast")
    r_glast = Res("glast")
    ydT = b.sb([128, 2, 512], BF16, "ydT")
    r_ydT = Res("ydT")
    Sf = b.sb([128, 256], F32, "Sf")
    Sb = b.sb([128, 256], BF16, "Sb")
    r_S = Res("S")
    Pacc = b.sb([128, 4], F32, "Pacc")
    r_Pacc = Res("Pacc")

    nbuf = {}

    def tmp(name, shape, dt, n=1):
        if name not in nbuf:
            nbuf[name] = [[(b.sb(shape, dt, "%s_%d" % (name, i)), Res(name)) for i in range(n)], 0]
        lst = nbuf[name]
        t, r = lst[0][lst[1] % n]
        lst[1] += 1
        return t, r


    Rs = b.sb([NS, D], BF16, "Rs")
    r_Rs = Res("Rs")
    xTs = b.sb([128, 8, NS], BF16, "xTs")
    r_xTs = Res("xTs")
    sb16 = b.sb([NS, NS], F32, "sb16")
    maskS = b.sb([NS, NS], F32, "maskS")
    maskS4 = b.sb([NS, 64], F32, "maskS4")
    maskN = b.sb([NS, 64], F32, "maskN")
    triSb = b.sb([NS, NS], BF16, "triSb")
    revSb = b.sb([NS, NS], BF16, "revSb")
    bm = b.sb([NS, 4], F32, "bm")
    mset("pool", sb16[:], 1.0, [rc])
    mset("pool", bm[:], 1.0, [rc])
    for bb in range(4):
        v = sb16[:, 4 * bb:4 * bb + 4]
        asel(v, v, [[0, 4]], ALU.is_ge, 0.0, -4 * bb, 1, [rc], [rc])
        asel(v, v, [[0, 4]], ALU.is_ge, 0.0, 4 * bb + 3, -1, [rc], [rc])
        v = bm[:, bb:bb + 1]
        asel(v, v, [[0, 1]], ALU.is_ge, 0.0, -4 * bb, 1, [rc], [rc])
        asel(v, v, [[0, 1]], ALU.is_ge, 0.0, 4 * bb + 3, -1, [rc], [rc])
    tt("dve", maskS[:], sb16[:], triO[0:NS, 0:NS], MUL, [rc], [rc])
    ts("dve", triSb[:], maskS[:], -1.0 / 16.0, None, MUL, None, [rc], [rc])
    tt("dve", sb16[:], sb16[:], maskS[:], SUB, [rc], [rc])
    ts("dve", revSb[:], sb16[:], -1.0 / 16.0, None, MUL, None, [rc], [rc])
    for h in range(4):
        cp("dve", maskS4[:, h * 16:(h + 1) * 16], maskS[:], [rc], [rc])
        cp("dve", maskN[:].rearrange("p (b h q) -> p b h q", b=4, h=4)[:, :, h, :],
           maskS[:].rearrange("p (b q) -> p b q", b=4), [rc], [rc])
    dma("pool", Rs[:], xs_d[:, :], [], [r_Rs], r_Rs)
    if has_cache:
        idxr = b.sb([128, 256], I32, "idxr")
        idx64 = b.sb([128, 4], I32, "idx64")
        r_idx = Res("idx")
        ia = xTf[:, 0:256].bitcast(I32)
        io = xTf[:, 256:512].bitcast(I32)
        for bb in range(4):
            dma("sp", ia[:, bb * 64:(bb + 1) * 64], pt_d[bb:bb + 1, :].broadcast_to([128, 64]), [], [r_xT], r_xT)
        P.op("pool", lambda e: e.iota(io, pattern=[[0, 256]], base=0, channel_multiplier=1), w=[r_xT])
        stt("dve", idxr[:], ia, 128, io, MUL, ADD, [r_xT], [r_idx])
        mset("pool", idx64[:], 0, [r_idx])
        dma("sp", idx64[0:64, :], pt_d.rearrange("b j -> j b"), [], [r_idx], r_idx, allow_slow_non_contiguous=True)
        ckf = ck.rearrange("l n r c -> (l n r) c")
        cvf = cv.rearrange("l n r c -> (l n r) c")
        clff = clf.rearrange("l n r h -> (l n) (r h)")
    fpc = [0]

    ALLR = []

    def switch():
        P.op("pool", lambda e: e.memset(dummy[:], 0.0), w=[rW, r_dummy, r_xT, r_qk, r_fq, r_ydT, r_alr] + r_gext + r_FK + r_VA + ALLR)

    class Arena:
        def __init__(self, aps):
            self.aps = aps
            self.i, self.pos = 0, 0

        def take(self, parts, cols, dt, name):
            w = cols if dt == F32 else (cols + 1) // 2
            while self.pos + w > self.aps[self.i][1]:
                self.i += 1
                self.pos = 0
            a = self.aps[self.i][0][0:parts, self.pos:self.pos + w]
            self.pos += w
            r = Res(name)
            ALLR.append(r)
            if dt != F32:
                a = a.bitcast(BF16)[:, 0:cols]
            return a, r

    FKf = FK[:].rearrange("p c n -> p (c n)").bitcast(F32)
    VAf = VAflat[:].bitcast(F32)
    qTff, kTff = qTf[:], kTf[:]
    fqf = fq[:].rearrange("p c n -> p (c n)").bitcast(F32)
    ydf = ydT[:].rearrange("p c n -> p (c n)").bitcast(F32)
    g0f = gext[0][:].rearrange("p c n -> p (c n)").bitcast(F32)
    g1f = gext[1][:].rearrange("p c n -> p (c n)").bitcast(F32)

    GP = 4
    fpd = {}
    if has_cache:
        fpd["Kg"] = [(b.sb([128, GP * 256], BF16, "Kg%d" % i), Res("Kg%d" % i)) for i in range(2)]
        fpd["Vg"] = [(b.sb([128, GP * 258], BF16, "Vg%d" % i), Res("Vg%d" % i)) for i in range(2)]
        fpd["KT"] = [(b.sb([128, 2 * GP * 128], BF16, "KTp%d" % i), Res("KTp%d" % i)) for i in range(2)]
        fpd["t"] = [(b.sb([128, GP * 16], F32, "fp_t%d" % i), Res("fp_t%d" % i)) for i in range(2)]
        fpd["pTb"] = [(b.sb([128, GP * 16], BF16, "fp_pTb%d" % i), Res("fp_pTb%d" % i)) for i in range(2)]
        fpd["tot"] = (b.sb([64, 4], F32, "fp_tot"), Res("fp_tot"))
        fpd["opast"] = (b.sb([NS, 4 * 257], F32, "opast"), Res("opast"))
        fpd["hsq"] = (b.sb([NS, 256], F32, "hsq"), Res("hsq"))
        fpd["fqs"] = (b.sb([128, 32], BF16, "fqs_p"), Res("fqs_p"))
        fpd["qblk"] = (b.sb([128, 64], BF16, "qblk_p"), Res("qblk_p"))
        for i in range(2):
            v, r = fpd["Vg"][i]
            mset("pool", v[:].rearrange("p (g c) -> p g c", g=GP)[:, :, 256:258], 1.0, [r])

    def sample_q_prework(l):
        hsq, r_hsq = fpd["hsq"]
        fqs, r_fqs = fpd["fqs"]
        qblk, r_qblk = fpd["qblk"]
        pt, rp = b.ps()
        ptb = pt[:].bitcast(BF16)
        for kc in range(8):
            tr(ptb[:, kc * NS:(kc + 1) * NS], Rs[:, kc * 128:(kc + 1) * 128], ident_bf[0:NS, 0:NS], [r_Rs, rc], [rp])
        cp("act", xTs[:], ptb[:, 0:8 * NS].rearrange("p (k n) -> p k n", k=8), [rp], [r_xTs])
        pq, rpq = b.ps()
        for kc in range(8):
            mm(pq[0:NS, 0:256], xTs[:, kc, :], Wi[:, kc, O_CQ:O_CQ + 256], kc == 0, kc == 7, [r_xTs, rW], [rpq])
        cp("act", hsq[:], pq[0:NS, 0:256], [rpq], [r_hsq])
        pt2, rp2 = b.ps()
        for c in range(2):
            tr(pt2[:, c * 16:(c + 1) * 16], hsq[:, c * 128:(c + 1) * 128], ident_f[0:NS, 0:NS], [r_hsq, rc], [rp2])
        cp("act", fqs[:], pt2[:, 0:32], [rp2], [r_fqs])
        mset("pool", qblk[:], 0.0, [r_qblk])
        qb4 = qblk[:].rearrange("p (c b m) -> p c b m", c=2, b=4)
        for h2 in range(2):
            cp("dve", qb4[64 * h2:64 * h2 + 64, :, :, 4 * h2:4 * h2 + 4],
               fqs[64 * h2:64 * h2 + 64, 0:32].rearrange("p (c b q) -> p c b q", c=2, b=4), [r_fqs], [r_qblk])

    def past_setup(l, bb):
        d = fpd
        sig_t, r_lfp = tmp("sig", [128, 512], F32)
        lfp = sig_t[0:64, :]
        msq_t, r_lfT = tmp("cmsq", [128, 512], F32)
        lfT = msq_t[:, 0:256]
        var_t, r_rhsB = tmp("cvar", [128, 512], F32)
        rhsB = var_t[0:64, 0:256]
        csq_t, r_eS = tmp("csq", [128, 512], F32)
        eS = csq_t[:, 0:256]
        tot, r_tot = d["tot"]
        P.op("pool", lambda e: e.indirect_dma_start(
            out=lfp, out_offset=None, in_=clff,
            in_offset=bass.IndirectOffsetOnAxis(ap=idx64[0:64, bb:bb + 1], axis=0), element_offset=l * NPOOL * 512),
            r=[r_idx], w=[r_lfp], dma=True, key=r_lfp)
        pt, rp = b.ps()
        for h in range(4):
            tr(pt[:, h * 64:(h + 1) * 64], lfp.rearrange("j (r h) -> j h r", h=4)[:, h, :], ident_f[0:64, 0:64],
               [r_lfp, rc], [rp])
        cp("act", lfT, pt[:, 0:256], [rp], [r_lfT])
        rsum("dve", tot[:], lfp.rearrange("j (r h) -> j h r", h=4), [r_lfp], [r_tot])
        for h in range(4):
            ts("dve", rhsB[:, h * 64:(h + 1) * 64], revM[0:64, 0:64], tot[:, h:h + 1], None, MUL, None, [rc, r_tot], [r_rhsB])
        pS, rpS = b.ps()
        mm(pS[:, 0:256], revM[:], lfT, True, False, [rc, r_lfT], [rpS])
        mm(pS[:, 0:256], ones_f[0:64, :], rhsB, False, True, [rc, r_rhsB], [rpS])
        act(eS, pS[:, 0:256], AF.Exp, [rpS], [r_eS], scale=-16.0)
        return eS.rearrange("s (h j) -> s h j", h=4), r_eS

    NG = 64 // GP

    def past_group(l, bb, g, eS3, r_eS):
        d = fpd
        pov, rpov = b.psb[6]
        qblk, r_qblk = d["qblk"]
        qb4 = qblk[:].rearrange("p (c b m) -> p c b m", c=2, b=4)
        k = fpc[0] % 2
        fpc[0] += 1
        Kg, r_Kg = d["Kg"][k]
        Vg, r_Vg = d["Vg"][k]
        KT, r_KT = d["KT"][k]
        t_, r_t = d["t"][k]
        pTb, r_pTb = d["pTb"][k]
        Kg3 = Kg[:].rearrange("p (g c) -> p g c", g=GP)
        Vg3 = Vg[:].rearrange("p (g c) -> p g c", g=GP)
        KT4 = KT[:].rearrange("p (c g s) -> p c g s", c=2, g=GP)

        def f_dmak():
            for p in range(GP):
                j = g * GP + p
                P.op("pool", lambda e, p=p, j=j: e.indirect_dma_start(
                    out=Kg3[:, p, :], out_offset=None, in_=ckf,
                    in_offset=bass.IndirectOffsetOnAxis(ap=idxr[:, bb * 64 + j:bb * 64 + j + 1], axis=0),
                    element_offset=l * NPOOL * 128 * 256),
                    r=[r_idx], w=[r_Kg], dma=True, key=r_Kg)

        def f_dmav():
            for p in range(GP):
                j = g * GP + p
                P.op("pool", lambda e, p=p, j=j: e.indirect_dma_start(
                    out=Vg3[:, p, 0:256], out_offset=None, in_=cvf,
                    in_offset=bass.IndirectOffsetOnAxis(ap=idxr[:, bb * 64 + j:bb * 64 + j + 1], axis=0),
                    element_offset=l * NPOOL * 128 * 256),
                    r=[r_idx], w=[r_Vg], dma=True, key=r_Vg)

        def f_a():
            pk, rpk = b.ps()
            pkb = pk[:].bitcast(BF16)
            for c in range(2):
                for p in range(GP):
                    tr(pkb[:, (c * GP + p) * 128:(c * GP + p + 1) * 128], Kg3[:, p, c * 128:(c + 1) * 128], ident_bf[:],
                       [r_Kg, rc], [rpk])
            cp("act" if g % 2 else "dve", KT[:], pkb[:, 0:2 * GP * 128], [rpk], [r_KT])

        def f_b():
            psc, rpsc = b.ps()
            for p in range(GP):
                for c in range(2):
                    mm(psc[:, p * 16 + c * 8:p * 16 + c * 8 + 8], KT4[:, c, p, :], qb4[:, c, bb, :], True, True,
                       [r_KT, r_qblk], [rpsc])
            act(t_[:], psc[:, 0:GP * 16], AF.Exp, [rpsc], [r_t], scale=0.125)
            t4 = t_[:].rearrange("s (p h q) -> s p h q", p=GP, h=4)
            pT4 = pTb[:].rearrange("s (p h q) -> s p h q", p=GP, h=4)
            eSg = eS3[:, :, g * GP:(g + 1) * GP].rearrange("s h p -> s p h")
            for q in range(4):
                tt("dve", pT4[:, :, :, q], t4[:, :, :, q], eSg, MUL, [r_t, r_eS], [r_pTb])

        def f_c():
            for p in range(GP):
                mm(pov[0:NS, 0:257], pTb[:, p * 16:(p + 1) * 16], Vg3[:, p, 0:257], g == 0 and p == 0,
                   g == NG - 1 and p == GP - 1, [r_pTb, r_Vg], [rpov])
        return f_dmak, f_dmav, f_a, f_b, f_c

    class PastPipe:
        def __init__(self, l, bb):
            self.l, self.bb = l, bb
            self.eS3, self.r_eS = past_setup(l, bb)
            self.groups = [past_group(l, bb, g, self.eS3, self.r_eS) for g in range(NG)]
            self.s = 0
            self.groups[0][0]()

        def issue(self, n):
            pass

        def step(self):
            s_ = self.s
            G = self.groups
            if 0 <= s_ - 2 < NG:
                G[s_ - 2][4]()
            if s_ < NG:
                G[s_][1]()
            if 0 <= s_ - 1 < NG:
                G[s_ - 1][3]()
            if s_ + 1 < NG:
                G[s_ + 1][0]()
            if s_ < NG:
                G[s_][2]()
            self.s += 1

        def drain(self):
            while self.s < NG + 2:
                self.step()
            opast, r_opast = fpd["opast"]
            pov, rpov = b.psb[6]
            cp("act", opast[:, self.bb * 257:(self.bb + 1) * 257], pov[0:NS, 0:257], [rpov], [r_opast])

    def sample_mixer(l):
        switch()
        ar = Arena([(FKf, 2048), (VAf, 2080), (xTf, 2048), (qTff, 512), (kTff, 512), (fqf, 512), (ydf, 512),
                    (g0f, 542), (g1f, 542)])
        T = ar.take
        hsA, r_hsA = T(NS, 1296, F32, "hsA")
        hsB, r_hsB = T(NS, 1284, F32, "hsB")

        def hc(a, e):
            if e <= 1296:
                return hsA[:, a:e], r_hsA
            return hsB[:, a - 1296:e - 1296], r_hsB
        WSf, r_WSf = T(NS, 64, F32, "WSf")
        WSb, r_WSb = T(NS, 64, BF16, "WSb")
        bsS, r_bsS = T(NS, 4, F32, "bsS")
        bff, r_bff = T(NS, 4, F32, "bff")
        S0f, r_S0f = T(128, 1024, F32, "S0f")
        S0b, r_S0b = T(128, 1024, BF16, "S0b")
        xxT, r_xxT = T(128, 2 * 4 * 34, F32, "xxT")
        xx4 = xxT.rearrange("p (c b t) -> p c b t", c=2, b=4)
        mset("pool", WSf, 0.0, [r_WSf])
        mset("pool", S0f, 0.0, [r_S0f])
        WS3 = WSf.rearrange("p (g t) -> p g t", g=4)
        S03 = S0f.rearrange("p (b n) -> p b n", b=4)
        NCD = dict(allow_slow_non_contiguous=True)
        for bb in range(4):
            for g in range(4):
                dma("sp", WS3[4 * bb:4 * bb + 4, g, 4 * bb:4 * bb + 4], sgu_w[l, g, 0:4, 0:4].rearrange("t s -> s t"),
                    [], [r_WSf], r_WSf, **NCD)
            dma("sp", bsS[4 * bb:4 * bb + 4, :], sgu_bs[l][:, 0:4].rearrange("g t -> t g"), [], [r_bsS], r_bsS, **NCD)
            for h in range(4):
                dma("sp", S03[32 * h:32 * h + 32, bb, 64 * h:64 * h + 64], st_gla[l, bb, h], [], [r_S0f], r_S0f)
            for c in range(2):
                dma("sp", xx4[:, c, bb, 0:30], st_conv[l, bb][:, c * 128:(c + 1) * 128].rearrange("t p -> p t"),
                    [], [r_xxT], r_xxT, **NCD)
            dma("sp", o_sconv[l, bb, 0:26, :], st_conv[l, bb, 4:30, :], [], [], r_xxT)
        dma("sp", bff, fox_bf[l:l + 1, :].broadcast_to([NS, 4]), [], [r_bff], r_bff)
        for g in range(4):
            tt("dve", WSb.rearrange("p (g t) -> p g t", g=4)[:, g, :], WS3[:, g, :], maskS[:], MUL, [r_WSf, rc], [r_WSb])
        cp("act", S0b, S0f, [r_S0f], [r_S0b])
        S0b3 = S0b.rearrange("p (b n) -> p b n", b=4)

        pt, rp = b.ps()
        ptb = pt[:].bitcast(BF16)
        for kc in range(8):
            tr(ptb[:, kc * NS:(kc + 1) * NS], Rs[:, kc * 128:(kc + 1) * 128], ident_bf[0:NS, 0:NS], [r_Rs, rc], [rp])
        cp("act", xTs[:], ptb[:, 0:8 * NS].rearrange("p (k n) -> p k n", k=8), [rp], [r_xTs])
        for g0 in (0, 512, 1024, 1296, 1808, 2320):
            e0 = {0: 512, 512: 1024, 1024: 1296, 1296: 1808, 1808: 2320, 2320: DIN}[g0]
            n = e0 - g0
            pt, rp = b.ps()
            for kc in range(8):
                mm(pt[0:NS, 0:n], xTs[:, kc, :], Wi[:, kc, g0:e0], kc == 0, kc == 7, [r_xTs, rW], [rp])
            dst, rdst = hc(g0, e0)
            cp("act" if (g0 // 512) % 2 else "dve", dst, pt[0:NS, 0:n], [rp], [rdst])
        v_, r_ = hc(O_CK, O_CK + 256)
        dma("sp", o_sfk[l], v_, [r_], [], r_)
        v_, r_ = hc(O_CV, O_CV + 256)
        dma("sp", o_sfv[l], v_, [r_], [], r_)
        switch()
        ar2 = Arena([(Wi[:].rearrange("p k n -> p (k n)").bitcast(F32), 10320)])

        def T(parts, cols, dt, name):
            for a_ in (ar2, ar):
                try:
                    return a_.take(parts, cols, dt, name)
                except IndexError:
                    a_.i = len(a_.aps) - 1
                    a_.pos = a_.aps[-1][1]
            raise RuntimeError("sample arenas exhausted: " + name)
        ys, r_ys = T(NS, 768, BF16, "ys")
        idf = ident_f[0:NS, 0:NS]

        pt, rp = b.ps()
        tr(pt[0:16, 0:NS], hsA[:, O_ALR:O_ALR + 16], idf, [r_hsA, rc], [rp])
        tr(pt[:, 16:32], hsA[:, O_AQ:O_AQ + 128], idf, [r_hsA, rc], [rp])
        tr(pt[:, 32:48], hsA[:, O_AK:O_AK + 128], idf, [r_hsA, rc], [rp])
        alrs, r_alrs = T(16, NS, BF16, "alrs")
        cp("act", alrs, pt[0:16, 0:NS], [rp], [r_alrs])
        qk_s, r_qks = T(128, 32, F32, "qk_s")
        cp("act", qk_s, pt[:, 16:48], [rp], [r_qks])
        pz, rpz = b.ps()
        mm(pz[0:NS, 0:128], alrs, Wa[:], True, False, [r_alrs, rW], [rpz])
        mm(pz[0:NS, 0:128], ones_bf[0:1, 0:NS], ba[:], False, True, [rc, rW], [rpz])
        e1, r_e1 = T(NS, 128, F32, "s_e1")
        act(e1, pz[0:NS, 0:128], AF.Exp, [rpz], [r_e1], scale=-1.0)
        spl, r_spl = T(NS, 128, F32, "s_spl")
        act(spl, e1, AF.Ln, [r_e1], [r_spl], bias=1.0, scale=1.0)
        shi, r_shi = T(NS, 128, BF16, "s_shi")
        slo, r_slo = T(NS, 128, BF16, "s_slo")
        cp("dve", shi, spl, [r_spl], [r_shi])
        tt("dve", slo, spl, shi, SUB, [r_spl, r_shi], [r_slo])
        pg_, rpg_ = b.ps()
        mm(pg_[:, 0:NS], shi, triSb[:], True, False, [r_shi, rc], [rpg_])
        mm(pg_[:, 0:NS], slo, triSb[:], False, True, [r_slo, rc], [rpg_])
        mm(pg_[0:NS, 128:256], revSb[:], shi, True, False, [r_shi, rc], [rpg_])
        mm(pg_[0:NS, 128:256], revSb[:], slo, False, True, [r_slo, rc], [rpg_])
        egs, r_egs = T(128, 32, F32, "s_eg")
        act(egs[:, 0:16], pg_[:, 0:NS], AF.Exp, [rpg_], [r_egs])
        act(egs[:, 16:32], pg_[:, 0:NS], AF.Exp, [rpg_], [r_egs], scale=-1.0)
        erev, r_erev = T(NS, 128, F32, "s_erev")
        act(erev, pg_[0:NS, 128:256], AF.Exp, [rpg_], [r_erev])
        qtl, r_qtl = T(128, NS, BF16, "s_qtl")
        stt("dve", qtl, qk_s[:, 0:16], 32.0 ** -0.5, egs[:, 0:16], MUL, MUL, [r_qks, r_egs], [r_qtl])
        kt4, r_kt4 = T(128, 64, BF16, "s_kt4")
        for h in range(4):
            stt("dve", kt4[:, h * 16:(h + 1) * 16], qk_s[:, 16:32], hm[:, h:h + 1], egs[:, 16:32], MUL, MUL,
                [r_qks, r_egs, rc], [r_kt4])
        qtb, r_qtb = T(128, 64, BF16, "s_qtb")
        mset("pool", qtb, 0.0, [r_qtb])
        for bb in range(4):
            cp("dve", qtb[:, bb * 16 + 4 * bb:bb * 16 + 4 * bb + 4], qtl[:, 4 * bb:4 * bb + 4], [r_qtl], [r_qtb])
        kpb, r_kpb = T(NS, 512, BF16, "s_kpb")
        for bb in range(4):
            stt("dve", kpb[:, bb * 128:(bb + 1) * 128], hsA[:, O_AK:O_AK + 128], bm[:, bb:bb + 1], erev, MUL, MUL,
                [r_hsA, r_erev, rc], [r_kpb])
        vb, r_vb = T(NS, 256, BF16, "s_vb")
        cp("act", vb, hsA[:, O_AV:O_AV + 256], [r_hsA], [r_vb])
        sgl, r_sgl = T(NS, 256, F32, "s_sgl")
        act(sgl, hsA[:, O_AG:O_AG + 256], AF.Silu, [r_hsA], [r_sgl])
        tt("dve", sgl, sgl, glag[0:NS, :], MUL, [r_sgl, rW], [r_sgl])
        pa_, rpa_ = b.ps()
        for h in range(4):
            mm(pa_[0:NS, h * 16:(h + 1) * 16], kt4[:, h * 16:(h + 1) * 16], qtl, True, True, [r_kt4, r_qtl], [rpa_])
        asb, r_asb = T(NS, 64, BF16, "s_asb")
        tt("dve", asb, pa_[0:NS, 0:64], maskS4[:], MUL, [rpa_, rc], [r_asb])
        po, rpo = b.ps()
        for bb in range(4):
            mm(po[0:NS, 0:256], qtb[:, bb * 16:(bb + 1) * 16], S0b3[:, bb, :], bb == 0, bb == 3, [r_qtb, r_S0b], [rpo])
        for h in range(4):
            mm(po[0:NS, 256 + 64 * h:320 + 64 * h], asb[:, h * 16:(h + 1) * 16], vb[:, 64 * h:64 * h + 64], True, True,
               [r_asb, r_vb], [rpo])
        of, r_of = T(NS, 256, F32, "s_of")
        cp("act", of, po[0:NS, 0:256], [rpo], [r_of])
        tt("dve", of, of, po[0:NS, 256:512], ADD, [r_of, rpo], [r_of])
        sn, r_sn = T(128, 256, F32, "s_sn")
        for bb in range(4):
            pn, rpn = b.ps()
            mm(pn[:, 0:256], kpb[:, bb * 128:(bb + 1) * 128], vb, True, True, [r_kpb, r_vb], [rpn])
            tt("dve", sn, pn[:, 0:256], blkmask[:], MUL, [rpn, rc], [r_sn])
            stt("dve", sn, S03[:, bb, :], egs[:, 4 * bb + 3:4 * bb + 4], sn, MUL, ADD, [r_S0f, r_egs, r_sn], [r_sn])
            for h in range(4):
                dma("sp", o_sgla[l, bb, h], sn[32 * h:32 * h + 32, 64 * h:64 * h + 64], [r_sn], [], r_sn)
        osq, r_osq = T(NS, 256, F32, "s_osq")
        act(osq, of, AF.Square, [r_of], [r_osq])
        gst, r_gst = T(NS, 8, F32, "s_gst")
        rsum("dve", gst[:, 0:4], osq.rearrange("p (h e) -> p h e", h=4), [r_osq], [r_gst])
        ts("dve", gst[:, 4:8], gst[:, 0:4], 1.0 / 64.0, EPS, MUL, ADD, [r_gst], [r_gst])
        ts("dve", gst[:, 4:8], gst[:, 4:8], -0.5, None, POW, None, [r_gst], [r_gst])
        for h in range(4):
            stt("dve", ys[:, 64 * h:64 * h + 64], of[:, 64 * h:64 * h + 64], gst[:, 4 + h:5 + h],
                sgl[:, 64 * h:64 * h + 64], MUL, MUL, [r_of, r_gst, r_sgl], [r_ys])

        uu = hsA[:, O_BU:O_BU + 256]
        vv = hsA[:, O_BV:O_BV + 256]
        vsq, r_vsq = T(NS, 256, F32, "s_vsq")
        act(vsq, vv, AF.Square, [r_hsA], [r_vsq])
        sst, r_sst = T(NS, 24, F32, "s_sst")
        rsum("dve", sst[:, 0:4], vv.rearrange("p (h e) -> p h e", h=4), [r_hsA], [r_sst])
        rsum("dve", sst[:, 4:8], vsq.rearrange("p (h e) -> p h e", h=4), [r_vsq], [r_sst])
        ts("dve", sst[:, 8:12], sst[:, 0:4], 1.0 / 64.0, None, MUL, None, [r_sst], [r_sst])
        tt("dve", sst[:, 12:16], sst[:, 8:12], sst[:, 8:12], MUL, [r_sst], [r_sst])
        stt("dve", sst[:, 12:16], sst[:, 4:8], 1.0 / 64.0, sst[:, 12:16], MUL, SUB, [r_sst], [r_sst])
        ts("dve", sst[:, 16:20], sst[:, 12:16], EPS, -0.5, ADD, POW, [r_sst], [r_sst])
        stt("dve", sst[:, 20:24], sst[:, 8:12], -1.0, sst[:, 16:20], MUL, MUL, [r_sst], [r_sst])
        vn, r_vn = T(NS, 256, F32, "s_vn")
        for g in range(4):
            ts("dve", vn[:, 64 * g:64 * g + 64], vv[:, 64 * g:64 * g + 64], sst[:, 16 + g:17 + g], sst[:, 20 + g:21 + g],
               MUL, ADD, [r_hsA, r_sst], [r_vn])
        tt("dve", vn, vn, sgg[0:NS, :], MUL, [r_vn, rW], [r_vn])
        tt("dve", vn, vn, sgb[0:NS, :], ADD, [r_vn, rW], [r_vn])
        dma("sp", o_ssgu[l], vn, [r_vn], [], r_vn)
        vlb, r_vlb = T(NS, 256, BF16, "s_vlb")
        cp("dve", vlb, vn, [r_vn], [r_vlb])
        pmx, rpmx = b.ps()
        for g in range(4):
            mm(pmx[0:NS, 64 * g:64 * g + 64], WSb[:, g * 16:(g + 1) * 16], vlb[:, 64 * g:64 * g + 64], True, True,
               [r_WSb, r_vlb], [rpmx])
        for g in range(4):
            stt("dve", ys[:, 256 + 64 * g:320 + 64 * g], pmx[0:NS, 64 * g:64 * g + 64], bsS[:, g:g + 1],
                uu[:, 64 * g:64 * g + 64], ADD, MUL, [rpmx, r_hsA, r_bsS], [r_ys])

        cf_, r_cf = hc(O_CF, O_CF + 4)
        fst, r_fst = T(NS, 16, F32, "s_fst")
        tt("dve", fst[:, 0:4], cf_, bff, ADD, [r_cf, r_bff], [r_fst])
        act(fst[:, 0:4], fst[:, 0:4], AF.Exp, [r_fst], [r_fst], scale=-1.0)
        act(fst[:, 4:8], fst[:, 0:4], AF.Ln, [r_fst], [r_fst], bias=1.0, scale=1.0)
        ts("dve", fst[:, 12:16], fst[:, 4:8], -1.0, None, MUL, None, [r_fst], [r_fst])
        dma("sp", o_sflf[l], fst[:, 12:16], [r_fst], [], r_fst)
        pd, rpd = b.ps()
        mm(pd[0:NS, 0:4], maskS[:], fst[:, 4:8], True, True, [rc, r_fst], [rpd])
        act(fst[:, 8:12], pd[0:NS, 0:4], AF.Exp, [rpd], [r_fst])
        pt, rp = b.ps()
        for c in range(2):
            v_, r_ = hc(O_CQ + 128 * c, O_CQ + 128 * c + 128)
            tr(pt[:, c * 16:(c + 1) * 16], v_, idf, [r_, rc], [rp])
            v_, r_ = hc(O_CK + 128 * c, O_CK + 128 * c + 128)
            tr(pt[:, 32 + c * 16:32 + (c + 1) * 16], v_, idf, [r_, rc], [rp])
        fqk, r_fqk = T(128, 64, BF16, "s_fqk")
        cp("act", fqk, pt[:, 0:64], [rp], [r_fqk])
        qblk, r_qblk = T(128, 2 * 4 * 8, BF16, "s_qblk")
        mset("pool", qblk, 0.0, [r_qblk])
        qb4 = qblk.rearrange("p (c b m) -> p c b m", c=2, b=4)
        for h2 in range(2):
            cp("dve", qb4[64 * h2:64 * h2 + 64, :, :, 4 * h2:4 * h2 + 4],
               fqk[64 * h2:64 * h2 + 64, 0:32].rearrange("p (c b q) -> p c b q", c=2, b=4), [r_fqk], [r_qblk])
        vaug, r_vaug = T(NS, 258, BF16, "s_vaug")
        v_, r_ = hc(O_CV, O_CV + 256)
        cp("act", vaug[:, 0:256], v_, [r_], [r_vaug])
        mset("pool", vaug[:, 256:258], 1.0, [r_vaug])
        psn, rpsn = b.ps()
        for bb in range(4):
            for c in range(2):
                mm(psn[0:NS, bb * 16 + c * 8:bb * 16 + c * 8 + 8], fqk[:, 32 + c * 16:32 + (c + 1) * 16], qb4[:, c, bb, :],
                   True, True, [r_fqk, r_qblk], [rpsn])
        t1, r_t1 = T(NS, 64, F32, "s_t1")
        act(t1, psn[0:NS, 0:64], AF.Exp, [rpsn], [r_t1], scale=0.125)
        tt("dve", t1, t1, maskN[:], MUL, [r_t1, rc], [r_t1])
        pnb, r_pnb = T(NS, 64, BF16, "s_pnb")
        t14 = t1.rearrange("p (b h q) -> p b h q", b=4, h=4)
        pn4 = pnb.rearrange("p (b h q) -> p b h q", b=4, h=4)
        for h in range(4):
            ts("dve", pn4[:, :, h, :], t14[:, :, h, :], fst[:, 8 + h:9 + h], None, MUL, None, [r_t1, r_fst], [r_pnb])
        ycs, r_ycs = T(NS, 256, F32, "s_ycs")
        osum, r_osum = T(NS, 258, F32, "s_osum")
        rd, r_rd = T(NS, 2, F32, "s_rd")
        on, r_on = T(NS, 256, F32, "s_on")
        for bb in range(4):
            pov, rpov = b.psb[7]
            mm(pov[0:NS, 0:257], pnb[:, bb * 16:(bb + 1) * 16], vaug[:, 0:257], True, True, [r_pnb, r_vaug], [rpov])
            if has_cache:
                tt("dve", osum[:, 0:257], pov[0:NS, 0:257], fpd["opast"][0][:, bb * 257:(bb + 1) * 257], ADD,
                   [rpov, fpd["opast"][1]], [r_osum])
            else:
                cp("dve", osum[:, 0:257], pov[0:NS, 0:257], [rpov], [r_osum])
            P.op("dve", lambda e, rd=rd, osum=osum: e.reciprocal(out=rd[:, 0:1], in_=osum[:, 256:257]), r=[r_osum], w=[r_rd])
            ts("dve", on, osum[:, 0:256], rd[:, 0:1], None, MUL, None, [r_osum, r_rd], [r_on])
            for h in range(4):
                dma("sp", ycs[4 * bb:4 * bb + 4, 64 * h:64 * h + 64], on[4 * h:4 * h + 4, 64 * h:64 * h + 64],
                    [r_on], [r_ycs], r_ycs)
        cp("dve", ys[:, 512:768], ycs, [r_ycs], [r_ys])

        ga_, r_ga = hc(O_DIN, O_DIN + 256)
        gg_, r_gg = hc(O_DIN + 256, O_DIN + 512)
        glu, r_glu = T(NS, 256, F32, "s_glu")
        act(glu, gg_, AF.Sigmoid, [r_gg], [r_glu])
        tt("dve", glu, glu, ga_, MUL, [r_glu, r_ga], [r_glu])
        for bb in range(4):
            dma("sp", o_sconv[l, bb, 26:30, :], glu[4 * bb:4 * bb + 4, :], [r_glu], [], r_glu)
        pt, rp = b.ps()
        for c in range(2):
            tr(pt[:, c * 16:(c + 1) * 16], glu[:, c * 128:(c + 1) * 128], idf, [r_glu, rc], [rp])
        for c in range(2):
            cp("act", xx4[:, c, :, 30:34], pt[:, c * 16:(c + 1) * 16].rearrange("p (b q) -> p b q", b=4), [rp], [r_xxT])
        acc, r_acc = T(128, 32, F32, "s_acc")
        for c in range(2):
            a3 = acc[:, c * 16:(c + 1) * 16].rearrange("p (b q) -> p b q", b=4)
            for j in range(31):
                if j == 0:
                    ts("dve", a3, xx4[:, c, :, 0:4], cw[:, c, 0:1], None, MUL, None, [r_xxT, rW], [r_acc])
                else:
                    stt("dve", a3, xx4[:, c, :, j:j + 4], cw[:, c, j:j + 1], a3, MUL, ADD, [r_xxT, rW, r_acc], [r_acc])
        ydTs, r_ydTs = T(128, 32, BF16, "s_ydTs")
        for c in range(2):
            ac = acc[:, c * 16:(c + 1) * 16]
            cof, r_cof = T(128, 16, F32, "s_cof%d" % c)
            act(cof, ac, AF.Identity, [r_acc, rW], [r_cof], bias=cb[:, c:c + 1], scale=1.0)
            sq, r_sq = T(128, 16, F32, "s_csq%d" % c)
            tt("dve", sq, cof, cof, MUL, [r_cof], [r_sq])
            pm, rpm = b.ps()
            mm(pm[:, 0:16], blk64[:], cof, True, True, [rc, r_cof], [rpm])
            mm(pm[:, 16:32], blk64[:], sq, True, True, [rc, r_sq], [rpm])
            mv, r_mv = T(128, 32, F32, "s_mv%d" % c)
            cp("act", mv, pm[:, 0:32], [rpm], [r_mv])
            tt("dve", sq, mv[:, 0:16], mv[:, 0:16], MUL, [r_mv], [r_sq])
            tt("dve", sq, mv[:, 16:32], sq, SUB, [r_mv, r_sq], [r_sq])
            ts("dve", sq, sq, EPS, -0.5, ADD, POW, [r_sq], [r_sq])
            tt("dve", cof, cof, mv[:, 0:16], SUB, [r_cof, r_mv], [r_cof])
            tt("dve", cof, cof, sq, MUL, [r_cof, r_sq], [r_cof])
            act(ydTs[:, c * 16:(c + 1) * 16], cof, AF.Silu, [r_cof, rW], [r_ydTs], scale=cng[:, c:c + 1], bias=cnb[:, c:c + 1])

        pty, rpty = b.ps()
        ptyb = pty[:].bitcast(BF16)
        for c in range(6):
            tr(ptyb[:, c * 16:(c + 1) * 16], ys[:, c * 128:(c + 1) * 128], ident_bf[0:NS, 0:NS], [r_ys, rc], [rpty])
        yTs, r_yTs = T(128, 96, BF16, "s_yTs")
        cp("act", yTs, ptyb[:, 0:96], [rpty], [r_yTs])
        ps2, rps2 = [], []
        for hf in range(2):
            pm_, rpm_ = b.ps()
            for kc in range(8):
                lh = yTs[:, kc * 16:(kc + 1) * 16] if kc < 6 else ydTs[:, (kc - 6) * 16:(kc - 5) * 16]
                mm(pm_[0:NS, :], lh, Wo[:, kc, hf * 512:(hf + 1) * 512], kc == 0, kc == 7, [r_yTs, r_ydTs, rW], [rpm_])
            ps2.append(pm_)
            rps2.append(rpm_)
        layer_norm(None, ps2, rps2, l1g, l1b, None, n=NS, res=Rs[:], r_res=r_Rs)
        switch()

    cur_wu = [None]

    def wu_chunk(l, fc, wus, cnt_f):
        if fc % 4 == 0:
            cols = min(4, NFC - fc) * 128
            wu, r_wu = wus[cnt_f[0] % 2]
            cnt_f[0] += 1
            cur_wu[0] = (wu, r_wu)
            dma("pool", wu[:, :, 0:cols], w_up[l, :, fc * 128:fc * 128 + cols].rearrange("(k p) n -> p k n", p=128),
                [], [r_wu], r_wu)
            dma("pool", wu[:, :, 512:512 + cols],
                w_up[l, :, DFF + fc * 128:DFF + fc * 128 + cols].rearrange("(k p) n -> p k n", p=128), [], [r_wu], r_wu)
        wu, r_wu = cur_wu[0]
        ci = fc % 4
        return wu, r_wu, ci * 128, 512 + ci * 128

    def sample_ffn(l, last, wus, wds, cnt_f):
        switch()
        ar = Arena([(fqf, 512), (ydf, 512), (g0f, 542), (g1f, 542)])
        T = ar.take
        pt, rp = b.ps()
        ptb = pt[:].bitcast(BF16)
        for kc in range(8):
            tr(ptb[:, kc * NS:(kc + 1) * NS], Rs[:, kc * 128:(kc + 1) * 128], ident_bf[0:NS, 0:NS], [r_Rs, rc], [rp])
        cp("act", xTs[:], ptb[:, 0:8 * NS].rearrange("p (k n) -> p k n", k=8), [rp], [r_xTs])
        bufT, r_bufT = T(128, NFC * 8, F32, "f_bufT")
        glo, r_glo = T(128, NFC * 8, F32, "f_glo")
        hTs, r_hTs = T(128, NFC * 16, BF16, "f_hTs")
        bu4 = bufT.rearrange("p (c b j) -> p c b j", c=NFC, b=4)
        gl4 = glo.rearrange("p (c b j) -> p c b j", c=NFC, b=4)
        NCD = dict(allow_slow_non_contiguous=True)
        for fc in range(NFC):
            for bb in range(4):
                dma("sp", bu4[:, fc, bb, :], st_ffc[l, bb][:, fc * 128:(fc + 1) * 128].rearrange("j p -> p j"),
                    [], [r_bufT], r_bufT, **NCD)
        gxl = [T(128, 24, F32, "f_gx%d" % i) for i in range(2)]
        gal = [T(128, 16, F32, "f_ga%d" % i) for i in range(2)]
        for fc in range(NFC):
            wu, r_wu, og, ov = wu_chunk(l, fc, wus, cnt_f)
            pg, rpg = b.ps()
            for kc in range(8):
                mm(pg[:, 0:16], wu[:, kc, og:og + 128], xTs[:, kc, :], kc == 0, kc == 7, [r_wu, r_xTs], [rpg])
            for kc in range(8):
                mm(pg[:, 16:32], wu[:, kc, ov:ov + 128], xTs[:, kc, :], kc == 0, kc == 7, [r_wu, r_xTs], [rpg])
            gx, r_gx = gxl[fc % 2]
            ga, r_ga = gal[fc % 2]
            gx3 = gx.rearrange("p (b t) -> p b t", b=4)
            ga3 = ga.rearrange("p (b t) -> p b t", b=4)
            cp("dve", gx3[:, :, 0:2], bu4[:, fc, :, :], [r_bufT], [r_gx])
            cp("act", gx3[:, :, 2:6], pg[:, 0:16].rearrange("p (b t) -> p b t", b=4), [rpg], [r_gx])
            cp("dve", gl4[:, fc, :, :], gx3[:, :, 4:6], [r_gx], [r_glo])
            ts("dve", ga3, gx3[:, :, 0:4], fw[:, fc, 0:1], None, MUL, None, [r_gx, rW], [r_ga])
            stt("dve", ga3, gx3[:, :, 1:5], fw[:, fc, 1:2], ga3, MUL, ADD, [r_gx, rW, r_ga], [r_ga])
            stt("dve", ga3, gx3[:, :, 2:6], fw[:, fc, 2:3], ga3, MUL, ADD, [r_gx, rW, r_ga], [r_ga])
            act(ga, ga, AF.Silu, [r_ga, rW], [r_ga], bias=fb[:, fc:fc + 1], scale=1.0)
            tt("dve", hTs[:, fc * 16:(fc + 1) * 16], ga, pg[:, 16:32], MUL, [r_ga, rpg], [r_hTs])
        for fc in range(NFC):
            for bb in range(4):
                dma("sp", o_sffc[l, bb][:, fc * 128:(fc + 1) * 128].rearrange("j p -> p j"), gl4[:, fc, bb, :],
                    [r_glo], [], r_glo, **NCD)
        bk = [b.ps(), b.ps()]
        for fc in range(NFC):
            wd, r_wd = wds[cnt_f[3] % 3]
            cnt_f[3] += 1
            dma("pool", wd[:], w_dn[l, fc * 128:(fc + 1) * 128, :], [], [r_wd], r_wd)
            for hf in range(2):
                mm(bk[hf][0][0:NS, :], hTs[:, fc * 16:(fc + 1) * 16], wd[:, hf * 512:(hf + 1) * 512], fc == 0, fc == NFC - 1,
                   [r_hTs, r_wd], [bk[hf][1]])
        layer_norm(None, [bk[0][0], bk[1][0]], [bk[0][1], bk[1][1]], l2g, l2b, o_ys[:, :] if last else None,
                   n=NS, res=Rs[:], r_res=r_Rs)
        switch()

    for t in range(NT):
        dma("pool", R[:, t, :], xp[t * 128:(t + 1) * 128, :], [], [r_R[t]], r_R[t])

    def layer_norm(t, ps2, rps2, g_t, b_t, out_dram, n=128, res=None, r_res=None):
        if res is None:
            res, r_res = R[:, t, :], r_R[t]
        rf, r_rf = tmp("ln_rf", [128, D], F32)
        for hf in range(2):
            stt("dve", rf[0:n, hf * 512:(hf + 1) * 512], res[:, hf * 512:(hf + 1) * 512], ALPHA, ps2[hf][0:n, :],
                MUL, ADD, [r_res, rps2[hf]], [r_rf])
        st, r_st = tmp("ln_st", [128, 8], F32)
        xn, r_xn = tmp("ln_xn", [128, D], F32)
        act(xn[0:n, :], rf[0:n, :], AF.Identity, [r_rf], [r_xn, r_st], accum_out=st[0:n, 0:1])
        act(xn[0:n, :], rf[0:n, :], AF.Square, [r_rf], [r_xn, r_st], accum_out=st[0:n, 1:2])
        ts("dve", st[0:n, 2:3], st[0:n, 0:1], 1.0 / D, None, MUL, None, [r_st], [r_st])
        tt("dve", st[0:n, 3:4], st[0:n, 2:3], st[0:n, 2:3], MUL, [r_st], [r_st])
        stt("dve", st[0:n, 4:5], st[0:n, 1:2], 1.0 / D, st[0:n, 3:4], MUL, SUB, [r_st], [r_st])
        ts("dve", st[0:n, 5:6], st[0:n, 4:5], EPS, -0.5, ADD, POW, [r_st], [r_st])
        stt("dve", st[0:n, 6:7], st[0:n, 2:3], -1.0, st[0:n, 5:6], MUL, MUL, [r_st], [r_st])
        act(xn[0:n, :], rf[0:n, :], AF.Identity, [r_rf, r_st], [r_xn], scale=st[0:n, 5:6], bias=st[0:n, 6:7])
        tt("dve", xn[0:n, :], xn[0:n, :], g_t[0:n, :], MUL, [r_xn, rW], [r_xn])
        if out_dram is None:
            tt("dve", res, xn[0:n, :], b_t[0:n, :], ADD, [r_xn, rW], [r_res])
        else:
            tt("dve", xn[0:n, :], xn[0:n, :], b_t[0:n, :], ADD, [r_xn, rW], [r_xn])
            dma("sp", out_dram, xn[0:n, :], [r_xn], [], r_xn)

    def make_xT(blk):
        for ti in range(4):
            t = blk * 4 + ti
            pt, rp = b.ps()
            ptb = pt[:].bitcast(BF16)
            for kc in range(8):
                tr(ptb[:, kc * 128:(kc + 1) * 128], R[:, t, kc * 128:(kc + 1) * 128], ident_bf[:], [r_R[t], rc], [rp])
            cp("act", xT[:, :, ti * 128:(ti + 1) * 128], ptb.rearrange("p (k n) -> p k n", k=8), [rp], [r_xT])

    for l in range(nlayers):
        last = (l == nlayers - 1)
        for kc in range(8):
            dma("pool", Wi[:, kc, :], w_in[l, kc * 128:(kc + 1) * 128, :], [], [rW], rW)
        for kc in range(8):
            dma("pool", Wo[:, kc, :], w_o[l, kc * 128:(kc + 1) * 128, :], [], [rW], rW)
        dma("pool", Wa[:], gla_w_a[l], [], [rW], rW)
        dma("pool", ba[:], gla_b_a[l:l + 1, :], [], [rW], rW)
        dma("pool", bfb[:], fox_bf[l:l + 1, :], [], [rW], rW)
        dma("pool", bs4[:], sgu_bs[l], [], [rW], rW)
        dma("sp", bsT[:], sgu_bs[l].rearrange("g t -> t g"), [], [rW], rW, allow_slow_non_contiguous=True)
        for (tl, src) in ((glag, gla_g), (sgg, sgu_g), (sgb, sgu_bb)):
            dma("sp", tl[:], src[l:l + 1, :].broadcast_to([128, 256]), [], [rW], rW)
        for (tl, src) in ((l1g, ln1_g), (l1b, ln1_b)):
            dma("sp", tl[:], src[l:l + 1, :].broadcast_to([128, D]), [], [rW], rW)
        NC_ = dict(allow_slow_non_contiguous=True)
        for c in range(2):
            dma("sp", cw[:, c, :], conv_w[l][:, c * 128:(c + 1) * 128].rearrange("j p -> p j"), [], [rW], rW, **NC_)
        for (tl, src) in ((cb, conv_b), (cng, cn_g), (cnb, cn_b)):
            dma("sp", tl[:], src[l].rearrange("(c p) -> p c", p=128), [], [rW], rW, **NC_)
        for c in range(NFC):
            dma("sp", fw[:, c, :], fcw[l][:, c * 128:(c + 1) * 128].rearrange("j p -> p j"), [], [rW], rW, **NC_)
        dma("sp", fb[:], fcb[l].rearrange("(c p) -> p c", p=128), [], [rW], rW, **NC_)
        dma("sp", sw, sgu_w[l].rearrange("g t s -> t g s"), [], [r_xT], r_xT)
        for g in range(4):
            pt, rp = b.ps()
            tr(pt[:, 0:128], sw[:, g, :], ident_f[:], [r_xT, rc], [rp])
            tt("dve", WT[:, g, :], pt[:, 0:128], triO[:], MUL, [rp, rc], [rW])
        mset("pool", Sf[:], 0.0, [r_S])
        mset("pool", Sb[:], 0.0, [r_S])
        mset("pool", Pacc[:], 0.0, [r_Pacc])
        mset("pool", gext[1][:, :, 512:542], 0.0, [r_gext[1]])

        if has_cache and 'sample' not in SKIP:
            sample_q_prework(l)
        for blk in range(4):
            make_xT(blk)
            c0 = blk * 512
            def fproj(col, m):
                pt, rp = b.ps()
                for kc in range(8):
                    mm(pt[0:m, :], Wi[:, kc, col:col + m], xT[:, kc, :], kc == 0, kc == 7, [rW, r_xT], [rp])
                return pt, rp
            pt, rp = fproj(O_AQ, 128)
            cp("act", qTf[:], pt[:], [rp], [r_qk])
            pt, rp = fproj(O_AK, 128)
            cp("dve", kTf[:], pt[:], [rp], [r_qk])
            pt, rp = fproj(O_ALR, 16)
            cp("act", alr[:], pt[0:16, :], [rp], [r_alr])
            for c in range(2):
                pt, rp = fproj(O_CQ + 128 * c, 128)
                cp("act", fq[:, c, :], pt[:], [rp], [r_fq])
                pt, rp = fproj(O_CK + 128 * c, 128)
                cp("dve", FK[:, c, c0:c0 + 512], pt[:], [rp], [r_FK[blk]])
            ge, r_ge = gext[blk % 2], r_gext[blk % 2]
            gp_, r_gp = gext[(blk + 1) % 2], r_gext[(blk + 1) % 2]
            cp("dve", ge[:, :, 0:30], gp_[:, :, 512:542], [r_gp], [r_ge])
            for c in range(2):
                pa, rpa = fproj(O_DIN + 128 * c, 128)
                pg, rpg = fproj(O_DIN + 256 + 128 * c, 128)
                sg, r_sg = tmp("sig", [128, 512], F32)
                act(sg[:], pg[:], AF.Sigmoid, [rpg], [r_sg])
                tt("dve", sg[:], pa[:], sg[:], MUL, [rpa, r_sg], [r_sg])
                cp("dve", ge[:, c, 30:542], sg[:], [r_sg], [r_ge])
                if blk == 3:
                    cp("dve", glast[:, c, :], sg[:, 482:512], [r_sg], [r_glast])
            if blk == 3:
                for c in range(2):
                    dma("sp", o_conv[l][:, c * 128:(c + 1) * 128].rearrange("t p -> p t"), glast[:, c, :], [r_glast], [],
                        r_glast, allow_slow_non_contiguous=True)
            if 'conv' in SKIP:
                mset('pool', ydT[:], 0.0, [r_ydT])
            for _once in ([] if 'conv' in SKIP else [0]):
                for c in range(2):
                    cof, r_cof = tmp("cof", [128, 512], F32)
                    ts("dve", cof[:], ge[:, c, 0:512], cw[:, c, 0:1], cb[:, c:c + 1], MUL, ADD, [r_ge, rW], [r_cof])
                    for j in range(1, 31):
                        stt("dve", cof[:], ge[:, c, j:j + 512], cw[:, c, j:j + 1], cof[:], MUL, ADD, [r_ge, rW, r_cof], [r_cof])
                    sq, r_sq = tmp("csq", [128, 512], F32)
                    act(sq[:], cof[:], AF.Square, [r_cof], [r_sq])
                    pm, rpm = b.ps()
                    mm(pm[:], blk64[:], cof[:], True, True, [rc, r_cof], [rpm])
                    pe2, rpe2 = b.ps()
                    mm(pe2[:], blk64[:], sq[:], True, True, [rc, r_sq], [rpe2])
                    msq, r_msq = tmp("cmsq", [128, 512], F32)
                    act(msq[:], pm[:], AF.Square, [rpm], [r_msq])
                    var, r_var = tmp("cvar", [128, 512], F32)
                    tt("dve", var[:], pe2[:], msq[:], SUB, [rpe2, r_msq], [r_var])
                    ts("dve", var[:], var[:], EPS, -0.5, ADD, POW, [r_var], [r_var])
                    tt("dve", cof[:], cof[:], pm[:], SUB, [r_cof, rpm], [r_cof])
                    tt("dve", cof[:], cof[:], var[:], MUL, [r_cof, r_var], [r_cof])
                    act(ydT[:, c, :], cof[:], AF.Silu, [r_cof, rW], [r_ydT], scale=cng[:, c:c + 1], bias=cnb[:, c:c + 1])

            pipe = PastPipe(l, blk) if (has_cache and 'sample' not in SKIP) else None
            if pipe:
                pipe.issue(2)
            for ti in range(4):
                t = blk * 4 + ti
                cs = slice(ti * 128, (ti + 1) * 128)
                ytok, r_ytok = tmp("ytok", [128, 768], BF16)

                def tproj(col, n):
                    pt, rp = b.ps()
                    for kc in range(8):
                        mm(pt[:, 0:n], xT[:, kc, cs], Wi[:, kc, col:col + n], kc == 0, kc == 7, [r_xT, rW], [rp])
                    return pt, rp

                if 'gla' in SKIP or GSTOP < 99:
                    mset('pool', ytok[:, 0:256], 0.0, [r_ytok])
                for _once in ([] if 'gla' in SKIP else [0]):
                    pz, rpz = b.ps()
                    mm(pz[:, 0:128], alr[0:16, cs], Wa[:], True, False, [r_alr, rW], [rpz])
                    mm(pz[:, 0:128], ones_bf[0:1, :], ba[:], False, True, [rc, rW], [rpz])
                    e1, r_e1 = tmp("g_e1", [128, 128], F32)
                    act(e1[:], pz[:, 0:128], AF.Exp, [rpz], [r_e1], scale=-1.0)
                    spl, r_spl = tmp("g_sp", [128, 128], F32)
                    act(spl[:], e1[:], AF.Ln, [r_e1], [r_spl], bias=1.0, scale=1.0)
                    if GSTOP <= 1:
                        break
                    pg_, rpg_ = b.ps()
                    shi, r_shi = tmp("g_shi", [128, 128], BF16)
                    slo, r_slo = tmp("g_slo", [128, 128], BF16)
                    cp("dve", shi[:], spl[:], [r_spl], [r_shi])
                    tt("dve", slo[:], spl[:], shi[:], SUB, [r_spl, r_shi], [r_slo])
                    mm(pg_[:, 0:128], shi[:], triMb[:], True, False, [r_shi, rc], [rpg_])
                    mm(pg_[:, 0:128], slo[:], triMb[:], False, True, [r_slo, rc], [rpg_])
                    mm(pg_[:, 128:256], revMb[:], shi[:], True, False, [r_shi, rc], [rpg_])
                    mm(pg_[:, 128:256], revMb[:], slo[:], False, True, [r_slo, rc], [rpg_])
                    eg, r_eg = tmp("g_eg", [128, 384], F32)
                    act(eg[:, 0:128], pg_[:, 0:128], AF.Exp, [rpg_], [r_eg])
                    act(eg[:, 128:256], pg_[:, 0:128], AF.Exp, [rpg_], [r_eg], scale=-1.0)
                    act(eg[:, 256:384], pg_[:, 128:256], AF.Exp, [rpg_], [r_eg])
                    if GSTOP <= 2:
                        break
                    qtl, r_qtl = tmp("g_qtl", [128, 128], BF16)
                    stt("dve", qtl[:], qTf[:, cs], 32.0 ** -0.5, eg[:, 0:128], MUL, MUL, [r_qk, r_eg], [r_qtl])
                    kt4, r_kt4 = tmp("g_kt4", [128, 4, 128], BF16)
                    for h in range(4):
                        stt("dve", kt4[:, h, :], kTf[:, cs], hm[:, h:h + 1], eg[:, 128:256], MUL, MUL,
                            [r_qk, r_eg, rc], [r_kt4])
                    if GSTOP <= 3:
                        break
                    p1, rp1 = tproj(O_AK, 512)
                    p2, rp2 = tproj(O_AK + 512, 128)
                    kp, r_kp = tmp("g_kp", [128, 128], BF16)
                    tt("dve", kp[:], p1[:, 0:128], eg[:, 256:384], MUL, [rp1, r_eg], [r_kp])
                    vb, r_vb = tmp("g_vb", [128, 256], BF16)
                    cp("act", vb[:], p1[:, 128:384], [rp1], [r_vb])
                    sgl, r_sgl = tmp("g_sg", [128, 256], F32)
                    act(sgl[:, 0:128], p1[:, 384:512], AF.Silu, [rp1], [r_sgl])
                    act(sgl[:, 128:256], p2[:, 0:128], AF.Silu, [rp2], [r_sgl])
                    tt("dve", sgl[:], sgl[:], glag[:], MUL, [r_sgl, rW], [r_sgl])
                    if GSTOP <= 4:
                        break
                    pa_, rpa_ = b.ps()
                    for h in range(4):
                        mm(pa_[:, h * 128:(h + 1) * 128], kt4[:, h, :], qtl[:], True, True, [r_kt4, r_qtl], [rpa_])
                    asb, r_asb = tmp("g_asb", [128, 512], BF16)
                    tt("dve", asb[:], pa_[:], mask4[:], MUL, [rpa_, rc], [r_asb])
                    if GSTOP <= 5:
                        break
                    po, rpo = b.ps()
                    mm(po[:, 0:256], qtl[:], Sb[:], True, True, [r_qtl, r_S], [rpo])
                    for h in range(4):
                        mm(po[:, 256 + 64 * h:320 + 64 * h], asb[:, h * 128:(h + 1) * 128], vb[:, 64 * h:64 * h + 64], True, True,
                           [r_asb, r_vb], [rpo])
                    of, r_of = tmp("g_of", [128, 256], F32)
                    cp("act", of[:], po[:, 0:256], [rpo], [r_of])
                    tt("dve", of[:], of[:], po[:, 256:512], ADD, [r_of, rpo], [r_of])
                    if GSTOP <= 6:
                        break
                    pn, rpn = b.ps()
                    mm(pn[:, 0:256], kp[:], vb[:], True, True, [r_kp, r_vb], [rpn])
                    stmp, r_stmp = tmp("g_stmp", [128, 256], F32)
                    tt("dve", stmp[:], pn[:, 0:256], blkmask[:], MUL, [rpn, rc], [r_stmp])
                    stt("dve", Sf[:], Sf[:], eg[:, 127:128], stmp[:], MUL, ADD, [r_S, r_eg, r_stmp], [r_S])
                    cp("act", Sb[:], Sf[:], [r_S], [r_S])
                    if GSTOP <= 7:
                        break
                    osq, r_osq = tmp("g_stmp", [128, 256], F32)
                    act(osq[:], of[:], AF.Square, [r_of], [r_osq])
                    gst, r_gst = tmp("g_st", [128, 8], F32)
                    rsum("dve", gst[:, 0:4], osq[:].rearrange("p (h e) -> p h e", h=4), [r_osq], [r_gst])
                    ts("dve", gst[:, 4:8], gst[:, 0:4], 1.0 / 64.0, EPS, MUL, ADD, [r_gst], [r_gst])
                    ts("dve", gst[:, 4:8], gst[:, 4:8], -0.5, None, POW, None, [r_gst], [r_gst])
                    for h in range(4):
                        stt("dve", ytok[:, 64 * h:64 * h + 64], of[:, 64 * h:64 * h + 64], gst[:, 4 + h:5 + h],
                            sgl[:, 64 * h:64 * h + 64], MUL, MUL, [r_of, r_gst, r_sgl], [r_ytok])

                if pipe:
                    pipe.step()
                if 'sgu' in SKIP:
                    mset('pool', ytok[:, 256:512], 0.0, [r_ytok])
                for _once in ([] if 'sgu' in SKIP else [0]):
                    pu, rpu = tproj(O_BU, 512)
                    us, r_us = tmp("s_u", [128, 512], F32)
                    cp("act", us[:], pu[:], [rpu], [r_us])
                    vsq, r_vsq = tmp("g_stmp", [128, 256], F32)
                    act(vsq[:], pu[:, 256:512], AF.Square, [rpu], [r_vsq])
                    sst, r_sst = tmp("s_st", [128, 24], F32)
                    rsum("dve", sst[:, 0:4], us[:, 256:512].rearrange("p (h e) -> p h e", h=4), [r_us], [r_sst])
                    rsum("dve", sst[:, 4:8], vsq[:].rearrange("p (h e) -> p h e", h=4), [r_vsq], [r_sst])
                    ts("dve", sst[:, 8:12], sst[:, 0:4], 1.0 / 64.0, None, MUL, None, [r_sst], [r_sst])
                    tt("dve", sst[:, 12:16], sst[:, 8:12], sst[:, 8:12], MUL, [r_sst], [r_sst])
                    stt("dve", sst[:, 12:16], sst[:, 4:8], 1.0 / 64.0, sst[:, 12:16], MUL, SUB, [r_sst], [r_sst])
                    ts("dve", sst[:, 16:20], sst[:, 12:16], EPS, -0.5, ADD, POW, [r_sst], [r_sst])
                    stt("dve", sst[:, 20:24], sst[:, 8:12], -1.0, sst[:, 16:20], MUL, MUL, [r_sst], [r_sst])
                    vn, r_vn = tmp("g_sg", [128, 256], F32)
                    for g in range(4):
                        ts("dve", vn[:, 64 * g:64 * g + 64], us[:, 256 + 64 * g:320 + 64 * g], sst[:, 16 + g:17 + g],
                           sst[:, 20 + g:21 + g], MUL, ADD, [r_us, r_sst], [r_vn])
                    tt("dve", vn[:], vn[:], sgg[:], MUL, [r_vn, rW], [r_vn])
                    vlb, r_vlb = tmp("s_vlb", [128, 256], BF16)
                    tt("dve", vlb[:], vn[:], sgb[:], ADD, [r_vn, rW], [r_vlb])
                    pmx, rpmx = b.ps()
                    for g in range(4):
                        mm(pmx[:, 64 * g:64 * g + 64], WT[:, g, :], vlb[:, 64 * g:64 * g + 64], True, True,
                           [rW, r_vlb], [rpmx])
                    for g in range(4):
                        stt("dve", ytok[:, 256 + 64 * g:320 + 64 * g], pmx[:, 64 * g:64 * g + 64], bsT[:, g:g + 1],
                            us[:, 64 * g:64 * g + 64], ADD, MUL, [rpmx, r_us, rW], [r_ytok])

                if pipe:
                    pipe.step()
                if 'fox' in SKIP:
                    mset('pool', ytok[:, 512:768], 0.0, [r_ytok])
                for _once in ([] if 'fox' in SKIP else [0]):
                    pkv, rpkv = tproj(O_CK, 512)
                    stg, r_stg = tmp("f_stg", [128, 512], F32)
                    cp("act", stg[:], pkv[:], [rpkv], [r_stg])
                    dma("sp", o_fk[l, t * 128:(t + 1) * 128, :], stg[:, 0:256], [r_stg], [], r_stg)
                    dma("sp", o_fv[l, t * 128:(t + 1) * 128, :], stg[:, 256:512], [r_stg], [], r_stg)
                    pcf, rpcf = b.ps()
                    for kc in range(8):
                        mm(pcf[:, 0:4], xT[:, kc, cs], Wi[:, kc, O_CF:O_CF + 4], kc == 0, False, [r_xT, rW], [rpcf])
                    mm(pcf[:, 0:4], ones_bf[0:1, :], bfb[:], False, True, [rc, rW], [rpcf])
                    fst, r_fst = tmp("f_st", [128, 16], F32)
                    act(fst[:, 0:4], pcf[:, 0:4], AF.Exp, [rpcf], [r_fst], scale=-1.0)
                    act(fst[:, 4:8], fst[:, 0:4], AF.Ln, [r_fst], [r_fst], bias=1.0, scale=1.0)
                    lfo, r_lfo = tmp("f_lfo", [128, 4], F32)
                    ts("dve", lfo[:], fst[:, 4:8], -1.0, None, MUL, None, [r_fst], [r_lfo])
                    dma("sp", o_flf[l, t * 128:(t + 1) * 128, :], lfo[:], [r_lfo], [], r_lfo)
                    pd, rpd = b.ps()
                    mm(pd[:, 0:4], triO[:], fst[:, 4:8], True, False, [rc, r_fst], [rpd])
                    mm(pd[:, 0:4], ones_f[:], Pacc[:], False, True, [rc, r_Pacc], [rpd])
                    act(fst[:, 8:12], pd[:, 0:4], AF.Exp, [rpd], [r_fst])
                    tt("dve", Pacc[:], Pacc[:], fst[:, 4:8], ADD, [r_Pacc, r_fst], [r_Pacc])
                    for h in range(4):
                        ts("dve", VA[:, t, h, 0:64], stg[:, 256 + 64 * h:320 + 64 * h], fst[:, 8 + h:9 + h], None, MUL, None,
                           [r_stg, r_fst], [r_VA[t]])
                    cp("dve", VA[:, t, :, 64], fst[:, 8:12], [r_fst], [r_VA[t]])
                    pov, rpov = b.psb[7]
                    for h in range(4):
                        if pipe and h == 2:
                            pipe.step()
                        hp = slice(64 * (h % 2), 64 * (h % 2) + 64)
                        hc = h // 2
                        for j0 in range(0, t + 1, 4):
                            js = list(range(j0, min(j0 + 4, t + 1)))
                            psc, rpsc = b.ps()
                            for j in js:
                                mm(psc[:, (j - j0) * 128:(j - j0 + 1) * 128], FK[hp, hc, j * 128:(j + 1) * 128], fq[hp, hc, cs],
                                   True, True, [r_FK[j // 4], r_fq], [rpsc])
                            pT, r_pT = tmp("f_pT", [128, 512], BF16, 3)
                            n = len(js) * 128
                            act(pT[:, 0:n], psc[:, 0:n], AF.Exp, [rpsc], [r_pT], scale=0.125)
                            if js[-1] == t:
                                sl = slice((t - j0) * 128, (t - j0 + 1) * 128)
                                tt("dve", pT[:, sl], pT[:, sl], mask_bf[:], MUL, [r_pT, rc], [r_pT])
                            for j in js:
                                mm(pov[:, 65 * h:65 * h + 65], pT[:, (j - j0) * 128:(j - j0 + 1) * 128], VA[:, j, h, :],
                                   j == 0, j == t, [r_pT, r_VA[j]], [rpov])
                    rd, r_rd = tmp("f_rd", [128, 4], F32)
                    P.op("dve", lambda e, rd=rd, pov=pov: e.reciprocal(
                        out=rd[:], in_=pov[:, 0:260].rearrange("p (h e) -> p h e", h=4)[:, :, 64]), r=[rpov], w=[r_rd])
                    for h in range(4):
                        ts("dve", ytok[:, 512 + 64 * h:576 + 64 * h], pov[:, 65 * h:65 * h + 64], rd[:, h:h + 1], None, MUL, None,
                           [rpov, r_rd], [r_ytok])

                if dbg and l == 0:
                    dma("pool", d_y[t * 128:(t + 1) * 128, :], ytok[:], [r_ytok], [], r_ytok)
                    if ti == 0:
                        for c in range(2):
                            dma("pool", d_yd[c * 128:(c + 1) * 128, c0:c0 + 512], ydT[:, c, :], [r_ydT], [], r_ydT)
                if pipe:
                    pipe.step()
                pty, rpty = b.ps()
                ptyb = pty[:].bitcast(BF16)
                for c in range(6):
                    tr(ptyb[:, c * 128:(c + 1) * 128], ytok[:, c * 128:(c + 1) * 128], ident_bf[:], [r_ytok, rc], [rpty])
                yT, r_yT = tmp("yT", [128, 6, 128], BF16)
                cp("act", yT[:], ptyb[:, 0:768].rearrange("p (k n) -> p k n", k=6), [rpty], [r_yT])
                ps2, rps2 = [], []
                for hf in range(2):
                    pm_, rpm_ = b.ps()
                    for kc in range(8):
                        lh = yT[:, kc, :] if kc < 6 else ydT[:, kc - 6, cs]
                        mm(pm_[:], lh, Wo[:, kc, hf * 512:(hf + 1) * 512], kc == 0, kc == 7, [r_yT, r_ydT, rW], [rpm_])
                    ps2.append(pm_)
                    rps2.append(rpm_)
                layer_norm(t, ps2, rps2, l1g, l1b, None)
            if pipe:
                pipe.drain()
        if dbg and l == 0:
            for t in range(NT):
                dma("pool", d_x1[t * 128:(t + 1) * 128, :], R[:, t, :], [r_R[t]], [], r_R[t])
        if 'sample' not in SKIP:
            sample_mixer(l)
        for h in range(4):
            dma("sp", o_gla[l, h], Sf[32 * h:32 * h + 32, 64 * h:64 * h + 64], [r_S], [], r_S)

        for _once in ([] if 'ffn' in SKIP else [0]):
            if l == 0:
                hal = [b.sb([128, NFC, 2], F32, "hal%d_%d" % (l, i)) for i in range(2)]
                r_hal = [Res("hal0"), Res("hal1")]
                Wif = Wi[:].rearrange("p k n -> p (k n)")
                hT = Wif[:, 0:NFC * 512].rearrange("p (c n) -> p c n", c=NFC)
                r_hT = Res("hT")
                wus = [(Wif[:, 10752:18944].rearrange("p (k n) -> p k n", k=8), Res("wu0")), (Wo[:], Res("wu1"))]
                wds = [(VAflat[:, i * 1024:(i + 1) * 1024], Res("wd%d" % i)) for i in range(3)]
                gxs = [(FKf[:, i * 514:(i + 1) * 514], Res("gx%d" % i)) for i in range(2)]
                gas = [(qTf[:], Res("ga0")), (kTf[:], Res("ga1"))]
                ali = [r_hT] + [x[1] for x in wus + wds + gxs + gas]
            ALLR.extend(ali)
            cnt_f = [0, 0, 0, 0]
            P.op("pool", lambda e: e.memset(dummy[:], 0.0), w=[rW, r_dummy] + ali)
            for (tl, src) in ((l2g, ln2_g), (l2b, ln2_b)):
                dma("sp", tl[:], src[l:l + 1, :].broadcast_to([128, D]), [], [rW], rW)
            mset("pool", hal[1][:], 0.0, [r_hal[1]])
            for blk in range(4):
                make_xT(blk)
                hcur, r_hcur = hal[blk % 2], r_hal[blk % 2]
                hprev, r_hprev = hal[(blk + 1) % 2], r_hal[(blk + 1) % 2]
                for fc in range(NFC):
                    wu, r_wu, og, ov = wu_chunk(l, fc, wus, cnt_f)
                    pg, rpg = b.ps()
                    for kc in range(8):
                        mm(pg[:], wu[:, kc, og:og + 128], xT[:, kc, :], kc == 0, kc == 7, [r_wu, r_xT], [rpg])
                    pv, rpv = b.ps()
                    for kc in range(8):
                        mm(pv[:], wu[:, kc, ov:ov + 128], xT[:, kc, :], kc == 0, kc == 7, [r_wu, r_xT], [rpv])
                    gx, r_gx = gxs[cnt_f[1] % 2]
                    cnt_f[1] += 1
                    cp("dve", gx[:, 0:2], hprev[:, fc, :], [r_hprev], [r_gx])
                    cp("act", gx[:, 2:514], pg[:], [rpg], [r_gx])
                    cp("dve", hcur[:, fc, :], gx[:, 512:514], [r_gx], [r_hcur])
                    ga, r_ga = gas[cnt_f[2] % 2]
                    cnt_f[2] += 1
                    act(ga[:], gx[:, 0:512], AF.Identity, [r_gx, rW], [r_ga], scale=fw[:, fc, 0:1])
                    stt("dve", ga[:], gx[:, 1:513], fw[:, fc, 1:2], ga[:], MUL, ADD, [r_gx, rW, r_ga], [r_ga])
                    stt("dve", ga[:], gx[:, 2:514], fw[:, fc, 2:3], ga[:], MUL, ADD, [r_gx, rW, r_ga], [r_ga])
                    act(ga[:], ga[:], AF.Silu, [r_ga, rW], [r_ga], bias=fb[:, fc:fc + 1], scale=1.0)
                    tt("dve", hT[:, fc, :], ga[:], pv[:], MUL, [r_ga, rpv], [r_hT])
                if blk == 3:
                    for c in range(NFC):
                        dma("sp", o_ffc[l][:, c * 128:(c + 1) * 128].rearrange("j p -> p j"), hcur[:, c, :], [r_hcur], [],
                            r_hcur, allow_slow_non_contiguous=True)
                if dbg and l == 0 and blk == 0:
                    for fc in range(NFC):
                        dma("pool", d_h[fc * 128:(fc + 1) * 128, :], hT[:, fc, :], [r_hT], [], r_hT)
                banks = [b.ps() for _ in range(6)] + [b.psb[6], b.psb[7]]
                for fc in range(NFC):
                    wd, r_wd = wds[cnt_f[3] % 3]
                    cnt_f[3] += 1
                    dma("pool", wd[:], w_dn[l, fc * 128:(fc + 1) * 128, :], [], [r_wd], r_wd)
                    for ti in range(4):
                        for hf in range(2):
                            pb, rpb = banks[ti * 2 + hf]
                            mm(pb[:], hT[:, fc, ti * 128:(ti + 1) * 128], wd[:, hf * 512:(hf + 1) * 512], fc == 0, fc == NFC - 1,
                               [r_hT, r_wd], [rpb])
                for ti in range(4):
                    t = blk * 4 + ti
                    layer_norm(t, [banks[ti * 2][0], banks[ti * 2 + 1][0]], [banks[ti * 2][1], banks[ti * 2 + 1][1]], l2g, l2b,
                               o_y[t * 128:(t + 1) * 128, :] if last else None)
        if 'ffn' not in SKIP:
            if 'sample' not in SKIP:
                sample_ffn(l, last, wus, wds, cnt_f)
            P.op("pool", lambda e: e.memset(dummy[:], 0.0), w=[rW, r_dummy] + ali)
        if dbg and l == 0 and not last:
            for t in range(NT):
                dma("pool", d_x2[t * 128:(t + 1) * 128, :], R[:, t, :], [r_R[t]], [], r_R[t])
    P.emit(stack)
    return nc, stack


IN_NAMES = ["w_in", "w_o", "gla_w_a", "gla_b_a", "gla_norm_g", "sgu_ln_g", "sgu_ln_b", "sgu_w", "fox_b_f",
            "conv_w", "conv_b", "conv_norm_g", "conv_norm_b", "ln1_g", "ln1_b", "ln2_g", "ln2_b",
            "ffn_conv_w", "ffn_conv_b"]


def run(inputs, nlayers=DEPTH, cores=8, dbg=False, has_cache=True, trace=False):
    nc, stack = build(nlayers=nlayers, dbg=dbg, has_cache=has_cache)
    with stack:
        pass
    in_maps = []
    for c in range(cores):
        m = {"xp": np.ascontiguousarray(inputs["x_prompt"][c]),
             "xs": np.ascontiguousarray(inputs["x_sample"][4 * c:4 * c + 4]).reshape(NS, D),
             "st_gla": np.ascontiguousarray(inputs["state_gla"][:, 4 * c:4 * c + 4]),
             "st_conv": np.ascontiguousarray(inputs["state_conv"][:, 4 * c:4 * c + 4]),
             "st_ffc": np.ascontiguousarray(inputs["state_ffn_conv"][:, 4 * c:4 * c + 4])}
        if has_cache:
            m["pt"] = np.ascontiguousarray(inputs["page_table"][4 * c:4 * c + 4]).astype(np.int32)
            m["ck"] = np.ascontiguousarray(inputs["cache_fox_k"]).reshape(DEPTH, NPOOL, 128, 256)
            m["cv"] = np.ascontiguousarray(inputs["cache_fox_v"]).reshape(DEPTH, NPOOL, 128, 256)
            m["clf"] = np.ascontiguousarray(inputs["cache_fox_logf"])
        for k in IN_NAMES:
            m[k] = np.ascontiguousarray(inputs[k])
        m["w_up"] = np.ascontiguousarray(inputs["ffn_w_up"])
        m["w_dn"] = np.ascontiguousarray(inputs["ffn_w_down"])
        m["sgu_b"] = np.ascontiguousarray(inputs["sgu_b"])
        in_maps.append(m)
    if trace:
        res = run_bass_kernel_spmd(nc, in_maps, core_ids=list(range(cores)), trace=True)
        print("EXEC_TIME_NS", res.exec_time_ns)
        return res.results
    res = run_bass_kernel_spmd(nc, in_maps, core_ids=list(range(cores)))
    return res.results


def kernel(**inputs):
    rs = run(inputs)
    f = np.float32
    y_p = np.stack([r["o_y"] for r in rs]).astype(f)
    p_fk = np.stack([r["o_fk"] for r in rs], axis=1).reshape(DEPTH, 8, SEQ, 4, 64).astype(f)
    p_fv = np.stack([r["o_fv"] for r in rs], axis=1).reshape(DEPTH, 8, SEQ, 4, 64).astype(f)
    p_lf = np.stack([r["o_flf"] for r in rs], axis=1).astype(f)
    p_gla = np.stack([r["o_gla"] for r in rs], axis=1).astype(f)
    p_conv = np.stack([r["o_conv"] for r in rs], axis=1).astype(f)
    p_ffc = np.stack([r["o_ffc"] for r in rs], axis=1).astype(f)
    y_s = np.concatenate([r["o_ys"] for r in rs]).reshape(32, 4, D).astype(f)

    def cat(nm, shp):
        return np.concatenate([r[nm].reshape((DEPTH, 4) + shp) for r in rs], axis=1).astype(f)
    s_fk = cat("o_sfk", (4, 4, 64))
    s_fv = cat("o_sfv", (4, 4, 64))
    s_lf = cat("o_sflf", (4, 4))
    s_gla = cat("o_sgla", (4, 32, 64))
    s_conv = cat("o_sconv", (30, 256))
    s_ffc = cat("o_sffc", (2, DFF))
    s_sgu = cat("o_ssgu", (4, 256))
    return (y_p, y_s, p_fk, p_fv, p_lf, p_gla, p_conv, p_ffc, s_fk, s_fv, s_lf, s_gla, s_conv, s_ffc, s_sgu)
```

```python
import numpy as np
import os
GSTOP = int(os.environ.get('GSTOP', '99'))
SKIP = set(os.environ.get('SKIP', '').split(','))
from contextlib import ExitStack
import concourse.bass as bass
import concourse.mybir as mybir
from concourse.bass_utils import run_bass_kernel_spmd

F32 = mybir.dt.float32
BF16 = mybir.dt.bfloat16
I32 = mybir.dt.int32
AF = mybir.ActivationFunctionType
ALU = mybir.AluOpType
AX = mybir.AxisListType

D = 1024
SEQ = 2048
NT = 16
DEPTH = 2
DIN = 2580
DFF = 2688
NFC = 21
EPS = 1e-5
ALPHA = (2 * DEPTH) ** 0.25
NS = 16

O_AQ, O_AK, O_AV, O_AG, O_ALR, O_BU, O_BV, O_CQ, O_CK, O_CV, O_CF, O_DIN = (
    0, 128, 256, 512, 768, 784, 1040, 1296, 1552, 1808, 2064, 2068)


class Res:
    __slots__ = ("name", "lw", "rd", "excl")

    def __init__(self, name="", excl=False):
        self.name = name
        self.lw = None
        self.rd = []
        self.excl = excl


class Op:
    __slots__ = ("eng", "fn", "raw", "oth", "pos", "dma", "key", "inc", "val", "users")


class Prog:
    ENGS = ("pe", "act", "dve", "pool", "sp")

    def __init__(self, nc):
        self.nc = nc
        self.by = {e: [] for e in self.ENGS}
        self.dcount = {}
        self.all_dma = []

    def op(self, eng, fn, r=(), w=(), dma=False, key=None):
        o = Op()
        o.eng, o.fn, o.dma, o.key = eng, fn, dma, key
        o.raw, o.oth = set(), set()
        o.inc, o.val, o.users = False, 0, 0
        o.pos = len(self.by[eng])
        for x in r:
            if x.lw is not None:
                o.raw.add(x.lw)
            if x.excl:
                for q in x.rd:
                    if q.eng != eng:
                        o.oth.add(q)
        for x in w:
            if x.lw is not None:
                if not (dma and x.lw.dma and x.lw.key is key and x.lw.eng == eng):
                    o.oth.add(x.lw)
            for q in x.rd:
                o.oth.add(q)
        for x in r:
            x.rd.append(o)
        for x in w:
            x.lw = o
            x.rd = []
        if dma:
            assert key is not None
            c = self.dcount.get(key, 0) + 16
            self.dcount[key] = c
            o.val = c
            self.all_dma.append(o)
        self.by[eng].append(o)
        return o

    def _needs(self, p, q):
        if p.dma:
            return True
        if p.eng != q.eng:
            return True
        if p.eng == "pe":
            return False
        return (p in q.raw) and (q.pos - p.pos <= 3)

    def emit(self, stack):
        nc = self.nc
        fin = self.op("sp", None)
        for d in self.all_dma:
            fin.oth.add(d)
        for e in self.ENGS:
            for q in self.by[e]:
                for p in (q.raw | q.oth):
                    if p is q:
                        continue
                    if self._needs(p, q):
                        p.inc = True
        esem = {}
        for e in self.ENGS:
            esem[e] = stack.enter_context(nc.semaphore("e_" + e))
            c = 0
            for o in self.by[e]:
                if o.inc and not o.dma:
                    c += 1
                    o.val = c
        dsem = {}
        for k in self.dcount:
            dsem[k] = stack.enter_context(nc.semaphore("d%d" % len(dsem)))
        print("sems used", len(dsem) + 5, "ops", {e: len(self.by[e]) for e in self.ENGS})

        def run(e, eng):
            waited = {}
            for q in self.by[e]:
                need = {}
                for p in (q.raw | q.oth):
                    if p is q or not self._needs(p, q):
                        continue
                    s = dsem[p.key] if p.dma else esem[p.eng]
                    if need.get(s, (0,))[0] < p.val:
                        need[s] = (p.val, s)
                for s, (v, _) in need.items():
                    if waited.get(s, 0) < v:
                        eng.wait_ge(s, v)
                        waited[s] = v
                if q.fn is None:
                    continue
                ins = q.fn(eng)
                if q.dma:
                    ins.then_inc(dsem[q.key], 16)
                elif q.inc:
                    ins.then_inc(esem[e], 1)

        with nc.Block() as block:
            @block.tensor
            def _(eng):
                run("pe", eng)

            @block.scalar
            def _(eng):
                run("act", eng)

            @block.vector
            def _(eng):
                run("dve", eng)

            @block.gpsimd
            def _(eng):
                run("pool", eng)

            @block.sync
            def _(eng):
                run("sp", eng)


class B:
    def __init__(self, nc, stack):
        self.nc, self.stack = nc, stack
        self.P = Prog(nc)
        self.nps = 0
        self.psb = []
        for i in range(8):
            t = stack.enter_context(nc.psum_tensor("ps%d" % i, [128, 512], F32))
            self.psb.append((t, Res("ps%d" % i, excl=True)))
        self.cnt = 0

    def sb(self, shape, dt, name=None):
        self.cnt += 1
        t = self.stack.enter_context(self.nc.sbuf_tensor(name or ("t%d" % self.cnt), list(shape), dt))
        return t

    def ps(self):
        t, r = self.psb[self.nps % 6]
        self.nps += 1
        return t, r


NPOOL = 2560


def build(has_cache=True, dbg=False, nlayers=DEPTH):
    nc = bass.Bass("TRN2", target_bir_lowering=False)
    stack = ExitStack()
    b = B(nc, stack)
    P = b.P

    def din(name, shape, dt=F32):
        return nc.dram_tensor(name, list(shape), dt, kind="ExternalInput").ap()

    def dout(name, shape, dt=F32):
        return nc.dram_tensor(name, list(shape), dt, kind="ExternalOutput").ap()

    def mm(out, lhsT, rhs, st, sp, r, w, **kw):
        P.op("pe", lambda e: e.matmul(out, lhsT, rhs, start=st, stop=sp, **kw), r=r, w=w)

    def tr(out, in_, ident, r, w):
        P.op("pe", lambda e: e.transpose(out=out, in_=in_, identity=ident), r=r, w=w)

    def act(out, in_, func, r, w, **kw):
        P.op("act", lambda e: e.activation(out=out, in_=in_, func=func, **kw), r=r, w=w)

    def tt(eng, out, in0, in1, op, r, w):
        P.op(eng, lambda e: e.tensor_tensor(out=out, in0=in0, in1=in1, op=op), r=r, w=w)

    def ts(eng, out, in0, s1, s2, op0, op1, r, w):
        if op1 == ALU.pow:
            act(out, in0, AF.Ln, r, w, bias=float(s1), scale=1.0)
            act(out, out, AF.Exp, list(r) + list(w), w, scale=float(s2))
            return
        if op0 == ALU.pow:
            act(out, in0, AF.Ln, r, w)
            act(out, out, AF.Exp, list(r) + list(w), w, scale=float(s1))
            return
        if s2 is None:
            P.op(eng, lambda e: e.tensor_scalar(out=out, in0=in0, scalar1=s1, scalar2=None, op0=op0), r=r, w=w)
        else:
            P.op(eng, lambda e: e.tensor_scalar(out=out, in0=in0, scalar1=s1, scalar2=s2, op0=op0, op1=op1), r=r, w=w)

    def stt(eng, out, in0, sc, in1, op0, op1, r, w):
        P.op("dve", lambda e: e.scalar_tensor_tensor(out=out, in0=in0, scalar=sc, in1=in1, op0=op0, op1=op1), r=r, w=w)

    def cp(eng, out, in_, r, w):
        if eng == "act":
            P.op(eng, lambda e: e.copy(out=out, in_=in_), r=r, w=w)
        else:
            P.op(eng, lambda e: e.tensor_copy(out=out, in_=in_), r=r, w=w)

    def rsum(eng, out, in_, r, w):
        P.op(eng, lambda e: e.reduce_sum(out=out, in_=in_, axis=AX.X), r=r, w=w)

    def mset(eng, ap, v, w):
        P.op(eng, lambda e: e.memset(ap, v), w=w)

    def asel(out, in_, pattern, cmp_, fill, base, cm, r, w):
        P.op("pool", lambda e: e.affine_select(out=out, in_=in_, pattern=pattern, compare_op=cmp_, fill=fill,
                                               base=base, channel_multiplier=cm), r=r, w=w)

    def dma(eng, out, in_, r, w, key, **kw):
        P.op(eng, lambda e: e.dma_start(out=out, in_=in_, **kw), r=r, w=w, dma=True, key=key)

    MUL, ADD, SUB, POW = ALU.mult, ALU.add, ALU.subtract, ALU.pow

    xp = din("xp", [SEQ, D])
    w_in = din("w_in", [DEPTH, D, DIN])
    w_o = din("w_o", [DEPTH, D, D])
    w_up = din("w_up", [DEPTH, D, 2 * DFF])
    w_dn = din("w_dn", [DEPTH, DFF, D])
    gla_w_a = din("gla_w_a", [DEPTH, 16, 128])
    gla_b_a = din("gla_b_a", [DEPTH, 128])
    gla_g = din("gla_norm_g", [DEPTH, 256])
    sgu_g = din("sgu_ln_g", [DEPTH, 256])
    sgu_bb = din("sgu_ln_b", [DEPTH, 256])
    sgu_w = din("sgu_w", [DEPTH, 4, 128, 128])
    sgu_bs = din("sgu_b", [DEPTH, 4, 128])
    fox_bf = din("fox_b_f", [DEPTH, 4])
    conv_w = din("conv_w", [DEPTH, 31, 256])
    conv_b = din("conv_b", [DEPTH, 256])
    cn_g = din("conv_norm_g", [DEPTH, 256])
    cn_b = din("conv_norm_b", [DEPTH, 256])
    ln1_g = din("ln1_g", [DEPTH, D])
    ln1_b = din("ln1_b", [DEPTH, D])
    ln2_g = din("ln2_g", [DEPTH, D])
    ln2_b = din("ln2_b", [DEPTH, D])
    fcw = din("ffn_conv_w", [DEPTH, 3, DFF])
    fcb = din("ffn_conv_b", [DEPTH, DFF])

    o_y = dout("o_y", [SEQ, D])
    o_fk = dout("o_fk", [DEPTH, SEQ, 256])
    o_fv = dout("o_fv", [DEPTH, SEQ, 256])
    o_flf = dout("o_flf", [DEPTH, SEQ, 4])
    o_gla = dout("o_gla", [DEPTH, 4, 32, 64])
    o_conv = dout("o_conv", [DEPTH, 30, 256])
    o_ffc = dout("o_ffc", [DEPTH, 2, DFF])
    if has_cache:
        pt_d = din("pt", [4, 64], I32)
        ck = din("ck", [DEPTH, NPOOL, 128, 256])
        cv = din("cv", [DEPTH, NPOOL, 128, 256])
        clf = din("clf", [DEPTH, NPOOL, 128, 4])
    xs_d = din("xs", [NS, D])
    st_gla = din("st_gla", [DEPTH, 4, 4, 32, 64])
    st_conv = din("st_conv", [DEPTH, 4, 30, 256])
    st_ffc = din("st_ffc", [DEPTH, 4, 2, DFF])
    o_ys = dout("o_ys", [NS, D])
    o_sfk = dout("o_sfk", [DEPTH, NS, 256])
    o_sfv = dout("o_sfv", [DEPTH, NS, 256])
    o_sflf = dout("o_sflf", [DEPTH, NS, 4])
    o_sgla = dout("o_sgla", [DEPTH, 4, 4, 32, 64])
    o_sconv = dout("o_sconv", [DEPTH, 4, 30, 256])
    o_sffc = dout("o_sffc", [DEPTH, 4, 2, DFF])
    o_ssgu = dout("o_ssgu", [DEPTH, NS, 256])
    if dbg:
        d_y = dout("d_y", [SEQ, 768])
        d_yd = dout("d_yd", [256, SEQ])
        d_x1 = dout("d_x1", [SEQ, D])
        d_h = dout("d_h", [DFF, 512])
        d_x2 = dout("d_x2", [SEQ, D])

    rc = Res("const")
    ident_f = b.sb([128, 128], F32, "ident_f")
    ident_bf = b.sb([128, 128], BF16, "ident_bf")
    revM = b.sb([128, 128], F32, "revM")
    triO = b.sb([128, 128], F32, "triO")
    ones_f = b.sb([128, 128], F32, "ones_f")
    ones_bf = b.sb([1, 128], BF16, "ones_bf")
    mask_bf = b.sb([128, 128], BF16, "mask_bf")
    mask4 = b.sb([128, 512], BF16, "mask4")
    blk64 = b.sb([128, 128], F32, "blk64")
    blkmask = b.sb([128, 256], F32, "blkmask")
    hm = b.sb([128, 4], F32, "hm")

    mset("pool", ident_f[:], 0.0, [rc])
    asel(ident_f[:], ident_f[:], [[-1, 128]], ALU.not_equal, 1.0, 0, 1, [rc], [rc])
    cp("dve", ident_bf[:], ident_f[:], [rc], [rc])
    mset("pool", revM[:], -1.0 / 16.0, [rc])
    asel(revM[:], revM[:], [[-1, 128]], ALU.is_gt, 0.0, 0, 1, [rc], [rc])
    mset("pool", triO[:], 1.0, [rc])
    asel(triO[:], triO[:], [[1, 128]], ALU.is_ge, 0.0, 0, -1, [rc], [rc])
    mset("pool", ones_f[:], 1.0, [rc])
    mset("pool", ones_bf[:], 1.0, [rc])
    cp("dve", mask_bf[:], triO[:], [rc], [rc])
    triMb = b.sb([128, 128], BF16, "triMb")
    revMb = b.sb([128, 128], BF16, "revMb")
    ts("dve", triMb[:], triO[:], -1.0 / 16.0, None, MUL, None, [rc], [rc])
    cp("dve", revMb[:], revM[:], [rc], [rc])
    for h in range(4):
        cp("dve", mask4[:, h * 128:(h + 1) * 128], triO[:], [rc], [rc])
    mset("pool", blk64[:], 0.0, [rc])
    mset("pool", blk64[0:64, 0:64], 1.0 / 64.0, [rc])
    mset("pool", blk64[64:128, 64:128], 1.0 / 64.0, [rc])
    mset("pool", blkmask[:], 1.0, [rc])
    mset("pool", hm[:], 1.0, [rc])
    for h in range(4):
        v = blkmask[:, 64 * h:64 * h + 64]
        asel(v, v, [[0, 64]], ALU.is_ge, 0.0, -32 * h, 1, [rc], [rc])
        asel(v, v, [[0, 64]], ALU.is_ge, 0.0, 32 * h + 31, -1, [rc], [rc])
        v = hm[:, h:h + 1]
        asel(v, v, [[0, 1]], ALU.is_ge, 0.0, -32 * h, 1, [rc], [rc])
        asel(v, v, [[0, 1]], ALU.is_ge, 0.0, 32 * h + 31, -1, [rc], [rc])

    R = b.sb([128, NT, D], BF16, "R")
    r_R = [Res("R%d" % t) for t in range(NT)]
    Wi = b.sb([128, 8, DIN], BF16, "Wi")
    Wo = b.sb([128, 8, D], BF16, "Wo")
    rW = Res("W")
    Wa = b.sb([16, 128], BF16, "Wa")
    ba = b.sb([1, 128], BF16, "ba")
    bfb = b.sb([1, 4], BF16, "bfb")
    glag = b.sb([128, 256], F32, "glag")
    sgg = b.sb([128, 256], F32, "sgg")
    sgb = b.sb([128, 256], F32, "sgb")
    l1g = b.sb([128, D], F32, "l1g")
    l1b = b.sb([128, D], F32, "l1b")
    l2g, l2b = l1g, l1b
    dummy = b.sb([128, 2], F32, "dummy_t")
    r_dummy = Res("dummy")
    cw = b.sb([128, 2, 31], F32, "cw")
    cb = b.sb([128, 2], F32, "cb")
    cng = b.sb([128, 2], F32, "cng")
    cnb = b.sb([128, 2], F32, "cnb")
    fw = b.sb([128, NFC, 3], F32, "fw")
    fb = b.sb([128, NFC], F32, "fb")
    bs4 = b.sb([4, 128], BF16, "bs4")
    bsT = b.sb([128, 4], F32, "bsT")
    WT = b.sb([128, 4, 128], BF16, "WT")

    xT = b.sb([128, 8, 512], BF16, "xT")
    r_xT = Res("xT")
    xTf = xT[:].rearrange("p k n -> p (k n)").bitcast(F32)
    sw = xTf[:, 0:512].rearrange("p (g s) -> p g s", g=4)
    FK = b.sb([128, 2, SEQ], BF16, "FK")
    r_FK = [Res("FK%d" % i) for i in range(4)]
    VAflat = b.sb([128, NT * 4 * 65], BF16, "VA")
    VA = VAflat[:].rearrange("p (t h e) -> p t h e", t=NT, h=4)
    r_VA = [Res("VA%d" % t) for t in range(NT)]
    fq = b.sb([128, 2, 512], BF16, "fq")
    r_fq = Res("fq")
    qTf = b.sb([128, 512], F32, "qTf")
    kTf = b.sb([128, 512], F32, "kTf")
    r_qk = Res("qk")
    alr = b.sb([16, 512], BF16, "alr")
    r_alr = Res("alr")
    gext = [b.sb([128, 2, 542], BF16, "gext%d" % i) for i in range(2)]
    r_gext = [Res("gext%d" % i) for i in range(2)]
    glast = b.sb([128, 2, 30], F32, "glast")
    r_glast = Res("glast")
    ydT = b.sb([128, 2, 512], BF16, "ydT")
    r_ydT = Res("ydT")
    Sf = b.sb([128, 256], F32, "Sf")
    Sb = b.sb([128, 256], BF16, "Sb")
    r_S = Res("S")
    Pacc = b.sb([128, 4], F32, "Pacc")
    r_Pacc = Res("Pacc")

    nbuf = {}

    def tmp(name, shape, dt, n=1):
        if name not in nbuf:
            nbuf[name] = [[(b.sb(shape, dt, "%s_%d" % (name, i)), Res(name)) for i in range(n)], 0]
        lst = nbuf[name]
        t, r = lst[0][lst[1] % n]
        lst[1] += 1
        return t, r


    Rs = b.sb([NS, D], BF16, "Rs")
    r_Rs = Res("Rs")
    xTs = b.sb([128, 8, NS], BF16, "xTs")
    r_xTs = Res("xTs")
    sb16 = b.sb([NS, NS], F32, "sb16")
    maskS = b.sb([NS, NS], F32, "maskS")
    maskS4 = b.sb([NS, 64], F32, "maskS4")
    maskN = b.sb([NS, 64], F32, "maskN")
    triSb = b.sb([NS, NS], BF16, "triSb")
    revSb = b.sb([NS, NS], BF16, "revSb")
    bm = b.sb([NS, 4], F32, "bm")
    mset("pool", sb16[:], 1.0, [rc])
    mset("pool", bm[:], 1.0, [rc])
    for bb in range(4):
        v = sb16[:, 4 * bb:4 * bb + 4]
        asel(v, v, [[0, 4]], ALU.is_ge, 0.0, -4 * bb, 1, [rc], [rc])
        asel(v, v, [[0, 4]], ALU.is_ge, 0.0, 4 * bb + 3, -1, [rc], [rc])
        v = bm[:, bb:bb + 1]
        asel(v, v, [[0, 1]], ALU.is_ge, 0.0, -4 * bb, 1, [rc], [rc])
        asel(v, v, [[0, 1]], ALU.is_ge, 0.0, 4 * bb + 3, -1, [rc], [rc])
    tt("dve", maskS[:], sb16[:], triO[0:NS, 0:NS], MUL, [rc], [rc])
    ts("dve", triSb[:], maskS[:], -1.0 / 16.0, None, MUL, None, [rc], [rc])
    tt("dve", sb16[:], sb16[:], maskS[:], SUB, [rc], [rc])
    ts("dve", revSb[:], sb16[:], -1.0 / 16.0, None, MUL, None, [rc], [rc])
    for h in range(4):
        cp("dve", maskS4[:, h * 16:(h + 1) * 16], maskS[:], [rc], [rc])
        cp("dve", maskN[:].rearrange("p (b h q) -> p b h q", b=4, h=4)[:, :, h, :],
           maskS[:].rearrange("p (b q) -> p b q", b=4), [rc], [rc])
    dma("pool", Rs[:], xs_d[:, :], [], [r_Rs], r_Rs)
    if has_cache:
        idxr = b.sb([128, 256], I32, "idxr")
        idx64 = b.sb([128, 4], I32, "idx64")
        r_idx = Res("idx")
        ia = xTf[:, 0:256].bitcast(I32)
        io = xTf[:, 256:512].bitcast(I32)
        for bb in range(4):
            dma("sp", ia[:, bb * 64:(bb + 1) * 64], pt_d[bb:bb + 1, :].broadcast_to([128, 64]), [], [r_xT], r_xT)
        P.op("pool", lambda e: e.iota(io, pattern=[[0, 256]], base=0, channel_multiplier=1), w=[r_xT])
        stt("dve", idxr[:], ia, 128, io, MUL, ADD, [r_xT], [r_idx])
        mset("pool", idx64[:], 0, [r_idx])
        dma("sp", idx64[0:64, :], pt_d.rearrange("b j -> j b"), [], [r_idx], r_idx, allow_slow_non_contiguous=True)
        ckf = ck.rearrange("l n r c -> (l n r) c")
        cvf = cv.rearrange("l n r c -> (l n r) c")
        clff = clf.rearrange("l n r h -> (l n) (r h)")
    fpc = [0]

    ALLR = []

    def switch():
        P.op("pool", lambda e: e.memset(dummy[:], 0.0), w=[rW, r_dummy, r_xT, r_qk, r_fq, r_ydT, r_alr] + r_gext + r_FK + r_VA + ALLR)

    class Arena:
        def __init__(self, aps):
            self.aps = aps
            self.i, self.pos = 0, 0

        def take(self, parts, cols, dt, name):
            w = cols if dt == F32 else (cols + 1) // 2
            while self.pos + w > self.aps[self.i][1]:
                self.i += 1
                self.pos = 0
            a = self.aps[self.i][0][0:parts, self.pos:self.pos + w]
            self.pos += w
            r = Res(name)
            ALLR.append(r)
            if dt != F32:
                a = a.bitcast(BF16)[:, 0:cols]
            return a, r

    FKf = FK[:].rearrange("p c n -> p (c n)").bitcast(F32)
    VAf = VAflat[:].bitcast(F32)
    qTff, kTff = qTf[:], kTf[:]
    fqf = fq[:].rearrange("p c n -> p (c n)").bitcast(F32)
    ydf = ydT[:].rearrange("p c n -> p (c n)").bitcast(F32)
    g0f = gext[0][:].rearrange("p c n -> p (c n)").bitcast(F32)
    g1f = gext[1][:].rearrange("p c n -> p (c n)").bitcast(F32)

    GP = 4
    fpd = {}
    if has_cache:
        fpd["Kg"] = [(b.sb([128, GP * 256], BF16, "Kg%d" % i), Res("Kg%d" % i)) for i in range(2)]
        fpd["Vg"] = [(b.sb([128, GP * 258], BF16, "Vg%d" % i), Res("Vg%d" % i)) for i in range(2)]
        fpd["KT"] = [(b.sb([128, 2 * GP * 128], BF16, "KTp%d" % i), Res("KTp%d" % i)) for i in range(2)]
        fpd["t"] = [(b.sb([128, GP * 16], F32, "fp_t%d" % i), Res("fp_t%d" % i)) for i in range(2)]
        fpd["pTb"] = [(b.sb([128, GP * 16], BF16, "fp_pTb%d" % i), Res("fp_pTb%d" % i)) for i in range(2)]
        fpd["tot"] = (b.sb([64, 4], F32, "fp_tot"), Res("fp_tot"))
        fpd["opast"] = (b.sb([128, 257], F32, "opast"), Res("opast"))
        fpd["hsq"] = (b.sb([NS, 258], F32, "hsq"), Res("hsq"))
        fpd["fqs"] = (b.sb([128, 32], BF16, "fqs_p"), Res("fqs_p"))
        fpd["qblk"] = (b.sb([128, 64], BF16, "qblk_p"), Res("qblk_p"))
        for i in range(2):
            v, r = fpd["Vg"][i]
            mset("pool", v[:].rearrange("p (g c) -> p g c", g=GP)[:, :, 256:258], 1.0, [r])

    def sample_q_prework(l):
        hsq, r_hsq = fpd["hsq"]
        fqs, r_fqs = fpd["fqs"]
        qblk, r_qblk = fpd["qblk"]
        pt, rp = b.ps()
        ptb = pt[:].bitcast(BF16)
        for kc in range(8):
            tr(ptb[:, kc * NS:(kc + 1) * NS], Rs[:, kc * 128:(kc + 1) * 128], ident_bf[0:NS, 0:NS], [r_Rs, rc], [rp])
        cp("act", xTs[:], ptb[:, 0:8 * NS].rearrange("p (k n) -> p k n", k=8), [rp], [r_xTs])
        pq, rpq = b.ps()
        for kc in range(8):
            mm(pq[0:NS, 0:256], xTs[:, kc, :], Wi[:, kc, O_CQ:O_CQ + 256], kc == 0, kc == 7, [r_xTs, rW], [rpq])
        cp("act", hsq[:, 0:256], pq[0:NS, 0:256], [rpq], [r_hsq])
        pt2, rp2 = b.ps()
        for c in range(2):
            tr(pt2[:, c * 16:(c + 1) * 16], hsq[:, c * 128:(c + 1) * 128], ident_f[0:NS, 0:NS], [r_hsq, rc], [rp2])
        cp("act", fqs[:], pt2[:, 0:32], [rp2], [r_fqs])
        mset("pool", qblk[:], 0.0, [r_qblk])
        qb4 = qblk[:].rearrange("p (c b m) -> p c b m", c=2, b=4)
        for h2 in range(2):
            cp("dve", qb4[64 * h2:64 * h2 + 64, :, :, 4 * h2:4 * h2 + 4],
               fqs[64 * h2:64 * h2 + 64, 0:32].rearrange("p (c b q) -> p c b q", c=2, b=4), [r_fqs], [r_qblk])

    def past_setup(l, bb):
        d = fpd
        sig_t, r_lfp = tmp("sig", [128, 512], F32)
        lfp = sig_t[0:64, :]
        msq_t, r_lfT = tmp("cmsq", [128, 512], F32)
        lfT = msq_t[:, 0:256]
        var_t, r_rhsB = tmp("cvar", [128, 512], F32)
        rhsB = var_t[0:64, 0:256]
        csq_t, r_eS = tmp("csq", [128, 512], F32)
        eS = csq_t[:, 0:256]
        tot, r_tot = d["tot"]
        P.op("pool", lambda e: e.indirect_dma_start(
            out=lfp, out_offset=None, in_=clff,
            in_offset=bass.IndirectOffsetOnAxis(ap=idx64[0:64, bb:bb + 1], axis=0), element_offset=l * NPOOL * 512),
            r=[r_idx], w=[r_lfp], dma=True, key=r_lfp)
        pt, rp = b.ps()
        for h in range(4):
            tr(pt[:, h * 64:(h + 1) * 64], lfp.rearrange("j (r h) -> j h r", h=4)[:, h, :], ident_f[0:64, 0:64],
               [r_lfp, rc], [rp])
        cp("act", lfT, pt[:, 0:256], [rp], [r_lfT])
        rsum("dve", tot[:], lfp.rearrange("j (r h) -> j h r", h=4), [r_lfp], [r_tot])
        for h in range(4):
            ts("dve", rhsB[:, h * 64:(h + 1) * 64], revM[0:64, 0:64], tot[:, h:h + 1], None, MUL, None, [rc, r_tot], [r_rhsB])
        pS, rpS = b.ps()
        mm(pS[:, 0:256], revM[:], lfT, True, False, [rc, r_lfT], [rpS])
        mm(pS[:, 0:256], ones_f[0:64, :], rhsB, False, True, [rc, r_rhsB], [rpS])
        act(eS, pS[:, 0:256], AF.Exp, [rpS], [r_eS], scale=-16.0)
        return eS.rearrange("s (h j) -> s h j", h=4), r_eS

    NG = 64 // GP

    def past_group(l, bb, g, eS3, r_eS):
        d = fpd
        pov, rpov = b.psb[6]
        qblk, r_qblk = d["qblk"]
        qb4 = qblk[:].rearrange("p (c b m) -> p c b m", c=2, b=4)
        k = fpc[0] % 2
        fpc[0] += 1
        Kg, r_Kg = d["Kg"][k]
        Vg, r_Vg = d["Vg"][k]
        KT, r_KT = d["KT"][k]
        t_, r_t = d["t"][k]
        pTb, r_pTb = d["pTb"][k]
        Kg3 = Kg[:].rearrange("p (g c) -> p g c", g=GP)
        Vg3 = Vg[:].rearrange("p (g c) -> p g c", g=GP)
        KT4 = KT[:].rearrange("p (c g s) -> p c g s", c=2, g=GP)

        def f_dmak():
            for p in range(GP):
                j = g * GP + p
                P.op("pool", lambda e, p=p, j=j: e.indirect_dma_start(
                    out=Kg3[:, p, :], out_offset=None, in_=ckf,
                    in_offset=bass.IndirectOffsetOnAxis(ap=idxr[:, bb * 64 + j:bb * 64 + j + 1], axis=0),
                    element_offset=l * NPOOL * 128 * 256),
                    r=[r_idx], w=[r_Kg], dma=True, key=r_Kg)

        def f_dmav():
            for p in range(GP):
                j = g * GP + p
                P.op("pool", lambda e, p=p, j=j: e.indirect_dma_start(
                    out=Vg3[:, p, 0:256], out_offset=None, in_=cvf,
                    in_offset=bass.IndirectOffsetOnAxis(ap=idxr[:, bb * 64 + j:bb * 64 + j + 1], axis=0),
                    element_offset=l * NPOOL * 128 * 256),
                    r=[r_idx], w=[r_Vg], dma=True, key=r_Vg)

        def f_a():
            pk, rpk = b.ps()
            pkb = pk[:].bitcast(BF16)
            for c in range(2):
                for p in range(GP):
                    tr(pkb[:, (c * GP + p) * 128:(c * GP + p + 1) * 128], Kg3[:, p, c * 128:(c + 1) * 128], ident_bf[:],
                       [r_Kg, rc], [rpk])
            cp("act" if g % 2 else "dve", KT[:], pkb[:, 0:2 * GP * 128], [rpk], [r_KT])

        def f_b():
            psc, rpsc = b.ps()
            for p in range(GP):
                for c in range(2):
                    mm(psc[:, p * 16 + c * 8:p * 16 + c * 8 + 8], KT4[:, c, p, :], qb4[:, c, bb, :], True, True,
                       [r_KT, r_qblk], [rpsc])
            act(t_[:], psc[:, 0:GP * 16], AF.Exp, [rpsc], [r_t], scale=0.125)
            t4 = t_[:].rearrange("s (p h q) -> s p h q", p=GP, h=4)
            pT4 = pTb[:].rearrange("s (p h q) -> s p h q", p=GP, h=4)
            eSg = eS3[:, :, g * GP:(g + 1) * GP].rearrange("s h p -> s p h")
            for q in range(4):
                tt("dve", pT4[:, :, :, q], t4[:, :, :, q], eSg, MUL, [r_t, r_eS], [r_pTb])

        def f_c():
            for p in range(GP):
                mm(pov[0:NS, 0:257], pTb[:, p * 16:(p + 1) * 16], Vg3[:, p, 0:257], g == 0 and p == 0,
                   g == NG - 1 and p == GP - 1, [r_pTb, r_Vg], [rpov])
        return f_dmak, f_dmav, f_a, f_b, f_c

    class PastPipe:
        def __init__(self, l, bb):
            self.l, self.bb = l, bb
            self.eS3, self.r_eS = past_setup(l, bb)
            self.groups = [past_group(l, bb, g, self.eS3, self.r_eS) for g in range(NG)]
            self.s = 0
            self.groups[0][0]()

        def issue(self, n):
            pass

        def step(self):
            s_ = self.s
            G = self.groups
            if 0 <= s_ - 2 < NG:
                G[s_ - 2][4]()
            if s_ < NG:
                G[s_][1]()
            if 0 <= s_ - 1 < NG:
                G[s_ - 1][3]()
            if s_ + 1 < NG:
                G[s_ + 1][0]()
            if s_ < NG:
                G[s_][2]()
            self.s += 1

        def drain(self):
            while self.s < NG + 2:
                self.step()
            opast, r_opast = fpd["opast"]
            pov, rpov = b.psb[6]
            hsq, r_hsq = fpd["hsq"]
            cp("act", hsq[:, 0:257], pov[0:NS, 0:257], [rpov], [r_hsq])
            dma("sp", opast[32 * self.bb:32 * self.bb + NS, :], hsq[:, 0:257], [r_hsq], [r_opast], r_opast)

    def sample_mixer(l):
        switch()
        ar = Arena([(FKf, 2048), (VAf, 2080), (xTf, 2048), (qTff, 512), (kTff, 512), (fqf, 512), (ydf, 512),
                    (g0f, 542), (g1f, 542)])
        T = ar.take
        hsA, r_hsA = T(NS, 1296, F32, "hsA")
        hsB, r_hsB = T(NS, 1284, F32, "hsB")

        def hc(a, e):
            if e <= 1296:
                return hsA[:, a:e], r_hsA
            return hsB[:, a - 1296:e - 1296], r_hsB
        WSf, r_WSf = T(NS, 64, F32, "WSf")
        WSb, r_WSb = T(NS, 64, BF16, "WSb")
        bsS, r_bsS = T(NS, 4, F32, "bsS")
        bff, r_bff = T(NS, 4, F32, "bff")
        S0f, r_S0f = T(128, 1024, F32, "S0f")
        S0b, r_S0b = T(128, 1024, BF16, "S0b")
        xxT, r_xxT = T(128, 2 * 4 * 34, F32, "xxT")
        xx4 = xxT.rearrange("p (c b t) -> p c b t", c=2, b=4)
        mset("pool", WSf, 0.0, [r_WSf])
        mset("pool", S0f, 0.0, [r_S0f])
        WS3 = WSf.rearrange("p (g t) -> p g t", g=4)
        S03 = S0f.rearrange("p (b n) -> p b n", b=4)
        NCD = dict(allow_slow_non_contiguous=True)
        for bb in range(4):
            for g in range(4):
                dma("sp", WS3[4 * bb:4 * bb + 4, g, 4 * bb:4 * bb + 4], sgu_w[l, g, 0:4, 0:4].rearrange("t s -> s t"),
                    [], [r_WSf], r_WSf, **NCD)
            dma("sp", bsS[4 * bb:4 * bb + 4, :], sgu_bs[l][:, 0:4].rearrange("g t -> t g"), [], [r_bsS], r_bsS, **NCD)
            for h in range(4):
                dma("sp", S03[32 * h:32 * h + 32, bb, 64 * h:64 * h + 64], st_gla[l, bb, h], [], [r_S0f], r_S0f)
            for c in range(2):
                dma("sp", xx4[:, c, bb, 0:30], st_conv[l, bb][:, c * 128:(c + 1) * 128].rearrange("t p -> p t"),
                    [], [r_xxT], r_xxT, **NCD)
            dma("sp", o_sconv[l, bb, 0:26, :], st_conv[l, bb, 4:30, :], [], [], r_xxT)
        dma("sp", bff, fox_bf[l:l + 1, :].broadcast_to([NS, 4]), [], [r_bff], r_bff)
        for g in range(4):
            tt("dve", WSb.rearrange("p (g t) -> p g t", g=4)[:, g, :], WS3[:, g, :], maskS[:], MUL, [r_WSf, rc], [r_WSb])
        cp("act", S0b, S0f, [r_S0f], [r_S0b])
        S0b3 = S0b.rearrange("p (b n) -> p b n", b=4)

        pt, rp = b.ps()
        ptb = pt[:].bitcast(BF16)
        for kc in range(8):
            tr(ptb[:, kc * NS:(kc + 1) * NS], Rs[:, kc * 128:(kc + 1) * 128], ident_bf[0:NS, 0:NS], [r_Rs, rc], [rp])
        cp("act", xTs[:], ptb[:, 0:8 * NS].rearrange("p (k n) -> p k n", k=8), [rp], [r_xTs])
        for g0 in (0, 512, 1024, 1296, 1808, 2320):
            e0 = {0: 512, 512: 1024, 1024: 1296, 1296: 1808, 1808: 2320, 2320: DIN}[g0]
            n = e0 - g0
            pt, rp = b.ps()
            for kc in range(8):
                mm(pt[0:NS, 0:n], xTs[:, kc, :], Wi[:, kc, g0:e0], kc == 0, kc == 7, [r_xTs, rW], [rp])
            dst, rdst = hc(g0, e0)
            cp("act" if (g0 // 512) % 2 else "dve", dst, pt[0:NS, 0:n], [rp], [rdst])
        v_, r_ = hc(O_CK, O_CK + 256)
        dma("sp", o_sfk[l], v_, [r_], [], r_)
        v_, r_ = hc(O_CV, O_CV + 256)
        dma("sp", o_sfv[l], v_, [r_], [], r_)
        switch()
        ar2 = Arena([(Wi[:].rearrange("p k n -> p (k n)").bitcast(F32), 10320)])

        def T(parts, cols, dt, name):
            for a_ in (ar2, ar):
                try:
                    return a_.take(parts, cols, dt, name)
                except IndexError:
                    a_.i = len(a_.aps) - 1
                    a_.pos = a_.aps[-1][1]
            raise RuntimeError("sample arenas exhausted: " + name)
        ys, r_ys = T(NS, 768, BF16, "ys")
        idf = ident_f[0:NS, 0:NS]

        ydTs, r_ydTs = T(128, 32, BF16, "s_ydTs")
        def _sg_gla():
            pt, rp = b.ps()
            tr(pt[0:16, 0:NS], hsA[:, O_ALR:O_ALR + 16], idf, [r_hsA, rc], [rp])
            tr(pt[:, 16:32], hsA[:, O_AQ:O_AQ + 128], idf, [r_hsA, rc], [rp])
            tr(pt[:, 32:48], hsA[:, O_AK:O_AK + 128], idf, [r_hsA, rc], [rp])
            alrs, r_alrs = T(16, NS, BF16, "alrs")
            cp("act", alrs, pt[0:16, 0:NS], [rp], [r_alrs])
            qk_s, r_qks = T(128, 32, F32, "qk_s")
            cp("act", qk_s, pt[:, 16:48], [rp], [r_qks])
            yield
            pz, rpz = b.ps()
            mm(pz[0:NS, 0:128], alrs, Wa[:], True, False, [r_alrs, rW], [rpz])
            mm(pz[0:NS, 0:128], ones_bf[0:1, 0:NS], ba[:], False, True, [rc, rW], [rpz])
            e1, r_e1 = T(NS, 128, F32, "s_e1")
            act(e1, pz[0:NS, 0:128], AF.Exp, [rpz], [r_e1], scale=-1.0)
            spl, r_spl = T(NS, 128, F32, "s_spl")
            act(spl, e1, AF.Ln, [r_e1], [r_spl], bias=1.0, scale=1.0)
            shi, r_shi = T(NS, 128, BF16, "s_shi")
            slo, r_slo = T(NS, 128, BF16, "s_slo")
            cp("dve", shi, spl, [r_spl], [r_shi])
            tt("dve", slo, spl, shi, SUB, [r_spl, r_shi], [r_slo])
            yield
            pg_, rpg_ = b.ps()
            mm(pg_[:, 0:NS], shi, triSb[:], True, False, [r_shi, rc], [rpg_])
            mm(pg_[:, 0:NS], slo, triSb[:], False, True, [r_slo, rc], [rpg_])
            mm(pg_[0:NS, 128:256], revSb[:], shi, True, False, [r_shi, rc], [rpg_])
            mm(pg_[0:NS, 128:256], revSb[:], slo, False, True, [r_slo, rc], [rpg_])
            egs, r_egs = T(128, 32, F32, "s_eg")
            act(egs[:, 0:16], pg_[:, 0:NS], AF.Exp, [rpg_], [r_egs])
            act(egs[:, 16:32], pg_[:, 0:NS], AF.Exp, [rpg_], [r_egs], scale=-1.0)
            erev, r_erev = T(NS, 128, F32, "s_erev")
            act(erev, pg_[0:NS, 128:256], AF.Exp, [rpg_], [r_erev])
            qtl, r_qtl = T(128, NS, BF16, "s_qtl")
            stt("dve", qtl, qk_s[:, 0:16], 32.0 ** -0.5, egs[:, 0:16], MUL, MUL, [r_qks, r_egs], [r_qtl])
            kt4, r_kt4 = T(128, 64, BF16, "s_kt4")
            for h in range(4):
                stt("dve", kt4[:, h * 16:(h + 1) * 16], qk_s[:, 16:32], hm[:, h:h + 1], egs[:, 16:32], MUL, MUL,
                    [r_qks, r_egs, rc], [r_kt4])
            qtb, r_qtb = T(128, 64, BF16, "s_qtb")
            mset("pool", qtb, 0.0, [r_qtb])
            for bb in range(4):
                cp("dve", qtb[:, bb * 16 + 4 * bb:bb * 16 + 4 * bb + 4], qtl[:, 4 * bb:4 * bb + 4], [r_qtl], [r_qtb])
            kpb, r_kpb = T(NS, 512, BF16, "s_kpb")
            for bb in range(4):
                stt("dve", kpb[:, bb * 128:(bb + 1) * 128], hsA[:, O_AK:O_AK + 128], bm[:, bb:bb + 1], erev, MUL, MUL,
                    [r_hsA, r_erev, rc], [r_kpb])
            vb, r_vb = T(NS, 256, BF16, "s_vb")
            cp("act", vb, hsA[:, O_AV:O_AV + 256], [r_hsA], [r_vb])
            sgl, r_sgl = T(NS, 256, F32, "s_sgl")
            act(sgl, hsA[:, O_AG:O_AG + 256], AF.Silu, [r_hsA], [r_sgl])
            tt("dve", sgl, sgl, glag[0:NS, :], MUL, [r_sgl, rW], [r_sgl])
            yield
            pa_, rpa_ = b.ps()
            for h in range(4):
                mm(pa_[0:NS, h * 16:(h + 1) * 16], kt4[:, h * 16:(h + 1) * 16], qtl, True, True, [r_kt4, r_qtl], [rpa_])
            asb, r_asb = T(NS, 64, BF16, "s_asb")
            tt("dve", asb, pa_[0:NS, 0:64], maskS4[:], MUL, [rpa_, rc], [r_asb])
            yield
            po, rpo = b.ps()
            for bb in range(4):
                mm(po[0:NS, 0:256], qtb[:, bb * 16:(bb + 1) * 16], S0b3[:, bb, :], bb == 0, bb == 3, [r_qtb, r_S0b], [rpo])
            for h in range(4):
                mm(po[0:NS, 256 + 64 * h:320 + 64 * h], asb[:, h * 16:(h + 1) * 16], vb[:, 64 * h:64 * h + 64], True, True,
                   [r_asb, r_vb], [rpo])
            of, r_of = T(NS, 256, F32, "s_of")
            cp("act", of, po[0:NS, 0:256], [rpo], [r_of])
            tt("dve", of, of, po[0:NS, 256:512], ADD, [r_of, rpo], [r_of])
            sn, r_sn = T(128, 256, F32, "s_sn")
            for bb in range(4):
                yield
                pn, rpn = b.ps()
                mm(pn[:, 0:256], kpb[:, bb * 128:(bb + 1) * 128], vb, True, True, [r_kpb, r_vb], [rpn])
                tt("dve", sn, pn[:, 0:256], blkmask[:], MUL, [rpn, rc], [r_sn])
                stt("dve", sn, S03[:, bb, :], egs[:, 4 * bb + 3:4 * bb + 4], sn, MUL, ADD, [r_S0f, r_egs, r_sn], [r_sn])
                for h in range(4):
                    dma("sp", o_sgla[l, bb, h], sn[32 * h:32 * h + 32, 64 * h:64 * h + 64], [r_sn], [], r_sn)
            osq, r_osq = T(NS, 256, F32, "s_osq")
            act(osq, of, AF.Square, [r_of], [r_osq])
            gst, r_gst = T(NS, 8, F32, "s_gst")
            rsum("dve", gst[:, 0:4], osq.rearrange("p (h e) -> p h e", h=4), [r_osq], [r_gst])
            ts("dve", gst[:, 4:8], gst[:, 0:4], 1.0 / 64.0, EPS, MUL, ADD, [r_gst], [r_gst])
            ts("dve", gst[:, 4:8], gst[:, 4:8], -0.5, None, POW, None, [r_gst], [r_gst])
            for h in range(4):
                stt("dve", ys[:, 64 * h:64 * h + 64], of[:, 64 * h:64 * h + 64], gst[:, 4 + h:5 + h],
                    sgl[:, 64 * h:64 * h + 64], MUL, MUL, [r_of, r_gst, r_sgl], [r_ys])
            yield

        def _sg_sgu():
            uu = hsA[:, O_BU:O_BU + 256]
            vv = hsA[:, O_BV:O_BV + 256]
            vsq, r_vsq = T(NS, 256, F32, "s_vsq")
            act(vsq, vv, AF.Square, [r_hsA], [r_vsq])
            sst, r_sst = T(NS, 24, F32, "s_sst")
            rsum("dve", sst[:, 0:4], vv.rearrange("p (h e) -> p h e", h=4), [r_hsA], [r_sst])
            rsum("dve", sst[:, 4:8], vsq.rearrange("p (h e) -> p h e", h=4), [r_vsq], [r_sst])
            ts("dve", sst[:, 8:12], sst[:, 0:4], 1.0 / 64.0, None, MUL, None, [r_sst], [r_sst])
            tt("dve", sst[:, 12:16], sst[:, 8:12], sst[:, 8:12], MUL, [r_sst], [r_sst])
            stt("dve", sst[:, 12:16], sst[:, 4:8], 1.0 / 64.0, sst[:, 12:16], MUL, SUB, [r_sst], [r_sst])
            ts("dve", sst[:, 16:20], sst[:, 12:16], EPS, -0.5, ADD, POW, [r_sst], [r_sst])
            stt("dve", sst[:, 20:24], sst[:, 8:12], -1.0, sst[:, 16:20], MUL, MUL, [r_sst], [r_sst])
            vn, r_vn = T(NS, 256, F32, "s_vn")
            for g in range(4):
                ts("dve", vn[:, 64 * g:64 * g + 64], vv[:, 64 * g:64 * g + 64], sst[:, 16 + g:17 + g], sst[:, 20 + g:21 + g],
                   MUL, ADD, [r_hsA, r_sst], [r_vn])
            tt("dve", vn, vn, sgg[0:NS, :], MUL, [r_vn, rW], [r_vn])
            tt("dve", vn, vn, sgb[0:NS, :], ADD, [r_vn, rW], [r_vn])
            dma("sp", o_ssgu[l], vn, [r_vn], [], r_vn)
            vlb, r_vlb = T(NS, 256, BF16, "s_vlb")
            cp("dve", vlb, vn, [r_vn], [r_vlb])
            pmx, rpmx = b.ps()
            for g in range(4):
                mm(pmx[0:NS, 64 * g:64 * g + 64], WSb[:, g * 16:(g + 1) * 16], vlb[:, 64 * g:64 * g + 64], True, True,
                   [r_WSb, r_vlb], [rpmx])
            for g in range(4):
                stt("dve", ys[:, 256 + 64 * g:320 + 64 * g], pmx[0:NS, 64 * g:64 * g + 64], bsS[:, g:g + 1],
                    uu[:, 64 * g:64 * g + 64], ADD, MUL, [rpmx, r_hsA, r_bsS], [r_ys])
            yield

        def _sg_fox():
            cf_, r_cf = hc(O_CF, O_CF + 4)
            fst, r_fst = T(NS, 16, F32, "s_fst")
            tt("dve", fst[:, 0:4], cf_, bff, ADD, [r_cf, r_bff], [r_fst])
            act(fst[:, 0:4], fst[:, 0:4], AF.Exp, [r_fst], [r_fst], scale=-1.0)
            act(fst[:, 4:8], fst[:, 0:4], AF.Ln, [r_fst], [r_fst], bias=1.0, scale=1.0)
            ts("dve", fst[:, 12:16], fst[:, 4:8], -1.0, None, MUL, None, [r_fst], [r_fst])
            dma("sp", o_sflf[l], fst[:, 12:16], [r_fst], [], r_fst)
            pd, rpd = b.ps()
            mm(pd[0:NS, 0:4], maskS[:], fst[:, 4:8], True, True, [rc, r_fst], [rpd])
            act(fst[:, 8:12], pd[0:NS, 0:4], AF.Exp, [rpd], [r_fst])
            yield
            pt, rp = b.ps()
            for c in range(2):
                v_, r_ = hc(O_CQ + 128 * c, O_CQ + 128 * c + 128)
                tr(pt[:, c * 16:(c + 1) * 16], v_, idf, [r_, rc], [rp])
                v_, r_ = hc(O_CK + 128 * c, O_CK + 128 * c + 128)
                tr(pt[:, 32 + c * 16:32 + (c + 1) * 16], v_, idf, [r_, rc], [rp])
            fqk, r_fqk = T(128, 64, BF16, "s_fqk")
            cp("act", fqk, pt[:, 0:64], [rp], [r_fqk])
            qblk, r_qblk = T(128, 2 * 4 * 8, BF16, "s_qblk")
            mset("pool", qblk, 0.0, [r_qblk])
            qb4 = qblk.rearrange("p (c b m) -> p c b m", c=2, b=4)
            for h2 in range(2):
                cp("dve", qb4[64 * h2:64 * h2 + 64, :, :, 4 * h2:4 * h2 + 4],
                   fqk[64 * h2:64 * h2 + 64, 0:32].rearrange("p (c b q) -> p c b q", c=2, b=4), [r_fqk], [r_qblk])
            vaug, r_vaug = T(NS, 258, BF16, "s_vaug")
            v_, r_ = hc(O_CV, O_CV + 256)
            cp("act", vaug[:, 0:256], v_, [r_], [r_vaug])
            mset("pool", vaug[:, 256:258], 1.0, [r_vaug])
            yield
            psn, rpsn = b.ps()
            for bb in range(4):
                for c in range(2):
                    mm(psn[0:NS, bb * 16 + c * 8:bb * 16 + c * 8 + 8], fqk[:, 32 + c * 16:32 + (c + 1) * 16], qb4[:, c, bb, :],
                       True, True, [r_fqk, r_qblk], [rpsn])
            t1, r_t1 = T(NS, 64, F32, "s_t1")
            act(t1, psn[0:NS, 0:64], AF.Exp, [rpsn], [r_t1], scale=0.125)
            tt("dve", t1, t1, maskN[:], MUL, [r_t1, rc], [r_t1])
            pnb, r_pnb = T(NS, 64, BF16, "s_pnb")
            t14 = t1.rearrange("p (b h q) -> p b h q", b=4, h=4)
            pn4 = pnb.rearrange("p (b h q) -> p b h q", b=4, h=4)
            for h in range(4):
                ts("dve", pn4[:, :, h, :], t14[:, :, h, :], fst[:, 8 + h:9 + h], None, MUL, None, [r_t1, r_fst], [r_pnb])
            ycs, r_ycs = T(NS, 256, F32, "s_ycs")
            osum, r_osum = T(NS, 258, F32, "s_osum")
            rd, r_rd = T(NS, 2, F32, "s_rd")
            on, r_on = T(NS, 256, F32, "s_on")
            for bb in range(4):
                pov, rpov = b.psb[7]
                mm(pov[0:NS, 0:257], pnb[:, bb * 16:(bb + 1) * 16], vaug[:, 0:257], True, True, [r_pnb, r_vaug], [rpov])
                if has_cache:
                    dma("sp", osum[:, 0:257], fpd["opast"][0][32 * bb:32 * bb + NS, :], [fpd["opast"][1]], [r_osum], r_osum)
                    tt("dve", osum[:, 0:257], osum[:, 0:257], pov[0:NS, 0:257], ADD, [rpov, r_osum], [r_osum])
                else:
                    cp("dve", osum[:, 0:257], pov[0:NS, 0:257], [rpov], [r_osum])
                P.op("dve", lambda e, rd=rd, osum=osum: e.reciprocal(out=rd[:, 0:1], in_=osum[:, 256:257]), r=[r_osum], w=[r_rd])
                ts("dve", on, osum[:, 0:256], rd[:, 0:1], None, MUL, None, [r_osum, r_rd], [r_on])
                for h in range(4):
                    dma("sp", ycs[4 * bb:4 * bb + 4, 64 * h:64 * h + 64], on[4 * h:4 * h + 4, 64 * h:64 * h + 64],
                        [r_on], [r_ycs], r_ycs)
            cp("dve", ys[:, 512:768], ycs, [r_ycs], [r_ys])
            yield

        def _sg_conv():
            ga_, r_ga = hc(O_DIN, O_DIN + 256)
            gg_, r_gg = hc(O_DIN + 256, O_DIN + 512)
            glu, r_glu = T(NS, 256, F32, "s_glu")
            act(glu, gg_, AF.Sigmoid, [r_gg], [r_glu])
            tt("dve", glu, glu, ga_, MUL, [r_glu, r_ga], [r_glu])
            for bb in range(4):
                dma("sp", o_sconv[l, bb, 26:30, :], glu[4 * bb:4 * bb + 4, :], [r_glu], [], r_glu)
            pt, rp = b.ps()
            for c in range(2):
                tr(pt[:, c * 16:(c + 1) * 16], glu[:, c * 128:(c + 1) * 128], idf, [r_glu, rc], [rp])
            for c in range(2):
                cp("act", xx4[:, c, :, 30:34], pt[:, c * 16:(c + 1) * 16].rearrange("p (b q) -> p b q", b=4), [rp], [r_xxT])
            acc, r_acc = T(128, 32, F32, "s_acc")
            for c in range(2):
                a3 = acc[:, c * 16:(c + 1) * 16].rearrange("p (b q) -> p b q", b=4)
                for j in range(31):
                    if j == 0:
                        ts("dve", a3, xx4[:, c, :, 0:4], cw[:, c, 0:1], None, MUL, None, [r_xxT, rW], [r_acc])
                    else:
                        stt("dve", a3, xx4[:, c, :, j:j + 4], cw[:, c, j:j + 1], a3, MUL, ADD, [r_xxT, rW, r_acc], [r_acc])
            for c in range(2):
                ac = acc[:, c * 16:(c + 1) * 16]
                cof, r_cof = T(128, 16, F32, "s_cof%d" % c)
                act(cof, ac, AF.Identity, [r_acc, rW], [r_cof], bias=cb[:, c:c + 1], scale=1.0)
                sq, r_sq = T(128, 16, F32, "s_csq%d" % c)
                tt("dve", sq, cof, cof, MUL, [r_cof], [r_sq])
                yield
                pm, rpm = b.ps()
                mm(pm[:, 0:16], blk64[:], cof, True, True, [rc, r_cof], [rpm])
                mm(pm[:, 16:32], blk64[:], sq, True, True, [rc, r_sq], [rpm])
                mv, r_mv = T(128, 32, F32, "s_mv%d" % c)
                cp("act", mv, pm[:, 0:32], [rpm], [r_mv])
                tt("dve", sq, mv[:, 0:16], mv[:, 0:16], MUL, [r_mv], [r_sq])
                tt("dve", sq, mv[:, 16:32], sq, SUB, [r_mv, r_sq], [r_sq])
                ts("dve", sq, sq, EPS, -0.5, ADD, POW, [r_sq], [r_sq])
                tt("dve", cof, cof, mv[:, 0:16], SUB, [r_cof, r_mv], [r_cof])
                tt("dve", cof, cof, sq, MUL, [r_cof, r_sq], [r_cof])
                act(ydTs[:, c * 16:(c + 1) * 16], cof, AF.Silu, [r_cof, rW], [r_ydTs], scale=cng[:, c:c + 1], bias=cnb[:, c:c + 1])
            yield

        _gens = [_sg_gla(), _sg_sgu(), _sg_fox(), _sg_conv()]
        while _gens:
            for g_ in list(_gens):
                try:
                    next(g_)
                except StopIteration:
                    _gens.remove(g_)
        pty, rpty = b.ps()
        ptyb = pty[:].bitcast(BF16)
        for c in range(6):
            tr(ptyb[:, c * 16:(c + 1) * 16], ys[:, c * 128:(c + 1) * 128], ident_bf[0:NS, 0:NS], [r_ys, rc], [rpty])
        yTs, r_yTs = T(128, 96, BF16, "s_yTs")
        cp("act", yTs, ptyb[:, 0:96], [rpty], [r_yTs])
        ps2, rps2 = [], []
        for hf in range(2):
            pm_, rpm_ = b.ps()
            for kc in range(8):
                lh = yTs[:, kc * 16:(kc + 1) * 16] if kc < 6 else ydTs[:, (kc - 6) * 16:(kc - 5) * 16]
                mm(pm_[0:NS, :], lh, Wo[:, kc, hf * 512:(hf + 1) * 512], kc == 0, kc == 7, [r_yTs, r_ydTs, rW], [rpm_])
            ps2.append(pm_)
            rps2.append(rpm_)
        layer_norm(None, ps2, rps2, l1g, l1b, None, n=NS, res=Rs[:], r_res=r_Rs)
        switch()

    cur_wu = [None]

    def wu_chunk(l, fc, wus, cnt_f):
        if fc % 4 == 0:
            cols = min(4, NFC - fc) * 128
            wu, r_wu = wus[cnt_f[0] % 2]
            cnt_f[0] += 1
            cur_wu[0] = (wu, r_wu)
            dma("pool", wu[:, :, 0:cols], w_up[l, :, fc * 128:fc * 128 + cols].rearrange("(k p) n -> p k n", p=128),
                [], [r_wu], r_wu)
            dma("pool", wu[:, :, 512:512 + cols],
                w_up[l, :, DFF + fc * 128:DFF + fc * 128 + cols].rearrange("(k p) n -> p k n", p=128), [], [r_wu], r_wu)
        wu, r_wu = cur_wu[0]
        ci = fc % 4
        return wu, r_wu, ci * 128, 512 + ci * 128

    def sample_ffn(l, last, wus, wds, cnt_f):
        switch()
        ar = Arena([(fqf, 512), (ydf, 512), (g0f, 542), (g1f, 542)])
        T = ar.take
        pt, rp = b.ps()
        ptb = pt[:].bitcast(BF16)
        for kc in range(8):
            tr(ptb[:, kc * NS:(kc + 1) * NS], Rs[:, kc * 128:(kc + 1) * 128], ident_bf[0:NS, 0:NS], [r_Rs, rc], [rp])
        cp("act", xTs[:], ptb[:, 0:8 * NS].rearrange("p (k n) -> p k n", k=8), [rp], [r_xTs])
        bufT, r_bufT = T(128, NFC * 8, F32, "f_bufT")
        glo, r_glo = T(128, NFC * 8, F32, "f_glo")
        hTs, r_hTs = T(128, NFC * 16, BF16, "f_hTs")
        bu4 = bufT.rearrange("p (c b j) -> p c b j", c=NFC, b=4)
        gl4 = glo.rearrange("p (c b j) -> p c b j", c=NFC, b=4)
        NCD = dict(allow_slow_non_contiguous=True)
        for fc in range(NFC):
            for bb in range(4):
                dma("sp", bu4[:, fc, bb, :], st_ffc[l, bb][:, fc * 128:(fc + 1) * 128].rearrange("j p -> p j"),
                    [], [r_bufT], r_bufT, **NCD)
        gxl = [T(128, 24, F32, "f_gx%d" % i) for i in range(2)]
        gal = [T(128, 16, F32, "f_ga%d" % i) for i in range(2)]
        for fc in range(NFC):
            wu, r_wu, og, ov = wu_chunk(l, fc, wus, cnt_f)
            pg, rpg = b.ps()
            for kc in range(8):
                mm(pg[:, 0:16], wu[:, kc, og:og + 128], xTs[:, kc, :], kc == 0, kc == 7, [r_wu, r_xTs], [rpg])
            for kc in range(8):
                mm(pg[:, 16:32], wu[:, kc, ov:ov + 128], xTs[:, kc, :], kc == 0, kc == 7, [r_wu, r_xTs], [rpg])
            gx, r_gx = gxl[fc % 2]
            ga, r_ga = gal[fc % 2]
            gx3 = gx.rearrange("p (b t) -> p b t", b=4)
            ga3 = ga.rearrange("p (b t) -> p b t", b=4)
            cp("dve", gx3[:, :, 0:2], bu4[:, fc, :, :], [r_bufT], [r_gx])
            cp("act", gx3[:, :, 2:6], pg[:, 0:16].rearrange("p (b t) -> p b t", b=4), [rpg], [r_gx])
            cp("dve", gl4[:, fc, :, :], gx3[:, :, 4:6], [r_gx], [r_glo])
            ts("dve", ga3, gx3[:, :, 0:4], fw[:, fc, 0:1], None, MUL, None, [r_gx, rW], [r_ga])
            stt("dve", ga3, gx3[:, :, 1:5], fw[:, fc, 1:2], ga3, MUL, ADD, [r_gx, rW, r_ga], [r_ga])
            stt("dve", ga3, gx3[:, :, 2:6], fw[:, fc, 2:3], ga3, MUL, ADD, [r_gx, rW, r_ga], [r_ga])
            act(ga, ga, AF.Silu, [r_ga, rW], [r_ga], bias=fb[:, fc:fc + 1], scale=1.0)
            tt("dve", hTs[:, fc * 16:(fc + 1) * 16], ga, pg[:, 16:32], MUL, [r_ga, rpg], [r_hTs])
        for fc in range(NFC):
            for bb in range(4):
                dma("sp", o_sffc[l, bb][:, fc * 128:(fc + 1) * 128].rearrange("j p -> p j"), gl4[:, fc, bb, :],
                    [r_glo], [], r_glo, **NCD)
        bk = [b.ps(), b.ps()]
        for fc in range(NFC):
            wd, r_wd = wds[cnt_f[3] % 3]
            cnt_f[3] += 1
            dma("pool", wd[:], w_dn[l, fc * 128:(fc + 1) * 128, :], [], [r_wd], r_wd)
            for hf in range(2):
                mm(bk[hf][0][0:NS, :], hTs[:, fc * 16:(fc + 1) * 16], wd[:, hf * 512:(hf + 1) * 512], fc == 0, fc == NFC - 1,
                   [r_hTs, r_wd], [bk[hf][1]])
        layer_norm(None, [bk[0][0], bk[1][0]], [bk[0][1], bk[1][1]], l2g, l2b, o_ys[:, :] if last else None,
                   n=NS, res=Rs[:], r_res=r_Rs)
        switch()

    for t in range(NT):
        dma("pool", R[:, t, :], xp[t * 128:(t + 1) * 128, :], [], [r_R[t]], r_R[t])

    def layer_norm(t, ps2, rps2, g_t, b_t, out_dram, n=128, res=None, r_res=None):
        if res is None:
            res, r_res = R[:, t, :], r_R[t]
        rf, r_rf = tmp("ln_rf", [128, D], F32)
        for hf in range(2):
            stt("dve", rf[0:n, hf * 512:(hf + 1) * 512], res[:, hf * 512:(hf + 1) * 512], ALPHA, ps2[hf][0:n, :],
                MUL, ADD, [r_res, rps2[hf]], [r_rf])
        st, r_st = tmp("ln_st", [128, 8], F32)
        xn, r_xn = tmp("ln_xn", [128, D], F32)
        act(xn[0:n, :], rf[0:n, :], AF.Identity, [r_rf], [r_xn, r_st], accum_out=st[0:n, 0:1])
        act(xn[0:n, :], rf[0:n, :], AF.Square, [r_rf], [r_xn, r_st], accum_out=st[0:n, 1:2])
        ts("dve", st[0:n, 2:3], st[0:n, 0:1], 1.0 / D, None, MUL, None, [r_st], [r_st])
        tt("dve", st[0:n, 3:4], st[0:n, 2:3], st[0:n, 2:3], MUL, [r_st], [r_st])
        stt("dve", st[0:n, 4:5], st[0:n, 1:2], 1.0 / D, st[0:n, 3:4], MUL, SUB, [r_st], [r_st])
        ts("dve", st[0:n, 5:6], st[0:n, 4:5], EPS, -0.5, ADD, POW, [r_st], [r_st])
        stt("dve", st[0:n, 6:7], st[0:n, 2:3], -1.0, st[0:n, 5:6], MUL, MUL, [r_st], [r_st])
        act(xn[0:n, :], rf[0:n, :], AF.Identity, [r_rf, r_st], [r_xn], scale=st[0:n, 5:6], bias=st[0:n, 6:7])
        tt("dve", xn[0:n, :], xn[0:n, :], g_t[0:n, :], MUL, [r_xn, rW], [r_xn])
        if out_dram is None:
            tt("dve", res, xn[0:n, :], b_t[0:n, :], ADD, [r_xn, rW], [r_res])
        else:
            tt("dve", xn[0:n, :], xn[0:n, :], b_t[0:n, :], ADD, [r_xn, rW], [r_xn])
            dma("sp", out_dram, xn[0:n, :], [r_xn], [], r_xn)

    def make_xT(blk):
        for ti in range(4):
            t = blk * 4 + ti
            pt, rp = b.ps()
            ptb = pt[:].bitcast(BF16)
            for kc in range(8):
                tr(ptb[:, kc * 128:(kc + 1) * 128], R[:, t, kc * 128:(kc + 1) * 128], ident_bf[:], [r_R[t], rc], [rp])
            cp("act", xT[:, :, ti * 128:(ti + 1) * 128], ptb.rearrange("p (k n) -> p k n", k=8), [rp], [r_xT])

    for l in range(nlayers):
        last = (l == nlayers - 1)
        for kc in range(8):
            dma("pool", Wi[:, kc, :], w_in[l, kc * 128:(kc + 1) * 128, :], [], [rW], rW)
        for kc in range(8):
            dma("pool", Wo[:, kc, :], w_o[l, kc * 128:(kc + 1) * 128, :], [], [rW], rW)
        dma("pool", Wa[:], gla_w_a[l], [], [rW], rW)
        dma("pool", ba[:], gla_b_a[l:l + 1, :], [], [rW], rW)
        dma("pool", bfb[:], fox_bf[l:l + 1, :], [], [rW], rW)
        dma("pool", bs4[:], sgu_bs[l], [], [rW], rW)
        dma("sp", bsT[:], sgu_bs[l].rearrange("g t -> t g"), [], [rW], rW, allow_slow_non_contiguous=True)
        for (tl, src) in ((glag, gla_g), (sgg, sgu_g), (sgb, sgu_bb)):
            dma("sp", tl[:], src[l:l + 1, :].broadcast_to([128, 256]), [], [rW], rW)
        for (tl, src) in ((l1g, ln1_g), (l1b, ln1_b)):
            dma("sp", tl[:], src[l:l + 1, :].broadcast_to([128, D]), [], [rW], rW)
        NC_ = dict(allow_slow_non_contiguous=True)
        for c in range(2):
            dma("sp", cw[:, c, :], conv_w[l][:, c * 128:(c + 1) * 128].rearrange("j p -> p j"), [], [rW], rW, **NC_)
        for (tl, src) in ((cb, conv_b), (cng, cn_g), (cnb, cn_b)):
            dma("sp", tl[:], src[l].rearrange("(c p) -> p c", p=128), [], [rW], rW, **NC_)
        for c in range(NFC):
            dma("sp", fw[:, c, :], fcw[l][:, c * 128:(c + 1) * 128].rearrange("j p -> p j"), [], [rW], rW, **NC_)
        dma("sp", fb[:], fcb[l].rearrange("(c p) -> p c", p=128), [], [rW], rW, **NC_)
        dma("sp", sw, sgu_w[l].rearrange("g t s -> t g s"), [], [r_xT], r_xT)
        for g in range(4):
            pt, rp = b.ps()
            tr(pt[:, 0:128], sw[:, g, :], ident_f[:], [r_xT, rc], [rp])
            tt("dve", WT[:, g, :], pt[:, 0:128], triO[:], MUL, [rp, rc], [rW])
        mset("pool", Sf[:], 0.0, [r_S])
        mset("pool", Sb[:], 0.0, [r_S])
        mset("pool", Pacc[:], 0.0, [r_Pacc])
        mset("pool", gext[1][:, :, 512:542], 0.0, [r_gext[1]])

        if has_cache and 'sample' not in SKIP:
            sample_q_prework(l)
        for blk in range(4):
            make_xT(blk)
            c0 = blk * 512
            def fproj(col, m):
                pt, rp = b.ps()
                for kc in range(8):
                    mm(pt[0:m, :], Wi[:, kc, col:col + m], xT[:, kc, :], kc == 0, kc == 7, [rW, r_xT], [rp])
                return pt, rp
            pt, rp = fproj(O_AQ, 128)
            cp("act", qTf[:], pt[:], [rp], [r_qk])
            pt, rp = fproj(O_AK, 128)
            cp("dve", kTf[:], pt[:], [rp], [r_qk])
            pt, rp = fproj(O_ALR, 16)
            cp("act", alr[:], pt[0:16, :], [rp], [r_alr])
            for c in range(2):
                pt, rp = fproj(O_CQ + 128 * c, 128)
                cp("act", fq[:, c, :], pt[:], [rp], [r_fq])
                pt, rp = fproj(O_CK + 128 * c, 128)
                cp("dve", FK[:, c, c0:c0 + 512], pt[:], [rp], [r_FK[blk]])
            ge, r_ge = gext[blk % 2], r_gext[blk % 2]
            gp_, r_gp = gext[(blk + 1) % 2], r_gext[(blk + 1) % 2]
            cp("dve", ge[:, :, 0:30], gp_[:, :, 512:542], [r_gp], [r_ge])
            for c in range(2):
                pa, rpa = fproj(O_DIN + 128 * c, 128)
                pg, rpg = fproj(O_DIN + 256 + 128 * c, 128)
                sg, r_sg = tmp("sig", [128, 512], F32)
                act(sg[:], pg[:], AF.Sigmoid, [rpg], [r_sg])
                tt("dve", sg[:], pa[:], sg[:], MUL, [rpa, r_sg], [r_sg])
                cp("dve", ge[:, c, 30:542], sg[:], [r_sg], [r_ge])
                if blk == 3:
                    cp("dve", glast[:, c, :], sg[:, 482:512], [r_sg], [r_glast])
            if blk == 3:
                for c in range(2):
                    dma("sp", o_conv[l][:, c * 128:(c + 1) * 128].rearrange("t p -> p t"), glast[:, c, :], [r_glast], [],
                        r_glast, allow_slow_non_contiguous=True)
            if 'conv' in SKIP:
                mset('pool', ydT[:], 0.0, [r_ydT])
            def gen_conv(ge=ge, r_ge=r_ge):
                for c in range(2):
                    cof, r_cof = tmp("cof", [128, 512], F32)
                    ts("dve", cof[:], ge[:, c, 0:512], cw[:, c, 0:1], cb[:, c:c + 1], MUL, ADD, [r_ge, rW], [r_cof])
                    for j in range(1, 31):
                        stt("dve", cof[:], ge[:, c, j:j + 512], cw[:, c, j:j + 1], cof[:], MUL, ADD, [r_ge, rW, r_cof], [r_cof])
                        if j % 5 == 0:
                            yield
                    sq, r_sq = tmp("csq", [128, 512], F32)
                    act(sq[:], cof[:], AF.Square, [r_cof], [r_sq])
                    pm, rpm = b.ps()
                    mm(pm[:], blk64[:], cof[:], True, True, [rc, r_cof], [rpm])
                    pe2, rpe2 = b.ps()
                    mm(pe2[:], blk64[:], sq[:], True, True, [rc, r_sq], [rpe2])
                    msq, r_msq = tmp("cmsq", [128, 512], F32)
                    act(msq[:], pm[:], AF.Square, [rpm], [r_msq])
                    var, r_var = tmp("cvar", [128, 512], F32)
                    tt("dve", var[:], pe2[:], msq[:], SUB, [rpe2, r_msq], [r_var])
                    ts("dve", var[:], var[:], EPS, -0.5, ADD, POW, [r_var], [r_var])
                    tt("dve", cof[:], cof[:], pm[:], SUB, [r_cof, rpm], [r_cof])
                    tt("dve", cof[:], cof[:], var[:], MUL, [r_cof, r_var], [r_cof])
                    act(ydT[:, c, :], cof[:], AF.Silu, [r_cof, rW], [r_ydT], scale=cng[:, c:c + 1], bias=cnb[:, c:c + 1])
                    yield

            pipe = None
            for ti in range(4):
                t = blk * 4 + ti
                cs = slice(ti * 128, (ti + 1) * 128)
                ytok, r_ytok = tmp("ytok", [128, 768], BF16)

                def tproj(col, n):
                    pt, rp = b.ps()
                    for kc in range(8):
                        mm(pt[:, 0:n], xT[:, kc, cs], Wi[:, kc, col:col + n], kc == 0, kc == 7, [r_xT, rW], [rp])
                    return pt, rp

                if 'gla' in SKIP or GSTOP < 99:
                    mset('pool', ytok[:, 0:256], 0.0, [r_ytok])
                def gen_gla():
                    pz, rpz = b.ps()
                    mm(pz[:, 0:128], alr[0:16, cs], Wa[:], True, False, [r_alr, rW], [rpz])
                    mm(pz[:, 0:128], ones_bf[0:1, :], ba[:], False, True, [rc, rW], [rpz])
                    e1, r_e1 = tmp("g_e1", [128, 128], F32)
                    act(e1[:], pz[:, 0:128], AF.Exp, [rpz], [r_e1], scale=-1.0)
                    spl, r_spl = tmp("g_sp", [128, 128], F32)
                    act(spl[:], e1[:], AF.Ln, [r_e1], [r_spl], bias=1.0, scale=1.0)
                    yield
                    pg_, rpg_ = b.ps()
                    shi, r_shi = tmp("g_shi", [128, 128], BF16)
                    slo, r_slo = tmp("g_slo", [128, 128], BF16)
                    cp("dve", shi[:], spl[:], [r_spl], [r_shi])
                    tt("dve", slo[:], spl[:], shi[:], SUB, [r_spl, r_shi], [r_slo])
                    mm(pg_[:, 0:128], shi[:], triMb[:], True, False, [r_shi, rc], [rpg_])
                    mm(pg_[:, 0:128], slo[:], triMb[:], False, True, [r_slo, rc], [rpg_])
                    mm(pg_[:, 128:256], revMb[:], shi[:], True, False, [r_shi, rc], [rpg_])
                    mm(pg_[:, 128:256], revMb[:], slo[:], False, True, [r_slo, rc], [rpg_])
                    eg, r_eg = tmp("g_eg", [128, 384], F32)
                    act(eg[:, 0:128], pg_[:, 0:128], AF.Exp, [rpg_], [r_eg])
                    act(eg[:, 128:256], pg_[:, 0:128], AF.Exp, [rpg_], [r_eg], scale=-1.0)
                    act(eg[:, 256:384], pg_[:, 128:256], AF.Exp, [rpg_], [r_eg])
                    yield
                    qtl, r_qtl = tmp("g_qtl", [128, 128], BF16)
                    stt("dve", qtl[:], qTf[:, cs], 32.0 ** -0.5, eg[:, 0:128], MUL, MUL, [r_qk, r_eg], [r_qtl])
                    kt4, r_kt4 = tmp("g_kt4", [128, 4, 128], BF16)
                    for h in range(4):
                        stt("dve", kt4[:, h, :], kTf[:, cs], hm[:, h:h + 1], eg[:, 128:256], MUL, MUL,
                            [r_qk, r_eg, rc], [r_kt4])
                    yield
                    p1, rp1 = tproj(O_AK, 512)
                    p2, rp2 = tproj(O_AK + 512, 128)
                    kp, r_kp = tmp("g_kp", [128, 128], BF16)
                    tt("dve", kp[:], p1[:, 0:128], eg[:, 256:384], MUL, [rp1, r_eg], [r_kp])
                    vb, r_vb = tmp("g_vb", [128, 256], BF16)
                    cp("act", vb[:], p1[:, 128:384], [rp1], [r_vb])
                    sgl, r_sgl = tmp("g_sg", [128, 256], F32)
                    act(sgl[:, 0:128], p1[:, 384:512], AF.Silu, [rp1], [r_sgl])
                    act(sgl[:, 128:256], p2[:, 0:128], AF.Silu, [rp2], [r_sgl])
                    tt("dve", sgl[:], sgl[:], glag[:], MUL, [r_sgl, rW], [r_sgl])
                    yield
                    pa_, rpa_ = b.ps()
                    for h in range(4):
                        mm(pa_[:, h * 128:(h + 1) * 128], kt4[:, h, :], qtl[:], True, True, [r_kt4, r_qtl], [rpa_])
                    asb, r_asb = tmp("g_asb", [128, 512], BF16)
                    tt("dve", asb[:], pa_[:], mask4[:], MUL, [rpa_, rc], [r_asb])
                    yield
                    po, rpo = b.ps()
                    mm(po[:, 0:256], qtl[:], Sb[:], True, True, [r_qtl, r_S], [rpo])
                    for h in range(4):
                        mm(po[:, 256 + 64 * h:320 + 64 * h], asb[:, h * 128:(h + 1) * 128], vb[:, 64 * h:64 * h + 64], True, True,
                           [r_asb, r_vb], [rpo])
                    of, r_of = tmp("g_of", [128, 256], F32)
                    cp("act", of[:], po[:, 0:256], [rpo], [r_of])
                    tt("dve", of[:], of[:], po[:, 256:512], ADD, [r_of, rpo], [r_of])
                    yield
                    pn, rpn = b.ps()
                    mm(pn[:, 0:256], kp[:], vb[:], True, True, [r_kp, r_vb], [rpn])
                    stmp, r_stmp = tmp("g_stmp", [128, 256], F32)
                    tt("dve", stmp[:], pn[:, 0:256], blkmask[:], MUL, [rpn, rc], [r_stmp])
                    stt("dve", Sf[:], Sf[:], eg[:, 127:128], stmp[:], MUL, ADD, [r_S, r_eg, r_stmp], [r_S])
                    cp("act", Sb[:], Sf[:], [r_S], [r_S])
                    yield
                    osq, r_osq = tmp("g_stmp", [128, 256], F32)
                    act(osq[:], of[:], AF.Square, [r_of], [r_osq])
                    gst, r_gst = tmp("g_st", [128, 8], F32)
                    rsum("dve", gst[:, 0:4], osq[:].rearrange("p (h e) -> p h e", h=4), [r_osq], [r_gst])
                    ts("dve", gst[:, 4:8], gst[:, 0:4], 1.0 / 64.0, EPS, MUL, ADD, [r_gst], [r_gst])
                    ts("dve", gst[:, 4:8], gst[:, 4:8], -0.5, None, POW, None, [r_gst], [r_gst])
                    for h in range(4):
                        stt("dve", ytok[:, 64 * h:64 * h + 64], of[:, 64 * h:64 * h + 64], gst[:, 4 + h:5 + h],
                            sgl[:, 64 * h:64 * h + 64], MUL, MUL, [r_of, r_gst, r_sgl], [r_ytok])

                if 'sgu' in SKIP:
                    mset('pool', ytok[:, 256:512], 0.0, [r_ytok])
                def gen_sgu():
                    pu, rpu = tproj(O_BU, 512)
                    us, r_us = tmp("s_u", [128, 512], F32)
                    cp("act", us[:], pu[:], [rpu], [r_us])
                    vsq, r_vsq = tmp("s_vsq", [128, 256], F32)
                    act(vsq[:], pu[:, 256:512], AF.Square, [rpu], [r_vsq])
                    yield
                    sst, r_sst = tmp("s_st", [128, 24], F32)
                    rsum("dve", sst[:, 0:4], us[:, 256:512].rearrange("p (h e) -> p h e", h=4), [r_us], [r_sst])
                    rsum("dve", sst[:, 4:8], vsq[:].rearrange("p (h e) -> p h e", h=4), [r_vsq], [r_sst])
                    ts("dve", sst[:, 8:12], sst[:, 0:4], 1.0 / 64.0, None, MUL, None, [r_sst], [r_sst])
                    tt("dve", sst[:, 12:16], sst[:, 8:12], sst[:, 8:12], MUL, [r_sst], [r_sst])
                    stt("dve", sst[:, 12:16], sst[:, 4:8], 1.0 / 64.0, sst[:, 12:16], MUL, SUB, [r_sst], [r_sst])
                    ts("dve", sst[:, 16:20], sst[:, 12:16], EPS, -0.5, ADD, POW, [r_sst], [r_sst])
                    stt("dve", sst[:, 20:24], sst[:, 8:12], -1.0, sst[:, 16:20], MUL, MUL, [r_sst], [r_sst])
                    vn, r_vn = tmp("s_vn", [128, 256], F32)
                    for g in range(4):
                        ts("dve", vn[:, 64 * g:64 * g + 64], us[:, 256 + 64 * g:320 + 64 * g], sst[:, 16 + g:17 + g],
                           sst[:, 20 + g:21 + g], MUL, ADD, [r_us, r_sst], [r_vn])
                    tt("dve", vn[:], vn[:], sgg[:], MUL, [r_vn, rW], [r_vn])
                    vlb, r_vlb = tmp("s_vlb", [128, 256], BF16)
                    tt("dve", vlb[:], vn[:], sgb[:], ADD, [r_vn, rW], [r_vlb])
                    pmx, rpmx = b.ps()
                    for g in range(4):
                        mm(pmx[:, 64 * g:64 * g + 64], WT[:, g, :], vlb[:, 64 * g:64 * g + 64], True, True,
                           [rW, r_vlb], [rpmx])
                    for g in range(4):
                        stt("dve", ytok[:, 256 + 64 * g:320 + 64 * g], pmx[:, 64 * g:64 * g + 64], bsT[:, g:g + 1],
                            us[:, 64 * g:64 * g + 64], ADD, MUL, [rpmx, r_us, rW], [r_ytok])

                if 'fox' in SKIP:
                    mset('pool', ytok[:, 512:768], 0.0, [r_ytok])
                def gen_fox():
                    pkv, rpkv = tproj(O_CK, 512)
                    stg, r_stg = tmp("f_stg", [128, 512], F32)
                    cp("act", stg[:], pkv[:], [rpkv], [r_stg])
                    dma("sp", o_fk[l, t * 128:(t + 1) * 128, :], stg[:, 0:256], [r_stg], [], r_stg)
                    dma("sp", o_fv[l, t * 128:(t + 1) * 128, :], stg[:, 256:512], [r_stg], [], r_stg)
                    pcf, rpcf = b.ps()
                    for kc in range(8):
                        mm(pcf[:, 0:4], xT[:, kc, cs], Wi[:, kc, O_CF:O_CF + 4], kc == 0, False, [r_xT, rW], [rpcf])
                    mm(pcf[:, 0:4], ones_bf[0:1, :], bfb[:], False, True, [rc, rW], [rpcf])
                    fst, r_fst = tmp("f_st", [128, 16], F32)
                    act(fst[:, 0:4], pcf[:, 0:4], AF.Exp, [rpcf], [r_fst], scale=-1.0)
                    act(fst[:, 4:8], fst[:, 0:4], AF.Ln, [r_fst], [r_fst], bias=1.0, scale=1.0)
                    lfo, r_lfo = tmp("f_lfo", [128, 4], F32)
                    ts("dve", lfo[:], fst[:, 4:8], -1.0, None, MUL, None, [r_fst], [r_lfo])
                    dma("sp", o_flf[l, t * 128:(t + 1) * 128, :], lfo[:], [r_lfo], [], r_lfo)
                    pd, rpd = b.ps()
                    mm(pd[:, 0:4], triO[:], fst[:, 4:8], True, False, [rc, r_fst], [rpd])
                    mm(pd[:, 0:4], ones_f[:], Pacc[:], False, True, [rc, r_Pacc], [rpd])
                    act(fst[:, 8:12], pd[:, 0:4], AF.Exp, [rpd], [r_fst])
                    tt("dve", Pacc[:], Pacc[:], fst[:, 4:8], ADD, [r_Pacc, r_fst], [r_Pacc])
                    for h in range(4):
                        ts("dve", VA[:, t, h, 0:64], stg[:, 256 + 64 * h:320 + 64 * h], fst[:, 8 + h:9 + h], None, MUL, None,
                           [r_stg, r_fst], [r_VA[t]])
                    cp("dve", VA[:, t, :, 64], fst[:, 8:12], [r_fst], [r_VA[t]])
                    pov, rpov = b.psb[7]
                    yield
                    for h in range(4):
                        yield
                        hp = slice(64 * (h % 2), 64 * (h % 2) + 64)
                        hc = h // 2
                        for j0 in range(0, t + 1, 4):
                            js = list(range(j0, min(j0 + 4, t + 1)))
                            psc, rpsc = b.ps()
                            for j in js:
                                mm(psc[:, (j - j0) * 128:(j - j0 + 1) * 128], FK[hp, hc, j * 128:(j + 1) * 128], fq[hp, hc, cs],
                                   True, True, [r_FK[j // 4], r_fq], [rpsc])
                            pT, r_pT = tmp("f_pT", [128, 512], BF16, 3)
                            n = len(js) * 128
                            act(pT[:, 0:n], psc[:, 0:n], AF.Exp, [rpsc], [r_pT], scale=0.125)
                            if js[-1] == t:
                                sl = slice((t - j0) * 128, (t - j0 + 1) * 128)
                                tt("dve", pT[:, sl], pT[:, sl], mask_bf[:], MUL, [r_pT, rc], [r_pT])
                            for j in js:
                                mm(pov[:, 65 * h:65 * h + 65], pT[:, (j - j0) * 128:(j - j0 + 1) * 128], VA[:, j, h, :],
                                   j == 0, j == t, [r_pT, r_VA[j]], [rpov])
                    rd, r_rd = tmp("f_rd", [128, 4], F32)
                    P.op("dve", lambda e, rd=rd, pov=pov: e.reciprocal(
                        out=rd[:], in_=pov[:, 0:260].rearrange("p (h e) -> p h e", h=4)[:, :, 64]), r=[rpov], w=[r_rd])
                    for h in range(4):
                        ts("dve", ytok[:, 512 + 64 * h:576 + 64 * h], pov[:, 65 * h:65 * h + 64], rd[:, h:h + 1], None, MUL, None,
                           [rpov, r_rd], [r_ytok])

                if dbg and l == 0:
                    dma("pool", d_y[t * 128:(t + 1) * 128, :], ytok[:], [r_ytok], [], r_ytok)
                    if ti == 0:
                        for c in range(2):
                            dma("pool", d_yd[c * 128:(c + 1) * 128, c0:c0 + 512], ydT[:, c, :], [r_ydT], [], r_ydT)
                gens = []
                if ti == 0 and 'conv' not in SKIP:
                    gens.append(gen_conv())
                if 'gla' not in SKIP:
                    gens.append(gen_gla())
                if 'sgu' not in SKIP:
                    gens.append(gen_sgu())
                if 'fox' not in SKIP:
                    gens.append(gen_fox())
                it_ = 0
                while gens:
                    for g_ in list(gens):
                        try:
                            next(g_)
                        except StopIteration:
                            gens.remove(g_)
                    if pipe and it_ < 6:
                        pipe.step()
                    it_ += 1
                while pipe and it_ < 6:
                    pipe.step()
                    it_ += 1
                if ti == 0 and has_cache and 'sample' not in SKIP:
                    pipe = PastPipe(l, blk)
                pty, rpty = b.ps()
                ptyb = pty[:].bitcast(BF16)
                for c in range(6):
                    tr(ptyb[:, c * 128:(c + 1) * 128], ytok[:, c * 128:(c + 1) * 128], ident_bf[:], [r_ytok, rc], [rpty])
                yT, r_yT = tmp("yT", [128, 6, 128], BF16)
                cp("act", yT[:], ptyb[:, 0:768].rearrange("p (k n) -> p k n", k=6), [rpty], [r_yT])
                ps2, rps2 = [], []
                for hf in range(2):
                    pm_, rpm_ = b.ps()
                    for kc in range(8):
                        lh = yT[:, kc, :] if kc < 6 else ydT[:, kc - 6, cs]
                        mm(pm_[:], lh, Wo[:, kc, hf * 512:(hf + 1) * 512], kc == 0, kc == 7, [r_yT, r_ydT, rW], [rpm_])
                    ps2.append(pm_)
                    rps2.append(rpm_)
                layer_norm(t, ps2, rps2, l1g, l1b, None)
            if pipe:
                pipe.drain()
        if dbg and l == 0:
            for t in range(NT):
                dma("pool", d_x1[t * 128:(t + 1) * 128, :], R[:, t, :], [r_R[t]], [], r_R[t])
        if 'sample' not in SKIP:
            sample_mixer(l)
        for h in range(4):
            dma("sp", o_gla[l, h], Sf[32 * h:32 * h + 32, 64 * h:64 * h + 64], [r_S], [], r_S)

        for _once in ([] if 'ffn' in SKIP else [0]):
            if l == 0:
                hal = [b.sb([128, NFC, 2], F32, "hal%d_%d" % (l, i)) for i in range(2)]
                r_hal = [Res("hal0"), Res("hal1")]
                Wif = Wi[:].rearrange("p k n -> p (k n)")
                hT = Wif[:, 0:NFC * 512].rearrange("p (c n) -> p c n", c=NFC)
                r_hT = Res("hT")
                wus = [(Wif[:, 10752:18944].rearrange("p (k n) -> p k n", k=8), Res("wu0")), (Wo[:], Res("wu1"))]
                wds = [(VAflat[:, i * 1024:(i + 1) * 1024], Res("wd%d" % i)) for i in range(3)]
                gxs = [(FKf[:, i * 514:(i + 1) * 514], Res("gx%d" % i)) for i in range(2)]
                gas = [(qTf[:], Res("ga0")), (kTf[:], Res("ga1"))]
                ali = [r_hT] + [x[1] for x in wus + wds + gxs + gas]
            ALLR.extend(ali)
            cnt_f = [0, 0, 0, 0]
            P.op("pool", lambda e: e.memset(dummy[:], 0.0), w=[rW, r_dummy] + ali)
            for (tl, src) in ((l2g, ln2_g), (l2b, ln2_b)):
                dma("sp", tl[:], src[l:l + 1, :].broadcast_to([128, D]), [], [rW], rW)
            mset("pool", hal[1][:], 0.0, [r_hal[1]])
            make_xT(0)
            for blk in range(4):
                hcur, r_hcur = hal[blk % 2], r_hal[blk % 2]
                hprev, r_hprev = hal[(blk + 1) % 2], r_hal[(blk + 1) % 2]
                for fc in range(NFC):
                    wu, r_wu, og, ov = wu_chunk(l, fc, wus, cnt_f)
                    pg, rpg = b.ps()
                    for kc in range(8):
                        mm(pg[:], wu[:, kc, og:og + 128], xT[:, kc, :], kc == 0, kc == 7, [r_wu, r_xT], [rpg])
                    pv, rpv = b.ps()
                    for kc in range(8):
                        mm(pv[:], wu[:, kc, ov:ov + 128], xT[:, kc, :], kc == 0, kc == 7, [r_wu, r_xT], [rpv])
                    gx, r_gx = gxs[cnt_f[1] % 2]
                    cnt_f[1] += 1
                    cp("dve", gx[:, 0:2], hprev[:, fc, :], [r_hprev], [r_gx])
                    cp("act", gx[:, 2:514], pg[:], [rpg], [r_gx])
                    cp("dve", hcur[:, fc, :], gx[:, 512:514], [r_gx], [r_hcur])
                    ga, r_ga = gas[cnt_f[2] % 2]
                    cnt_f[2] += 1
                    act(ga[:], gx[:, 0:512], AF.Identity, [r_gx, rW], [r_ga], scale=fw[:, fc, 0:1])
                    stt("dve", ga[:], gx[:, 1:513], fw[:, fc, 1:2], ga[:], MUL, ADD, [r_gx, rW, r_ga], [r_ga])
                    stt("dve", ga[:], gx[:, 2:514], fw[:, fc, 2:3], ga[:], MUL, ADD, [r_gx, rW, r_ga], [r_ga])
                    act(ga[:], ga[:], AF.Silu, [r_ga, rW], [r_ga], bias=fb[:, fc:fc + 1], scale=1.0)
                    tt("dve", hT[:, fc, :], ga[:], pv[:], MUL, [r_ga, rpv], [r_hT])
                if blk == 3:
                    for c in range(NFC):
                        dma("sp", o_ffc[l][:, c * 128:(c + 1) * 128].rearrange("j p -> p j"), hcur[:, c, :], [r_hcur], [],
                            r_hcur, allow_slow_non_contiguous=True)
                if dbg and l == 0 and blk == 0:
                    for fc in range(NFC):
                        dma("pool", d_h[fc * 128:(fc + 1) * 128, :], hT[:, fc, :], [r_hT], [], r_hT)
                if blk < 3:
                    make_xT(blk + 1)
                banks = [b.psb[6], b.psb[7]] + [b.ps() for _ in range(6)]
                for fc in range(NFC):
                    wd, r_wd = wds[cnt_f[3] % 3]
                    cnt_f[3] += 1
                    dma("pool", wd[:], w_dn[l, fc * 128:(fc + 1) * 128, :], [], [r_wd], r_wd)
                    for ti in range(4):
                        for hf in range(2):
                            pb, rpb = banks[ti * 2 + hf]
                            mm(pb[:], hT[:, fc, ti * 128:(ti + 1) * 128], wd[:, hf * 512:(hf + 1) * 512], fc == 0, fc == NFC - 1,
                               [r_hT, r_wd], [rpb])
                for ti in range(4):
                    t = blk * 4 + ti
                    layer_norm(t, [banks[ti * 2][0], banks[ti * 2 + 1][0]], [banks[ti * 2][1], banks[ti * 2 + 1][1]], l2g, l2b,
                               o_y[t * 128:(t + 1) * 128, :] if last else None)
        if 'ffn' not in SKIP:
            if 'sample' not in SKIP:
                sample_ffn(l, last, wus, wds, cnt_f)
            P.op("pool", lambda e: e.memset(dummy[:], 0.0), w=[rW, r_dummy] + ali)
        if dbg and l == 0 and not last:
            for t in range(NT):
                dma("pool", d_x2[t * 128:(t + 1) * 128, :], R[:, t, :], [r_R[t]], [], r_R[t])
    P.emit(stack)
    return nc, stack


IN_NAMES = ["w_in", "w_o", "gla_w_a", "gla_b_a", "gla_norm_g", "sgu_ln_g", "sgu_ln_b", "sgu_w", "fox_b_f",
            "conv_w", "conv_b", "conv_norm_g", "conv_norm_b", "ln1_g", "ln1_b", "ln2_g", "ln2_b",
            "ffn_conv_w", "ffn_conv_b"]


def run(inputs, nlayers=DEPTH, cores=8, dbg=False, has_cache=True, trace=False):
    nc, stack = build(nlayers=nlayers, dbg=dbg, has_cache=has_cache)
    with stack:
        pass
    in_maps = []
    for c in range(cores):
        m = {"xp": np.ascontiguousarray(inputs["x_prompt"][c]),
             "xs": np.ascontiguousarray(inputs["x_sample"][4 * c:4 * c + 4]).reshape(NS, D),
             "st_gla": np.ascontiguousarray(inputs["state_gla"][:, 4 * c:4 * c + 4]),
             "st_conv": np.ascontiguousarray(inputs["state_conv"][:, 4 * c:4 * c + 4]),
             "st_ffc": np.ascontiguousarray(inputs["state_ffn_conv"][:, 4 * c:4 * c + 4])}
        if has_cache:
            m["pt"] = np.ascontiguousarray(inputs["page_table"][4 * c:4 * c + 4]).astype(np.int32)
            m["ck"] = np.ascontiguousarray(inputs["cache_fox_k"]).reshape(DEPTH, NPOOL, 128, 256)
            m["cv"] = np.ascontiguousarray(inputs["cache_fox_v"]).reshape(DEPTH, NPOOL, 128, 256)
            m["clf"] = np.ascontiguousarray(inputs["cache_fox_logf"])
        for k in IN_NAMES:
            m[k] = np.ascontiguousarray(inputs[k])
        m["w_up"] = np.ascontiguousarray(inputs["ffn_w_up"])
        m["w_dn"] = np.ascontiguousarray(inputs["ffn_w_down"])
        m["sgu_b"] = np.ascontiguousarray(inputs["sgu_b"])
        in_maps.append(m)
    if trace:
        res = run_bass_kernel_spmd(nc, in_maps, core_ids=list(range(cores)), trace=True)
        print("EXEC_TIME_NS", res.exec_time_ns)
        return res.results
    res = run_bass_kernel_spmd(nc, in_maps, core_ids=list(range(cores)))
    return res.results


def kernel(**inputs):
    rs = run(inputs)
    f = np.float32
    y_p = np.stack([r["o_y"] for r in rs]).astype(f)
    p_fk = np.stack([r["o_fk"] for r in rs], axis=1).reshape(DEPTH, 8, SEQ, 4, 64).astype(f)
    p_fv = np.stack([r["o_fv"] for r in rs], axis=1).reshape(DEPTH, 8, SEQ, 4, 64).astype(f)
    p_lf = np.stack([r["o_flf"] for r in rs], axis=1).astype(f)
    p_gla = np.stack([r["o_gla"] for r in rs], axis=1).astype(f)
    p_conv = np.stack([r["o_conv"] for r in rs], axis=1).astype(f)
    p_ffc = np.stack([r["o_ffc"] for r in rs], axis=1).astype(f)
    y_s = np.concatenate([r["o_ys"] for r in rs]).reshape(32, 4, D).astype(f)

    def cat(nm, shp):
        return np.concatenate([r[nm].reshape((DEPTH, 4) + shp) for r in rs], axis=1).astype(f)
    s_fk = cat("o_sfk", (4, 4, 64))
    s_fv = cat("o_sfv", (4, 4, 64))
    s_lf = cat("o_sflf", (4, 4))
    s_gla = cat("o_sgla", (4, 32, 64))
    s_conv = cat("o_sconv", (30, 256))
    s_ffc = cat("o_sffc", (2, DFF))
    s_sgu = cat("o_ssgu", (4, 256))
    return (y_p, y_s, p_fk, p_fv, p_lf, p_gla, p_conv, p_ffc, s_fk, s_fv, s_lf, s_gla, s_conv, s_ffc, s_sgu)
```

```python
import numpy as np
import os
GSTOP = int(os.environ.get('GSTOP', '99'))
SKIP = set(os.environ.get('SKIP', '').split(','))
from contextlib import ExitStack
import concourse.bass as bass
import concourse.mybir as mybir
from concourse.bass_utils import run_bass_kernel_spmd

F32 = mybir.dt.float32
BF16 = mybir.dt.bfloat16
I32 = mybir.dt.int32
AF = mybir.ActivationFunctionType
ALU = mybir.AluOpType
AX = mybir.AxisListType

D = 1024
SEQ = 2048
NT = 16
DEPTH = 2
DIN = 2580
DFF = 2688
NFC = 21
EPS = 1e-5
ALPHA = (2 * DEPTH) ** 0.25
NS = 16

O_AQ, O_AK, O_AV, O_AG, O_ALR, O_BU, O_BV, O_CQ, O_CK, O_CV, O_CF, O_DIN = (
    0, 128, 256, 512, 768, 784, 1040, 1296, 1552, 1808, 2064, 2068)


class Res:
    __slots__ = ("name", "lw", "rd", "excl")

    def __init__(self, name="", excl=False):
        self.name = name
        self.lw = None
        self.rd = []
        self.excl = excl


class Op:
    __slots__ = ("eng", "fn", "raw", "oth", "pos", "dma", "key", "inc", "val", "users")


class Prog:
    ENGS = ("pe", "act", "dve", "pool", "sp")

    def __init__(self, nc):
        self.nc = nc
        self.by = {e: [] for e in self.ENGS}
        self.dcount = {}
        self.all_dma = []

    def op(self, eng, fn, r=(), w=(), dma=False, key=None):
        o = Op()
        o.eng, o.fn, o.dma, o.key = eng, fn, dma, key
        o.raw, o.oth = set(), set()
        o.inc, o.val, o.users = False, 0, 0
        o.pos = len(self.by[eng])
        for x in r:
            if x.lw is not None:
                o.raw.add(x.lw)
            if x.excl:
                for q in x.rd:
                    if q.eng != eng:
                        o.oth.add(q)
        for x in w:
            if x.lw is not None:
                if not (dma and x.lw.dma and x.lw.key is key and x.lw.eng == eng):
                    o.oth.add(x.lw)
            for q in x.rd:
                o.oth.add(q)
        for x in r:
            x.rd.append(o)
        for x in w:
            x.lw = o
            x.rd = []
        if dma:
            assert key is not None
            c = self.dcount.get(key, 0) + 16
            self.dcount[key] = c
            o.val = c
            self.all_dma.append(o)
        self.by[eng].append(o)
        return o

    def _needs(self, p, q):
        if p.dma:
            return True
        if p.eng != q.eng:
            return True
        if p.eng == "pe":
            return False
        return (p in q.raw) and (q.pos - p.pos <= 3)

    def emit(self, stack):
        nc = self.nc
        fin = self.op("sp", None)
        for d in self.all_dma:
            fin.oth.add(d)
        for e in self.ENGS:
            for q in self.by[e]:
                for p in (q.raw | q.oth):
                    if p is q:
                        continue
                    if self._needs(p, q):
                        p.inc = True
        esem = {}
        for e in self.ENGS:
            esem[e] = stack.enter_context(nc.semaphore("e_" + e))
            c = 0
            for o in self.by[e]:
                if o.inc and not o.dma:
                    c += 1
                    o.val = c
        dsem = {}
        for k in self.dcount:
            dsem[k] = stack.enter_context(nc.semaphore("d%d" % len(dsem)))
        print("sems used", len(dsem) + 5, "ops", {e: len(self.by[e]) for e in self.ENGS})

        def run(e, eng):
            waited = {}
            for q in self.by[e]:
                need = {}
                for p in (q.raw | q.oth):
                    if p is q or not self._needs(p, q):
                        continue
                    s = dsem[p.key] if p.dma else esem[p.eng]
                    if need.get(s, (0,))[0] < p.val:
                        need[s] = (p.val, s)
                for s, (v, _) in need.items():
                    if waited.get(s, 0) < v:
                        eng.wait_ge(s, v)
                        waited[s] = v
                if q.fn is None:
                    continue
                ins = q.fn(eng)
                if q.dma:
                    ins.then_inc(dsem[q.key], 16)
                elif q.inc:
                    ins.then_inc(esem[e], 1)

        with nc.Block() as block:
            @block.tensor
            def _(eng):
                run("pe", eng)

            @block.scalar
            def _(eng):
                run("act", eng)

            @block.vector
            def _(eng):
                run("dve", eng)

            @block.gpsimd
            def _(eng):
                run("pool", eng)

            @block.sync
            def _(eng):
                run("sp", eng)


class B:
    def __init__(self, nc, stack):
        self.nc, self.stack = nc, stack
        self.P = Prog(nc)
        self.nps = 0
        self.psb = []
        for i in range(8):
            t = stack.enter_context(nc.psum_tensor("ps%d" % i, [128, 512], F32))
            self.psb.append((t, Res("ps%d" % i, excl=True)))
        self.cnt = 0

    def sb(self, shape, dt, name=None):
        self.cnt += 1
        t = self.stack.enter_context(self.nc.sbuf_tensor(name or ("t%d" % self.cnt), list(shape), dt))
        return t

    def ps(self):
        t, r = self.psb[self.nps % 6]
        self.nps += 1
        return t, r


NPOOL = 2560


def build(has_cache=True, dbg=False, nlayers=DEPTH):
    nc = bass.Bass("TRN2", target_bir_lowering=False)
    stack = ExitStack()
    b = B(nc, stack)
    P = b.P

    def din(name, shape, dt=F32):
        return nc.dram_tensor(name, list(shape), dt, kind="ExternalInput").ap()

    def dout(name, shape, dt=F32):
        return nc.dram_tensor(name, list(shape), dt, kind="ExternalOutput").ap()

    def mm(out, lhsT, rhs, st, sp, r, w, **kw):
        P.op("pe", lambda e: e.matmul(out, lhsT, rhs, start=st, stop=sp, **kw), r=r, w=w)

    def tr(out, in_, ident, r, w):
        P.op("pe", lambda e: e.transpose(out=out, in_=in_, identity=ident), r=r, w=w)

    def act(out, in_, func, r, w, **kw):
        P.op("act", lambda e: e.activation(out=out, in_=in_, func=func, **kw), r=r, w=w)

    def tt(eng, out, in0, in1, op, r, w):
        P.op(eng, lambda e: e.tensor_tensor(out=out, in0=in0, in1=in1, op=op), r=r, w=w)

    def ts(eng, out, in0, s1, s2, op0, op1, r, w):
        if op1 == ALU.pow:
            act(out, in0, AF.Ln, r, w, bias=float(s1), scale=1.0)
            act(out, out, AF.Exp, list(r) + list(w), w, scale=float(s2))
            return
        if op0 == ALU.pow:
            act(out, in0, AF.Ln, r, w)
            act(out, out, AF.Exp, list(r) + list(w), w, scale=float(s1))
            return
        if s2 is None:
            P.op(eng, lambda e: e.tensor_scalar(out=out, in0=in0, scalar1=s1, scalar2=None, op0=op0), r=r, w=w)
        else:
            P.op(eng, lambda e: e.tensor_scalar(out=out, in0=in0, scalar1=s1, scalar2=s2, op0=op0, op1=op1), r=r, w=w)

    def stt(eng, out, in0, sc, in1, op0, op1, r, w):
        P.op("dve", lambda e: e.scalar_tensor_tensor(out=out, in0=in0, scalar=sc, in1=in1, op0=op0, op1=op1), r=r, w=w)

    def cp(eng, out, in_, r, w):
        if eng == "act":
            P.op(eng, lambda e: e.copy(out=out, in_=in_), r=r, w=w)
        else:
            P.op(eng, lambda e: e.tensor_copy(out=out, in_=in_), r=r, w=w)

    def rsum(eng, out, in_, r, w):
        P.op(eng, lambda e: e.reduce_sum(out=out, in_=in_, axis=AX.X), r=r, w=w)

    def mset(eng, ap, v, w):
        P.op(eng, lambda e: e.memset(ap, v), w=w)

    def asel(out, in_, pattern, cmp_, fill, base, cm, r, w):
        P.op("pool", lambda e: e.affine_select(out=out, in_=in_, pattern=pattern, compare_op=cmp_, fill=fill,
                                               base=base, channel_multiplier=cm), r=r, w=w)

    def dma(eng, out, in_, r, w, key, **kw):
        P.op(eng, lambda e: e.dma_start(out=out, in_=in_, **kw), r=r, w=w, dma=True, key=key)

    MUL, ADD, SUB, POW = ALU.mult, ALU.add, ALU.subtract, ALU.pow

    xp = din("xp", [SEQ, D])
    w_in = din("w_in", [DEPTH, D, DIN])
    w_o = din("w_o", [DEPTH, D, D])
    w_up = din("w_up", [DEPTH, D, 2 * DFF])
    w_dn = din("w_dn", [DEPTH, DFF, D])
    gla_w_a = din("gla_w_a", [DEPTH, 16, 128])
    gla_b_a = din("gla_b_a", [DEPTH, 128])
    gla_g = din("gla_norm_g", [DEPTH, 256])
    sgu_g = din("sgu_ln_g", [DEPTH, 256])
    sgu_bb = din("sgu_ln_b", [DEPTH, 256])
    sgu_w = din("sgu_w", [DEPTH, 4, 128, 128])
    sgu_bs = din("sgu_b", [DEPTH, 4, 128])
    fox_bf = din("fox_b_f", [DEPTH, 4])
    conv_w = din("conv_w", [DEPTH, 31, 256])
    conv_b = din("conv_b", [DEPTH, 256])
    cn_g = din("conv_norm_g", [DEPTH, 256])
    cn_b = din("conv_norm_b", [DEPTH, 256])
    ln1_g = din("ln1_g", [DEPTH, D])
    ln1_b = din("ln1_b", [DEPTH, D])
    ln2_g = din("ln2_g", [DEPTH, D])
    ln2_b = din("ln2_b", [DEPTH, D])
    fcw = din("ffn_conv_w", [DEPTH, 3, DFF])
    fcb = din("ffn_conv_b", [DEPTH, DFF])

    o_y = dout("o_y", [SEQ, D])
    o_fk = dout("o_fk", [DEPTH, SEQ, 256])
    o_fv = dout("o_fv", [DEPTH, SEQ, 256])
    o_flf = dout("o_flf", [DEPTH, SEQ, 4])
    o_gla = dout("o_gla", [DEPTH, 4, 32, 64])
    o_conv = dout("o_conv", [DEPTH, 30, 256])
    o_ffc = dout("o_ffc", [DEPTH, 2, DFF])
    if has_cache:
        pt_d = din("pt", [4, 64], I32)
        ck = din("ck", [DEPTH, NPOOL, 128, 256])
        cv = din("cv", [DEPTH, NPOOL, 128, 256])
        clf = din("clf", [DEPTH, NPOOL, 128, 4])
    xs_d = din("xs", [NS, D])
    st_gla = din("st_gla", [DEPTH, 4, 4, 32, 64])
    st_conv = din("st_conv", [DEPTH, 4, 30, 256])
    st_ffc = din("st_ffc", [DEPTH, 4, 2, DFF])
    o_ys = dout("o_ys", [NS, D])
    o_sfk = dout("o_sfk", [DEPTH, NS, 256])
    o_sfv = dout("o_sfv", [DEPTH, NS, 256])
    o_sflf = dout("o_sflf", [DEPTH, NS, 4])
    o_sgla = dout("o_sgla", [DEPTH, 4, 4, 32, 64])
    o_sconv = dout("o_sconv", [DEPTH, 4, 30, 256])
    o_sffc = dout("o_sffc", [DEPTH, 4, 2, DFF])
    o_ssgu = dout("o_ssgu", [DEPTH, NS, 256])
    if dbg:
        d_y = dout("d_y", [SEQ, 768])
        d_yd = dout("d_yd", [256, SEQ])
        d_x1 = dout("d_x1", [SEQ, D])
        d_h = dout("d_h", [DFF, 512])
        d_x2 = dout("d_x2", [SEQ, D])

    rc = Res("const")
    ident_f = b.sb([128, 128], F32, "ident_f")
    ident_bf = b.sb([128, 128], BF16, "ident_bf")
    revM = b.sb([128, 128], F32, "revM")
    triO = b.sb([128, 128], F32, "triO")
    ones_f = b.sb([128, 128], F32, "ones_f")
    ones_bf = b.sb([1, 128], BF16, "ones_bf")
    mask_bf = b.sb([128, 128], BF16, "mask_bf")
    mask4 = b.sb([128, 512], BF16, "mask4")
    blk64 = b.sb([128, 128], F32, "blk64")
    blkmask = b.sb([128, 256], F32, "blkmask")
    hm = b.sb([128, 4], F32, "hm")

    mset("pool", ident_f[:], 0.0, [rc])
    asel(ident_f[:], ident_f[:], [[-1, 128]], ALU.not_equal, 1.0, 0, 1, [rc], [rc])
    cp("dve", ident_bf[:], ident_f[:], [rc], [rc])
    mset("pool", revM[:], -1.0 / 16.0, [rc])
    asel(revM[:], revM[:], [[-1, 128]], ALU.is_gt, 0.0, 0, 1, [rc], [rc])
    mset("pool", triO[:], 1.0, [rc])
    asel(triO[:], triO[:], [[1, 128]], ALU.is_ge, 0.0, 0, -1, [rc], [rc])
    mset("pool", ones_f[:], 1.0, [rc])
    mset("pool", ones_bf[:], 1.0, [rc])
    cp("dve", mask_bf[:], triO[:], [rc], [rc])
    triMb = b.sb([128, 128], BF16, "triMb")
    revMb = b.sb([128, 128], BF16, "revMb")
    ts("dve", triMb[:], triO[:], -1.0 / 16.0, None, MUL, None, [rc], [rc])
    cp("dve", revMb[:], revM[:], [rc], [rc])
    for h in range(4):
        cp("dve", mask4[:, h * 128:(h + 1) * 128], triO[:], [rc], [rc])
    mset("pool", blk64[:], 0.0, [rc])
    mset("pool", blk64[0:64, 0:64], 1.0 / 64.0, [rc])
    mset("pool", blk64[64:128, 64:128], 1.0 / 64.0, [rc])
    mset("pool", blkmask[:], 1.0, [rc])
    mset("pool", hm[:], 1.0, [rc])
    for h in range(4):
        v = blkmask[:, 64 * h:64 * h + 64]
        asel(v, v, [[0, 64]], ALU.is_ge, 0.0, -32 * h, 1, [rc], [rc])
        asel(v, v, [[0, 64]], ALU.is_ge, 0.0, 32 * h + 31, -1, [rc], [rc])
        v = hm[:, h:h + 1]
        asel(v, v, [[0, 1]], ALU.is_ge, 0.0, -32 * h, 1, [rc], [rc])
        asel(v, v, [[0, 1]], ALU.is_ge, 0.0, 32 * h + 31, -1, [rc], [rc])

    R = b.sb([128, NT, D], BF16, "R")
    r_R = [Res("R%d" % t) for t in range(NT)]
    Wi = b.sb([128, 8, DIN], BF16, "Wi")
    Wo = b.sb([128, 8, D], BF16, "Wo")
    rW = Res("W")
    Wa = b.sb([16, 128], BF16, "Wa")
    ba = b.sb([1, 128], BF16, "ba")
    bfb = b.sb([1, 4], BF16, "bfb")
    glag = b.sb([128, 256], F32, "glag")
    sgg = b.sb([128, 256], F32, "sgg")
    sgb = b.sb([128, 256], F32, "sgb")
    l1g = b.sb([128, D], F32, "l1g")
    l1b = b.sb([128, D], F32, "l1b")
    l2g, l2b = l1g, l1b
    dummy = b.sb([128, 2], F32, "dummy_t")
    r_dummy = Res("dummy")
    cw = b.sb([128, 2, 31], F32, "cw")
    cb = b.sb([128, 2], F32, "cb")
    cng = b.sb([128, 2], F32, "cng")
    cnb = b.sb([128, 2], F32, "cnb")
    fw = b.sb([128, NFC, 3], F32, "fw")
    fb = b.sb([128, NFC], F32, "fb")
    bs4 = b.sb([4, 128], BF16, "bs4")
    bsT = b.sb([128, 4], F32, "bsT")
    WT = b.sb([128, 4, 128], BF16, "WT")

    xT = b.sb([128, 8, 512], BF16, "xT")
    r_xT = Res("xT")
    xTf = xT[:].rearrange("p k n -> p (k n)").bitcast(F32)
    sw = xTf[:, 0:512].rearrange("p (g s) -> p g s", g=4)
    FK = b.sb([128, 2, SEQ], BF16, "FK")
    r_FK = [Res("FK%d" % i) for i in range(4)]
    VAflat = b.sb([128, NT * 4 * 65], BF16, "VA")
    VA = VAflat[:].rearrange("p (t h e) -> p t h e", t=NT, h=4)
    r_VA = [Res("VA%d" % t) for t in range(NT)]
    fq = b.sb([128, 2, 512], BF16, "fq")
    r_fq = Res("fq")
    qTf = b.sb([128, 512], F32, "qTf")
    kTf = b.sb([128, 512], F32, "kTf")
    r_qk = Res("qk")
    alr = b.sb([16, 512], BF16, "alr")
    r_alr = Res("alr")
    gext = [b.sb([128, 2, 542], BF16, "gext%d" % i) for i in range(2)]
    r_gext = [Res("gext%d" % i) for i in range(2)]
    glast = b.sb([128, 2, 30], F32, "glast")
    r_glast = Res("glast")
    ydT = b.sb([128, 2, 512], BF16, "ydT")
    r_ydT = Res("ydT")
    Sf = b.sb([128, 256], F32, "Sf")
    Sb = b.sb([128, 256], BF16, "Sb")
    r_S = Res("S")
    Pacc = b.sb([128, 4], F32, "Pacc")
    r_Pacc = Res("Pacc")

    nbuf = {}

    def tmp(name, shape, dt, n=1):
        if name not in nbuf:
            nbuf[name] = [[(b.sb(shape, dt, "%s_%d" % (name, i)), Res(name)) for i in range(n)], 0]
        lst = nbuf[name]
        t, r = lst[0][lst[1] % n]
        lst[1] += 1
        return t, r


    Rs = b.sb([NS, D], BF16, "Rs")
    r_Rs = Res("Rs")
    xTs = b.sb([128, 8, NS], BF16, "xTs")
    r_xTs = Res("xTs")
    sb16 = b.sb([NS, NS], F32, "sb16")
    maskS = b.sb([NS, NS], F32, "maskS")
    maskS4 = b.sb([NS, 64], F32, "maskS4")
    maskN = b.sb([NS, 64], F32, "maskN")
    triSb = b.sb([NS, NS], BF16, "triSb")
    revSb = b.sb([NS, NS], BF16, "revSb")
    bm = b.sb([NS, 4], F32, "bm")
    mset("pool", sb16[:], 1.0, [rc])
    mset("pool", bm[:], 1.0, [rc])
    for bb in range(4):
        v = sb16[:, 4 * bb:4 * bb + 4]
        asel(v, v, [[0, 4]], ALU.is_ge, 0.0, -4 * bb, 1, [rc], [rc])
        asel(v, v, [[0, 4]], ALU.is_ge, 0.0, 4 * bb + 3, -1, [rc], [rc])
        v = bm[:, bb:bb + 1]
        asel(v, v, [[0, 1]], ALU.is_ge, 0.0, -4 * bb, 1, [rc], [rc])
        asel(v, v, [[0, 1]], ALU.is_ge, 0.0, 4 * bb + 3, -1, [rc], [rc])
    tt("dve", maskS[:], sb16[:], triO[0:NS, 0:NS], MUL, [rc], [rc])
    ts("dve", triSb[:], maskS[:], -1.0 / 16.0, None, MUL, None, [rc], [rc])
    tt("dve", sb16[:], sb16[:], maskS[:], SUB, [rc], [rc])
    ts("dve", revSb[:], sb16[:], -1.0 / 16.0, None, MUL, None, [rc], [rc])
    for h in range(4):
        cp("dve", maskS4[:, h * 16:(h + 1) * 16], maskS[:], [rc], [rc])
        cp("dve", maskN[:].rearrange("p (b h q) -> p b h q", b=4, h=4)[:, :, h, :],
           maskS[:].rearrange("p (b q) -> p b q", b=4), [rc], [rc])
    dma("pool", Rs[:], xs_d[:, :], [], [r_Rs], r_Rs)
    if has_cache:
        idxr = b.sb([128, 256], I32, "idxr")
        idx64 = b.sb([128, 4], I32, "idx64")
        r_idx = Res("idx")
        ia = xTf[:, 0:256].bitcast(I32)
        io = xTf[:, 256:512].bitcast(I32)
        for bb in range(4):
            dma("sp", ia[:, bb * 64:(bb + 1) * 64], pt_d[bb:bb + 1, :].broadcast_to([128, 64]), [], [r_xT], r_xT)
        P.op("pool", lambda e: e.iota(io, pattern=[[0, 256]], base=0, channel_multiplier=1), w=[r_xT])
        stt("dve", idxr[:], ia, 128, io, MUL, ADD, [r_xT], [r_idx])
        mset("pool", idx64[:], 0, [r_idx])
        dma("sp", idx64[0:64, :], pt_d.rearrange("b j -> j b"), [], [r_idx], r_idx, allow_slow_non_contiguous=True)
        ckf = ck.rearrange("l n r c -> (l n r) c")
        cvf = cv.rearrange("l n r c -> (l n r) c")
        clff = clf.rearrange("l n r h -> (l n) (r h)")
    fpc = [0]

    ALLR = []

    def switch():
        P.op("pool", lambda e: e.memset(dummy[:], 0.0), w=[rW, r_dummy, r_xT, r_qk, r_fq, r_ydT, r_alr] + r_gext + r_FK + r_VA + ALLR)

    class Arena:
        def __init__(self, aps):
            self.aps = aps
            self.i, self.pos = 0, 0

        def take(self, parts, cols, dt, name):
            w = cols if dt == F32 else (cols + 1) // 2
            while self.pos + w > self.aps[self.i][1]:
                self.i += 1
                self.pos = 0
            a = self.aps[self.i][0][0:parts, self.pos:self.pos + w]
            self.pos += w
            r = Res(name)
            ALLR.append(r)
            if dt != F32:
                a = a.bitcast(BF16)[:, 0:cols]
            return a, r

    FKf = FK[:].rearrange("p c n -> p (c n)").bitcast(F32)
    VAf = VAflat[:].bitcast(F32)
    qTff, kTff = qTf[:], kTf[:]
    fqf = fq[:].rearrange("p c n -> p (c n)").bitcast(F32)
    ydf = ydT[:].rearrange("p c n -> p (c n)").bitcast(F32)
    g0f = gext[0][:].rearrange("p c n -> p (c n)").bitcast(F32)
    g1f = gext[1][:].rearrange("p c n -> p (c n)").bitcast(F32)

    GP = 4
    fpd = {}
    if has_cache:
        fpd["Kg"] = [(b.sb([128, GP * 256], BF16, "Kg%d" % i), Res("Kg%d" % i)) for i in range(2)]
        fpd["Vg"] = [(b.sb([128, GP * 258], BF16, "Vg%d" % i), Res("Vg%d" % i)) for i in range(2)]
        fpd["KT"] = [(b.sb([128, 2 * GP * 128], BF16, "KTp%d" % i), Res("KTp%d" % i)) for i in range(2)]
        fpd["t"] = [(b.sb([128, GP * 16], F32, "fp_t%d" % i), Res("fp_t%d" % i)) for i in range(2)]
        fpd["pTb"] = [(b.sb([128, GP * 16], BF16, "fp_pTb%d" % i), Res("fp_pTb%d" % i)) for i in range(2)]
        fpd["tot"] = (b.sb([64, 4], F32, "fp_tot"), Res("fp_tot"))
        fpd["opast"] = (b.sb([128, 257], F32, "opast"), Res("opast"))
        fpd["hsq"] = (b.sb([NS, 258], F32, "hsq"), Res("hsq"))
        fpd["fqs"] = (b.sb([128, 32], BF16, "fqs_p"), Res("fqs_p"))
        fpd["qblk"] = (b.sb([128, 64], BF16, "qblk_p"), Res("qblk_p"))
        for i in range(2):
            v, r = fpd["Vg"][i]
            mset("pool", v[:].rearrange("p (g c) -> p g c", g=GP)[:, :, 256:258], 1.0, [r])

    def sample_q_prework(l):
        hsq, r_hsq = fpd["hsq"]
        fqs, r_fqs = fpd["fqs"]
        qblk, r_qblk = fpd["qblk"]
        pt, rp = b.ps()
        ptb = pt[:].bitcast(BF16)
        for kc in range(8):
            tr(ptb[:, kc * NS:(kc + 1) * NS], Rs[:, kc * 128:(kc + 1) * 128], ident_bf[0:NS, 0:NS], [r_Rs, rc], [rp])
        cp("act", xTs[:], ptb[:, 0:8 * NS].rearrange("p (k n) -> p k n", k=8), [rp], [r_xTs])
        pq, rpq = b.ps()
        for kc in range(8):
            mm(pq[0:NS, 0:256], xTs[:, kc, :], Wi[:, kc, O_CQ:O_CQ + 256], kc == 0, kc == 7, [r_xTs, rW], [rpq])
        cp("act", hsq[:, 0:256], pq[0:NS, 0:256], [rpq], [r_hsq])
        pt2, rp2 = b.ps()
        for c in range(2):
            tr(pt2[:, c * 16:(c + 1) * 16], hsq[:, c * 128:(c + 1) * 128], ident_f[0:NS, 0:NS], [r_hsq, rc], [rp2])
        cp("act", fqs[:], pt2[:, 0:32], [rp2], [r_fqs])
        mset("pool", qblk[:], 0.0, [r_qblk])
        qb4 = qblk[:].rearrange("p (c b m) -> p c b m", c=2, b=4)
        for h2 in range(2):
            cp("dve", qb4[64 * h2:64 * h2 + 64, :, :, 4 * h2:4 * h2 + 4],
               fqs[64 * h2:64 * h2 + 64, 0:32].rearrange("p (c b q) -> p c b q", c=2, b=4), [r_fqs], [r_qblk])

    def past_setup(l, bb):
        d = fpd
        sig_t, r_lfp = tmp("sig", [128, 512], F32)
        lfp = sig_t[0:64, :]
        msq_t, r_lfT = tmp("cmsq", [128, 512], F32)
        lfT = msq_t[:, 0:256]
        var_t, r_rhsB = tmp("cvar", [128, 512], F32)
        rhsB = var_t[0:64, 0:256]
        csq_t, r_eS = tmp("csq", [128, 512], F32)
        eS = csq_t[:, 0:256]
        tot, r_tot = d["tot"]
        P.op("pool", lambda e: e.indirect_dma_start(
            out=lfp, out_offset=None, in_=clff,
            in_offset=bass.IndirectOffsetOnAxis(ap=idx64[0:64, bb:bb + 1], axis=0), element_offset=l * NPOOL * 512),
            r=[r_idx], w=[r_lfp], dma=True, key=r_lfp)
        pt, rp = b.ps()
        for h in range(4):
            tr(pt[:, h * 64:(h + 1) * 64], lfp.rearrange("j (r h) -> j h r", h=4)[:, h, :], ident_f[0:64, 0:64],
               [r_lfp, rc], [rp])
        cp("act", lfT, pt[:, 0:256], [rp], [r_lfT])
        rsum("dve", tot[:], lfp.rearrange("j (r h) -> j h r", h=4), [r_lfp], [r_tot])
        for h in range(4):
            ts("dve", rhsB[:, h * 64:(h + 1) * 64], revM[0:64, 0:64], tot[:, h:h + 1], None, MUL, None, [rc, r_tot], [r_rhsB])
        pS, rpS = b.ps()
        mm(pS[:, 0:256], revM[:], lfT, True, False, [rc, r_lfT], [rpS])
        mm(pS[:, 0:256], ones_f[0:64, :], rhsB, False, True, [rc, r_rhsB], [rpS])
        act(eS, pS[:, 0:256], AF.Exp, [rpS], [r_eS], scale=-16.0)
        return eS.rearrange("s (h j) -> s h j", h=4), r_eS

    NG = 64 // GP

    def past_group(l, bb, g, eS3, r_eS):
        d = fpd
        pov, rpov = b.psb[6]
        qblk, r_qblk = d["qblk"]
        qb4 = qblk[:].rearrange("p (c b m) -> p c b m", c=2, b=4)
        k = fpc[0] % 2
        fpc[0] += 1
        Kg, r_Kg = d["Kg"][k]
        Vg, r_Vg = d["Vg"][k]
        KT, r_KT = d["KT"][k]
        t_, r_t = d["t"][k]
        pTb, r_pTb = d["pTb"][k]
        Kg3 = Kg[:].rearrange("p (g c) -> p g c", g=GP)
        Vg3 = Vg[:].rearrange("p (g c) -> p g c", g=GP)
        KT4 = KT[:].rearrange("p (c g s) -> p c g s", c=2, g=GP)

        def f_dmak():
            for p in range(GP):
                j = g * GP + p
                P.op("pool", lambda e, p=p, j=j: e.indirect_dma_start(
                    out=Kg3[:, p, :], out_offset=None, in_=ckf,
                    in_offset=bass.IndirectOffsetOnAxis(ap=idxr[:, bb * 64 + j:bb * 64 + j + 1], axis=0),
                    element_offset=l * NPOOL * 128 * 256),
                    r=[r_idx], w=[r_Kg], dma=True, key=r_Kg)

        def f_dmav():
            for p in range(GP):
                j = g * GP + p
                P.op("pool", lambda e, p=p, j=j: e.indirect_dma_start(
                    out=Vg3[:, p, 0:256], out_offset=None, in_=cvf,
                    in_offset=bass.IndirectOffsetOnAxis(ap=idxr[:, bb * 64 + j:bb * 64 + j + 1], axis=0),
                    element_offset=l * NPOOL * 128 * 256),
                    r=[r_idx], w=[r_Vg], dma=True, key=r_Vg)

        def f_a():
            pk, rpk = b.ps()
            pkb = pk[:].bitcast(BF16)
            for c in range(2):
                for p in range(GP):
                    tr(pkb[:, (c * GP + p) * 128:(c * GP + p + 1) * 128], Kg3[:, p, c * 128:(c + 1) * 128], ident_bf[:],
                       [r_Kg, rc], [rpk])
            cp("act" if g % 2 else "dve", KT[:], pkb[:, 0:2 * GP * 128], [rpk], [r_KT])

        def f_b():
            psc, rpsc = b.ps()
            for p in range(GP):
                for c in range(2):
                    mm(psc[:, p * 16 + c * 8:p * 16 + c * 8 + 8], KT4[:, c, p, :], qb4[:, c, bb, :], True, True,
                       [r_KT, r_qblk], [rpsc])
            act(t_[:], psc[:, 0:GP * 16], AF.Exp, [rpsc], [r_t], scale=0.125)
            t4 = t_[:].rearrange("s (p h q) -> s p h q", p=GP, h=4)
            pT4 = pTb[:].rearrange("s (p h q) -> s p h q", p=GP, h=4)
            eSg = eS3[:, :, g * GP:(g + 1) * GP].rearrange("s h p -> s p h")
            for q in range(4):
                tt("dve", pT4[:, :, :, q], t4[:, :, :, q], eSg, MUL, [r_t, r_eS], [r_pTb])

        def f_c():
            for p in range(GP):
                mm(pov[0:NS, 0:257], pTb[:, p * 16:(p + 1) * 16], Vg3[:, p, 0:257], g == 0 and p == 0,
                   g == NG - 1 and p == GP - 1, [r_pTb, r_Vg], [rpov])
        return f_dmak, f_dmav, f_a, f_b, f_c

    class PastPipe:
        def __init__(self, l, bb):
            self.l, self.bb = l, bb
            self.eS3, self.r_eS = past_setup(l, bb)
            self.groups = [past_group(l, bb, g, self.eS3, self.r_eS) for g in range(NG)]
            self.s = 0
            self.groups[0][0]()

        def issue(self, n):
            pass

        def step(self):
            s_ = self.s
            G = self.groups
            if 0 <= s_ - 2 < NG:
                G[s_ - 2][4]()
            if s_ < NG:
                G[s_][1]()
            if 0 <= s_ - 1 < NG:
                G[s_ - 1][3]()
            if s_ + 1 < NG:
                G[s_ + 1][0]()
            if s_ < NG:
                G[s_][2]()
            self.s += 1

        def drain(self):
            while self.s < NG + 2:
                self.step()
            opast, r_opast = fpd["opast"]
            pov, rpov = b.psb[6]
            hsq, r_hsq = fpd["hsq"]
            cp("act", hsq[:, 0:257], pov[0:NS, 0:257], [rpov], [r_hsq])
            dma("sp", opast[32 * self.bb:32 * self.bb + NS, :], hsq[:, 0:257], [r_hsq], [r_opast], r_opast)

    def sample_mixer(l):
        switch()
        ar = Arena([(FKf, 2048), (VAf, 2080), (xTf, 2048), (qTff, 512), (kTff, 512), (fqf, 512), (ydf, 512),
                    (g0f, 542), (g1f, 542)])
        T = ar.take
        hsA, r_hsA = T(NS, 1296, F32, "hsA")
        hsB, r_hsB = T(NS, 1284, F32, "hsB")

        def hc(a, e):
            if e <= 1296:
                return hsA[:, a:e], r_hsA
            return hsB[:, a - 1296:e - 1296], r_hsB
        WSf, r_WSf = T(NS, 64, F32, "WSf")
        WSb, r_WSb = T(NS, 64, BF16, "WSb")
        bsS, r_bsS = T(NS, 4, F32, "bsS")
        bff, r_bff = T(NS, 4, F32, "bff")
        S0f, r_S0f = T(128, 1024, F32, "S0f")
        S0b, r_S0b = T(128, 1024, BF16, "S0b")
        xxT, r_xxT = T(128, 2 * 4 * 34, F32, "xxT")
        xx4 = xxT.rearrange("p (c b t) -> p c b t", c=2, b=4)
        mset("pool", WSf, 0.0, [r_WSf])
        mset("pool", S0f, 0.0, [r_S0f])
        WS3 = WSf.rearrange("p (g t) -> p g t", g=4)
        S03 = S0f.rearrange("p (b n) -> p b n", b=4)
        NCD = dict(allow_slow_non_contiguous=True)
        for bb in range(4):
            for g in range(4):
                dma("sp", WS3[4 * bb:4 * bb + 4, g, 4 * bb:4 * bb + 4], sgu_w[l, g, 0:4, 0:4].rearrange("t s -> s t"),
                    [], [r_WSf], r_WSf, **NCD)
            dma("sp", bsS[4 * bb:4 * bb + 4, :], sgu_bs[l][:, 0:4].rearrange("g t -> t g"), [], [r_bsS], r_bsS, **NCD)
            for h in range(4):
                dma("sp", S03[32 * h:32 * h + 32, bb, 64 * h:64 * h + 64], st_gla[l, bb, h], [], [r_S0f], r_S0f)
            for c in range(2):
                dma("sp", xx4[:, c, bb, 0:30], st_conv[l, bb][:, c * 128:(c + 1) * 128].rearrange("t p -> p t"),
                    [], [r_xxT], r_xxT, **NCD)
            dma("sp", o_sconv[l, bb, 0:26, :], st_conv[l, bb, 4:30, :], [], [], r_xxT)
        dma("sp", bff, fox_bf[l:l + 1, :].broadcast_to([NS, 4]), [], [r_bff], r_bff)
        for g in range(4):
            tt("dve", WSb.rearrange("p (g t) -> p g t", g=4)[:, g, :], WS3[:, g, :], maskS[:], MUL, [r_WSf, rc], [r_WSb])
        cp("act", S0b, S0f, [r_S0f], [r_S0b])
        S0b3 = S0b.rearrange("p (b n) -> p b n", b=4)

        pt, rp = b.ps()
        ptb = pt[:].bitcast(BF16)
        for kc in range(8):
            tr(ptb[:, kc * NS:(kc + 1) * NS], Rs[:, kc * 128:(kc + 1) * 128], ident_bf[0:NS, 0:NS], [r_Rs, rc], [rp])
        cp("act", xTs[:], ptb[:, 0:8 * NS].rearrange("p (k n) -> p k n", k=8), [rp], [r_xTs])
        for g0 in (0, 512, 1024, 1296, 1808, 2320):
            e0 = {0: 512, 512: 1024, 1024: 1296, 1296: 1808, 1808: 2320, 2320: DIN}[g0]
            n = e0 - g0
            pt, rp = b.ps()
            for kc in range(8):
                mm(pt[0:NS, 0:n], xTs[:, kc, :], Wi[:, kc, g0:e0], kc == 0, kc == 7, [r_xTs, rW], [rp])
            dst, rdst = hc(g0, e0)
            cp("act" if (g0 // 512) % 2 else "dve", dst, pt[0:NS, 0:n], [rp], [rdst])
        v_, r_ = hc(O_CK, O_CK + 256)
        dma("sp", o_sfk[l], v_, [r_], [], r_)
        v_, r_ = hc(O_CV, O_CV + 256)
        dma("sp", o_sfv[l], v_, [r_], [], r_)
        switch()
        ar2 = Arena([(Wi[:].rearrange("p k n -> p (k n)").bitcast(F32), 10320)])

        def T(parts, cols, dt, name):
            for a_ in (ar2, ar):
                try:
                    return a_.take(parts, cols, dt, name)
                except IndexError:
                    a_.i = len(a_.aps) - 1
                    a_.pos = a_.aps[-1][1]
            raise RuntimeError("sample arenas exhausted: " + name)
        ys, r_ys = T(NS, 768, BF16, "ys")
        idf = ident_f[0:NS, 0:NS]

        ydTs, r_ydTs = T(128, 32, BF16, "s_ydTs")
        def _sg_gla():
            pt, rp = b.ps()
            tr(pt[0:16, 0:NS], hsA[:, O_ALR:O_ALR + 16], idf, [r_hsA, rc], [rp])
            tr(pt[:, 16:32], hsA[:, O_AQ:O_AQ + 128], idf, [r_hsA, rc], [rp])
            tr(pt[:, 32:48], hsA[:, O_AK:O_AK + 128], idf, [r_hsA, rc], [rp])
            alrs, r_alrs = T(16, NS, BF16, "alrs")
            cp("act", alrs, pt[0:16, 0:NS], [rp], [r_alrs])
            qk_s, r_qks = T(128, 32, F32, "qk_s")
            cp("act", qk_s, pt[:, 16:48], [rp], [r_qks])
            yield
            pz, rpz = b.ps()
            mm(pz[0:NS, 0:128], alrs, Wa[:], True, False, [r_alrs, rW], [rpz])
            mm(pz[0:NS, 0:128], ones_bf[0:1, 0:NS], ba[:], False, True, [rc, rW], [rpz])
            e1, r_e1 = T(NS, 128, F32, "s_e1")
            act(e1, pz[0:NS, 0:128], AF.Exp, [rpz], [r_e1], scale=-1.0)
            spl, r_spl = T(NS, 128, F32, "s_spl")
            act(spl, e1, AF.Ln, [r_e1], [r_spl], bias=1.0, scale=1.0)
            shi, r_shi = T(NS, 128, BF16, "s_shi")
            slo, r_slo = T(NS, 128, BF16, "s_slo")
            cp("dve", shi, spl, [r_spl], [r_shi])
            tt("dve", slo, spl, shi, SUB, [r_spl, r_shi], [r_slo])
            yield
            pg_, rpg_ = b.ps()
            mm(pg_[:, 0:NS], shi, triSb[:], True, False, [r_shi, rc], [rpg_])
            mm(pg_[:, 0:NS], slo, triSb[:], False, True, [r_slo, rc], [rpg_])
            mm(pg_[0:NS, 128:256], revSb[:], shi, True, False, [r_shi, rc], [rpg_])
            mm(pg_[0:NS, 128:256], revSb[:], slo, False, True, [r_slo, rc], [rpg_])
            egs, r_egs = T(128, 32, F32, "s_eg")
            act(egs[:, 0:16], pg_[:, 0:NS], AF.Exp, [rpg_], [r_egs])
            act(egs[:, 16:32], pg_[:, 0:NS], AF.Exp, [rpg_], [r_egs], scale=-1.0)
            erev, r_erev = T(NS, 128, F32, "s_erev")
            act(erev, pg_[0:NS, 128:256], AF.Exp, [rpg_], [r_erev])
            qtl, r_qtl = T(128, NS, BF16, "s_qtl")
            stt("dve", qtl, qk_s[:, 0:16], 32.0 ** -0.5, egs[:, 0:16], MUL, MUL, [r_qks, r_egs], [r_qtl])
            kt4, r_kt4 = T(128, 64, BF16, "s_kt4")
            for h in range(4):
                stt("dve", kt4[:, h * 16:(h + 1) * 16], qk_s[:, 16:32], hm[:, h:h + 1], egs[:, 16:32], MUL, MUL,
                    [r_qks, r_egs, rc], [r_kt4])
            qtb, r_qtb = T(128, 64, BF16, "s_qtb")
            mset("pool", qtb, 0.0, [r_qtb])
            for bb in range(4):
                cp("dve", qtb[:, bb * 16 + 4 * bb:bb * 16 + 4 * bb + 4], qtl[:, 4 * bb:4 * bb + 4], [r_qtl], [r_qtb])
            kpb, r_kpb = T(NS, 512, BF16, "s_kpb")
            for bb in range(4):
                stt("dve", kpb[:, bb * 128:(bb + 1) * 128], hsA[:, O_AK:O_AK + 128], bm[:, bb:bb + 1], erev, MUL, MUL,
                    [r_hsA, r_erev, rc], [r_kpb])
            vb, r_vb = T(NS, 256, BF16, "s_vb")
            cp("act", vb, hsA[:, O_AV:O_AV + 256], [r_hsA], [r_vb])
            sgl, r_sgl = T(NS, 256, F32, "s_sgl")
            act(sgl, hsA[:, O_AG:O_AG + 256], AF.Silu, [r_hsA], [r_sgl])
            tt("dve", sgl, sgl, glag[0:NS, :], MUL, [r_sgl, rW], [r_sgl])
            yield
            pa_, rpa_ = b.ps()
            for h in range(4):
                mm(pa_[0:NS, h * 16:(h + 1) * 16], kt4[:, h * 16:(h + 1) * 16], qtl, True, True, [r_kt4, r_qtl], [rpa_])
            asb, r_asb = T(NS, 64, BF16, "s_asb")
            tt("dve", asb, pa_[0:NS, 0:64], maskS4[:], MUL, [rpa_, rc], [r_asb])
            yield
            po, rpo = b.ps()
            for bb in range(4):
                mm(po[0:NS, 0:256], qtb[:, bb * 16:(bb + 1) * 16], S0b3[:, bb, :], bb == 0, bb == 3, [r_qtb, r_S0b], [rpo])
            for h in range(4):
                mm(po[0:NS, 256 + 64 * h:320 + 64 * h], asb[:, h * 16:(h + 1) * 16], vb[:, 64 * h:64 * h + 64], True, True,
                   [r_asb, r_vb], [rpo])
            of, r_of = T(NS, 256, F32, "s_of")
            cp("act", of, po[0:NS, 0:256], [rpo], [r_of])
            tt("dve", of, of, po[0:NS, 256:512], ADD, [r_of, rpo], [r_of])
            sn, r_sn = T(128, 256, F32, "s_sn")
            for bb in range(4):
                yield
                pn, rpn = b.ps()
                mm(pn[:, 0:256], kpb[:, bb * 128:(bb + 1) * 128], vb, True, True, [r_kpb, r_vb], [rpn])
                tt("dve", sn, pn[:, 0:256], blkmask[:], MUL, [rpn, rc], [r_sn])
                stt("dve", sn, S03[:, bb, :], egs[:, 4 * bb + 3:4 * bb + 4], sn, MUL, ADD, [r_S0f, r_egs, r_sn], [r_sn])
                for h in range(4):
                    dma("sp", o_sgla[l, bb, h], sn[32 * h:32 * h + 32, 64 * h:64 * h + 64], [r_sn], [], r_sn)
            osq, r_osq = T(NS, 256, F32, "s_osq")
            act(osq, of, AF.Square, [r_of], [r_osq])
            gst, r_gst = T(NS, 8, F32, "s_gst")
            rsum("dve", gst[:, 0:4], osq.rearrange("p (h e) -> p h e", h=4), [r_osq], [r_gst])
            ts("dve", gst[:, 4:8], gst[:, 0:4], 1.0 / 64.0, EPS, MUL, ADD, [r_gst], [r_gst])
            ts("dve", gst[:, 4:8], gst[:, 4:8], -0.5, None, POW, None, [r_gst], [r_gst])
            for h in range(4):
                stt("dve", ys[:, 64 * h:64 * h + 64], of[:, 64 * h:64 * h + 64], gst[:, 4 + h:5 + h],
                    sgl[:, 64 * h:64 * h + 64], MUL, MUL, [r_of, r_gst, r_sgl], [r_ys])
            yield

        def _sg_sgu():
            uu = hsA[:, O_BU:O_BU + 256]
            vv = hsA[:, O_BV:O_BV + 256]
            vsq, r_vsq = T(NS, 256, F32, "s_vsq")
            act(vsq, vv, AF.Square, [r_hsA], [r_vsq])
            sst, r_sst = T(NS, 24, F32, "s_sst")
            rsum("dve", sst[:, 0:4], vv.rearrange("p (h e) -> p h e", h=4), [r_hsA], [r_sst])
            rsum("dve", sst[:, 4:8], vsq.rearrange("p (h e) -> p h e", h=4), [r_vsq], [r_sst])
            ts("dve", sst[:, 8:12], sst[:, 0:4], 1.0 / 64.0, None, MUL, None, [r_sst], [r_sst])
            tt("dve", sst[:, 12:16], sst[:, 8:12], sst[:, 8:12], MUL, [r_sst], [r_sst])
            stt("dve", sst[:, 12:16], sst[:, 4:8], 1.0 / 64.0, sst[:, 12:16], MUL, SUB, [r_sst], [r_sst])
            ts("dve", sst[:, 16:20], sst[:, 12:16], EPS, -0.5, ADD, POW, [r_sst], [r_sst])
            stt("dve", sst[:, 20:24], sst[:, 8:12], -1.0, sst[:, 16:20], MUL, MUL, [r_sst], [r_sst])
            vn, r_vn = T(NS, 256, F32, "s_vn")
            for g in range(4):
                ts("dve", vn[:, 64 * g:64 * g + 64], vv[:, 64 * g:64 * g + 64], sst[:, 16 + g:17 + g], sst[:, 20 + g:21 + g],
                   MUL, ADD, [r_hsA, r_sst], [r_vn])
            tt("dve", vn, vn, sgg[0:NS, :], MUL, [r_vn, rW], [r_vn])
            tt("dve", vn, vn, sgb[0:NS, :], ADD, [r_vn, rW], [r_vn])
            dma("sp", o_ssgu[l], vn, [r_vn], [], r_vn)
            vlb, r_vlb = T(NS, 256, BF16, "s_vlb")
            cp("dve", vlb, vn, [r_vn], [r_vlb])
            pmx, rpmx = b.ps()
            for g in range(4):
                mm(pmx[0:NS, 64 * g:64 * g + 64], WSb[:, g * 16:(g + 1) * 16], vlb[:, 64 * g:64 * g + 64], True, True,
                   [r_WSb, r_vlb], [rpmx])
            for g in range(4):
                stt("dve", ys[:, 256 + 64 * g:320 + 64 * g], pmx[0:NS, 64 * g:64 * g + 64], bsS[:, g:g + 1],
                    uu[:, 64 * g:64 * g + 64], ADD, MUL, [rpmx, r_hsA, r_bsS], [r_ys])
            yield

        def _sg_fox():
            cf_, r_cf = hc(O_CF, O_CF + 4)
            fst, r_fst = T(NS, 16, F32, "s_fst")
            tt("dve", fst[:, 0:4], cf_, bff, ADD, [r_cf, r_bff], [r_fst])
            act(fst[:, 0:4], fst[:, 0:4], AF.Exp, [r_fst], [r_fst], scale=-1.0)
            act(fst[:, 4:8], fst[:, 0:4], AF.Ln, [r_fst], [r_fst], bias=1.0, scale=1.0)
            ts("dve", fst[:, 12:16], fst[:, 4:8], -1.0, None, MUL, None, [r_fst], [r_fst])
            dma("sp", o_sflf[l], fst[:, 12:16], [r_fst], [], r_fst)
            pd, rpd = b.ps()
            mm(pd[0:NS, 0:4], maskS[:], fst[:, 4:8], True, True, [rc, r_fst], [rpd])
            act(fst[:, 8:12], pd[0:NS, 0:4], AF.Exp, [rpd], [r_fst])
            yield
            pt, rp = b.ps()
            for c in range(2):
                v_, r_ = hc(O_CQ + 128 * c, O_CQ + 128 * c + 128)
                tr(pt[:, c * 16:(c + 1) * 16], v_, idf, [r_, rc], [rp])
                v_, r_ = hc(O_CK + 128 * c, O_CK + 128 * c + 128)
                tr(pt[:, 32 + c * 16:32 + (c + 1) * 16], v_, idf, [r_, rc], [rp])
            fqk, r_fqk = T(128, 64, BF16, "s_fqk")
            cp("act", fqk, pt[:, 0:64], [rp], [r_fqk])
            qblk, r_qblk = T(128, 2 * 4 * 8, BF16, "s_qblk")
            mset("pool", qblk, 0.0, [r_qblk])
            qb4 = qblk.rearrange("p (c b m) -> p c b m", c=2, b=4)
            for h2 in range(2):
                cp("dve", qb4[64 * h2:64 * h2 + 64, :, :, 4 * h2:4 * h2 + 4],
                   fqk[64 * h2:64 * h2 + 64, 0:32].rearrange("p (c b q) -> p c b q", c=2, b=4), [r_fqk], [r_qblk])
            vaug, r_vaug = T(NS, 258, BF16, "s_vaug")
            v_, r_ = hc(O_CV, O_CV + 256)
            cp("act", vaug[:, 0:256], v_, [r_], [r_vaug])
            mset("pool", vaug[:, 256:258], 1.0, [r_vaug])
            yield
            psn, rpsn = b.ps()
            for bb in range(4):
                for c in range(2):
                    mm(psn[0:NS, bb * 16 + c * 8:bb * 16 + c * 8 + 8], fqk[:, 32 + c * 16:32 + (c + 1) * 16], qb4[:, c, bb, :],
                       True, True, [r_fqk, r_qblk], [rpsn])
            t1, r_t1 = T(NS, 64, F32, "s_t1")
            act(t1, psn[0:NS, 0:64], AF.Exp, [rpsn], [r_t1], scale=0.125)
            tt("dve", t1, t1, maskN[:], MUL, [r_t1, rc], [r_t1])
            pnb, r_pnb = T(NS, 64, BF16, "s_pnb")
            t14 = t1.rearrange("p (b h q) -> p b h q", b=4, h=4)
            pn4 = pnb.rearrange("p (b h q) -> p b h q", b=4, h=4)
            for h in range(4):
                ts("dve", pn4[:, :, h, :], t14[:, :, h, :], fst[:, 8 + h:9 + h], None, MUL, None, [r_t1, r_fst], [r_pnb])
            ycs, r_ycs = T(NS, 256, F32, "s_ycs")
            osum, r_osum = T(NS, 258, F32, "s_osum")
            rd, r_rd = T(NS, 2, F32, "s_rd")
            on, r_on = T(NS, 256, F32, "s_on")
            for bb in range(4):
                pov, rpov = b.psb[7]
                mm(pov[0:NS, 0:257], pnb[:, bb * 16:(bb + 1) * 16], vaug[:, 0:257], True, True, [r_pnb, r_vaug], [rpov])
                if has_cache:
                    dma("sp", osum[:, 0:257], fpd["opast"][0][32 * bb:32 * bb + NS, :], [fpd["opast"][1]], [r_osum], r_osum)
                    tt("dve", osum[:, 0:257], osum[:, 0:257], pov[0:NS, 0:257], ADD, [rpov, r_osum], [r_osum])
                else:
                    cp("dve", osum[:, 0:257], pov[0:NS, 0:257], [rpov], [r_osum])
                P.op("dve", lambda e, rd=rd, osum=osum: e.reciprocal(out=rd[:, 0:1], in_=osum[:, 256:257]), r=[r_osum], w=[r_rd])
                ts("dve", on, osum[:, 0:256], rd[:, 0:1], None, MUL, None, [r_osum, r_rd], [r_on])
                for h in range(4):
                    dma("sp", ycs[4 * bb:4 * bb + 4, 64 * h:64 * h + 64], on[4 * h:4 * h + 4, 64 * h:64 * h + 64],
                        [r_on], [r_ycs], r_ycs)
            cp("dve", ys[:, 512:768], ycs, [r_ycs], [r_ys])
            yield

        def _sg_conv():
            ga_, r_ga = hc(O_DIN, O_DIN + 256)
            gg_, r_gg = hc(O_DIN + 256, O_DIN + 512)
            glu, r_glu = T(NS, 256, F32, "s_glu")
            act(glu, gg_, AF.Sigmoid, [r_gg], [r_glu])
            tt("dve", glu, glu, ga_, MUL, [r_glu, r_ga], [r_glu])
            for bb in range(4):
                dma("sp", o_sconv[l, bb, 26:30, :], glu[4 * bb:4 * bb + 4, :], [r_glu], [], r_glu)
            pt, rp = b.ps()
            for c in range(2):
                tr(pt[:, c * 16:(c + 1) * 16], glu[:, c * 128:(c + 1) * 128], idf, [r_glu, rc], [rp])
            for c in range(2):
                cp("act", xx4[:, c, :, 30:34], pt[:, c * 16:(c + 1) * 16].rearrange("p (b q) -> p b q", b=4), [rp], [r_xxT])
            acc, r_acc = T(128, 32, F32, "s_acc")
            for c in range(2):
                a3 = acc[:, c * 16:(c + 1) * 16].rearrange("p (b q) -> p b q", b=4)
                for j in range(31):
                    if j == 0:
                        ts("dve", a3, xx4[:, c, :, 0:4], cw[:, c, 0:1], None, MUL, None, [r_xxT, rW], [r_acc])
                    else:
                        stt("dve", a3, xx4[:, c, :, j:j + 4], cw[:, c, j:j + 1], a3, MUL, ADD, [r_xxT, rW, r_acc], [r_acc])
            for c in range(2):
                ac = acc[:, c * 16:(c + 1) * 16]
                cof, r_cof = T(128, 16, F32, "s_cof%d" % c)
                act(cof, ac, AF.Identity, [r_acc, rW], [r_cof], bias=cb[:, c:c + 1], scale=1.0)
                sq, r_sq = T(128, 16, F32, "s_csq%d" % c)
                tt("dve", sq, cof, cof, MUL, [r_cof], [r_sq])
                yield
                pm, rpm = b.ps()
                mm(pm[:, 0:16], blk64[:], cof, True, True, [rc, r_cof], [rpm])
                mm(pm[:, 16:32], blk64[:], sq, True, True, [rc, r_sq], [rpm])
                mv, r_mv = T(128, 32, F32, "s_mv%d" % c)
                cp("act", mv, pm[:, 0:32], [rpm], [r_mv])
                tt("dve", sq, mv[:, 0:16], mv[:, 0:16], MUL, [r_mv], [r_sq])
                tt("dve", sq, mv[:, 16:32], sq, SUB, [r_mv, r_sq], [r_sq])
                ts("dve", sq, sq, EPS, -0.5, ADD, POW, [r_sq], [r_sq])
                tt("dve", cof, cof, mv[:, 0:16], SUB, [r_cof, r_mv], [r_cof])
                tt("dve", cof, cof, sq, MUL, [r_cof, r_sq], [r_cof])
                act(ydTs[:, c * 16:(c + 1) * 16], cof, AF.Silu, [r_cof, rW], [r_ydTs], scale=cng[:, c:c + 1], bias=cnb[:, c:c + 1])
            yield

        _gens = [_sg_gla(), _sg_sgu(), _sg_fox(), _sg_conv()]
        while _gens:
            for g_ in list(_gens):
                try:
                    next(g_)
                except StopIteration:
                    _gens.remove(g_)
        pty, rpty = b.ps()
        ptyb = pty[:].bitcast(BF16)
        for c in range(6):
            tr(ptyb[:, c * 16:(c + 1) * 16], ys[:, c * 128:(c + 1) * 128], ident_bf[0:NS, 0:NS], [r_ys, rc], [rpty])
        yTs, r_yTs = T(128, 96, BF16, "s_yTs")
        cp("act", yTs, ptyb[:, 0:96], [rpty], [r_yTs])
        ps2, rps2 = [], []
        for hf in range(2):
            pm_, rpm_ = b.ps()
            for kc in range(8):
                lh = yTs[:, kc * 16:(kc + 1) * 16] if kc < 6 else ydTs[:, (kc - 6) * 16:(kc - 5) * 16]
                mm(pm_[0:NS, :], lh, Wo[:, kc, hf * 512:(hf + 1) * 512], kc == 0, kc == 7, [r_yTs, r_ydTs, rW], [rpm_])
            ps2.append(pm_)
            rps2.append(rpm_)
        layer_norm(None, ps2, rps2, l1g, l1b, None, n=NS, res=Rs[:], r_res=r_Rs)
        switch()

    cur_wu = [None]

    def wu_chunk(l, fc, wus, cnt_f):
        if fc % 4 == 0:
            cols = min(4, NFC - fc) * 128
            wu, r_wu = wus[cnt_f[0] % 2]
            cnt_f[0] += 1
            cur_wu[0] = (wu, r_wu)
            dma("pool", wu[:, :, 0:cols], w_up[l, :, fc * 128:fc * 128 + cols].rearrange("(k p) n -> p k n", p=128),
                [], [r_wu], r_wu)
            dma("pool", wu[:, :, 512:512 + cols],
                w_up[l, :, DFF + fc * 128:DFF + fc * 128 + cols].rearrange("(k p) n -> p k n", p=128), [], [r_wu], r_wu)
        wu, r_wu = cur_wu[0]
        ci = fc % 4
        return wu, r_wu, ci * 128, 512 + ci * 128

    def sample_ffn_prep(l):
        switch()
        ar = Arena([(fqf, 512), (ydf, 512), (g0f, 542), (g1f, 542)])
        T = ar.take
        pt, rp = b.ps()
        ptb = pt[:].bitcast(BF16)
        for kc in range(8):
            tr(ptb[:, kc * NS:(kc + 1) * NS], Rs[:, kc * 128:(kc + 1) * 128], ident_bf[0:NS, 0:NS], [r_Rs, rc], [rp])
        cp("act", xTs[:], ptb[:, 0:8 * NS].rearrange("p (k n) -> p k n", k=8), [rp], [r_xTs])
        d = {}
        d["bufT"] = T(128, NFC * 8, F32, "f_bufT")
        d["glo"] = T(128, NFC * 8, F32, "f_glo")
        d["hTs"] = T(128, NFC * 16, BF16, "f_hTs")
        bufT, r_bufT = d["bufT"]
        bu4 = bufT.rearrange("p (c b j) -> p c b j", c=NFC, b=4)
        NCD = dict(allow_slow_non_contiguous=True)
        for fc in range(NFC):
            for bb in range(4):
                dma("sp", bu4[:, fc, bb, :], st_ffc[l, bb][:, fc * 128:(fc + 1) * 128].rearrange("j p -> p j"),
                    [], [r_bufT], r_bufT, **NCD)
        d["gxl"] = [T(128, 24, F32, "f_gx%d" % i) for i in range(2)]
        d["gal"] = [T(128, 16, F32, "f_ga%d" % i) for i in range(2)]
        return d

    def sample_ffn_chunk(l, fc, wu, r_wu, og, ov, d):
        bufT, r_bufT = d["bufT"]
        glo, r_glo = d["glo"]
        hTs, r_hTs = d["hTs"]
        bu4 = bufT.rearrange("p (c b j) -> p c b j", c=NFC, b=4)
        gl4 = glo.rearrange("p (c b j) -> p c b j", c=NFC, b=4)
        pg, rpg = b.ps()
        for kc in range(8):
            mm(pg[:, 0:16], wu[:, kc, og:og + 128], xTs[:, kc, :], kc == 0, kc == 7, [r_wu, r_xTs], [rpg])
        for kc in range(8):
            mm(pg[:, 16:32], wu[:, kc, ov:ov + 128], xTs[:, kc, :], kc == 0, kc == 7, [r_wu, r_xTs], [rpg])
        gx, r_gx = d["gxl"][fc % 2]
        ga, r_ga = d["gal"][fc % 2]
        gx3 = gx.rearrange("p (b t) -> p b t", b=4)
        ga3 = ga.rearrange("p (b t) -> p b t", b=4)
        cp("dve", gx3[:, :, 0:2], bu4[:, fc, :, :], [r_bufT], [r_gx])
        cp("act", gx3[:, :, 2:6], pg[:, 0:16].rearrange("p (b t) -> p b t", b=4), [rpg], [r_gx])
        cp("dve", gl4[:, fc, :, :], gx3[:, :, 4:6], [r_gx], [r_glo])
        ts("dve", ga3, gx3[:, :, 0:4], fw[:, fc, 0:1], None, MUL, None, [r_gx, rW], [r_ga])
        stt("dve", ga3, gx3[:, :, 1:5], fw[:, fc, 1:2], ga3, MUL, ADD, [r_gx, rW, r_ga], [r_ga])
        stt("dve", ga3, gx3[:, :, 2:6], fw[:, fc, 2:3], ga3, MUL, ADD, [r_gx, rW, r_ga], [r_ga])
        act(ga, ga, AF.Silu, [r_ga, rW], [r_ga], bias=fb[:, fc:fc + 1], scale=1.0)
        tt("dve", hTs[:, fc * 16:(fc + 1) * 16], ga, pg[:, 16:32], MUL, [r_ga, rpg], [r_hTs])

    def sample_ffn_down(l, last, wds, cnt_f, d):
        glo, r_glo = d["glo"]
        hTs, r_hTs = d["hTs"]
        gl4 = glo.rearrange("p (c b j) -> p c b j", c=NFC, b=4)
        NCD = dict(allow_slow_non_contiguous=True)
        for fc in range(NFC):
            for bb in range(4):
                dma("sp", o_sffc[l, bb][:, fc * 128:(fc + 1) * 128].rearrange("j p -> p j"), gl4[:, fc, bb, :],
                    [r_glo], [], r_glo, **NCD)
        bk = [b.ps(), b.ps()]
        for fc in range(NFC):
            wd, r_wd = wds[cnt_f[3] % 3]
            cnt_f[3] += 1
            dma("pool", wd[:], w_dn[l, fc * 128:(fc + 1) * 128, :], [], [r_wd], r_wd)
            for hf in range(2):
                mm(bk[hf][0][0:NS, :], hTs[:, fc * 16:(fc + 1) * 16], wd[:, hf * 512:(hf + 1) * 512], fc == 0, fc == NFC - 1,
                   [r_hTs, r_wd], [bk[hf][1]])
        layer_norm(None, [bk[0][0], bk[1][0]], [bk[0][1], bk[1][1]], l2g, l2b, o_ys[:, :] if last else None,
                   n=NS, res=Rs[:], r_res=r_Rs)
        switch()

    for t in range(NT):
        dma("pool", R[:, t, :], xp[t * 128:(t + 1) * 128, :], [], [r_R[t]], r_R[t])

    def layer_norm_gen(t, ps2, rps2, g_t, b_t, out_dram, n=128, res=None, r_res=None):
        if res is None:
            res, r_res = R[:, t, :], r_R[t]
        rf, r_rf = tmp("ln_rf", [128, D], F32)
        for hf in range(2):
            stt("dve", rf[0:n, hf * 512:(hf + 1) * 512], res[:, hf * 512:(hf + 1) * 512], ALPHA, ps2[hf][0:n, :],
                MUL, ADD, [r_res, rps2[hf]], [r_rf])
        yield
        st, r_st = tmp("ln_st", [128, 8], F32)
        xn, r_xn = tmp("ln_xn", [128, D], F32)
        act(xn[0:n, :], rf[0:n, :], AF.Identity, [r_rf], [r_xn, r_st], accum_out=st[0:n, 0:1])
        act(xn[0:n, :], rf[0:n, :], AF.Square, [r_rf], [r_xn, r_st], accum_out=st[0:n, 1:2])
        yield
        ts("dve", st[0:n, 2:3], st[0:n, 0:1], 1.0 / D, None, MUL, None, [r_st], [r_st])
        tt("dve", st[0:n, 3:4], st[0:n, 2:3], st[0:n, 2:3], MUL, [r_st], [r_st])
        stt("dve", st[0:n, 4:5], st[0:n, 1:2], 1.0 / D, st[0:n, 3:4], MUL, SUB, [r_st], [r_st])
        yield
        ts("dve", st[0:n, 5:6], st[0:n, 4:5], EPS, -0.5, ADD, POW, [r_st], [r_st])
        stt("dve", st[0:n, 6:7], st[0:n, 2:3], -1.0, st[0:n, 5:6], MUL, MUL, [r_st], [r_st])
        yield
        act(xn[0:n, :], rf[0:n, :], AF.Identity, [r_rf, r_st], [r_xn], scale=st[0:n, 5:6], bias=st[0:n, 6:7])
        yield
        tt("dve", xn[0:n, :], xn[0:n, :], g_t[0:n, :], MUL, [r_xn, rW], [r_xn])
        if out_dram is None:
            tt("dve", res, xn[0:n, :], b_t[0:n, :], ADD, [r_xn, rW], [r_res])
        else:
            tt("dve", xn[0:n, :], xn[0:n, :], b_t[0:n, :], ADD, [r_xn, rW], [r_xn])
            dma("sp", out_dram, xn[0:n, :], [r_xn], [], r_xn)

    def layer_norm(*a, **kw):
        for _ in layer_norm_gen(*a, **kw):
            pass

    def make_xT(blk):
        for ti in range(4):
            t = blk * 4 + ti
            pt, rp = b.ps()
            ptb = pt[:].bitcast(BF16)
            for kc in range(8):
                tr(ptb[:, kc * 128:(kc + 1) * 128], R[:, t, kc * 128:(kc + 1) * 128], ident_bf[:], [r_R[t], rc], [rp])
            cp("act", xT[:, :, ti * 128:(ti + 1) * 128], ptb.rearrange("p (k n) -> p k n", k=8), [rp], [r_xT])

    for l in range(nlayers):
        last = (l == nlayers - 1)
        for kc in range(8):
            dma("pool", Wi[:, kc, :], w_in[l, kc * 128:(kc + 1) * 128, :], [], [rW], rW)
        for kc in range(8):
            dma("pool", Wo[:, kc, :], w_o[l, kc * 128:(kc + 1) * 128, :], [], [rW], rW)
        dma("pool", Wa[:], gla_w_a[l], [], [rW], rW)
        dma("pool", ba[:], gla_b_a[l:l + 1, :], [], [rW], rW)
        dma("pool", bfb[:], fox_bf[l:l + 1, :], [], [rW], rW)
        dma("pool", bs4[:], sgu_bs[l], [], [rW], rW)
        dma("sp", bsT[:], sgu_bs[l].rearrange("g t -> t g"), [], [rW], rW, allow_slow_non_contiguous=True)
        for (tl, src) in ((glag, gla_g), (sgg, sgu_g), (sgb, sgu_bb)):
            dma("sp", tl[:], src[l:l + 1, :].broadcast_to([128, 256]), [], [rW], rW)
        for (tl, src) in ((l1g, ln1_g), (l1b, ln1_b)):
            dma("sp", tl[:], src[l:l + 1, :].broadcast_to([128, D]), [], [rW], rW)
        NC_ = dict(allow_slow_non_contiguous=True)
        for c in range(2):
            dma("sp", cw[:, c, :], conv_w[l][:, c * 128:(c + 1) * 128].rearrange("j p -> p j"), [], [rW], rW, **NC_)
        for (tl, src) in ((cb, conv_b), (cng, cn_g), (cnb, cn_b)):
            dma("sp", tl[:], src[l].rearrange("(c p) -> p c", p=128), [], [rW], rW, **NC_)
        for c in range(NFC):
            dma("sp", fw[:, c, :], fcw[l][:, c * 128:(c + 1) * 128].rearrange("j p -> p j"), [], [rW], rW, **NC_)
        dma("sp", fb[:], fcb[l].rearrange("(c p) -> p c", p=128), [], [rW], rW, **NC_)
        dma("sp", sw, sgu_w[l].rearrange("g t s -> t g s"), [], [r_xT], r_xT)
        for g in range(4):
            pt, rp = b.ps()
            tr(pt[:, 0:128], sw[:, g, :], ident_f[:], [r_xT, rc], [rp])
            tt("dve", WT[:, g, :], pt[:, 0:128], triO[:], MUL, [rp, rc], [rW])
        mset("pool", Sf[:], 0.0, [r_S])
        mset("pool", Sb[:], 0.0, [r_S])
        mset("pool", Pacc[:], 0.0, [r_Pacc])
        mset("pool", gext[1][:, :, 512:542], 0.0, [r_gext[1]])

        if has_cache and 'sample' not in SKIP:
            sample_q_prework(l)
        for blk in range(4):
            make_xT(blk)
            c0 = blk * 512
            def fproj(col, m):
                pt, rp = b.ps()
                for kc in range(8):
                    mm(pt[0:m, :], Wi[:, kc, col:col + m], xT[:, kc, :], kc == 0, kc == 7, [rW, r_xT], [rp])
                return pt, rp
            pt, rp = fproj(O_AQ, 128)
            cp("act", qTf[:], pt[:], [rp], [r_qk])
            pt, rp = fproj(O_AK, 128)
            cp("dve", kTf[:], pt[:], [rp], [r_qk])
            pt, rp = fproj(O_ALR, 16)
            cp("act", alr[:], pt[0:16, :], [rp], [r_alr])
            for c in range(2):
                pt, rp = fproj(O_CQ + 128 * c, 128)
                cp("act", fq[:, c, :], pt[:], [rp], [r_fq])
                pt, rp = fproj(O_CK + 128 * c, 128)
                cp("dve", FK[:, c, c0:c0 + 512], pt[:], [rp], [r_FK[blk]])
            ge, r_ge = gext[blk % 2], r_gext[blk % 2]
            gp_, r_gp = gext[(blk + 1) % 2], r_gext[(blk + 1) % 2]
            cp("dve", ge[:, :, 0:30], gp_[:, :, 512:542], [r_gp], [r_ge])
            for c in range(2):
                pa, rpa = fproj(O_DIN + 128 * c, 128)
                pg, rpg = fproj(O_DIN + 256 + 128 * c, 128)
                sg, r_sg = tmp("sig", [128, 512], F32)
                act(sg[:], pg[:], AF.Sigmoid, [rpg], [r_sg])
                tt("dve", sg[:], pa[:], sg[:], MUL, [rpa, r_sg], [r_sg])
                cp("dve", ge[:, c, 30:542], sg[:], [r_sg], [r_ge])
                if blk == 3:
                    cp("dve", glast[:, c, :], sg[:, 482:512], [r_sg], [r_glast])
            if blk == 3:
                for c in range(2):
                    dma("sp", o_conv[l][:, c * 128:(c + 1) * 128].rearrange("t p -> p t"), glast[:, c, :], [r_glast], [],
                        r_glast, allow_slow_non_contiguous=True)
            if 'conv' in SKIP:
                mset('pool', ydT[:], 0.0, [r_ydT])
            def gen_conv(ge=ge, r_ge=r_ge):
                for c in range(2):
                    cof, r_cof = tmp("cof", [128, 512], F32)
                    ts("dve", cof[:], ge[:, c, 0:512], cw[:, c, 0:1], cb[:, c:c + 1], MUL, ADD, [r_ge, rW], [r_cof])
                    for j in range(1, 31):
                        stt("dve", cof[:], ge[:, c, j:j + 512], cw[:, c, j:j + 1], cof[:], MUL, ADD, [r_ge, rW, r_cof], [r_cof])
                        if j % 5 == 0:
                            yield
                    sq, r_sq = tmp("csq", [128, 512], F32)
                    act(sq[:], cof[:], AF.Square, [r_cof], [r_sq])
                    pm, rpm = b.ps()
                    mm(pm[:], blk64[:], cof[:], True, True, [rc, r_cof], [rpm])
                    pe2, rpe2 = b.ps()
                    mm(pe2[:], blk64[:], sq[:], True, True, [rc, r_sq], [rpe2])
                    msq, r_msq = tmp("cmsq", [128, 512], F32)
                    act(msq[:], pm[:], AF.Square, [rpm], [r_msq])
                    var, r_var = tmp("cvar", [128, 512], F32)
                    tt("dve", var[:], pe2[:], msq[:], SUB, [rpe2, r_msq], [r_var])
                    ts("dve", var[:], var[:], EPS, -0.5, ADD, POW, [r_var], [r_var])
                    tt("dve", cof[:], cof[:], pm[:], SUB, [r_cof, rpm], [r_cof])
                    tt("dve", cof[:], cof[:], var[:], MUL, [r_cof, r_var], [r_cof])
                    act(ydT[:, c, :], cof[:], AF.Silu, [r_cof, rW], [r_ydT], scale=cng[:, c:c + 1], bias=cnb[:, c:c + 1])
                    yield

            pipe = None
            pending_tail = None
            for ti in range(4):
                t = blk * 4 + ti
                cs = slice(ti * 128, (ti + 1) * 128)
                ytok, r_ytok = tmp("ytok", [128, 768], BF16, 2)

                def tproj(col, n):
                    pt, rp = b.ps()
                    for kc in range(8):
                        mm(pt[:, 0:n], xT[:, kc, cs], Wi[:, kc, col:col + n], kc == 0, kc == 7, [r_xT, rW], [rp])
                    return pt, rp

                if 'gla' in SKIP or GSTOP < 99:
                    mset('pool', ytok[:, 0:256], 0.0, [r_ytok])
                def gen_gla():
                    pz, rpz = b.ps()
                    mm(pz[:, 0:128], alr[0:16, cs], Wa[:], True, False, [r_alr, rW], [rpz])
                    mm(pz[:, 0:128], ones_bf[0:1, :], ba[:], False, True, [rc, rW], [rpz])
                    e1, r_e1 = tmp("g_e1", [128, 128], F32)
                    act(e1[:], pz[:, 0:128], AF.Exp, [rpz], [r_e1], scale=-1.0)
                    spl, r_spl = tmp("g_sp", [128, 128], F32)
                    act(spl[:], e1[:], AF.Ln, [r_e1], [r_spl], bias=1.0, scale=1.0)
                    yield
                    pg_, rpg_ = b.ps()
                    shi, r_shi = tmp("g_shi", [128, 128], BF16)
                    slo, r_slo = tmp("g_slo", [128, 128], BF16)
                    cp("dve", shi[:], spl[:], [r_spl], [r_shi])
                    tt("dve", slo[:], spl[:], shi[:], SUB, [r_spl, r_shi], [r_slo])
                    mm(pg_[:, 0:128], shi[:], triMb[:], True, False, [r_shi, rc], [rpg_])
                    mm(pg_[:, 0:128], slo[:], triMb[:], False, True, [r_slo, rc], [rpg_])
                    mm(pg_[:, 128:256], revMb[:], shi[:], True, False, [r_shi, rc], [rpg_])
                    mm(pg_[:, 128:256], revMb[:], slo[:], False, True, [r_slo, rc], [rpg_])
                    eg, r_eg = tmp("g_eg", [128, 384], F32)
                    act(eg[:, 0:128], pg_[:, 0:128], AF.Exp, [rpg_], [r_eg])
                    act(eg[:, 128:256], pg_[:, 0:128], AF.Exp, [rpg_], [r_eg], scale=-1.0)
                    act(eg[:, 256:384], pg_[:, 128:256], AF.Exp, [rpg_], [r_eg])
                    yield
                    qtl, r_qtl = tmp("g_qtl", [128, 128], BF16)
                    stt("dve", qtl[:], qTf[:, cs], 32.0 ** -0.5, eg[:, 0:128], MUL, MUL, [r_qk, r_eg], [r_qtl])
                    kt4, r_kt4 = tmp("g_kt4", [128, 4, 128], BF16)
                    for h in range(4):
                        stt("dve", kt4[:, h, :], kTf[:, cs], hm[:, h:h + 1], eg[:, 128:256], MUL, MUL,
                            [r_qk, r_eg, rc], [r_kt4])
                    yield
                    p1, rp1 = tproj(O_AK, 512)
                    p2, rp2 = tproj(O_AK + 512, 128)
                    kp, r_kp = tmp("g_kp", [128, 128], BF16)
                    tt("dve", kp[:], p1[:, 0:128], eg[:, 256:384], MUL, [rp1, r_eg], [r_kp])
                    vb, r_vb = tmp("g_vb", [128, 256], BF16)
                    cp("act", vb[:], p1[:, 128:384], [rp1], [r_vb])
                    sgl, r_sgl = tmp("g_sg", [128, 256], F32)
                    act(sgl[:, 0:128], p1[:, 384:512], AF.Silu, [rp1], [r_sgl])
                    act(sgl[:, 128:256], p2[:, 0:128], AF.Silu, [rp2], [r_sgl])
                    tt("dve", sgl[:], sgl[:], glag[:], MUL, [r_sgl, rW], [r_sgl])
                    yield
                    pa_, rpa_ = b.ps()
                    for h in range(4):
                        mm(pa_[:, h * 128:(h + 1) * 128], kt4[:, h, :], qtl[:], True, True, [r_kt4, r_qtl], [rpa_])
                    asb, r_asb = tmp("g_asb", [128, 512], BF16)
                    tt("dve", asb[:], pa_[:], mask4[:], MUL, [rpa_, rc], [r_asb])
                    yield
                    po, rpo = b.ps()
                    mm(po[:, 0:256], qtl[:], Sb[:], True, True, [r_qtl, r_S], [rpo])
                    for h in range(4):
                        mm(po[:, 256 + 64 * h:320 + 64 * h], asb[:, h * 128:(h + 1) * 128], vb[:, 64 * h:64 * h + 64], True, True,
                           [r_asb, r_vb], [rpo])
                    of, r_of = tmp("g_of", [128, 256], F32)
                    cp("act", of[:], po[:, 0:256], [rpo], [r_of])
                    tt("dve", of[:], of[:], po[:, 256:512], ADD, [r_of, rpo], [r_of])
                    yield
                    pn, rpn = b.ps()
                    mm(pn[:, 0:256], kp[:], vb[:], True, True, [r_kp, r_vb], [rpn])
                    stmp, r_stmp = tmp("g_stmp", [128, 256], F32)
                    tt("dve", stmp[:], pn[:, 0:256], blkmask[:], MUL, [rpn, rc], [r_stmp])
                    stt("dve", Sf[:], Sf[:], eg[:, 127:128], stmp[:], MUL, ADD, [r_S, r_eg, r_stmp], [r_S])
                    cp("act", Sb[:], Sf[:], [r_S], [r_S])
                    yield
                    osq, r_osq = tmp("g_stmp", [128, 256], F32)
                    act(osq[:], of[:], AF.Square, [r_of], [r_osq])
                    gst, r_gst = tmp("g_st", [128, 8], F32)
                    rsum("dve", gst[:, 0:4], osq[:].rearrange("p (h e) -> p h e", h=4), [r_osq], [r_gst])
                    ts("dve", gst[:, 4:8], gst[:, 0:4], 1.0 / 64.0, EPS, MUL, ADD, [r_gst], [r_gst])
                    ts("dve", gst[:, 4:8], gst[:, 4:8], -0.5, None, POW, None, [r_gst], [r_gst])
                    for h in range(4):
                        stt("dve", ytok[:, 64 * h:64 * h + 64], of[:, 64 * h:64 * h + 64], gst[:, 4 + h:5 + h],
                            sgl[:, 64 * h:64 * h + 64], MUL, MUL, [r_of, r_gst, r_sgl], [r_ytok])

                if 'sgu' in SKIP:
                    mset('pool', ytok[:, 256:512], 0.0, [r_ytok])
                def gen_sgu():
                    pu, rpu = tproj(O_BU, 512)
                    us, r_us = tmp("s_u", [128, 512], F32)
                    cp("act", us[:], pu[:], [rpu], [r_us])
                    vsq, r_vsq = tmp("s_vsq", [128, 256], F32)
                    act(vsq[:], pu[:, 256:512], AF.Square, [rpu], [r_vsq])
                    yield
                    sst, r_sst = tmp("s_st", [128, 24], F32)
                    rsum("dve", sst[:, 0:4], us[:, 256:512].rearrange("p (h e) -> p h e", h=4), [r_us], [r_sst])
                    rsum("dve", sst[:, 4:8], vsq[:].rearrange("p (h e) -> p h e", h=4), [r_vsq], [r_sst])
                    ts("dve", sst[:, 8:12], sst[:, 0:4], 1.0 / 64.0, None, MUL, None, [r_sst], [r_sst])
                    tt("dve", sst[:, 12:16], sst[:, 8:12], sst[:, 8:12], MUL, [r_sst], [r_sst])
                    stt("dve", sst[:, 12:16], sst[:, 4:8], 1.0 / 64.0, sst[:, 12:16], MUL, SUB, [r_sst], [r_sst])
                    ts("dve", sst[:, 16:20], sst[:, 12:16], EPS, -0.5, ADD, POW, [r_sst], [r_sst])
                    stt("dve", sst[:, 20:24], sst[:, 8:12], -1.0, sst[:, 16:20], MUL, MUL, [r_sst], [r_sst])
                    vn, r_vn = tmp("s_vn", [128, 256], F32)
                    for g in range(4):
                        ts("dve", vn[:, 64 * g:64 * g + 64], us[:, 256 + 64 * g:320 + 64 * g], sst[:, 16 + g:17 + g],
                           sst[:, 20 + g:21 + g], MUL, ADD, [r_us, r_sst], [r_vn])
                    tt("dve", vn[:], vn[:], sgg[:], MUL, [r_vn, rW], [r_vn])
                    vlb, r_vlb = tmp("s_vlb", [128, 256], BF16)
                    tt("dve", vlb[:], vn[:], sgb[:], ADD, [r_vn, rW], [r_vlb])
                    pmx, rpmx = b.ps()
                    for g in range(4):
                        mm(pmx[:, 64 * g:64 * g + 64], WT[:, g, :], vlb[:, 64 * g:64 * g + 64], True, True,
                           [rW, r_vlb], [rpmx])
                    for g in range(4):
                        stt("dve", ytok[:, 256 + 64 * g:320 + 64 * g], pmx[:, 64 * g:64 * g + 64], bsT[:, g:g + 1],
                            us[:, 64 * g:64 * g + 64], ADD, MUL, [rpmx, r_us, rW], [r_ytok])

                if 'fox' in SKIP:
                    mset('pool', ytok[:, 512:768], 0.0, [r_ytok])
                def gen_fox():
                    pkv, rpkv = tproj(O_CK, 512)
                    stg, r_stg = tmp("f_stg", [128, 512], F32)
                    cp("act", stg[:], pkv[:], [rpkv], [r_stg])
                    dma("sp", o_fk[l, t * 128:(t + 1) * 128, :], stg[:, 0:256], [r_stg], [], r_stg)
                    dma("sp", o_fv[l, t * 128:(t + 1) * 128, :], stg[:, 256:512], [r_stg], [], r_stg)
                    pcf, rpcf = b.ps()
                    for kc in range(8):
                        mm(pcf[:, 0:4], xT[:, kc, cs], Wi[:, kc, O_CF:O_CF + 4], kc == 0, False, [r_xT, rW], [rpcf])
                    mm(pcf[:, 0:4], ones_bf[0:1, :], bfb[:], False, True, [rc, rW], [rpcf])
                    fst, r_fst = tmp("f_st", [128, 16], F32)
                    act(fst[:, 0:4], pcf[:, 0:4], AF.Exp, [rpcf], [r_fst], scale=-1.0)
                    act(fst[:, 4:8], fst[:, 0:4], AF.Ln, [r_fst], [r_fst], bias=1.0, scale=1.0)
                    lfo, r_lfo = tmp("f_lfo", [128, 4], F32)
                    ts("dve", lfo[:], fst[:, 4:8], -1.0, None, MUL, None, [r_fst], [r_lfo])
                    dma("sp", o_flf[l, t * 128:(t + 1) * 128, :], lfo[:], [r_lfo], [], r_lfo)
                    pd, rpd = b.ps()
                    mm(pd[:, 0:4], triO[:], fst[:, 4:8], True, False, [rc, r_fst], [rpd])
                    mm(pd[:, 0:4], ones_f[:], Pacc[:], False, True, [rc, r_Pacc], [rpd])
                    act(fst[:, 8:12], pd[:, 0:4], AF.Exp, [rpd], [r_fst])
                    tt("dve", Pacc[:], Pacc[:], fst[:, 4:8], ADD, [r_Pacc, r_fst], [r_Pacc])
                    for h in range(4):
                        ts("dve", VA[:, t, h, 0:64], stg[:, 256 + 64 * h:320 + 64 * h], fst[:, 8 + h:9 + h], None, MUL, None,
                           [r_stg, r_fst], [r_VA[t]])
                    cp("dve", VA[:, t, :, 64], fst[:, 8:12], [r_fst], [r_VA[t]])
                    pov, rpov = b.psb[7]
                    yield
                    for h in range(4):
                        yield
                        hp = slice(64 * (h % 2), 64 * (h % 2) + 64)
                        hc = h // 2
                        for j0 in range(0, t + 1, 4):
                            js = list(range(j0, min(j0 + 4, t + 1)))
                            psc, rpsc = b.ps()
                            for j in js:
                                mm(psc[:, (j - j0) * 128:(j - j0 + 1) * 128], FK[hp, hc, j * 128:(j + 1) * 128], fq[hp, hc, cs],
                                   True, True, [r_FK[j // 4], r_fq], [rpsc])
                            pT, r_pT = tmp("f_pT", [128, 512], BF16, 3)
                            n = len(js) * 128
                            act(pT[:, 0:n], psc[:, 0:n], AF.Exp, [rpsc], [r_pT], scale=0.125)
                            if js[-1] == t:
                                sl = slice((t - j0) * 128, (t - j0 + 1) * 128)
                                tt("dve", pT[:, sl], pT[:, sl], mask_bf[:], MUL, [r_pT, rc], [r_pT])
                            for j in js:
                                mm(pov[:, 65 * h:65 * h + 65], pT[:, (j - j0) * 128:(j - j0 + 1) * 128], VA[:, j, h, :],
                                   j == 0, j == t, [r_pT, r_VA[j]], [rpov])
                    rd, r_rd = tmp("f_rd", [128, 4], F32)
                    P.op("dve", lambda e, rd=rd, pov=pov: e.reciprocal(
                        out=rd[:], in_=pov[:, 0:260].rearrange("p (h e) -> p h e", h=4)[:, :, 64]), r=[rpov], w=[r_rd])
                    for h in range(4):
                        ts("dve", ytok[:, 512 + 64 * h:576 + 64 * h], pov[:, 65 * h:65 * h + 64], rd[:, h:h + 1], None, MUL, None,
                           [rpov, r_rd], [r_ytok])

                if dbg and l == 0:
                    dma("pool", d_y[t * 128:(t + 1) * 128, :], ytok[:], [r_ytok], [], r_ytok)
                    if ti == 0:
                        for c in range(2):
                            dma("pool", d_yd[c * 128:(c + 1) * 128, c0:c0 + 512], ydT[:, c, :], [r_ydT], [], r_ydT)
                gens = []
                if ti > 0 and pending_tail is not None:
                    gens.append(pending_tail)
                if ti == 0 and 'conv' not in SKIP:
                    gens.append(gen_conv())
                if 'gla' not in SKIP:
                    gens.append(gen_gla())
                if 'sgu' not in SKIP:
                    gens.append(gen_sgu())
                if 'fox' not in SKIP:
                    gens.append(gen_fox())
                it_ = 0
                while gens:
                    for g_ in list(gens):
                        try:
                            next(g_)
                        except StopIteration:
                            gens.remove(g_)
                    if pipe and it_ < 6:
                        pipe.step()
                    it_ += 1
                while pipe and it_ < 6:
                    pipe.step()
                    it_ += 1
                if ti == 0 and has_cache and 'sample' not in SKIP:
                    pipe = PastPipe(l, blk)
                def gen_tail(t=t, cs=cs, ytok=ytok, r_ytok=r_ytok):
                    pty, rpty = b.ps()
                    ptyb = pty[:].bitcast(BF16)
                    for c in range(6):
                        tr(ptyb[:, c * 128:(c + 1) * 128], ytok[:, c * 128:(c + 1) * 128], ident_bf[:], [r_ytok, rc], [rpty])
                    yT, r_yT = tmp("yT", [128, 6, 128], BF16)
                    cp("act", yT[:], ptyb[:, 0:768].rearrange("p (k n) -> p k n", k=6), [rpty], [r_yT])
                    yield
                    ps2, rps2 = [], []
                    for hf in range(2):
                        pm_, rpm_ = b.ps()
                        for kc in range(8):
                            lh = yT[:, kc, :] if kc < 6 else ydT[:, kc - 6, cs]
                            mm(pm_[:], lh, Wo[:, kc, hf * 512:(hf + 1) * 512], kc == 0, kc == 7, [r_yT, r_ydT, rW], [rpm_])
                        ps2.append(pm_)
                        rps2.append(rpm_)
                    lg = layer_norm_gen(t, ps2, rps2, l1g, l1b, None)
                    next(lg)
                    yield
                    for _ in lg:
                        yield
                pending_tail = gen_tail()
            for _ in pending_tail:
                pass
            pending_tail = None
            if pipe:
                pipe.drain()
        if dbg and l == 0:
            for t in range(NT):
                dma("pool", d_x1[t * 128:(t + 1) * 128, :], R[:, t, :], [r_R[t]], [], r_R[t])
        if 'sample' not in SKIP:
            sample_mixer(l)
        for h in range(4):
            dma("sp", o_gla[l, h], Sf[32 * h:32 * h + 32, 64 * h:64 * h + 64], [r_S], [], r_S)

        for _once in ([] if 'ffn' in SKIP else [0]):
            if l == 0:
                hal = [b.sb([128, NFC, 2], F32, "hal%d_%d" % (l, i)) for i in range(2)]
                r_hal = [Res("hal0"), Res("hal1")]
                Wif = Wi[:].rearrange("p k n -> p (k n)")
                hT = Wif[:, 0:NFC * 512].rearrange("p (c n) -> p c n", c=NFC)
                r_hT = Res("hT")
                wus = [(Wif[:, 10752:18944].rearrange("p (k n) -> p k n", k=8), Res("wu0")), (Wo[:], Res("wu1"))]
                wds = [(VAflat[:, i * 1024:(i + 1) * 1024], Res("wd%d" % i)) for i in range(3)]
                gxs = [(FKf[:, i * 514:(i + 1) * 514], Res("gx%d" % i)) for i in range(2)]
                gas = [(qTf[:], Res("ga0")), (kTf[:], Res("ga1"))]
                ali = [r_hT] + [x[1] for x in wus + wds + gxs + gas]
            ALLR.extend(ali)
            cnt_f = [0, 0, 0, 0]
            P.op("pool", lambda e: e.memset(dummy[:], 0.0), w=[rW, r_dummy] + ali)
            for (tl, src) in ((l2g, ln2_g), (l2b, ln2_b)):
                dma("sp", tl[:], src[l:l + 1, :].broadcast_to([128, D]), [], [rW], rW)
            mset("pool", hal[1][:], 0.0, [r_hal[1]])
            make_xT(0)
            sfa = sample_ffn_prep(l) if 'sample' not in SKIP else None
            for blk in range(4):
                hcur, r_hcur = hal[blk % 2], r_hal[blk % 2]
                hprev, r_hprev = hal[(blk + 1) % 2], r_hal[(blk + 1) % 2]
                for fc in range(NFC):
                    wu, r_wu, og, ov = wu_chunk(l, fc, wus, cnt_f)
                    pg, rpg = b.ps()
                    for kc in range(8):
                        mm(pg[:], wu[:, kc, og:og + 128], xT[:, kc, :], kc == 0, kc == 7, [r_wu, r_xT], [rpg])
                    pv, rpv = b.ps()
                    for kc in range(8):
                        mm(pv[:], wu[:, kc, ov:ov + 128], xT[:, kc, :], kc == 0, kc == 7, [r_wu, r_xT], [rpv])
                    gx, r_gx = gxs[cnt_f[1] % 2]
                    cnt_f[1] += 1
                    cp("dve", gx[:, 0:2], hprev[:, fc, :], [r_hprev], [r_gx])
                    cp("act", gx[:, 2:514], pg[:], [rpg], [r_gx])
                    cp("dve", hcur[:, fc, :], gx[:, 512:514], [r_gx], [r_hcur])
                    ga, r_ga = gas[cnt_f[2] % 2]
                    cnt_f[2] += 1
                    act(ga[:], gx[:, 0:512], AF.Identity, [r_gx, rW], [r_ga], scale=fw[:, fc, 0:1])
                    stt("dve", ga[:], gx[:, 1:513], fw[:, fc, 1:2], ga[:], MUL, ADD, [r_gx, rW, r_ga], [r_ga])
                    stt("dve", ga[:], gx[:, 2:514], fw[:, fc, 2:3], ga[:], MUL, ADD, [r_gx, rW, r_ga], [r_ga])
                    act(ga[:], ga[:], AF.Silu, [r_ga, rW], [r_ga], bias=fb[:, fc:fc + 1], scale=1.0)
                    tt("dve", hT[:, fc, :], ga[:], pv[:], MUL, [r_ga, rpv], [r_hT])
                    if blk == 3 and sfa is not None:
                        sample_ffn_chunk(l, fc, wu, r_wu, og, ov, sfa)
                if blk == 3:
                    for c in range(NFC):
                        dma("sp", o_ffc[l][:, c * 128:(c + 1) * 128].rearrange("j p -> p j"), hcur[:, c, :], [r_hcur], [],
                            r_hcur, allow_slow_non_contiguous=True)
                if dbg and l == 0 and blk == 0:
                    for fc in range(NFC):
                        dma("pool", d_h[fc * 128:(fc + 1) * 128, :], hT[:, fc, :], [r_hT], [], r_hT)
                if blk < 3:
                    make_xT(blk + 1)
                banks = [b.psb[6], b.psb[7]] + [b.ps() for _ in range(6)]
                for fc in range(NFC):
                    wd, r_wd = wds[cnt_f[3] % 3]
                    cnt_f[3] += 1
                    dma("pool", wd[:], w_dn[l, fc * 128:(fc + 1) * 128, :], [], [r_wd], r_wd)
                    for ti in range(4):
                        for hf in range(2):
                            pb, rpb = banks[ti * 2 + hf]
                            mm(pb[:], hT[:, fc, ti * 128:(ti + 1) * 128], wd[:, hf * 512:(hf + 1) * 512], fc == 0, fc == NFC - 1,
                               [r_hT, r_wd], [rpb])
                for ti in range(4):
                    t = blk * 4 + ti
                    layer_norm(t, [banks[ti * 2][0], banks[ti * 2 + 1][0]], [banks[ti * 2][1], banks[ti * 2 + 1][1]], l2g, l2b,
                               o_y[t * 128:(t + 1) * 128, :] if last else None)
        if 'ffn' not in SKIP:
            if 'sample' not in SKIP:
                sample_ffn_down(l, last, wds, cnt_f, sfa)
            P.op("pool", lambda e: e.memset(dummy[:], 0.0), w=[rW, r_dummy] + ali)
        if dbg and l == 0 and not last:
            for t in range(NT):
                dma("pool", d_x2[t * 128:(t + 1) * 128, :], R[:, t, :], [r_R[t]], [], r_R[t])
    P.emit(stack)
    return nc, stack


IN_NAMES = ["w_in", "w_o", "gla_w_a", "gla_b_a", "gla_norm_g", "sgu_ln_g", "sgu_ln_b", "sgu_w", "fox_b_f",
            "conv_w", "conv_b", "conv_norm_g", "conv_norm_b", "ln1_g", "ln1_b", "ln2_g", "ln2_b",
            "ffn_conv_w", "ffn_conv_b"]


def run(inputs, nlayers=DEPTH, cores=8, dbg=False, has_cache=True, trace=False):
    nc, stack = build(nlayers=nlayers, dbg=dbg, has_cache=has_cache)
    with stack:
        pass
    in_maps = []
    for c in range(cores):
        m = {"xp": np.ascontiguousarray(inputs["x_prompt"][c]),
             "xs": np.ascontiguousarray(inputs["x_sample"][4 * c:4 * c + 4]).reshape(NS, D),
             "st_gla": np.ascontiguousarray(inputs["state_gla"][:, 4 * c:4 * c + 4]),
             "st_conv": np.ascontiguousarray(inputs["state_conv"][:, 4 * c:4 * c + 4]),
             "st_ffc": np.ascontiguousarray(inputs["state_ffn_conv"][:, 4 * c:4 * c + 4])}
        if has_cache:
            m["pt"] = np.ascontiguousarray(inputs["page_table"][4 * c:4 * c + 4]).astype(np.int32)
            m["ck"] = np.ascontiguousarray(inputs["cache_fox_k"]).reshape(DEPTH, NPOOL, 128, 256)
            m["cv"] = np.ascontiguousarray(inputs["cache_fox_v"]).reshape(DEPTH, NPOOL, 128, 256)
            m["clf"] = np.ascontiguousarray(inputs["cache_fox_logf"])
        for k in IN_NAMES:
            m[k] = np.ascontiguousarray(inputs[k])
        m["w_up"] = np.ascontiguousarray(inputs["ffn_w_up"])
        m["w_dn"] = np.ascontiguousarray(inputs["ffn_w_down"])
        m["sgu_b"] = np.ascontiguousarray(inputs["sgu_b"])
        in_maps.append(m)
    if trace:
        res = run_bass_kernel_spmd(nc, in_maps, core_ids=list(range(cores)), trace=True)
        print("EXEC_TIME_NS", res.exec_time_ns)
        return res.results
    res = run_bass_kernel_spmd(nc, in_maps, core_ids=list(range(cores)))
    return res.results


def kernel(**inputs):
    rs = run(inputs)
    f = np.float32
    y_p = np.stack([r["o_y"] for r in rs]).astype(f)
    p_fk = np.stack([r["o_fk"] for r in rs], axis=1).reshape(DEPTH, 8, SEQ, 4, 64).astype(f)
    p_fv = np.stack([r["o_fv"] for r in rs], axis=1).reshape(DEPTH, 8, SEQ, 4, 64).astype(f)
    p_lf = np.stack([r["o_flf"] for r in rs], axis=1).astype(f)
    p_gla = np.stack([r["o_gla"] for r in rs], axis=1).astype(f)
    p_conv = np.stack([r["o_conv"] for r in rs], axis=1).astype(f)
    p_ffc = np.stack([r["o_ffc"] for r in rs], axis=1).astype(f)
    y_s = np.concatenate([r["o_ys"] for r in rs]).reshape(32, 4, D).astype(f)

    def cat(nm, shp):
        return np.concatenate([r[nm].reshape((DEPTH, 4) + shp) for r in rs], axis=1).astype(f)
    s_fk = cat("o_sfk", (4, 4, 64))
    s_fv = cat("o_sfv", (4, 4, 64))
    s_lf = cat("o_sflf", (4, 4))
    s_gla = cat("o_sgla", (4, 32, 64))
    s_conv = cat("o_sconv", (30, 256))
    s_ffc = cat("o_sffc", (2, DFF))
    s_sgu = cat("o_ssgu", (4, 256))
    return (y_p, y_s, p_fk, p_fv, p_lf, p_gla, p_conv, p_ffc, s_fk, s_fv, s_lf, s_gla, s_conv, s_ffc, s_sgu)
```

```python
import numpy as np
import os
GSTOP = int(os.environ.get('GSTOP', '99'))
SKIP = set(os.environ.get('SKIP', '').split(','))
from contextlib import ExitStack
import concourse.bass as bass
import concourse.mybir as mybir
from concourse.bass_utils import run_bass_kernel_spmd

F32 = mybir.dt.float32
BF16 = mybir.dt.bfloat16
I32 = mybir.dt.int32
AF = mybir.ActivationFunctionType
ALU = mybir.AluOpType
AX = mybir.AxisListType

D = 1024
SEQ = 2048
NT = 16
DEPTH = 2
DIN = 2580
DFF = 2688
NFC = 21
EPS = 1e-5
ALPHA = (2 * DEPTH) ** 0.25
NS = 16

O_AQ, O_AK, O_AV, O_AG, O_ALR, O_BU, O_BV, O_CQ, O_CK, O_CV, O_CF, O_DIN = (
    0, 128, 256, 512, 768, 784, 1040, 1296, 1552, 1808, 2064, 2068)


class Res:
    __slots__ = ("name", "lw", "rd", "excl")

    def __init__(self, name="", excl=False):
        self.name = name
        self.lw = None
        self.rd = []
        self.excl = excl


class Op:
    __slots__ = ("eng", "fn", "raw", "oth", "pos", "dma", "key", "inc", "val", "users")


class Prog:
    ENGS = ("pe", "act", "dve", "pool", "sp")

    def __init__(self, nc):
        self.nc = nc
        self.by = {e: [] for e in self.ENGS}
        self.dcount = {}
        self.all_dma = []

    def op(self, eng, fn, r=(), w=(), dma=False, key=None):
        o = Op()
        o.eng, o.fn, o.dma, o.key = eng, fn, dma, key
        o.raw, o.oth = set(), set()
        o.inc, o.val, o.users = False, 0, 0
        o.pos = len(self.by[eng])
        for x in r:
            if x.lw is not None:
                o.raw.add(x.lw)
            if x.excl:
                for q in x.rd:
                    if q.eng != eng:
                        o.oth.add(q)
        for x in w:
            if x.lw is not None:
                if not (dma and x.lw.dma and x.lw.key is key and x.lw.eng == eng):
                    o.oth.add(x.lw)
            for q in x.rd:
                o.oth.add(q)
        for x in r:
            x.rd.append(o)
        for x in w:
            x.lw = o
            x.rd = []
        if dma:
            assert key is not None
            c = self.dcount.get(key, 0) + 16
            self.dcount[key] = c
            o.val = c
            self.all_dma.append(o)
        self.by[eng].append(o)
        return o

    def _needs(self, p, q):
        if p.dma:
            return True
        if p.eng != q.eng:
            return True
        if p.eng == "pe":
            return False
        return (p in q.raw) and (q.pos - p.pos <= 3)

    def emit(self, stack):
        nc = self.nc
        fin = self.op("sp", None)
        for d in self.all_dma:
            fin.oth.add(d)
        for e in self.ENGS:
            for q in self.by[e]:
                for p in (q.raw | q.oth):
                    if p is q:
                        continue
                    if self._needs(p, q):
                        p.inc = True
        esem = {}
        for e in self.ENGS:
            esem[e] = stack.enter_context(nc.semaphore("e_" + e))
            c = 0
            for o in self.by[e]:
                if o.inc and not o.dma:
                    c += 1
                    o.val = c
        dsem = {}
        for k in self.dcount:
            dsem[k] = stack.enter_context(nc.semaphore("d%d" % len(dsem)))
        print("sems used", len(dsem) + 5, "ops", {e: len(self.by[e]) for e in self.ENGS})

        def run(e, eng):
            waited = {}
            for q in self.by[e]:
                need = {}
                for p in (q.raw | q.oth):
                    if p is q or not self._needs(p, q):
                        continue
                    s = dsem[p.key] if p.dma else esem[p.eng]
                    if need.get(s, (0,))[0] < p.val:
                        need[s] = (p.val, s)
                for s, (v, _) in need.items():
                    if waited.get(s, 0) < v:
                        eng.wait_ge(s, v)
                        waited[s] = v
                if q.fn is None:
                    continue
                ins = q.fn(eng)
                if q.dma:
                    ins.then_inc(dsem[q.key], 16)
                elif q.inc:
                    ins.then_inc(esem[e], 1)

        with nc.Block() as block:
            @block.tensor
            def _(eng):
                run("pe", eng)

            @block.scalar
            def _(eng):
                run("act", eng)

            @block.vector
            def _(eng):
                run("dve", eng)

            @block.gpsimd
            def _(eng):
                run("pool", eng)

            @block.sync
            def _(eng):
                run("sp", eng)


class B:
    def __init__(self, nc, stack):
        self.nc, self.stack = nc, stack
        self.P = Prog(nc)
        self.nps = 0
        self.psb = []
        for i in range(8):
            t = stack.enter_context(nc.psum_tensor("ps%d" % i, [128, 512], F32))
            self.psb.append((t, Res("ps%d" % i, excl=True)))
        self.cnt = 0

    def sb(self, shape, dt, name=None):
        self.cnt += 1
        t = self.stack.enter_context(self.nc.sbuf_tensor(name or ("t%d" % self.cnt), list(shape), dt))
        return t

    def ps(self):
        t, r = self.psb[self.nps % 6]
        self.nps += 1
        return t, r


NPOOL = 2560


def build(has_cache=True, dbg=False, nlayers=DEPTH):
    nc = bass.Bass("TRN2", target_bir_lowering=False)
    stack = ExitStack()
    b = B(nc, stack)
    P = b.P

    def din(name, shape, dt=F32):
        return nc.dram_tensor(name, list(shape), dt, kind="ExternalInput").ap()

    def dout(name, shape, dt=F32):
        return nc.dram_tensor(name, list(shape), dt, kind="ExternalOutput").ap()

    def mm(out, lhsT, rhs, st, sp, r, w, **kw):
        P.op("pe", lambda e: e.matmul(out, lhsT, rhs, start=st, stop=sp, **kw), r=r, w=w)

    def tr(out, in_, ident, r, w):
        P.op("pe", lambda e: e.transpose(out=out, in_=in_, identity=ident), r=r, w=w)

    def act(out, in_, func, r, w, **kw):
        P.op("act", lambda e: e.activation(out=out, in_=in_, func=func, **kw), r=r, w=w)

    def tt(eng, out, in0, in1, op, r, w):
        P.op(eng, lambda e: e.tensor_tensor(out=out, in0=in0, in1=in1, op=op), r=r, w=w)

    def ts(eng, out, in0, s1, s2, op0, op1, r, w):
        if op1 == ALU.pow:
            act(out, in0, AF.Ln, r, w, bias=float(s1), scale=1.0)
            act(out, out, AF.Exp, list(r) + list(w), w, scale=float(s2))
            return
        if op0 == ALU.pow:
            act(out, in0, AF.Ln, r, w)
            act(out, out, AF.Exp, list(r) + list(w), w, scale=float(s1))
            return
        if s2 is None:
            P.op(eng, lambda e: e.tensor_scalar(out=out, in0=in0, scalar1=s1, scalar2=None, op0=op0), r=r, w=w)
        else:
            P.op(eng, lambda e: e.tensor_scalar(out=out, in0=in0, scalar1=s1, scalar2=s2, op0=op0, op1=op1), r=r, w=w)

    def stt(eng, out, in0, sc, in1, op0, op1, r, w):
        P.op("dve", lambda e: e.scalar_tensor_tensor(out=out, in0=in0, scalar=sc, in1=in1, op0=op0, op1=op1), r=r, w=w)

    def cp(eng, out, in_, r, w):
        if eng == "act":
            P.op(eng, lambda e: e.copy(out=out, in_=in_), r=r, w=w)
        else:
            P.op(eng, lambda e: e.tensor_copy(out=out, in_=in_), r=r, w=w)

    def rsum(eng, out, in_, r, w):
        P.op(eng, lambda e: e.reduce_sum(out=out, in_=in_, axis=AX.X), r=r, w=w)

    def mset(eng, ap, v, w):
        P.op(eng, lambda e: e.memset(ap, v), w=w)

    def asel(out, in_, pattern, cmp_, fill, base, cm, r, w):
        P.op("pool", lambda e: e.affine_select(out=out, in_=in_, pattern=pattern, compare_op=cmp_, fill=fill,
                                               base=base, channel_multiplier=cm), r=r, w=w)

    def dma(eng, out, in_, r, w, key, **kw):
        P.op(eng, lambda e: e.dma_start(out=out, in_=in_, **kw), r=r, w=w, dma=True, key=key)

    MUL, ADD, SUB, POW = ALU.mult, ALU.add, ALU.subtract, ALU.pow

    xp = din("xp", [SEQ, D])
    w_in = din("w_in", [DEPTH, D, DIN])
    w_o = din("w_o", [DEPTH, D, D])
    w_up = din("w_up", [DEPTH, D, 2 * DFF])
    w_dn = din("w_dn", [DEPTH, DFF, D])
    gla_w_a = din("gla_w_a", [DEPTH, 16, 128])
    gla_b_a = din("gla_b_a", [DEPTH, 128])
    gla_g = din("gla_norm_g", [DEPTH, 256])
    sgu_g = din("sgu_ln_g", [DEPTH, 256])
    sgu_bb = din("sgu_ln_b", [DEPTH, 256])
    sgu_w = din("sgu_w", [DEPTH, 4, 128, 128])
    sgu_bs = din("sgu_b", [DEPTH, 4, 128])
    fox_bf = din("fox_b_f", [DEPTH, 4])
    conv_w = din("conv_w", [DEPTH, 31, 256])
    conv_b = din("conv_b", [DEPTH, 256])
    cn_g = din("conv_norm_g", [DEPTH, 256])
    cn_b = din("conv_norm_b", [DEPTH, 256])
    ln1_g = din("ln1_g", [DEPTH, D])
    ln1_b = din("ln1_b", [DEPTH, D])
    ln2_g = din("ln2_g", [DEPTH, D])
    ln2_b = din("ln2_b", [DEPTH, D])
    fcw = din("ffn_conv_w", [DEPTH, 3, DFF])
    fcb = din("ffn_conv_b", [DEPTH, DFF])

    o_y = dout("o_y", [SEQ, D])
    o_fk = dout("o_fk", [DEPTH, SEQ, 256])
    o_fv = dout("o_fv", [DEPTH, SEQ, 256])
    o_flf = dout("o_flf", [DEPTH, SEQ, 4])
    o_gla = dout("o_gla", [DEPTH, 4, 32, 64])
    o_conv = dout("o_conv", [DEPTH, 30, 256])
    o_ffc = dout("o_ffc", [DEPTH, 2, DFF])
    if has_cache:
        pt_d = din("pt", [4, 64], I32)
        ck = din("ck", [DEPTH, NPOOL, 128, 256])
        cv = din("cv", [DEPTH, NPOOL, 128, 256])
        clf = din("clf", [DEPTH, NPOOL, 128, 4])
    xs_d = din("xs", [NS, D])
    st_gla = din("st_gla", [DEPTH, 4, 4, 32, 64])
    st_conv = din("st_conv", [DEPTH, 4, 30, 256])
    st_ffc = din("st_ffc", [DEPTH, 4, 2, DFF])
    o_ys = dout("o_ys", [NS, D])
    o_sfk = dout("o_sfk", [DEPTH, NS, 256])
    o_sfv = dout("o_sfv", [DEPTH, NS, 256])
    o_sflf = dout("o_sflf", [DEPTH, NS, 4])
    o_sgla = dout("o_sgla", [DEPTH, 4, 4, 32, 64])
    o_sconv = dout("o_sconv", [DEPTH, 4, 30, 256])
    o_sffc = dout("o_sffc", [DEPTH, 4, 2, DFF])
    o_ssgu = dout("o_ssgu", [DEPTH, NS, 256])
    if dbg:
        d_y = dout("d_y", [SEQ, 768])
        d_yd = dout("d_yd", [256, SEQ])
        d_x1 = dout("d_x1", [SEQ, D])
        d_h = dout("d_h", [DFF, 512])
        d_x2 = dout("d_x2", [SEQ, D])

    rc = Res("const")
    ident_f = b.sb([128, 128], F32, "ident_f")
    ident_bf = b.sb([128, 128], BF16, "ident_bf")
    revM = b.sb([128, 128], F32, "revM")
    triO = b.sb([128, 128], F32, "triO")
    ones_f = b.sb([128, 128], F32, "ones_f")
    ones_bf = b.sb([1, 128], BF16, "ones_bf")
    mask_bf = b.sb([128, 128], BF16, "mask_bf")
    mask4 = b.sb([128, 512], BF16, "mask4")
    blk64 = b.sb([128, 128], F32, "blk64")
    blkmask = b.sb([128, 256], F32, "blkmask")
    hm = b.sb([128, 4], F32, "hm")

    mset("pool", ident_f[:], 0.0, [rc])
    asel(ident_f[:], ident_f[:], [[-1, 128]], ALU.not_equal, 1.0, 0, 1, [rc], [rc])
    cp("dve", ident_bf[:], ident_f[:], [rc], [rc])
    mset("pool", revM[:], -1.0 / 16.0, [rc])
    asel(revM[:], revM[:], [[-1, 128]], ALU.is_gt, 0.0, 0, 1, [rc], [rc])
    mset("pool", triO[:], 1.0, [rc])
    asel(triO[:], triO[:], [[1, 128]], ALU.is_ge, 0.0, 0, -1, [rc], [rc])
    mset("pool", ones_f[:], 1.0, [rc])
    mset("pool", ones_bf[:], 1.0, [rc])
    cp("dve", mask_bf[:], triO[:], [rc], [rc])
    triMb = b.sb([128, 128], BF16, "triMb")
    revMb = b.sb([128, 128], BF16, "revMb")
    ts("dve", triMb[:], triO[:], -1.0 / 16.0, None, MUL, None, [rc], [rc])
    cp("dve", revMb[:], revM[:], [rc], [rc])
    for h in range(4):
        cp("dve", mask4[:, h * 128:(h + 1) * 128], triO[:], [rc], [rc])
    mset("pool", blk64[:], 0.0, [rc])
    mset("pool", blk64[0:64, 0:64], 1.0 / 64.0, [rc])
    mset("pool", blk64[64:128, 64:128], 1.0 / 64.0, [rc])
    mset("pool", blkmask[:], 1.0, [rc])
    mset("pool", hm[:], 1.0, [rc])
    for h in range(4):
        v = blkmask[:, 64 * h:64 * h + 64]
        asel(v, v, [[0, 64]], ALU.is_ge, 0.0, -32 * h, 1, [rc], [rc])
        asel(v, v, [[0, 64]], ALU.is_ge, 0.0, 32 * h + 31, -1, [rc], [rc])
        v = hm[:, h:h + 1]
        asel(v, v, [[0, 1]], ALU.is_ge, 0.0, -32 * h, 1, [rc], [rc])
        asel(v, v, [[0, 1]], ALU.is_ge, 0.0, 32 * h + 31, -1, [rc], [rc])

    R = b.sb([128, NT, D], BF16, "R")
    r_R = [Res("R%d" % t) for t in range(NT)]
    Wi = b.sb([128, 8, DIN], BF16, "Wi")
    Wo = b.sb([128, 8, D], BF16, "Wo")
    rW = Res("W")
    Wa = b.sb([16, 128], BF16, "Wa")
    ba = b.sb([1, 128], BF16, "ba")
    bfb = b.sb([1, 4], BF16, "bfb")
    glag = b.sb([128, 256], F32, "glag")
    sgg = b.sb([128, 256], F32, "sgg")
    sgb = b.sb([128, 256], F32, "sgb")
    l1g = b.sb([128, D], F32, "l1g")
    l1b = b.sb([128, D], F32, "l1b")
    l2g, l2b = l1g, l1b
    dummy = b.sb([128, 2], F32, "dummy_t")
    r_dummy = Res("dummy")
    cw = b.sb([128, 2, 31], F32, "cw")
    cb = b.sb([128, 2], F32, "cb")
    cng = b.sb([128, 2], F32, "cng")
    cnb = b.sb([128, 2], F32, "cnb")
    fw = b.sb([128, NFC, 3], F32, "fw")
    fb = b.sb([128, NFC], F32, "fb")
    bs4 = b.sb([4, 128], BF16, "bs4")
    bsT = b.sb([128, 4], F32, "bsT")
    WT = b.sb([128, 4, 128], BF16, "WT")

    xT = b.sb([128, 8, 512], BF16, "xT")
    r_xT = Res("xT")
    xTf = xT[:].rearrange("p k n -> p (k n)").bitcast(F32)
    sw = xTf[:, 0:512].rearrange("p (g s) -> p g s", g=4)
    FK = b.sb([128, 2, SEQ], BF16, "FK")
    r_FK = [Res("FK%d" % i) for i in range(4)]
    VAflat = b.sb([128, NT * 4 * 65], BF16, "VA")
    VA = VAflat[:].rearrange("p (t h e) -> p t h e", t=NT, h=4)
    r_VA = [Res("VA%d" % t) for t in range(NT)]
    fq = b.sb([128, 2, 512], BF16, "fq")
    r_fq = Res("fq")
    qTf = b.sb([128, 512], F32, "qTf")
    kTf = b.sb([128, 512], F32, "kTf")
    r_qk = Res("qk")
    alr = b.sb([16, 512], BF16, "alr")
    r_alr = Res("alr")
    gext = [b.sb([128, 2, 542], BF16, "gext%d" % i) for i in range(2)]
    r_gext = [Res("gext%d" % i) for i in range(2)]
    glast = b.sb([128, 2, 30], F32, "glast")
    r_glast = Res("glast")
    ydT = b.sb([128, 2, 512], BF16, "ydT")
    r_ydT = Res("ydT")
    Sf = b.sb([128, 256], F32, "Sf")
    Sb = b.sb([128, 256], BF16, "Sb")
    r_S = Res("S")
    Pacc = b.sb([128, 4], F32, "Pacc")
    r_Pacc = Res("Pacc")

    nbuf = {}

    def tmp(name, shape, dt, n=1):
        if name not in nbuf:
            nbuf[name] = [[(b.sb(shape, dt, "%s_%d" % (name, i)), Res(name)) for i in range(n)], 0]
        lst = nbuf[name]
        t, r = lst[0][lst[1] % n]
        lst[1] += 1
        return t, r


    Rs = b.sb([NS, D], BF16, "Rs")
    r_Rs = Res("Rs")
    xTs = b.sb([128, 8, NS], BF16, "xTs")
    r_xTs = Res("xTs")
    sb16 = b.sb([NS, NS], F32, "sb16")
    maskS = b.sb([NS, NS], F32, "maskS")
    maskS4 = b.sb([NS, 64], F32, "maskS4")
    maskN = b.sb([NS, 64], F32, "maskN")
    triSb = b.sb([NS, NS], BF16, "triSb")
    revSb = b.sb([NS, NS], BF16, "revSb")
    bm = b.sb([NS, 4], F32, "bm")
    mset("pool", sb16[:], 1.0, [rc])
    mset("pool", bm[:], 1.0, [rc])
    for bb in range(4):
        v = sb16[:, 4 * bb:4 * bb + 4]
        asel(v, v, [[0, 4]], ALU.is_ge, 0.0, -4 * bb, 1, [rc], [rc])
        asel(v, v, [[0, 4]], ALU.is_ge, 0.0, 4 * bb + 3, -1, [rc], [rc])
        v = bm[:, bb:bb + 1]
        asel(v, v, [[0, 1]], ALU.is_ge, 0.0, -4 * bb, 1, [rc], [rc])
        asel(v, v, [[0, 1]], ALU.is_ge, 0.0, 4 * bb + 3, -1, [rc], [rc])
    tt("dve", maskS[:], sb16[:], triO[0:NS, 0:NS], MUL, [rc], [rc])
    ts("dve", triSb[:], maskS[:], -1.0 / 16.0, None, MUL, None, [rc], [rc])
    tt("dve", sb16[:], sb16[:], maskS[:], SUB, [rc], [rc])
    ts("dve", revSb[:], sb16[:], -1.0 / 16.0, None, MUL, None, [rc], [rc])
    for h in range(4):
        cp("dve", maskS4[:, h * 16:(h + 1) * 16], maskS[:], [rc], [rc])
        cp("dve", maskN[:].rearrange("p (b h q) -> p b h q", b=4, h=4)[:, :, h, :],
           maskS[:].rearrange("p (b q) -> p b q", b=4), [rc], [rc])
    dma("pool", Rs[:], xs_d[:, :], [], [r_Rs], r_Rs)
    if has_cache:
        idxr = b.sb([128, 256], I32, "idxr")
        idx64 = b.sb([128, 4], I32, "idx64")
        r_idx = Res("idx")
        ia = xTf[:, 0:256].bitcast(I32)
        io = xTf[:, 256:512].bitcast(I32)
        for bb in range(4):
            dma("sp", ia[:, bb * 64:(bb + 1) * 64], pt_d[bb:bb + 1, :].broadcast_to([128, 64]), [], [r_xT], r_xT)
        P.op("pool", lambda e: e.iota(io, pattern=[[0, 256]], base=0, channel_multiplier=1), w=[r_xT])
        stt("dve", idxr[:], ia, 128, io, MUL, ADD, [r_xT], [r_idx])
        mset("pool", idx64[:], 0, [r_idx])
        dma("sp", idx64[0:64, :], pt_d.rearrange("b j -> j b"), [], [r_idx], r_idx, allow_slow_non_contiguous=True)
        ckf = ck.rearrange("l n r c -> (l n r) c")
        cvf = cv.rearrange("l n r c -> (l n r) c")
        clff = clf.rearrange("l n r h -> (l n) (r h)")
    fpc = [0]

    ALLR = []

    def switch():
        P.op("pool", lambda e: e.memset(dummy[:], 0.0), w=[rW, r_dummy, r_xT, r_qk, r_fq, r_ydT, r_alr] + r_gext + r_FK + r_VA + ALLR)

    class Arena:
        def __init__(self, aps):
            self.aps = aps
            self.i, self.pos = 0, 0

        def take(self, parts, cols, dt, name):
            w = cols if dt == F32 else (cols + 1) // 2
            while self.pos + w > self.aps[self.i][1]:
                self.i += 1
                self.pos = 0
            a = self.aps[self.i][0][0:parts, self.pos:self.pos + w]
            self.pos += w
            r = Res(name)
            ALLR.append(r)
            if dt != F32:
                a = a.bitcast(BF16)[:, 0:cols]
            return a, r

    FKf = FK[:].rearrange("p c n -> p (c n)").bitcast(F32)
    VAf = VAflat[:].bitcast(F32)
    qTff, kTff = qTf[:], kTf[:]
    fqf = fq[:].rearrange("p c n -> p (c n)").bitcast(F32)
    ydf = ydT[:].rearrange("p c n -> p (c n)").bitcast(F32)
    g0f = gext[0][:].rearrange("p c n -> p (c n)").bitcast(F32)
    g1f = gext[1][:].rearrange("p c n -> p (c n)").bitcast(F32)

    GP = 4
    fpd = {}
    if has_cache:
        fpd["Kg"] = [(b.sb([128, GP * 256], BF16, "Kg%d" % i), Res("Kg%d" % i)) for i in range(2)]
        fpd["Vg"] = [(b.sb([128, GP * 258], BF16, "Vg%d" % i), Res("Vg%d" % i)) for i in range(2)]
        fpd["KT"] = [(b.sb([128, 2 * GP * 128], BF16, "KTp%d" % i), Res("KTp%d" % i)) for i in range(2)]
        fpd["t"] = [(b.sb([128, GP * 16], F32, "fp_t%d" % i), Res("fp_t%d" % i)) for i in range(2)]
        fpd["pTb"] = [(b.sb([128, GP * 16], BF16, "fp_pTb%d" % i), Res("fp_pTb%d" % i)) for i in range(2)]
        fpd["tot"] = (b.sb([64, 4], F32, "fp_tot"), Res("fp_tot"))
        fpd["opast"] = (b.sb([128, 257], F32, "opast"), Res("opast"))
        fpd["hsq"] = (b.sb([NS, 258], F32, "hsq"), Res("hsq"))
        fpd["fqs"] = (b.sb([128, 32], BF16, "fqs_p"), Res("fqs_p"))
        fpd["qblk"] = (b.sb([128, 64], BF16, "qblk_p"), Res("qblk_p"))
        for i in range(2):
            v, r = fpd["Vg"][i]
            mset("pool", v[:].rearrange("p (g c) -> p g c", g=GP)[:, :, 256:258], 1.0, [r])

    def sample_q_prework(l):
        hsq, r_hsq = fpd["hsq"]
        fqs, r_fqs = fpd["fqs"]
        qblk, r_qblk = fpd["qblk"]
        pt, rp = b.ps()
        ptb = pt[:].bitcast(BF16)
        for kc in range(8):
            tr(ptb[:, kc * NS:(kc + 1) * NS], Rs[:, kc * 128:(kc + 1) * 128], ident_bf[0:NS, 0:NS], [r_Rs, rc], [rp])
        cp("act", xTs[:], ptb[:, 0:8 * NS].rearrange("p (k n) -> p k n", k=8), [rp], [r_xTs])
        pq, rpq = b.ps()
        for kc in range(8):
            mm(pq[0:NS, 0:256], xTs[:, kc, :], Wi[:, kc, O_CQ:O_CQ + 256], kc == 0, kc == 7, [r_xTs, rW], [rpq])
        cp("act", hsq[:, 0:256], pq[0:NS, 0:256], [rpq], [r_hsq])
        pt2, rp2 = b.ps()
        for c in range(2):
            tr(pt2[:, c * 16:(c + 1) * 16], hsq[:, c * 128:(c + 1) * 128], ident_f[0:NS, 0:NS], [r_hsq, rc], [rp2])
        cp("act", fqs[:], pt2[:, 0:32], [rp2], [r_fqs])
        mset("pool", qblk[:], 0.0, [r_qblk])
        qb4 = qblk[:].rearrange("p (c b m) -> p c b m", c=2, b=4)
        for h2 in range(2):
            cp("dve", qb4[64 * h2:64 * h2 + 64, :, :, 4 * h2:4 * h2 + 4],
               fqs[64 * h2:64 * h2 + 64, 0:32].rearrange("p (c b q) -> p c b q", c=2, b=4), [r_fqs], [r_qblk])

    def past_setup(l, bb):
        d = fpd
        sig_t, r_lfp = tmp("sig", [128, 512], F32)
        lfp = sig_t[0:64, :]
        msq_t, r_lfT = tmp("cmsq", [128, 512], F32)
        lfT = msq_t[:, 0:256]
        var_t, r_rhsB = tmp("cvar", [128, 512], F32)
        rhsB = var_t[0:64, 0:256]
        csq_t, r_eS = tmp("csq", [128, 512], F32)
        eS = csq_t[:, 0:256]
        tot, r_tot = d["tot"]
        P.op("pool", lambda e: e.indirect_dma_start(
            out=lfp, out_offset=None, in_=clff,
            in_offset=bass.IndirectOffsetOnAxis(ap=idx64[0:64, bb:bb + 1], axis=0), element_offset=l * NPOOL * 512),
            r=[r_idx], w=[r_lfp], dma=True, key=r_lfp)
        pt, rp = b.ps()
        for h in range(4):
            tr(pt[:, h * 64:(h + 1) * 64], lfp.rearrange("j (r h) -> j h r", h=4)[:, h, :], ident_f[0:64, 0:64],
               [r_lfp, rc], [rp])
        cp("act", lfT, pt[:, 0:256], [rp], [r_lfT])
        rsum("dve", tot[:], lfp.rearrange("j (r h) -> j h r", h=4), [r_lfp], [r_tot])
        for h in range(4):
            ts("dve", rhsB[:, h * 64:(h + 1) * 64], revM[0:64, 0:64], tot[:, h:h + 1], None, MUL, None, [rc, r_tot], [r_rhsB])
        pS, rpS = b.ps()
        mm(pS[:, 0:256], revM[:], lfT, True, False, [rc, r_lfT], [rpS])
        mm(pS[:, 0:256], ones_f[0:64, :], rhsB, False, True, [rc, r_rhsB], [rpS])
        act(eS, pS[:, 0:256], AF.Exp, [rpS], [r_eS], scale=-16.0)
        return eS.rearrange("s (h j) -> s h j", h=4), r_eS

    NG = 64 // GP

    def past_group(l, bb, g, eS3, r_eS):
        d = fpd
        pov, rpov = b.psb[6]
        qblk, r_qblk = d["qblk"]
        qb4 = qblk[:].rearrange("p (c b m) -> p c b m", c=2, b=4)
        k = fpc[0] % 2
        fpc[0] += 1
        Kg, r_Kg = d["Kg"][k]
        Vg, r_Vg = d["Vg"][k]
        KT, r_KT = d["KT"][k]
        t_, r_t = d["t"][k]
        pTb, r_pTb = d["pTb"][k]
        Kg3 = Kg[:].rearrange("p (g c) -> p g c", g=GP)
        Vg3 = Vg[:].rearrange("p (g c) -> p g c", g=GP)
        KT4 = KT[:].rearrange("p (c g s) -> p c g s", c=2, g=GP)

        def f_dmak():
            for p in range(GP):
                j = g * GP + p
                P.op("pool", lambda e, p=p, j=j: e.indirect_dma_start(
                    out=Kg3[:, p, :], out_offset=None, in_=ckf,
                    in_offset=bass.IndirectOffsetOnAxis(ap=idxr[:, bb * 64 + j:bb * 64 + j + 1], axis=0),
                    element_offset=l * NPOOL * 128 * 256),
                    r=[r_idx], w=[r_Kg], dma=True, key=r_Kg)

        def f_dmav():
            for p in range(GP):
                j = g * GP + p
                P.op("pool", lambda e, p=p, j=j: e.indirect_dma_start(
                    out=Vg3[:, p, 0:256], out_offset=None, in_=cvf,
                    in_offset=bass.IndirectOffsetOnAxis(ap=idxr[:, bb * 64 + j:bb * 64 + j + 1], axis=0),
                    element_offset=l * NPOOL * 128 * 256),
                    r=[r_idx], w=[r_Vg], dma=True, key=r_Vg)

        def f_a():
            pk, rpk = b.ps()
            pkb = pk[:].bitcast(BF16)
            for c in range(2):
                for p in range(GP):
                    tr(pkb[:, (c * GP + p) * 128:(c * GP + p + 1) * 128], Kg3[:, p, c * 128:(c + 1) * 128], ident_bf[:],
                       [r_Kg, rc], [rpk])
            cp("act" if g % 2 else "dve", KT[:], pkb[:, 0:2 * GP * 128], [rpk], [r_KT])

        def f_b():
            psc, rpsc = b.ps()
            for p in range(GP):
                for c in range(2):
                    mm(psc[:, p * 16 + c * 8:p * 16 + c * 8 + 8], KT4[:, c, p, :], qb4[:, c, bb, :], True, True,
                       [r_KT, r_qblk], [rpsc])
            act(t_[:], psc[:, 0:GP * 16], AF.Exp, [rpsc], [r_t], scale=0.125)
            t4 = t_[:].rearrange("s (p h q) -> s p h q", p=GP, h=4)
            pT4 = pTb[:].rearrange("s (p h q) -> s p h q", p=GP, h=4)
            eSg = eS3[:, :, g * GP:(g + 1) * GP].rearrange("s h p -> s p h")
            for q in range(4):
                tt("dve", pT4[:, :, :, q], t4[:, :, :, q], eSg, MUL, [r_t, r_eS], [r_pTb])

        def f_c():
            for p in range(GP):
                mm(pov[0:NS, 0:257], pTb[:, p * 16:(p + 1) * 16], Vg3[:, p, 0:257], g == 0 and p == 0,
                   g == NG - 1 and p == GP - 1, [r_pTb, r_Vg], [rpov])
        return f_dmak, f_dmav, f_a, f_b, f_c

    class PastPipe:
        def __init__(self, l, bb):
            self.l, self.bb = l, bb
            self.eS3, self.r_eS = past_setup(l, bb)
            self.groups = [past_group(l, bb, g, self.eS3, self.r_eS) for g in range(NG)]
            self.s = 0
            self.groups[0][0]()

        def issue(self, n):
            pass

        def step(self):
            s_ = self.s
            G = self.groups
            if 0 <= s_ - 2 < NG:
                G[s_ - 2][4]()
            if s_ < NG:
                G[s_][1]()
            if 0 <= s_ - 1 < NG:
                G[s_ - 1][3]()
            if s_ + 1 < NG:
                G[s_ + 1][0]()
            if s_ < NG:
                G[s_][2]()
            self.s += 1

        def drain(self):
            while self.s < NG + 2:
                self.step()
            opast, r_opast = fpd["opast"]
            pov, rpov = b.psb[6]
            hsq, r_hsq = fpd["hsq"]
            cp("act", hsq[:, 0:257], pov[0:NS, 0:257], [rpov], [r_hsq])
            dma("sp", opast[32 * self.bb:32 * self.bb + NS, :], hsq[:, 0:257], [r_hsq], [r_opast], r_opast)

    def sample_mixer(l):
        switch()
        ar = Arena([(FKf, 2048), (VAf, 2080), (xTf, 2048), (qTff, 512), (kTff, 512), (fqf, 512), (ydf, 512),
                    (g0f, 542), (g1f, 542)])
        T = ar.take
        hsA, r_hsA = T(NS, 1296, F32, "hsA")
        hsB, r_hsB = T(NS, 1284, F32, "hsB")

        def hc(a, e):
            if e <= 1296:
                return hsA[:, a:e], r_hsA
            return hsB[:, a - 1296:e - 1296], r_hsB
        WSf, r_WSf = T(NS, 64, F32, "WSf")
        WSb, r_WSb = T(NS, 64, BF16, "WSb")
        bsS, r_bsS = T(NS, 4, F32, "bsS")
        bff, r_bff = T(NS, 4, F32, "bff")
        S0f, r_S0f = T(128, 1024, F32, "S0f")
        S0b, r_S0b = T(128, 1024, BF16, "S0b")
        xxT, r_xxT = T(128, 2 * 4 * 34, F32, "xxT")
        xx4 = xxT.rearrange("p (c b t) -> p c b t", c=2, b=4)
        mset("pool", WSf, 0.0, [r_WSf])
        mset("pool", S0f, 0.0, [r_S0f])
        WS3 = WSf.rearrange("p (g t) -> p g t", g=4)
        S03 = S0f.rearrange("p (b n) -> p b n", b=4)
        NCD = dict(allow_slow_non_contiguous=True)
        for bb in range(4):
            for g in range(4):
                dma("sp", WS3[4 * bb:4 * bb + 4, g, 4 * bb:4 * bb + 4], sgu_w[l, g, 0:4, 0:4].rearrange("t s -> s t"),
                    [], [r_WSf], r_WSf, **NCD)
            dma("sp", bsS[4 * bb:4 * bb + 4, :], sgu_bs[l][:, 0:4].rearrange("g t -> t g"), [], [r_bsS], r_bsS, **NCD)
            for h in range(4):
                dma("sp", S03[32 * h:32 * h + 32, bb, 64 * h:64 * h + 64], st_gla[l, bb, h], [], [r_S0f], r_S0f)
            for c in range(2):
                dma("sp", xx4[:, c, bb, 0:30], st_conv[l, bb][:, c * 128:(c + 1) * 128].rearrange("t p -> p t"),
                    [], [r_xxT], r_xxT, **NCD)
            dma("sp", o_sconv[l, bb, 0:26, :], st_conv[l, bb, 4:30, :], [], [], r_xxT)
        dma("sp", bff, fox_bf[l:l + 1, :].broadcast_to([NS, 4]), [], [r_bff], r_bff)
        for g in range(4):
            tt("dve", WSb.rearrange("p (g t) -> p g t", g=4)[:, g, :], WS3[:, g, :], maskS[:], MUL, [r_WSf, rc], [r_WSb])
        cp("act", S0b, S0f, [r_S0f], [r_S0b])
        S0b3 = S0b.rearrange("p (b n) -> p b n", b=4)

        pt, rp = b.ps()
        ptb = pt[:].bitcast(BF16)
        for kc in range(8):
            tr(ptb[:, kc * NS:(kc + 1) * NS], Rs[:, kc * 128:(kc + 1) * 128], ident_bf[0:NS, 0:NS], [r_Rs, rc], [rp])
        cp("act", xTs[:], ptb[:, 0:8 * NS].rearrange("p (k n) -> p k n", k=8), [rp], [r_xTs])
        for g0 in (0, 512, 1024, 1296, 1808, 2320):
            e0 = {0: 512, 512: 1024, 1024: 1296, 1296: 1808, 1808: 2320, 2320: DIN}[g0]
            n = e0 - g0
            pt, rp = b.ps()
            for kc in range(8):
                mm(pt[0:NS, 0:n], xTs[:, kc, :], Wi[:, kc, g0:e0], kc == 0, kc == 7, [r_xTs, rW], [rp])
            dst, rdst = hc(g0, e0)
            cp("act" if (g0 // 512) % 2 else "dve", dst, pt[0:NS, 0:n], [rp], [rdst])
        v_, r_ = hc(O_CK, O_CK + 256)
        dma("sp", o_sfk[l], v_, [r_], [], r_)
        v_, r_ = hc(O_CV, O_CV + 256)
        dma("sp", o_sfv[l], v_, [r_], [], r_)
        switch()
        ar2 = Arena([(Wi[:].rearrange("p k n -> p (k n)").bitcast(F32), 10320)])

        def T(parts, cols, dt, name):
            for a_ in (ar2, ar):
                try:
                    return a_.take(parts, cols, dt, name)
                except IndexError:
                    a_.i = len(a_.aps) - 1
                    a_.pos = a_.aps[-1][1]
            raise RuntimeError("sample arenas exhausted: " + name)
        ys, r_ys = T(NS, 768, BF16, "ys")
        idf = ident_f[0:NS, 0:NS]

        ydTs, r_ydTs = T(128, 32, BF16, "s_ydTs")
        def _sg_gla():
            pt, rp = b.ps()
            tr(pt[0:16, 0:NS], hsA[:, O_ALR:O_ALR + 16], idf, [r_hsA, rc], [rp])
            tr(pt[:, 16:32], hsA[:, O_AQ:O_AQ + 128], idf, [r_hsA, rc], [rp])
            tr(pt[:, 32:48], hsA[:, O_AK:O_AK + 128], idf, [r_hsA, rc], [rp])
            alrs, r_alrs = T(16, NS, BF16, "alrs")
            cp("act", alrs, pt[0:16, 0:NS], [rp], [r_alrs])
            qk_s, r_qks = T(128, 32, F32, "qk_s")
            cp("act", qk_s, pt[:, 16:48], [rp], [r_qks])
            yield
            pz, rpz = b.ps()
            mm(pz[0:NS, 0:128], alrs, Wa[:], True, False, [r_alrs, rW], [rpz])
            mm(pz[0:NS, 0:128], ones_bf[0:1, 0:NS], ba[:], False, True, [rc, rW], [rpz])
            e1, r_e1 = T(NS, 128, F32, "s_e1")
            act(e1, pz[0:NS, 0:128], AF.Exp, [rpz], [r_e1], scale=-1.0)
            spl, r_spl = T(NS, 128, F32, "s_spl")
            act(spl, e1, AF.Ln, [r_e1], [r_spl], bias=1.0, scale=1.0)
            shi, r_shi = T(NS, 128, BF16, "s_shi")
            slo, r_slo = T(NS, 128, BF16, "s_slo")
            cp("dve", shi, spl, [r_spl], [r_shi])
            tt("dve", slo, spl, shi, SUB, [r_spl, r_shi], [r_slo])
            yield
            pg_, rpg_ = b.ps()
            mm(pg_[:, 0:NS], shi, triSb[:], True, False, [r_shi, rc], [rpg_])
            mm(pg_[:, 0:NS], slo, triSb[:], False, True, [r_slo, rc], [rpg_])
            mm(pg_[0:NS, 128:256], revSb[:], shi, True, False, [r_shi, rc], [rpg_])
            mm(pg_[0:NS, 128:256], revSb[:], slo, False, True, [r_slo, rc], [rpg_])
            egs, r_egs = T(128, 32, F32, "s_eg")
            act(egs[:, 0:16], pg_[:, 0:NS], AF.Exp, [rpg_], [r_egs])
            act(egs[:, 16:32], pg_[:, 0:NS], AF.Exp, [rpg_], [r_egs], scale=-1.0)
            erev, r_erev = T(NS, 128, F32, "s_erev")
            act(erev, pg_[0:NS, 128:256], AF.Exp, [rpg_], [r_erev])
            qtl, r_qtl = T(128, NS, BF16, "s_qtl")
            stt("dve", qtl, qk_s[:, 0:16], 32.0 ** -0.5, egs[:, 0:16], MUL, MUL, [r_qks, r_egs], [r_qtl])
            kt4, r_kt4 = T(128, 64, BF16, "s_kt4")
            for h in range(4):
                stt("dve", kt4[:, h * 16:(h + 1) * 16], qk_s[:, 16:32], hm[:, h:h + 1], egs[:, 16:32], MUL, MUL,
                    [r_qks, r_egs, rc], [r_kt4])
            qtb, r_qtb = T(128, 64, BF16, "s_qtb")
            mset("pool", qtb, 0.0, [r_qtb])
            for bb in range(4):
                cp("dve", qtb[:, bb * 16 + 4 * bb:bb * 16 + 4 * bb + 4], qtl[:, 4 * bb:4 * bb + 4], [r_qtl], [r_qtb])
            kpb, r_kpb = T(NS, 512, BF16, "s_kpb")
            for bb in range(4):
                stt("dve", kpb[:, bb * 128:(bb + 1) * 128], hsA[:, O_AK:O_AK + 128], bm[:, bb:bb + 1], erev, MUL, MUL,
                    [r_hsA, r_erev, rc], [r_kpb])
            vb, r_vb = T(NS, 256, BF16, "s_vb")
            cp("act", vb, hsA[:, O_AV:O_AV + 256], [r_hsA], [r_vb])
            sgl, r_sgl = T(NS, 256, F32, "s_sgl")
            act(sgl, hsA[:, O_AG:O_AG + 256], AF.Silu, [r_hsA], [r_sgl])
            tt("dve", sgl, sgl, glag[0:NS, :], MUL, [r_sgl, rW], [r_sgl])
            yield
            pa_, rpa_ = b.ps()
            for h in range(4):
                mm(pa_[0:NS, h * 16:(h + 1) * 16], kt4[:, h * 16:(h + 1) * 16], qtl, True, True, [r_kt4, r_qtl], [rpa_])
            asb, r_asb = T(NS, 64, BF16, "s_asb")
            tt("dve", asb, pa_[0:NS, 0:64], maskS4[:], MUL, [rpa_, rc], [r_asb])
            yield
            po, rpo = b.ps()
            for bb in range(4):
                mm(po[0:NS, 0:256], qtb[:, bb * 16:(bb + 1) * 16], S0b3[:, bb, :], bb == 0, bb == 3, [r_qtb, r_S0b], [rpo])
            for h in range(4):
                mm(po[0:NS, 256 + 64 * h:320 + 64 * h], asb[:, h * 16:(h + 1) * 16], vb[:, 64 * h:64 * h + 64], True, True,
                   [r_asb, r_vb], [rpo])
            of, r_of = T(NS, 256, F32, "s_of")
            cp("act", of, po[0:NS, 0:256], [rpo], [r_of])
            tt("dve", of, of, po[0:NS, 256:512], ADD, [r_of, rpo], [r_of])
            sn, r_sn = T(128, 256, F32, "s_sn")
            for bb in range(4):
                yield
                pn, rpn = b.ps()
                mm(pn[:, 0:256], kpb[:, bb * 128:(bb + 1) * 128], vb, True, True, [r_kpb, r_vb], [rpn])
                tt("dve", sn, pn[:, 0:256], blkmask[:], MUL, [rpn, rc], [r_sn])
                stt("dve", sn, S03[:, bb, :], egs[:, 4 * bb + 3:4 * bb + 4], sn, MUL, ADD, [r_S0f, r_egs, r_sn], [r_sn])
                for h in range(4):
                    dma("sp", o_sgla[l, bb, h], sn[32 * h:32 * h + 32, 64 * h:64 * h + 64], [r_sn], [], r_sn)
            osq, r_osq = T(NS, 256, F32, "s_osq")
            act(osq, of, AF.Square, [r_of], [r_osq])
            gst, r_gst = T(NS, 8, F32, "s_gst")
            rsum("dve", gst[:, 0:4], osq.rearrange("p (h e) -> p h e", h=4), [r_osq], [r_gst])
            ts("dve", gst[:, 4:8], gst[:, 0:4], 1.0 / 64.0, EPS, MUL, ADD, [r_gst], [r_gst])
            ts("dve", gst[:, 4:8], gst[:, 4:8], -0.5, None, POW, None, [r_gst], [r_gst])
            for h in range(4):
                stt("dve", ys[:, 64 * h:64 * h + 64], of[:, 64 * h:64 * h + 64], gst[:, 4 + h:5 + h],
                    sgl[:, 64 * h:64 * h + 64], MUL, MUL, [r_of, r_gst, r_sgl], [r_ys])
            yield

        def _sg_sgu():
            uu = hsA[:, O_BU:O_BU + 256]
            vv = hsA[:, O_BV:O_BV + 256]
            vsq, r_vsq = T(NS, 256, F32, "s_vsq")
            act(vsq, vv, AF.Square, [r_hsA], [r_vsq])
            sst, r_sst = T(NS, 24, F32, "s_sst")
            rsum("dve", sst[:, 0:4], vv.rearrange("p (h e) -> p h e", h=4), [r_hsA], [r_sst])
            rsum("dve", sst[:, 4:8], vsq.rearrange("p (h e) -> p h e", h=4), [r_vsq], [r_sst])
            ts("dve", sst[:, 8:12], sst[:, 0:4], 1.0 / 64.0, None, MUL, None, [r_sst], [r_sst])
            tt("dve", sst[:, 12:16], sst[:, 8:12], sst[:, 8:12], MUL, [r_sst], [r_sst])
            stt("dve", sst[:, 12:16], sst[:, 4:8], 1.0 / 64.0, sst[:, 12:16], MUL, SUB, [r_sst], [r_sst])
            ts("dve", sst[:, 16:20], sst[:, 12:16], EPS, -0.5, ADD, POW, [r_sst], [r_sst])
            stt("dve", sst[:, 20:24], sst[:, 8:12], -1.0, sst[:, 16:20], MUL, MUL, [r_sst], [r_sst])
            vn, r_vn = T(NS, 256, F32, "s_vn")
            for g in range(4):
                ts("dve", vn[:, 64 * g:64 * g + 64], vv[:, 64 * g:64 * g + 64], sst[:, 16 + g:17 + g], sst[:, 20 + g:21 + g],
                   MUL, ADD, [r_hsA, r_sst], [r_vn])
            tt("dve", vn, vn, sgg[0:NS, :], MUL, [r_vn, rW], [r_vn])
            tt("dve", vn, vn, sgb[0:NS, :], ADD, [r_vn, rW], [r_vn])
            dma("sp", o_ssgu[l], vn, [r_vn], [], r_vn)
            vlb, r_vlb = T(NS, 256, BF16, "s_vlb")
            cp("dve", vlb, vn, [r_vn], [r_vlb])
            pmx, rpmx = b.ps()
            for g in range(4):
                mm(pmx[0:NS, 64 * g:64 * g + 64], WSb[:, g * 16:(g + 1) * 16], vlb[:, 64 * g:64 * g + 64], True, True,
                   [r_WSb, r_vlb], [rpmx])
            for g in range(4):
                stt("dve", ys[:, 256 + 64 * g:320 + 64 * g], pmx[0:NS, 64 * g:64 * g + 64], bsS[:, g:g + 1],
                    uu[:, 64 * g:64 * g + 64], ADD, MUL, [rpmx, r_hsA, r_bsS], [r_ys])
            yield

        def _sg_fox():
            cf_, r_cf = hc(O_CF, O_CF + 4)
            fst, r_fst = T(NS, 16, F32, "s_fst")
            tt("dve", fst[:, 0:4], cf_, bff, ADD, [r_cf, r_bff], [r_fst])
            act(fst[:, 0:4], fst[:, 0:4], AF.Exp, [r_fst], [r_fst], scale=-1.0)
            act(fst[:, 4:8], fst[:, 0:4], AF.Ln, [r_fst], [r_fst], bias=1.0, scale=1.0)
            ts("dve", fst[:, 12:16], fst[:, 4:8], -1.0, None, MUL, None, [r_fst], [r_fst])
            dma("sp", o_sflf[l], fst[:, 12:16], [r_fst], [], r_fst)
            pd, rpd = b.ps()
            mm(pd[0:NS, 0:4], maskS[:], fst[:, 4:8], True, True, [rc, r_fst], [rpd])
            act(fst[:, 8:12], pd[0:NS, 0:4], AF.Exp, [rpd], [r_fst])
            yield
            pt, rp = b.ps()
            for c in range(2):
                v_, r_ = hc(O_CQ + 128 * c, O_CQ + 128 * c + 128)
                tr(pt[:, c * 16:(c + 1) * 16], v_, idf, [r_, rc], [rp])
                v_, r_ = hc(O_CK + 128 * c, O_CK + 128 * c + 128)
                tr(pt[:, 32 + c * 16:32 + (c + 1) * 16], v_, idf, [r_, rc], [rp])
            fqk, r_fqk = T(128, 64, BF16, "s_fqk")
            cp("act", fqk, pt[:, 0:64], [rp], [r_fqk])
            qblk, r_qblk = T(128, 2 * 4 * 8, BF16, "s_qblk")
            mset("pool", qblk, 0.0, [r_qblk])
            qb4 = qblk.rearrange("p (c b m) -> p c b m", c=2, b=4)
            for h2 in range(2):
                cp("dve", qb4[64 * h2:64 * h2 + 64, :, :, 4 * h2:4 * h2 + 4],
                   fqk[64 * h2:64 * h2 + 64, 0:32].rearrange("p (c b q) -> p c b q", c=2, b=4), [r_fqk], [r_qblk])
            vaug, r_vaug = T(NS, 258, BF16, "s_vaug")
            v_, r_ = hc(O_CV, O_CV + 256)
            cp("act", vaug[:, 0:256], v_, [r_], [r_vaug])
            mset("pool", vaug[:, 256:258], 1.0, [r_vaug])
            yield
            psn, rpsn = b.ps()
            for bb in range(4):
                for c in range(2):
                    mm(psn[0:NS, bb * 16 + c * 8:bb * 16 + c * 8 + 8], fqk[:, 32 + c * 16:32 + (c + 1) * 16], qb4[:, c, bb, :],
                       True, True, [r_fqk, r_qblk], [rpsn])
            t1, r_t1 = T(NS, 64, F32, "s_t1")
            act(t1, psn[0:NS, 0:64], AF.Exp, [rpsn], [r_t1], scale=0.125)
            tt("dve", t1, t1, maskN[:], MUL, [r_t1, rc], [r_t1])
            pnb, r_pnb = T(NS, 64, BF16, "s_pnb")
            t14 = t1.rearrange("p (b h q) -> p b h q", b=4, h=4)
            pn4 = pnb.rearrange("p (b h q) -> p b h q", b=4, h=4)
            for h in range(4):
                ts("dve", pn4[:, :, h, :], t14[:, :, h, :], fst[:, 8 + h:9 + h], None, MUL, None, [r_t1, r_fst], [r_pnb])
            ycs, r_ycs = T(NS, 256, F32, "s_ycs")
            osum, r_osum = T(NS, 258, F32, "s_osum")
            rd, r_rd = T(NS, 2, F32, "s_rd")
            on, r_on = T(NS, 256, F32, "s_on")
            for bb in range(4):
                pov, rpov = b.psb[7]
                mm(pov[0:NS, 0:257], pnb[:, bb * 16:(bb + 1) * 16], vaug[:, 0:257], True, True, [r_pnb, r_vaug], [rpov])
                if has_cache:
                    dma("sp", osum[:, 0:257], fpd["opast"][0][32 * bb:32 * bb + NS, :], [fpd["opast"][1]], [r_osum], r_osum)
                    tt("dve", osum[:, 0:257], osum[:, 0:257], pov[0:NS, 0:257], ADD, [rpov, r_osum], [r_osum])
                else:
                    cp("dve", osum[:, 0:257], pov[0:NS, 0:257], [rpov], [r_osum])
                P.op("dve", lambda e, rd=rd, osum=osum: e.reciprocal(out=rd[:, 0:1], in_=osum[:, 256:257]), r=[r_osum], w=[r_rd])
                ts("dve", on, osum[:, 0:256], rd[:, 0:1], None, MUL, None, [r_osum, r_rd], [r_on])
                for h in range(4):
                    dma("sp", ycs[4 * bb:4 * bb + 4, 64 * h:64 * h + 64], on[4 * h:4 * h + 4, 64 * h:64 * h + 64],
                        [r_on], [r_ycs], r_ycs)
            cp("dve", ys[:, 512:768], ycs, [r_ycs], [r_ys])
            yield

        def _sg_conv():
            ga_, r_ga = hc(O_DIN, O_DIN + 256)
            gg_, r_gg = hc(O_DIN + 256, O_DIN + 512)
            glu, r_glu = T(NS, 256, F32, "s_glu")
            act(glu, gg_, AF.Sigmoid, [r_gg], [r_glu])
            tt("dve", glu, glu, ga_, MUL, [r_glu, r_ga], [r_glu])
            for bb in range(4):
                dma("sp", o_sconv[l, bb, 26:30, :], glu[4 * bb:4 * bb + 4, :], [r_glu], [], r_glu)
            pt, rp = b.ps()
            for c in range(2):
                tr(pt[:, c * 16:(c + 1) * 16], glu[:, c * 128:(c + 1) * 128], idf, [r_glu, rc], [rp])
            for c in range(2):
                cp("act", xx4[:, c, :, 30:34], pt[:, c * 16:(c + 1) * 16].rearrange("p (b q) -> p b q", b=4), [rp], [r_xxT])
            acc, r_acc = T(128, 32, F32, "s_acc")
            for c in range(2):
                a3 = acc[:, c * 16:(c + 1) * 16].rearrange("p (b q) -> p b q", b=4)
                for j in range(31):
                    if j == 0:
                        ts("dve", a3, xx4[:, c, :, 0:4], cw[:, c, 0:1], None, MUL, None, [r_xxT, rW], [r_acc])
                    else:
                        stt("dve", a3, xx4[:, c, :, j:j + 4], cw[:, c, j:j + 1], a3, MUL, ADD, [r_xxT, rW, r_acc], [r_acc])
            for c in range(2):
                ac = acc[:, c * 16:(c + 1) * 16]
                cof, r_cof = T(128, 16, F32, "s_cof%d" % c)
                act(cof, ac, AF.Identity, [r_acc, rW], [r_cof], bias=cb[:, c:c + 1], scale=1.0)
                sq, r_sq = T(128, 16, F32, "s_csq%d" % c)
                tt("dve", sq, cof, cof, MUL, [r_cof], [r_sq])
                yield
                pm, rpm = b.ps()
                mm(pm[:, 0:16], blk64[:], cof, True, True, [rc, r_cof], [rpm])
                mm(pm[:, 16:32], blk64[:], sq, True, True, [rc, r_sq], [rpm])
                mv, r_mv = T(128, 32, F32, "s_mv%d" % c)
                cp("act", mv, pm[:, 0:32], [rpm], [r_mv])
                tt("dve", sq, mv[:, 0:16], mv[:, 0:16], MUL, [r_mv], [r_sq])
                tt("dve", sq, mv[:, 16:32], sq, SUB, [r_mv, r_sq], [r_sq])
                ts("dve", sq, sq, EPS, -0.5, ADD, POW, [r_sq], [r_sq])
                tt("dve", cof, cof, mv[:, 0:16], SUB, [r_cof, r_mv], [r_cof])
                tt("dve", cof, cof, sq, MUL, [r_cof, r_sq], [r_cof])
                act(ydTs[:, c * 16:(c + 1) * 16], cof, AF.Silu, [r_cof, rW], [r_ydTs], scale=cng[:, c:c + 1], bias=cnb[:, c:c + 1])
            yield

        _gens = [_sg_gla(), _sg_sgu(), _sg_fox(), _sg_conv()]
        while _gens:
            for g_ in list(_gens):
                try:
                    next(g_)
                except StopIteration:
                    _gens.remove(g_)
        pty, rpty = b.ps()
        ptyb = pty[:].bitcast(BF16)
        for c in range(6):
            tr(ptyb[:, c * 16:(c + 1) * 16], ys[:, c * 128:(c + 1) * 128], ident_bf[0:NS, 0:NS], [r_ys, rc], [rpty])
        yTs, r_yTs = T(128, 96, BF16, "s_yTs")
        cp("act", yTs, ptyb[:, 0:96], [rpty], [r_yTs])
        ps2, rps2 = [], []
        for hf in range(2):
            pm_, rpm_ = b.ps()
            for kc in range(8):
                lh = yTs[:, kc * 16:(kc + 1) * 16] if kc < 6 else ydTs[:, (kc - 6) * 16:(kc - 5) * 16]
                mm(pm_[0:NS, :], lh, Wo[:, kc, hf * 512:(hf + 1) * 512], kc == 0, kc == 7, [r_yTs, r_ydTs, rW], [rpm_])
            ps2.append(pm_)
            rps2.append(rpm_)
        layer_norm(None, ps2, rps2, l1g, l1b, None, n=NS, res=Rs[:], r_res=r_Rs)
        switch()

    cur_wu = [None]

    def wu_chunk(l, fc, wus, cnt_f):
        if fc % 4 == 0:
            cols = min(4, NFC - fc) * 128
            wu, r_wu = wus[cnt_f[0] % 2]
            cnt_f[0] += 1
            cur_wu[0] = (wu, r_wu)
            dma("pool", wu[:, :, 0:cols], w_up[l, :, fc * 128:fc * 128 + cols].rearrange("(k p) n -> p k n", p=128),
                [], [r_wu], r_wu)
            dma("pool", wu[:, :, 512:512 + cols],
                w_up[l, :, DFF + fc * 128:DFF + fc * 128 + cols].rearrange("(k p) n -> p k n", p=128), [], [r_wu], r_wu)
        wu, r_wu = cur_wu[0]
        ci = fc % 4
        return wu, r_wu, ci * 128, 512 + ci * 128

    def sample_ffn_prep(l):
        switch()
        ar = Arena([(fqf, 512), (ydf, 512), (g0f, 542), (g1f, 542)])
        T = ar.take
        pt, rp = b.ps()
        ptb = pt[:].bitcast(BF16)
        for kc in range(8):
            tr(ptb[:, kc * NS:(kc + 1) * NS], Rs[:, kc * 128:(kc + 1) * 128], ident_bf[0:NS, 0:NS], [r_Rs, rc], [rp])
        cp("act", xTs[:], ptb[:, 0:8 * NS].rearrange("p (k n) -> p k n", k=8), [rp], [r_xTs])
        d = {}
        d["bufT"] = T(128, NFC * 8, F32, "f_bufT")
        d["glo"] = T(128, NFC * 8, F32, "f_glo")
        d["hTs"] = T(128, NFC * 16, BF16, "f_hTs")
        bufT, r_bufT = d["bufT"]
        bu4 = bufT.rearrange("p (c b j) -> p c b j", c=NFC, b=4)
        NCD = dict(allow_slow_non_contiguous=True)
        for fc in range(NFC):
            for bb in range(4):
                dma("sp", bu4[:, fc, bb, :], st_ffc[l, bb][:, fc * 128:(fc + 1) * 128].rearrange("j p -> p j"),
                    [], [r_bufT], r_bufT, **NCD)
        d["gxl"] = [T(128, 24, F32, "f_gx%d" % i) for i in range(2)]
        d["gal"] = [T(128, 16, F32, "f_ga%d" % i) for i in range(2)]
        return d

    def sample_ffn_chunk(l, fc, wu, r_wu, og, ov, d):
        bufT, r_bufT = d["bufT"]
        glo, r_glo = d["glo"]
        hTs, r_hTs = d["hTs"]
        bu4 = bufT.rearrange("p (c b j) -> p c b j", c=NFC, b=4)
        gl4 = glo.rearrange("p (c b j) -> p c b j", c=NFC, b=4)
        pg, rpg = b.ps()
        for kc in range(8):
            mm(pg[:, 0:16], wu[:, kc, og:og + 128], xTs[:, kc, :], kc == 0, kc == 7, [r_wu, r_xTs], [rpg])
        for kc in range(8):
            mm(pg[:, 16:32], wu[:, kc, ov:ov + 128], xTs[:, kc, :], kc == 0, kc == 7, [r_wu, r_xTs], [rpg])
        gx, r_gx = d["gxl"][fc % 2]
        ga, r_ga = d["gal"][fc % 2]
        gx3 = gx.rearrange("p (b t) -> p b t", b=4)
        ga3 = ga.rearrange("p (b t) -> p b t", b=4)
        cp("dve", gx3[:, :, 0:2], bu4[:, fc, :, :], [r_bufT], [r_gx])
        cp("act", gx3[:, :, 2:6], pg[:, 0:16].rearrange("p (b t) -> p b t", b=4), [rpg], [r_gx])
        cp("dve", gl4[:, fc, :, :], gx3[:, :, 4:6], [r_gx], [r_glo])
        ts("dve", ga3, gx3[:, :, 0:4], fw[:, fc, 0:1], None, MUL, None, [r_gx, rW], [r_ga])
        stt("dve", ga3, gx3[:, :, 1:5], fw[:, fc, 1:2], ga3, MUL, ADD, [r_gx, rW, r_ga], [r_ga])
        stt("dve", ga3, gx3[:, :, 2:6], fw[:, fc, 2:3], ga3, MUL, ADD, [r_gx, rW, r_ga], [r_ga])
        act(ga, ga, AF.Silu, [r_ga, rW], [r_ga], bias=fb[:, fc:fc + 1], scale=1.0)
        tt("dve", hTs[:, fc * 16:(fc + 1) * 16], ga, pg[:, 16:32], MUL, [r_ga, rpg], [r_hTs])

    def sample_ffn_down(l, last, wds, cnt_f, d):
        glo, r_glo = d["glo"]
        hTs, r_hTs = d["hTs"]
        gl4 = glo.rearrange("p (c b j) -> p c b j", c=NFC, b=4)
        NCD = dict(allow_slow_non_contiguous=True)
        for fc in range(NFC):
            for bb in range(4):
                dma("sp", o_sffc[l, bb][:, fc * 128:(fc + 1) * 128].rearrange("j p -> p j"), gl4[:, fc, bb, :],
                    [r_glo], [], r_glo, **NCD)
        bk = [b.ps(), b.ps()]
        for fc in range(NFC):
            wd, r_wd = wds[cnt_f[3] % 3]
            cnt_f[3] += 1
            dma("pool", wd[:], w_dn[l, fc * 128:(fc + 1) * 128, :], [], [r_wd], r_wd)
            for hf in range(2):
                mm(bk[hf][0][0:NS, :], hTs[:, fc * 16:(fc + 1) * 16], wd[:, hf * 512:(hf + 1) * 512], fc == 0, fc == NFC - 1,
                   [r_hTs, r_wd], [bk[hf][1]])
        layer_norm(None, [bk[0][0], bk[1][0]], [bk[0][1], bk[1][1]], l2g, l2b, o_ys[:, :] if last else None,
                   n=NS, res=Rs[:], r_res=r_Rs)
        switch()

    for t in range(NT):
        dma("pool", R[:, t, :], xp[t * 128:(t + 1) * 128, :], [], [r_R[t]], r_R[t])

    def layer_norm_gen(t, ps2, rps2, g_t, b_t, out_dram, n=128, res=None, r_res=None):
        if res is None:
            res, r_res = R[:, t, :], r_R[t]
        rf, r_rf = tmp("ln_rf", [128, D], F32)
        for hf in range(2):
            stt("dve", rf[0:n, hf * 512:(hf + 1) * 512], res[:, hf * 512:(hf + 1) * 512], ALPHA, ps2[hf][0:n, :],
                MUL, ADD, [r_res, rps2[hf]], [r_rf])
        yield
        st, r_st = tmp("ln_st", [128, 8], F32)
        xn, r_xn = tmp("ln_xn", [128, D], F32)
        act(xn[0:n, :], rf[0:n, :], AF.Identity, [r_rf], [r_xn, r_st], accum_out=st[0:n, 0:1])
        act(xn[0:n, :], rf[0:n, :], AF.Square, [r_rf], [r_xn, r_st], accum_out=st[0:n, 1:2])
        yield
        ts("dve", st[0:n, 2:3], st[0:n, 0:1], 1.0 / D, None, MUL, None, [r_st], [r_st])
        tt("dve", st[0:n, 3:4], st[0:n, 2:3], st[0:n, 2:3], MUL, [r_st], [r_st])
        stt("dve", st[0:n, 4:5], st[0:n, 1:2], 1.0 / D, st[0:n, 3:4], MUL, SUB, [r_st], [r_st])
        yield
        ts("dve", st[0:n, 5:6], st[0:n, 4:5], EPS, -0.5, ADD, POW, [r_st], [r_st])
        stt("dve", st[0:n, 6:7], st[0:n, 2:3], -1.0, st[0:n, 5:6], MUL, MUL, [r_st], [r_st])
        yield
        act(xn[0:n, :], rf[0:n, :], AF.Identity, [r_rf, r_st], [r_xn], scale=st[0:n, 5:6], bias=st[0:n, 6:7])
        yield
        tt("dve", xn[0:n, :], xn[0:n, :], g_t[0:n, :], MUL, [r_xn, rW], [r_xn])
        if out_dram is None:
            tt("dve", res, xn[0:n, :], b_t[0:n, :], ADD, [r_xn, rW], [r_res])
        else:
            tt("dve", xn[0:n, :], xn[0:n, :], b_t[0:n, :], ADD, [r_xn, rW], [r_xn])
            dma("sp", out_dram, xn[0:n, :], [r_xn], [], r_xn)

    def layer_norm(*a, **kw):
        for _ in layer_norm_gen(*a, **kw):
            pass

    def make_xT(blk):
        for ti in range(4):
            t = blk * 4 + ti
            pt, rp = b.ps()
            ptb = pt[:].bitcast(BF16)
            for kc in range(8):
                tr(ptb[:, kc * 128:(kc + 1) * 128], R[:, t, kc * 128:(kc + 1) * 128], ident_bf[:], [r_R[t], rc], [rp])
            cp("act", xT[:, :, ti * 128:(ti + 1) * 128], ptb.rearrange("p (k n) -> p k n", k=8), [rp], [r_xT])

    for l in range(nlayers):
        last = (l == nlayers - 1)
        for kc in range(8):
            dma("pool", Wi[:, kc, :], w_in[l, kc * 128:(kc + 1) * 128, :], [], [rW], rW)
        for kc in range(8):
            dma("pool", Wo[:, kc, :], w_o[l, kc * 128:(kc + 1) * 128, :], [], [rW], rW)
        dma("pool", Wa[:], gla_w_a[l], [], [rW], rW)
        dma("pool", ba[:], gla_b_a[l:l + 1, :], [], [rW], rW)
        dma("pool", bfb[:], fox_bf[l:l + 1, :], [], [rW], rW)
        dma("pool", bs4[:], sgu_bs[l], [], [rW], rW)
        dma("sp", bsT[:], sgu_bs[l].rearrange("g t -> t g"), [], [rW], rW, allow_slow_non_contiguous=True)
        for (tl, src) in ((glag, gla_g), (sgg, sgu_g), (sgb, sgu_bb)):
            dma("sp", tl[:], src[l:l + 1, :].broadcast_to([128, 256]), [], [rW], rW)
        for (tl, src) in ((l1g, ln1_g), (l1b, ln1_b)):
            dma("sp", tl[:], src[l:l + 1, :].broadcast_to([128, D]), [], [rW], rW)
        NC_ = dict(allow_slow_non_contiguous=True)
        for c in range(2):
            dma("sp", cw[:, c, :], conv_w[l][:, c * 128:(c + 1) * 128].rearrange("j p -> p j"), [], [rW], rW, **NC_)
        for (tl, src) in ((cb, conv_b), (cng, cn_g), (cnb, cn_b)):
            dma("sp", tl[:], src[l].rearrange("(c p) -> p c", p=128), [], [rW], rW, **NC_)
        for c in range(NFC):
            dma("sp", fw[:, c, :], fcw[l][:, c * 128:(c + 1) * 128].rearrange("j p -> p j"), [], [rW], rW, **NC_)
        dma("sp", fb[:], fcb[l].rearrange("(c p) -> p c", p=128), [], [rW], rW, **NC_)
        dma("sp", sw, sgu_w[l].rearrange("g t s -> t g s"), [], [r_xT], r_xT)
        for g in range(4):
            pt, rp = b.ps()
            tr(pt[:, 0:128], sw[:, g, :], ident_f[:], [r_xT, rc], [rp])
            tt("dve", WT[:, g, :], pt[:, 0:128], triO[:], MUL, [rp, rc], [rW])
        mset("pool", Sf[:], 0.0, [r_S])
        mset("pool", Sb[:], 0.0, [r_S])
        mset("pool", Pacc[:], 0.0, [r_Pacc])
        mset("pool", gext[1][:, :, 512:542], 0.0, [r_gext[1]])

        if has_cache and 'sample' not in SKIP:
            sample_q_prework(l)
        pending_tail = None
        for blk in range(4):
            make_xT(blk)
            c0 = blk * 512
            def fproj(col, m):
                pt, rp = b.ps()
                for kc in range(8):
                    mm(pt[0:m, :], Wi[:, kc, col:col + m], xT[:, kc, :], kc == 0, kc == 7, [rW, r_xT], [rp])
                return pt, rp
            pt, rp = fproj(O_AQ, 128)
            cp("act", qTf[:], pt[:], [rp], [r_qk])
            pt, rp = fproj(O_AK, 128)
            cp("dve", kTf[:], pt[:], [rp], [r_qk])
            pt, rp = fproj(O_ALR, 16)
            cp("act", alr[:], pt[0:16, :], [rp], [r_alr])
            for c in range(2):
                pt, rp = fproj(O_CQ + 128 * c, 128)
                cp("act", fq[:, c, :], pt[:], [rp], [r_fq])
                pt, rp = fproj(O_CK + 128 * c, 128)
                cp("dve", FK[:, c, c0:c0 + 512], pt[:], [rp], [r_FK[blk]])
            ge, r_ge = gext[blk % 2], r_gext[blk % 2]
            gp_, r_gp = gext[(blk + 1) % 2], r_gext[(blk + 1) % 2]
            cp("dve", ge[:, :, 0:30], gp_[:, :, 512:542], [r_gp], [r_ge])
            for c in range(2):
                pa, rpa = fproj(O_DIN + 128 * c, 128)
                pg, rpg = fproj(O_DIN + 256 + 128 * c, 128)
                sg, r_sg = tmp("sig", [128, 512], F32)
                act(sg[:], pg[:], AF.Sigmoid, [rpg], [r_sg])
                tt("dve", sg[:], pa[:], sg[:], MUL, [rpa, r_sg], [r_sg])
                cp("dve", ge[:, c, 30:542], sg[:], [r_sg], [r_ge])
                if blk == 3:
                    cp("dve", glast[:, c, :], sg[:, 482:512], [r_sg], [r_glast])
            if blk == 3:
                for c in range(2):
                    dma("sp", o_conv[l][:, c * 128:(c + 1) * 128].rearrange("t p -> p t"), glast[:, c, :], [r_glast], [],
                        r_glast, allow_slow_non_contiguous=True)
            if 'conv' in SKIP:
                mset('pool', ydT[:], 0.0, [r_ydT])
            def gen_conv(ge=ge, r_ge=r_ge):
                for c in range(2):
                    cof, r_cof = tmp("cof", [128, 512], F32)
                    ts("dve", cof[:], ge[:, c, 0:512], cw[:, c, 0:1], cb[:, c:c + 1], MUL, ADD, [r_ge, rW], [r_cof])
                    for j in range(1, 31):
                        stt("dve", cof[:], ge[:, c, j:j + 512], cw[:, c, j:j + 1], cof[:], MUL, ADD, [r_ge, rW, r_cof], [r_cof])
                        if j % 5 == 0:
                            yield
                    sq, r_sq = tmp("csq", [128, 512], F32)
                    act(sq[:], cof[:], AF.Square, [r_cof], [r_sq])
                    pm, rpm = b.ps()
                    mm(pm[:], blk64[:], cof[:], True, True, [rc, r_cof], [rpm])
                    pe2, rpe2 = b.ps()
                    mm(pe2[:], blk64[:], sq[:], True, True, [rc, r_sq], [rpe2])
                    msq, r_msq = tmp("cmsq", [128, 512], F32)
                    act(msq[:], pm[:], AF.Square, [rpm], [r_msq])
                    var, r_var = tmp("cvar", [128, 512], F32)
                    tt("dve", var[:], pe2[:], msq[:], SUB, [rpe2, r_msq], [r_var])
                    ts("dve", var[:], var[:], EPS, -0.5, ADD, POW, [r_var], [r_var])
                    tt("dve", cof[:], cof[:], pm[:], SUB, [r_cof, rpm], [r_cof])
                    tt("dve", cof[:], cof[:], var[:], MUL, [r_cof, r_var], [r_cof])
                    act(ydT[:, c, :], cof[:], AF.Silu, [r_cof, rW], [r_ydT], scale=cng[:, c:c + 1], bias=cnb[:, c:c + 1])
                    yield

            pipe = None
            for ti in range(4):
                t = blk * 4 + ti
                cs = slice(ti * 128, (ti + 1) * 128)
                ytok, r_ytok = tmp("ytok", [128, 768], BF16, 2)

                def tproj(col, n):
                    pt, rp = b.ps()
                    for kc in range(8):
                        mm(pt[:, 0:n], xT[:, kc, cs], Wi[:, kc, col:col + n], kc == 0, kc == 7, [r_xT, rW], [rp])
                    return pt, rp

                if 'gla' in SKIP or GSTOP < 99:
                    mset('pool', ytok[:, 0:256], 0.0, [r_ytok])
                def gen_gla():
                    pz, rpz = b.ps()
                    mm(pz[:, 0:128], alr[0:16, cs], Wa[:], True, False, [r_alr, rW], [rpz])
                    mm(pz[:, 0:128], ones_bf[0:1, :], ba[:], False, True, [rc, rW], [rpz])
                    e1, r_e1 = tmp("g_e1", [128, 128], F32)
                    act(e1[:], pz[:, 0:128], AF.Exp, [rpz], [r_e1], scale=-1.0)
                    spl, r_spl = tmp("g_sp", [128, 128], F32)
                    act(spl[:], e1[:], AF.Ln, [r_e1], [r_spl], bias=1.0, scale=1.0)
                    yield
                    pg_, rpg_ = b.ps()
                    shi, r_shi = tmp("g_shi", [128, 128], BF16)
                    slo, r_slo = tmp("g_slo", [128, 128], BF16)
                    cp("dve", shi[:], spl[:], [r_spl], [r_shi])
                    tt("dve", slo[:], spl[:], shi[:], SUB, [r_spl, r_shi], [r_slo])
                    mm(pg_[:, 0:128], shi[:], triMb[:], True, False, [r_shi, rc], [rpg_])
                    mm(pg_[:, 0:128], slo[:], triMb[:], False, True, [r_slo, rc], [rpg_])
                    mm(pg_[:, 128:256], revMb[:], shi[:], True, False, [r_shi, rc], [rpg_])
                    mm(pg_[:, 128:256], revMb[:], slo[:], False, True, [r_slo, rc], [rpg_])
                    eg, r_eg = tmp("g_eg", [128, 384], F32)
                    act(eg[:, 0:128], pg_[:, 0:128], AF.Exp, [rpg_], [r_eg])
                    act(eg[:, 128:256], pg_[:, 0:128], AF.Exp, [rpg_], [r_eg], scale=-1.0)
                    act(eg[:, 256:384], pg_[:, 128:256], AF.Exp, [rpg_], [r_eg])
                    yield
                    qtl, r_qtl = tmp("g_qtl", [128, 128], BF16)
                    stt("dve", qtl[:], qTf[:, cs], 32.0 ** -0.5, eg[:, 0:128], MUL, MUL, [r_qk, r_eg], [r_qtl])
                    kt4, r_kt4 = tmp("g_kt4", [128, 4, 128], BF16)
                    for h in range(4):
                        stt("dve", kt4[:, h, :], kTf[:, cs], hm[:, h:h + 1], eg[:, 128:256], MUL, MUL,
                            [r_qk, r_eg, rc], [r_kt4])
                    yield
                    p1, rp1 = tproj(O_AK, 512)
                    p2, rp2 = tproj(O_AK + 512, 128)
                    kp, r_kp = tmp("g_kp", [128, 128], BF16)
                    tt("dve", kp[:], p1[:, 0:128], eg[:, 256:384], MUL, [rp1, r_eg], [r_kp])
                    vb, r_vb = tmp("g_vb", [128, 256], BF16)
                    cp("act", vb[:], p1[:, 128:384], [rp1], [r_vb])
                    sgl, r_sgl = tmp("g_sg", [128, 256], F32)
                    act(sgl[:, 0:128], p1[:, 384:512], AF.Silu, [rp1], [r_sgl])
                    act(sgl[:, 128:256], p2[:, 0:128], AF.Silu, [rp2], [r_sgl])
                    tt("dve", sgl[:], sgl[:], glag[:], MUL, [r_sgl, rW], [r_sgl])
                    yield
                    pa_, rpa_ = b.ps()
                    for h in range(4):
                        mm(pa_[:, h * 128:(h + 1) * 128], kt4[:, h, :], qtl[:], True, True, [r_kt4, r_qtl], [rpa_])
                    asb, r_asb = tmp("g_asb", [128, 512], BF16)
                    tt("dve", asb[:], pa_[:], mask4[:], MUL, [rpa_, rc], [r_asb])
                    yield
                    po, rpo = b.ps()
                    mm(po[:, 0:256], qtl[:], Sb[:], True, True, [r_qtl, r_S], [rpo])
                    for h in range(4):
                        mm(po[:, 256 + 64 * h:320 + 64 * h], asb[:, h * 128:(h + 1) * 128], vb[:, 64 * h:64 * h + 64], True, True,
                           [r_asb, r_vb], [rpo])
                    of, r_of = tmp("g_of", [128, 256], F32)
                    cp("act", of[:], po[:, 0:256], [rpo], [r_of])
                    tt("dve", of[:], of[:], po[:, 256:512], ADD, [r_of, rpo], [r_of])
                    yield
                    pn, rpn = b.ps()
                    mm(pn[:, 0:256], kp[:], vb[:], True, True, [r_kp, r_vb], [rpn])
                    stmp, r_stmp = tmp("g_stmp", [128, 256], F32)
                    tt("dve", stmp[:], pn[:, 0:256], blkmask[:], MUL, [rpn, rc], [r_stmp])
                    stt("dve", Sf[:], Sf[:], eg[:, 127:128], stmp[:], MUL, ADD, [r_S, r_eg, r_stmp], [r_S])
                    cp("act", Sb[:], Sf[:], [r_S], [r_S])
                    yield
                    osq, r_osq = tmp("g_stmp", [128, 256], F32)
                    act(osq[:], of[:], AF.Square, [r_of], [r_osq])
                    gst, r_gst = tmp("g_st", [128, 8], F32)
                    rsum("dve", gst[:, 0:4], osq[:].rearrange("p (h e) -> p h e", h=4), [r_osq], [r_gst])
                    ts("dve", gst[:, 4:8], gst[:, 0:4], 1.0 / 64.0, EPS, MUL, ADD, [r_gst], [r_gst])
                    ts("dve", gst[:, 4:8], gst[:, 4:8], -0.5, None, POW, None, [r_gst], [r_gst])
                    for h in range(4):
                        stt("dve", ytok[:, 64 * h:64 * h + 64], of[:, 64 * h:64 * h + 64], gst[:, 4 + h:5 + h],
                            sgl[:, 64 * h:64 * h + 64], MUL, MUL, [r_of, r_gst, r_sgl], [r_ytok])

                if 'sgu' in SKIP:
                    mset('pool', ytok[:, 256:512], 0.0, [r_ytok])
                def gen_sgu():
                    pu, rpu = tproj(O_BU, 512)
                    us, r_us = tmp("s_u", [128, 512], F32)
                    cp("act", us[:], pu[:], [rpu], [r_us])
                    vsq, r_vsq = tmp("s_vsq", [128, 256], F32)
                    act(vsq[:], pu[:, 256:512], AF.Square, [rpu], [r_vsq])
                    yield
                    sst, r_sst = tmp("s_st", [128, 24], F32)
                    rsum("dve", sst[:, 0:4], us[:, 256:512].rearrange("p (h e) -> p h e", h=4), [r_us], [r_sst])
                    rsum("dve", sst[:, 4:8], vsq[:].rearrange("p (h e) -> p h e", h=4), [r_vsq], [r_sst])
                    ts("dve", sst[:, 8:12], sst[:, 0:4], 1.0 / 64.0, None, MUL, None, [r_sst], [r_sst])
                    tt("dve", sst[:, 12:16], sst[:, 8:12], sst[:, 8:12], MUL, [r_sst], [r_sst])
                    stt("dve", sst[:, 12:16], sst[:, 4:8], 1.0 / 64.0, sst[:, 12:16], MUL, SUB, [r_sst], [r_sst])
                    ts("dve", sst[:, 16:20], sst[:, 12:16], EPS, -0.5, ADD, POW, [r_sst], [r_sst])
                    stt("dve", sst[:, 20:24], sst[:, 8:12], -1.0, sst[:, 16:20], MUL, MUL, [r_sst], [r_sst])
                    vn, r_vn = tmp("s_vn", [128, 256], F32)
                    for g in range(4):
                        ts("dve", vn[:, 64 * g:64 * g + 64], us[:, 256 + 64 * g:320 + 64 * g], sst[:, 16 + g:17 + g],
                           sst[:, 20 + g:21 + g], MUL, ADD, [r_us, r_sst], [r_vn])
                    tt("dve", vn[:], vn[:], sgg[:], MUL, [r_vn, rW], [r_vn])
                    vlb, r_vlb = tmp("s_vlb", [128, 256], BF16)
                    tt("dve", vlb[:], vn[:], sgb[:], ADD, [r_vn, rW], [r_vlb])
                    pmx, rpmx = b.ps()
                    for g in range(4):
                        mm(pmx[:, 64 * g:64 * g + 64], WT[:, g, :], vlb[:, 64 * g:64 * g + 64], True, True,
                           [rW, r_vlb], [rpmx])
                    for g in range(4):
                        stt("dve", ytok[:, 256 + 64 * g:320 + 64 * g], pmx[:, 64 * g:64 * g + 64], bsT[:, g:g + 1],
                            us[:, 64 * g:64 * g + 64], ADD, MUL, [rpmx, r_us, rW], [r_ytok])

                if 'fox' in SKIP:
                    mset('pool', ytok[:, 512:768], 0.0, [r_ytok])
                def gen_fox():
                    pkv, rpkv = tproj(O_CK, 512)
                    stg, r_stg = tmp("f_stg", [128, 512], F32)
                    cp("act", stg[:], pkv[:], [rpkv], [r_stg])
                    dma("sp", o_fk[l, t * 128:(t + 1) * 128, :], stg[:, 0:256], [r_stg], [], r_stg)
                    dma("sp", o_fv[l, t * 128:(t + 1) * 128, :], stg[:, 256:512], [r_stg], [], r_stg)
                    pcf, rpcf = b.ps()
                    for kc in range(8):
                        mm(pcf[:, 0:4], xT[:, kc, cs], Wi[:, kc, O_CF:O_CF + 4], kc == 0, False, [r_xT, rW], [rpcf])
                    mm(pcf[:, 0:4], ones_bf[0:1, :], bfb[:], False, True, [rc, rW], [rpcf])
                    fst, r_fst = tmp("f_st", [128, 16], F32)
                    act(fst[:, 0:4], pcf[:, 0:4], AF.Exp, [rpcf], [r_fst], scale=-1.0)
                    act(fst[:, 4:8], fst[:, 0:4], AF.Ln, [r_fst], [r_fst], bias=1.0, scale=1.0)
                    lfo, r_lfo = tmp("f_lfo", [128, 4], F32)
                    ts("dve", lfo[:], fst[:, 4:8], -1.0, None, MUL, None, [r_fst], [r_lfo])
                    dma("sp", o_flf[l, t * 128:(t + 1) * 128, :], lfo[:], [r_lfo], [], r_lfo)
                    pd, rpd = b.ps()
                    mm(pd[:, 0:4], triO[:], fst[:, 4:8], True, False, [rc, r_fst], [rpd])
                    mm(pd[:, 0:4], ones_f[:], Pacc[:], False, True, [rc, r_Pacc], [rpd])
                    act(fst[:, 8:12], pd[:, 0:4], AF.Exp, [rpd], [r_fst])
                    tt("dve", Pacc[:], Pacc[:], fst[:, 4:8], ADD, [r_Pacc, r_fst], [r_Pacc])
                    for h in range(4):
                        ts("dve", VA[:, t, h, 0:64], stg[:, 256 + 64 * h:320 + 64 * h], fst[:, 8 + h:9 + h], None, MUL, None,
                           [r_stg, r_fst], [r_VA[t]])
                    cp("dve", VA[:, t, :, 64], fst[:, 8:12], [r_fst], [r_VA[t]])
                    pov, rpov = b.psb[7]
                    yield
                    for h in range(4):
                        yield
                        hp = slice(64 * (h % 2), 64 * (h % 2) + 64)
                        hc = h // 2
                        for j0 in range(0, t + 1, 4):
                            js = list(range(j0, min(j0 + 4, t + 1)))
                            psc, rpsc = b.ps()
                            for j in js:
                                mm(psc[:, (j - j0) * 128:(j - j0 + 1) * 128], FK[hp, hc, j * 128:(j + 1) * 128], fq[hp, hc, cs],
                                   True, True, [r_FK[j // 4], r_fq], [rpsc])
                            pT, r_pT = tmp("f_pT", [128, 512], BF16, 3)
                            n = len(js) * 128
                            act(pT[:, 0:n], psc[:, 0:n], AF.Exp, [rpsc], [r_pT], scale=0.125)
                            if js[-1] == t:
                                sl = slice((t - j0) * 128, (t - j0 + 1) * 128)
                                tt("dve", pT[:, sl], pT[:, sl], mask_bf[:], MUL, [r_pT, rc], [r_pT])
                            for j in js:
                                mm(pov[:, 65 * h:65 * h + 65], pT[:, (j - j0) * 128:(j - j0 + 1) * 128], VA[:, j, h, :],
                                   j == 0, j == t, [r_pT, r_VA[j]], [rpov])
                    rd, r_rd = tmp("f_rd", [128, 4], F32)
                    P.op("dve", lambda e, rd=rd, pov=pov: e.reciprocal(
                        out=rd[:], in_=pov[:, 0:260].rearrange("p (h e) -> p h e", h=4)[:, :, 64]), r=[rpov], w=[r_rd])
                    for h in range(4):
                        ts("dve", ytok[:, 512 + 64 * h:576 + 64 * h], pov[:, 65 * h:65 * h + 64], rd[:, h:h + 1], None, MUL, None,
                           [rpov, r_rd], [r_ytok])

                if dbg and l == 0:
                    dma("pool", d_y[t * 128:(t + 1) * 128, :], ytok[:], [r_ytok], [], r_ytok)
                    if ti == 0:
                        for c in range(2):
                            dma("pool", d_yd[c * 128:(c + 1) * 128, c0:c0 + 512], ydT[:, c, :], [r_ydT], [], r_ydT)
                gens = []
                if pending_tail is not None:
                    gens.append(pending_tail)
                if ti == 0 and 'conv' not in SKIP:
                    gens.append(gen_conv())
                if 'gla' not in SKIP:
                    gens.append(gen_gla())
                if 'sgu' not in SKIP:
                    gens.append(gen_sgu())
                if 'fox' not in SKIP:
                    gens.append(gen_fox())
                it_ = 0
                while gens:
                    for g_ in list(gens):
                        try:
                            next(g_)
                        except StopIteration:
                            gens.remove(g_)
                    if pipe and it_ < 6:
                        pipe.step()
                    it_ += 1
                while pipe and it_ < 6:
                    pipe.step()
                    it_ += 1
                if ti == 0 and has_cache and 'sample' not in SKIP:
                    pipe = PastPipe(l, blk)
                def gen_tail(t=t, cs=cs, ytok=ytok, r_ytok=r_ytok):
                    pty, rpty = b.ps()
                    ptyb = pty[:].bitcast(BF16)
                    for c in range(6):
                        tr(ptyb[:, c * 128:(c + 1) * 128], ytok[:, c * 128:(c + 1) * 128], ident_bf[:], [r_ytok, rc], [rpty])
                    yT, r_yT = tmp("yT", [128, 6, 128], BF16)
                    cp("act", yT[:], ptyb[:, 0:768].rearrange("p (k n) -> p k n", k=6), [rpty], [r_yT])
                    yield
                    ps2, rps2 = [], []
                    for hf in range(2):
                        pm_, rpm_ = b.ps()
                        for kc in range(8):
                            lh = yT[:, kc, :] if kc < 6 else ydT[:, kc - 6, cs]
                            mm(pm_[:], lh, Wo[:, kc, hf * 512:(hf + 1) * 512], kc == 0, kc == 7, [r_yT, r_ydT, rW], [rpm_])
                        ps2.append(pm_)
                        rps2.append(rpm_)
                    lg = layer_norm_gen(t, ps2, rps2, l1g, l1b, None)
                    next(lg)
                    yield
                    for _ in lg:
                        yield
                pending_tail = gen_tail()
            if blk == 3:
                for _ in pending_tail:
                    pass
                pending_tail = None
            if pipe:
                pipe.drain()
        if dbg and l == 0:
            for t in range(NT):
                dma("pool", d_x1[t * 128:(t + 1) * 128, :], R[:, t, :], [r_R[t]], [], r_R[t])
        if 'sample' not in SKIP:
            sample_mixer(l)
        for h in range(4):
            dma("sp", o_gla[l, h], Sf[32 * h:32 * h + 32, 64 * h:64 * h + 64], [r_S], [], r_S)

        for _once in ([] if 'ffn' in SKIP else [0]):
            if l == 0:
                hal = [b.sb([128, NFC, 2], F32, "hal%d_%d" % (l, i)) for i in range(2)]
                r_hal = [Res("hal0"), Res("hal1")]
                Wif = Wi[:].rearrange("p k n -> p (k n)")
                hT = Wif[:, 0:NFC * 512].rearrange("p (c n) -> p c n", c=NFC)
                r_hT = Res("hT")
                wus = [(Wif[:, 10752:18944].rearrange("p (k n) -> p k n", k=8), Res("wu0")), (Wo[:], Res("wu1"))]
                wds = [(VAflat[:, i * 1024:(i + 1) * 1024], Res("wd%d" % i)) for i in range(3)]
                gxs = [(FKf[:, i * 514:(i + 1) * 514], Res("gx%d" % i)) for i in range(2)]
                gas = [(qTf[:], Res("ga0")), (kTf[:], Res("ga1"))]
                ali = [r_hT] + [x[1] for x in wus + wds + gxs + gas]
            ALLR.extend(ali)
            cnt_f = [0, 0, 0, 0]
            P.op("pool", lambda e: e.memset(dummy[:], 0.0), w=[rW, r_dummy] + ali)
            for (tl, src) in ((l2g, ln2_g), (l2b, ln2_b)):
                dma("sp", tl[:], src[l:l + 1, :].broadcast_to([128, D]), [], [rW], rW)
            mset("pool", hal[1][:], 0.0, [r_hal[1]])
            make_xT(0)
            sfa = sample_ffn_prep(l) if 'sample' not in SKIP else None
            for blk in range(4):
                hcur, r_hcur = hal[blk % 2], r_hal[blk % 2]
                hprev, r_hprev = hal[(blk + 1) % 2], r_hal[(blk + 1) % 2]
                for fc in range(NFC):
                    wu, r_wu, og, ov = wu_chunk(l, fc, wus, cnt_f)
                    pg, rpg = b.ps()
                    for kc in range(8):
                        mm(pg[:], wu[:, kc, og:og + 128], xT[:, kc, :], kc == 0, kc == 7, [r_wu, r_xT], [rpg])
                    pv, rpv = b.ps()
                    for kc in range(8):
                        mm(pv[:], wu[:, kc, ov:ov + 128], xT[:, kc, :], kc == 0, kc == 7, [r_wu, r_xT], [rpv])
                    gx, r_gx = gxs[cnt_f[1] % 2]
                    cnt_f[1] += 1
                    cp("dve", gx[:, 0:2], hprev[:, fc, :], [r_hprev], [r_gx])
                    cp("act", gx[:, 2:514], pg[:], [rpg], [r_gx])
                    cp("dve", hcur[:, fc, :], gx[:, 512:514], [r_gx], [r_hcur])
                    ga, r_ga = gas[cnt_f[2] % 2]
                    cnt_f[2] += 1
                    act(ga[:], gx[:, 0:512], AF.Identity, [r_gx, rW], [r_ga], scale=fw[:, fc, 0:1])
                    stt("dve", ga[:], gx[:, 1:513], fw[:, fc, 1:2], ga[:], MUL, ADD, [r_gx, rW, r_ga], [r_ga])
                    stt("dve", ga[:], gx[:, 2:514], fw[:, fc, 2:3], ga[:], MUL, ADD, [r_gx, rW, r_ga], [r_ga])
                    act(ga[:], ga[:], AF.Silu, [r_ga, rW], [r_ga], bias=fb[:, fc:fc + 1], scale=1.0)
                    tt("dve", hT[:, fc, :], ga[:], pv[:], MUL, [r_ga, rpv], [r_hT])
                    if blk == 3 and sfa is not None:
                        sample_ffn_chunk(l, fc, wu, r_wu, og, ov, sfa)
                if blk == 3:
                    for c in range(NFC):
                        dma("sp", o_ffc[l][:, c * 128:(c + 1) * 128].rearrange("j p -> p j"), hcur[:, c, :], [r_hcur], [],
                            r_hcur, allow_slow_non_contiguous=True)
                if dbg and l == 0 and blk == 0:
                    for fc in range(NFC):
                        dma("pool", d_h[fc * 128:(fc + 1) * 128, :], hT[:, fc, :], [r_hT], [], r_hT)
                if blk < 3:
                    make_xT(blk + 1)
                banks = [b.psb[6], b.psb[7]] + [b.ps() for _ in range(6)]
                for fc in range(NFC):
                    wd, r_wd = wds[cnt_f[3] % 3]
                    cnt_f[3] += 1
                    dma("pool", wd[:], w_dn[l, fc * 128:(fc + 1) * 128, :], [], [r_wd], r_wd)
                    for ti in range(4):
                        for hf in range(2):
                            pb, rpb = banks[ti * 2 + hf]
                            mm(pb[:], hT[:, fc, ti * 128:(ti + 1) * 128], wd[:, hf * 512:(hf + 1) * 512], fc == 0, fc == NFC - 1,
                               [r_hT, r_wd], [rpb])
                for ti in range(4):
                    t = blk * 4 + ti
                    layer_norm(t, [banks[ti * 2][0], banks[ti * 2 + 1][0]], [banks[ti * 2][1], banks[ti * 2 + 1][1]], l2g, l2b,
                               o_y[t * 128:(t + 1) * 128, :] if last else None)
        if 'ffn' not in SKIP:
            if 'sample' not in SKIP:
                sample_ffn_down(l, last, wds, cnt_f, sfa)
            P.op("pool", lambda e: e.memset(dummy[:], 0.0), w=[rW, r_dummy] + ali)
        if dbg and l == 0 and not last:
            for t in range(NT):
                dma("pool", d_x2[t * 128:(t + 1) * 128, :], R[:, t, :], [r_R[t]], [], r_R[t])
    P.emit(stack)
    return nc, stack


IN_NAMES = ["w_in", "w_o", "gla_w_a", "gla_b_a", "gla_norm_g", "sgu_ln_g", "sgu_ln_b", "sgu_w", "fox_b_f",
            "conv_w", "conv_b", "conv_norm_g", "conv_norm_b", "ln1_g", "ln1_b", "ln2_g", "ln2_b",
            "ffn_conv_w", "ffn_conv_b"]


def run(inputs, nlayers=DEPTH, cores=8, dbg=False, has_cache=True, trace=False):
    nc, stack = build(nlayers=nlayers, dbg=dbg, has_cache=has_cache)
    with stack:
        pass
    in_maps = []
    for c in range(cores):
        m = {"xp": np.ascontiguousarray(inputs["x_prompt"][c]),
             "xs": np.ascontiguousarray(inputs["x_sample"][4 * c:4 * c + 4]).reshape(NS, D),
             "st_gla": np.ascontiguousarray(inputs["state_gla"][:, 4 * c:4 * c + 4]),
             "st_conv": np.ascontiguousarray(inputs["state_conv"][:, 4 * c:4 * c + 4]),
             "st_ffc": np.ascontiguousarray(inputs["state_ffn_conv"][:, 4 * c:4 * c + 4])}
        if has_cache:
            m["pt"] = np.ascontiguousarray(inputs["page_table"][4 * c:4 * c + 4]).astype(np.int32)
            m["ck"] = np.ascontiguousarray(inputs["cache_fox_k"]).reshape(DEPTH, NPOOL, 128, 256)
            m["cv"] = np.ascontiguousarray(inputs["cache_fox_v"]).reshape(DEPTH, NPOOL, 128, 256)
            m["clf"] = np.ascontiguousarray(inputs["cache_fox_logf"])
        for k in IN_NAMES:
            m[k] = np.ascontiguousarray(inputs[k])
        m["w_up"] = np.ascontiguousarray(inputs["ffn_w_up"])
        m["w_dn"] = np.ascontiguousarray(inputs["ffn_w_down"])
        m["sgu_b"] = np.ascontiguousarray(inputs["sgu_b"])
        in_maps.append(m)
    if trace:
        res = run_bass_kernel_spmd(nc, in_maps, core_ids=list(range(cores)), trace=True)
        print("EXEC_TIME_NS", res.exec_time_ns)
        return res.results
    res = run_bass_kernel_spmd(nc, in_maps, core_ids=list(range(cores)))
    return res.results


def kernel(**inputs):
    rs = run(inputs)
    f = np.float32
    y_p = np.stack([r["o_y"] for r in rs]).astype(f)
    p_fk = np.stack([r["o_fk"] for r in rs], axis=1).reshape(DEPTH, 8, SEQ, 4, 64).astype(f)
    p_fv = np.stack([r["o_fv"] for r in rs], axis=1).reshape(DEPTH, 8, SEQ, 4, 64).astype(f)
    p_lf = np.stack([r["o_flf"] for r in rs], axis=1).astype(f)
    p_gla = np.stack([r["o_gla"] for r in rs], axis=1).astype(f)
    p_conv = np.stack([r["o_conv"] for r in rs], axis=1).astype(f)
    p_ffc = np.stack([r["o_ffc"] for r in rs], axis=1).astype(f)
    y_s = np.concatenate([r["o_ys"] for r in rs]).reshape(32, 4, D).astype(f)

    def cat(nm, shp):
        return np.concatenate([r[nm].reshape((DEPTH, 4) + shp) for r in rs], axis=1).astype(f)
    s_fk = cat("o_sfk", (4, 4, 64))
    s_fv = cat("o_sfv", (4, 4, 64))
    s_lf = cat("o_sflf", (4, 4))
    s_gla = cat("o_sgla", (4, 32, 64))
    s_conv = cat("o_sconv", (30, 256))
    s_ffc = cat("o_sffc", (2, DFF))
    s_sgu = cat("o_ssgu", (4, 256))
    return (y_p, y_s, p_fk, p_fv, p_lf, p_gla, p_conv, p_ffc, s_fk, s_fv, s_lf, s_gla, s_conv, s_ffc, s_sgu)
```
